# Optimizing a Trainium2 kernel written in Bass

```python
import math
import jax, jax.numpy as jnp
from jax import lax
import numpy as np

D_MODEL = 1024
BATCH = 16
SEQ = 4096
DEPTH = 1
DEC_BATCH = 32
DEC_SEQ = 16
PAST_LEN = 2048

CHUNK = 64
MIX_WIDTH = D_MODEL
RET_WIDTH = MIX_WIDTH // 2
RET_HEADS = 4
RET_HEAD_DIM = RET_WIDTH // RET_HEADS
SSM_WIDTH = MIX_WIDTH - RET_WIDTH
SSM_GROUP = 16
SSM_GROUPS = SSM_WIDTH // SSM_GROUP
SSM_STATE = 64
IN_WIDTH = 4 * RET_WIDTH + 2 * SSM_WIDTH
ROPE_BASE = 10000.0
EPS = 1e-6
DT_MIN = 1e-3
DT_MAX = 1e-1
LAMBDA_RE_MAX = -1e-4

kernel_name = "retnet_s5_parallel_heads_stream_step"


def _rmsnorm(x, g):
    xf = x.astype(jnp.float32)
    return xf * lax.rsqrt(jnp.mean(xf * xf, axis=-1, keepdims=True) + EPS) * g.astype(jnp.float32)


def _rotary(x, pos):
    half = x.shape[-1] // 2
    inv = ROPE_BASE ** (-jnp.arange(half, dtype=jnp.float32) / half)
    ang = pos.astype(jnp.float32)[:, None] * inv[None, :]
    cos = jnp.cos(ang)[None, :, None, :]
    sin = jnp.sin(ang)[None, :, None, :]
    x1, x2 = x[..., :half], x[..., half:]
    return jnp.concatenate([x1 * cos - x2 * sin, x1 * sin + x2 * cos], axis=-1)


def _retention(q, k, v, s0):
    b, t, h, dh = q.shape
    blk = min(t, CHUNK)
    n = t // blk
    log_g = jnp.log1p(-jnp.power(2.0, -5.0 - jnp.arange(RET_HEADS, dtype=jnp.float32)))
    idx = jnp.arange(blk, dtype=jnp.float32)
    diff = idx[:, None] - idx[None, :]
    causal = diff >= 0
    intra_decay = jnp.where(causal[None], jnp.exp(jnp.where(causal, diff, 0.0)[None] * log_g[:, None, None]), 0.0)
    q_decay = jnp.exp((idx + 1.0)[:, None] * log_g[None, :])
    k_decay = jnp.exp((blk - 1.0 - idx)[:, None] * log_g[None, :])
    s_decay = jnp.exp(blk * log_g)

    def step(s, qkv):
        qc, kc, vc = qkv
        scores = jnp.einsum('blhd,bmhd->bhlm', qc, kc) * intra_decay[None]
        o = (jnp.einsum('bhlm,bmhe->blhe', scores, vc)
             + jnp.einsum('blhd,bhde->blhe', qc, s) * q_decay[None, :, :, None])
        s_new = s_decay[None, :, None, None] * s + jnp.einsum('blhd,blhe->bhde', kc * k_decay[None, :, :, None], vc)
        return s_new, o

    split = lambda a: a.reshape(b, n, blk, h, dh).swapaxes(0, 1)
    s_fin, o = lax.scan(step, s0, (split(q), split(k), split(v)))
    return o.swapaxes(0, 1).reshape(b, t, h, dh), s_fin


def _s5(u, h0, lambda_re, lambda_im, log_step, b_re, b_im, c_re, c_im, d_skip):
    bsz, t, _ = u.shape
    f32 = jnp.float32
    lam = lax.complex(jnp.minimum(lambda_re.astype(f32), LAMBDA_RE_MAX), lambda_im.astype(f32))
    dt = jnp.exp(log_step.astype(f32))[:, None]
    lam_bar = jnp.exp(lam * dt)
    b_mat = lax.complex(b_re.astype(f32), b_im.astype(f32))
    b_bar = ((lam_bar - 1.0) / lam)[..., None] * b_mat
    c_mat = lax.complex(c_re.astype(f32), c_im.astype(f32))
    ug = u.reshape(bsz, t, SSM_GROUPS, SSM_GROUP).astype(jnp.complex64)
    bu = jnp.einsum('gpc,btgc->btgp', b_bar, ug)
    bu = bu.at[:, 0].add(lam_bar[None] * h0)
    a = jnp.broadcast_to(lam_bar, bu.shape)

    def combine(left, right):
        a1, b1 = left
        a2, b2 = right
        return a2 * a1, a2 * b1 + b2

    _, hs = lax.associative_scan(combine, (a, bu), axis=1)
    y = jnp.einsum('gcp,btgp->btgc', c_mat, hs).real.reshape(bsz, t, SSM_WIDTH)
    y = y + d_skip.astype(f32) * u
    return y, hs[:, -1]


def _layer(x, pos, ret_s0, ssm_h0, norm_g, w_in, ret_norm_g, lambda_re, lambda_im, log_step,
           b_re, b_im, c_re, c_im, d_skip, w_glu, b_glu, w_out):
    bsz, t, _ = x.shape
    f32 = jnp.float32
    hn = _rmsnorm(x, norm_g).astype(x.dtype)
    proj = (hn @ w_in).astype(f32)
    q, k, v, g_ret, u, g_ssm = jnp.split(
        proj, np.cumsum([RET_WIDTH, RET_WIDTH, RET_WIDTH, RET_WIDTH, SSM_WIDTH]).tolist(), axis=-1)
    heads = lambda a: a.reshape(bsz, t, RET_HEADS, RET_HEAD_DIM)
    q = _rotary(heads(q), pos)
    k = _rotary(heads(k), pos) * (RET_HEAD_DIM ** -0.5)
    o, ret_s = _retention(q, k, heads(v), ret_s0.astype(f32))
    o = _rmsnorm(o, ret_norm_g).reshape(bsz, t, RET_WIDTH) * jax.nn.silu(g_ret)
    y, ssm_h = _s5(u, ssm_h0, lambda_re, lambda_im, log_step, b_re, b_im, c_re, c_im, d_skip)
    z = jax.nn.gelu(y)
    z = z * jax.nn.sigmoid(z @ w_glu.astype(f32) + b_glu.astype(f32))
    z = z * jax.nn.silu(g_ssm)
    mix = jnp.concatenate([o, z], axis=-1).astype(x.dtype)
    return x + mix @ w_out, ret_s, ssm_h


def setup_inputs(seed: int = 0) -> dict:
    key = jax.random.key(seed)
    ks = jax.random.split(key, 20)
    f32 = jnp.float32
    nrm = lambda k, shp, s: jax.random.normal(k, shp, f32) * s
    n_idx = jnp.arange(SSM_STATE, dtype=f32)
    return {
        "x_prompt": nrm(ks[0], (BATCH, SEQ, D_MODEL), 1.0),
        "x_sample": nrm(ks[1], (DEC_BATCH, DEC_SEQ, D_MODEL), 1.0),
        "state_ret": nrm(ks[2], (DEPTH, DEC_BATCH, RET_HEADS, RET_HEAD_DIM, RET_HEAD_DIM), 0.3),
        "state_ssm_re": nrm(ks[3], (DEPTH, DEC_BATCH, SSM_GROUPS, SSM_STATE), 0.1),
        "state_ssm_im": nrm(ks[4], (DEPTH, DEC_BATCH, SSM_GROUPS, SSM_STATE), 0.1),
        "norm_g": 1.0 + nrm(ks[5], (DEPTH, D_MODEL), 0.01),
        "w_in": nrm(ks[6], (DEPTH, D_MODEL, IN_WIDTH), D_MODEL ** -0.5),
        "ret_norm_g": 1.0 + nrm(ks[7], (DEPTH, RET_HEADS, RET_HEAD_DIM), 0.01),
        "ssm_lambda_re": -0.5 + nrm(ks[8], (DEPTH, SSM_GROUPS, SSM_STATE), 0.01),
        "ssm_lambda_im": math.pi * n_idx + nrm(ks[9], (DEPTH, SSM_GROUPS, SSM_STATE), 0.01),
        "ssm_log_step": jax.random.uniform(ks[10], (DEPTH, SSM_GROUPS), f32, math.log(DT_MIN), math.log(DT_MAX)),
        "ssm_b_re": nrm(ks[11], (DEPTH, SSM_GROUPS, SSM_STATE, SSM_GROUP), (2.0 * SSM_GROUP) ** -0.5),
        "ssm_b_im": nrm(ks[12], (DEPTH, SSM_GROUPS, SSM_STATE, SSM_GROUP), (2.0 * SSM_GROUP) ** -0.5),
        "ssm_c_re": nrm(ks[13], (DEPTH, SSM_GROUPS, SSM_GROUP, SSM_STATE), (2.0 * SSM_STATE) ** -0.5),
        "ssm_c_im": nrm(ks[14], (DEPTH, SSM_GROUPS, SSM_GROUP, SSM_STATE), (2.0 * SSM_STATE) ** -0.5),
        "ssm_d": nrm(ks[15], (DEPTH, SSM_WIDTH), 1.0),
        "w_glu": nrm(ks[16], (DEPTH, SSM_WIDTH, SSM_WIDTH), SSM_WIDTH ** -0.5),
        "b_glu": nrm(ks[17], (DEPTH, SSM_WIDTH), 0.01),
        "w_out": nrm(ks[18], (DEPTH, MIX_WIDTH, D_MODEL), MIX_WIDTH ** -0.5),
        "final_norm_g": 1.0 + nrm(ks[19], (D_MODEL,), 0.01),
    }


def reference(x_prompt, x_sample, state_ret, state_ssm_re, state_ssm_im, norm_g, w_in, ret_norm_g,
              ssm_lambda_re, ssm_lambda_im, ssm_log_step, ssm_b_re, ssm_b_im, ssm_c_re, ssm_c_im,
              ssm_d, w_glu, b_glu, w_out, final_norm_g):
    f32 = jnp.float32
    bp, tp, _ = x_prompt.shape
    bs, ts, _ = x_sample.shape
    pos_p = jnp.arange(tp, dtype=f32)
    pos_s = PAST_LEN + jnp.arange(ts, dtype=f32)
    xp, xs = x_prompt, x_sample
    rp, hp_re, hp_im, rs, hs_re, hs_im = [], [], [], [], [], []
    for l in range(DEPTH):
        w = (norm_g[l], w_in[l], ret_norm_g[l], ssm_lambda_re[l], ssm_lambda_im[l], ssm_log_step[l],
             ssm_b_re[l], ssm_b_im[l], ssm_c_re[l], ssm_c_im[l], ssm_d[l], w_glu[l], b_glu[l], w_out[l])
        ret0_p = jnp.zeros((bp, RET_HEADS, RET_HEAD_DIM, RET_HEAD_DIM), f32)
        h0_p = jnp.zeros((bp, SSM_GROUPS, SSM_STATE), jnp.complex64)
        xp, r_p, h_p = _layer(xp, pos_p, ret0_p, h0_p, *w)
        h0_s = lax.complex(state_ssm_re[l].astype(f32), state_ssm_im[l].astype(f32))
        xs, r_s, h_s = _layer(xs, pos_s, state_ret[l], h0_s, *w)
        rp.append(r_p); hp_re.append(h_p.real); hp_im.append(h_p.imag)
        rs.append(r_s); hs_re.append(h_s.real); hs_im.append(h_s.imag)
    y_prompt = _rmsnorm(xp, final_norm_g).astype(x_prompt.dtype)
    y_sample = _rmsnorm(xs, final_norm_g).astype(x_sample.dtype)
    return (y_prompt, y_sample, jnp.stack(rp), jnp.stack(hp_re), jnp.stack(hp_im),
            jnp.stack(rs), jnp.stack(hs_re), jnp.stack(hs_im))
```

```python
import math
from contextlib import ExitStack

import numpy as np
import concourse.bass as bass
import concourse.mybir as mybir
from concourse.bass_utils import run_bass_kernel_spmd

F32 = mybir.dt.float32
BF16 = mybir.dt.bfloat16
ALU = mybir.AluOpType
AF = mybir.ActivationFunctionType

D = 1024
SEQ = 4096
NSEQ_P = 2
NSEQ_S = 4
TS = 16
PAST = 2048
NB = 256
NT = 2
NBLK_SEQ = SEQ // NB
L = 4
NJ = NB // L
NC = NB // 64
EPS = 1e-6
HD = 128
STRICT = True
DBG = {}


class Trk:
    def __init__(self, nc, es):
        self.nc = nc
        self.es = es
        self.eng = {"pe": nc.tensor, "act": nc.scalar, "dve": nc.vector, "pool": nc.gpsimd, "sp": nc.sync}
        self.sems = {}
        self.cnt = {}
        for e in ("pe", "act", "dve", "pool"):
            self.sems[e] = es.enter_context(nc.semaphore("sem_" + e))
            self.cnt[e] = 0
        self.known = {e: {} for e in self.eng}
        self.last_w = {}
        self.readers = {}
        self.n_ops = 0
        self.n_waits = 0

    def _deps(self, reads, writes):
        deps = {}

        def add(s, v):
            if v > deps.get(s, 0):
                deps[s] = v

        for k in reads:
            lw = self.last_w.get(k)
            if lw:
                add(*lw)
        for k in writes:
            lw = self.last_w.get(k)
            if lw:
                add(*lw)
            for s, v in self.readers.get(k, {}).items():
                add(s, v)
        return deps

    def _wait(self, e, deps, own=None):
        for s, v in deps.items():
            if s == own and (own == "pe" or not STRICT):
                continue
            if self.known[e].get(s, 0) >= v:
                continue
            self.eng[e].wait_ge(self.sems[s], v)
            self.known[e][s] = v
            self.n_waits += 1

    def op(self, e, fn, reads=(), writes=()):
        deps = self._deps(reads, writes)
        self._wait(e, deps, own=e)
        ins = fn(self.eng[e])
        self.cnt[e] += 1
        n = self.cnt[e]
        ins.then_inc(self.sems[e], 1)
        for k in reads:
            self.readers.setdefault(k, {})[e] = n
        for k in writes:
            self.last_w[k] = (e, n)
            self.readers[k] = {}
        self.n_ops += 1
        return ins

    def dma(self, out, in_, sem, reads=(), writes=(), q="sp", **kw):
        if sem not in self.sems:
            self.sems[sem] = self.es.enter_context(self.nc.semaphore("d_" + sem))
            self.cnt[sem] = 0
        deps = self._deps(reads, writes)
        if sem == "ld_setup":
            deps.pop(sem, None)
        self._wait(q, deps, own=None)
        ins = self.eng[q].dma_start(out=out, in_=in_, **kw)
        self.cnt[sem] += 16
        n = self.cnt[sem]
        ins.then_inc(self.sems[sem], 16)
        for k in reads:
            self.readers.setdefault(k, {})[sem] = n
        for k in writes:
            self.last_w[k] = (sem, n)
            self.readers[k] = {}
        self.n_ops += 1
        return ins

    def fence_group(self, sem):
        tot = self.cnt[sem]
        for k, lw in list(self.last_w.items()):
            if lw[0] == sem:
                self.last_w[k] = (sem, tot)

    def barrier(self):
        for e in self.eng:
            for s_, c in self.cnt.items():
                if s_ == e or c == 0:
                    continue
                if self.known[e].get(s_, 0) < c:
                    self.eng[e].wait_ge(self.sems[s_], c)
                    self.known[e][s_] = c

    def finish(self, e="sp"):
        for s, c in self.cnt.items():
            if s in ("pe", "act", "dve", "pool"):
                continue
            if c > 0 and self.known[e].get(s, 0) < c:
                self.eng[e].wait_ge(self.sems[s], c)
                self.known[e][s] = c


def AP(t, off, dims):
    return bass.AP(t, off, [list(d) for d in dims])


def build_nc():
    nc = bass.Bass("TRN2", target_bir_lowering=False)

    def din(name, shape):
        return nc.dram_tensor(name, list(shape), F32, kind="ExternalInput").ap()

    def dout(name, shape):
        return nc.dram_tensor(name, list(shape), F32, kind="ExternalOutput").ap()

    xp = din("xp", [NSEQ_P * SEQ, D])
    xs = din("xs", [NSEQ_S * TS, D])
    sret = din("sret", [NSEQ_S * 4 * HD, HD])
    sre = din("sre", [NSEQ_S, 2048])
    sim = din("sim", [NSEQ_S, 2048])
    norm_g = din("norm_g", [1, D])
    w_in = din("w_in", [D, 3072])
    ret_norm_g = din("ret_norm_g", [4, HD])
    lre = din("lre", [32, 64])
    lim = din("lim", [32, 64])
    lstep = din("lstep", [1, 32])
    bre = din("bre", [2048, 16])
    bim = din("bim", [2048, 16])
    cre = din("cre", [512, 64])
    cim = din("cim", [512, 64])
    ssm_d = din("ssm_d", [1, 512])
    w_glu = din("w_glu", [512, 512])
    b_glu = din("b_glu", [1, 512])
    w_out = din("w_out", [D, D])
    fng = din("fng", [1, D])
    cident = din("cident", [128, 128])
    crot_p = din("crot_p", [SEQ, 192])
    crot_s = din("crot_s", [128, 192])
    cmask = din("cmask", [2 * 128, 256])
    cqdec = din("cqdec", [2, 256])
    ckdec = din("ckdec", [2 * 128, 4])
    cdup = din("cdup", [64, 256])
    cpar = din("cpar", [2, 512])
    cwmask = din("cwmask", [128, 2])
    csdec = din("csdec", [1, 8])

    yp = dout("yp", [NSEQ_P * SEQ, D])
    ys = dout("ys", [NSEQ_S * TS, D])
    rp = dout("rp", [NSEQ_P * 4 * HD, HD])
    hre_p = dout("hre_p", [NSEQ_P, 2048])
    him_p = dout("him_p", [NSEQ_P, 2048])
    rs = dout("rs", [NSEQ_S * 4 * HD, HD])
    hre_s = dout("hre_s", [NSEQ_S, 2048])
    him_s = dout("him_s", [NSEQ_S, 2048])

    with ExitStack() as es:
        es.enter_context(nc.allow_non_contiguous_dma(reason="small parameter layouts"))
        T = Trk(nc, es)

        def sb(name, shape, dt=F32):
            return es.enter_context(nc.sbuf_tensor(name, list(shape), dt))

        def tap(name, ap_, shape, keys, dt=F32):
            if not DBG.get("taps"):
                return
            d = nc.dram_tensor("tap_" + name, list(shape), dt, kind="ExternalOutput").ap()
            T.dma(d, ap_, "dbg", reads=keys)

        win_b = sb("win_b", [128, 8, 3072], BF16)
        wout_b = sb("wout_b", [128, 8, 1024], BF16)
        wglu_b = sb("wglu_b", [128, 4, 512], BF16)
        ident_f = sb("ident_f", [128, 128], F32)
        ident_b = sb("ident_b", [128, 128], BF16)
        ones_b = sb("ones_b", [128, 128], BF16)
        ng = sb("ng", [128, 8], F32)
        rng_ = sb("rng", [128, 4], F32)
        bglu = sb("bglu", [128, 4], F32)
        dvec = sb("dvec", [128, 4], F32)
        gfin = sb("gfin", [128, D], F32)
        maskT = sb("maskT", [128, 2, 256], F32)
        qdec = sb("qdec", [128, 2, 256], F32)
        kdec = sb("kdec", [128, 2, 4], F32)
        sdec_t = sb("sdec_t", [128, 2, 4], F32)
        Kbd = sb("Kbd", [128, 4, L, 128], BF16)
        Wt = sb("Wt", [128, 4, L, 2, 128], BF16)
        Et = sb("Et", [128, 16, L, 2, 32], BF16)
        cosT = sb("cosT", [128, 16, NJ], F32)
        sinT = sb("sinT", [128, 16, NJ], F32)
        rtab = sb("rtab", [128, 16], F32)
        Rt = sb("Rt", [128, 16, NJ], F32)
        h0s = sb("h0s", [128, NSEQ_S, 2, 16], F32)

        NXS = 4
        xslot = [sb(f"xslot{i}", [128, D], F32) for i in range(NXS)]

        ps = [es.enter_context(nc.psum_tensor(f"ps{i}", [128, 512], F32)) for i in range(8)]
        psb = [p.bitcast(BF16) for p in ps]
        bank_ctr = [0]

        pinned = set()

        def bank(pin=True):
            for _ in range(9):
                b = bank_ctr[0] % 8
                bank_ctr[0] += 1
                if b not in pinned:
                    break
            else:
                raise RuntimeError("out of PSUM banks")
            if pin:
                pinned.add(b)
            return b

        def rel(*bs):
            for b in bs:
                pinned.discard(b)

        def act(out, in_, func, reads, writes, **kw):
            return T.op("act", lambda e: e.activation(out=out, in_=in_, func=func, **kw), reads, writes)

        def tt(eng, out, in0, in1, op, reads, writes):
            return T.op(eng, lambda e: e.tensor_tensor(out=out, in0=in0, in1=in1, op=op), reads, writes)

        def ts(eng, out, in0, s1, s2, op0, op1, reads, writes):
            return T.op(eng, lambda e: e.tensor_scalar(out=out, in0=in0, scalar1=s1, scalar2=s2, op0=op0, op1=op1),
                        reads, writes)

        def stt(out, in0, scalar, in1, op0, op1, reads, writes):
            return T.op("dve", lambda e: e.scalar_tensor_tensor(out=out, in0=in0, scalar=scalar, in1=in1,
                                                                 op0=op0, op1=op1), reads, writes)

        def cp(eng, out, in_, reads, writes):
            if eng == "act":
                return act(out, in_, AF.Copy, reads, writes)
            return T.op(eng, lambda e: e.tensor_copy(out=out, in_=in_), reads, writes)

        def mm(out, lhsT, rhs, start, stop, reads, writes, **kw):
            return T.op("pe", lambda e: e.matmul(out, lhsT=lhsT, rhs=rhs, start=start, stop=stop, **kw), reads, writes)

        def tr(out, in_, reads, writes):
            return T.op("pe", lambda e: e.transpose(out=out, in_=in_, identity=ident_b[:]), reads, writes)

        def memset(eng, ap_, val, writes):
            return T.op(eng, lambda e: e.memset(ap_, val), (), writes)

        T.dma(ident_f[:], cident, "ld_setup", writes=["ident_f"])
        T.dma(ng[:], AP(norm_g.tensor, 0, [[1, 128], [128, 8]]), "ld_setup", writes=["ng"])
        T.dma(rng_[:], AP(ret_norm_g.tensor, 0, [[1, 128], [128, 4]]), "ld_setup", writes=["rng"])
        T.dma(bglu[:], AP(b_glu.tensor, 0, [[1, 128], [128, 4]]), "ld_setup", writes=["bglu"])
        T.dma(dvec[:], AP(ssm_d.tensor, 0, [[1, 128], [128, 4]]), "ld_setup", writes=["dvec"])
        T.dma(gfin[:], AP(fng.tensor, 0, [[0, 128], [1, D]]), "ld_setup", writes=["gfin"])
        T.dma(sdec_t[:, :, :], AP(csdec.tensor, 0, [[0, 128], [4, 2], [1, 4]]), "ld_setup", writes=["sdec_t"])
        for c in range(2):
            T.dma(maskT[:, c, :], cmask[c * 128:(c + 1) * 128, :], "ld_setup", writes=["maskT"])
            T.dma(qdec[:, c, :], AP(cqdec.tensor, c * 256, [[0, 128], [1, 256]]), "ld_setup", writes=["qdec"])
            T.dma(kdec[:, c, :], ckdec[c * 128:(c + 1) * 128, :], "ld_setup", writes=["kdec"])

        with ExitStack() as es2:
            def sb2(name, shape, dt=F32):
                return es2.enter_context(nc.sbuf_tensor(name, list(shape), dt))

            K1 = "s5"
            lre1 = sb2("lre1", [64, 32]); lim1 = sb2("lim1", [64, 32]); dt1 = sb2("dt1", [64, 32])
            T.dma(lre1[:], AP(lre.tensor, 0, [[1, 64], [64, 32]]), "ld_setup", writes=["lre1"])
            T.dma(lim1[:], AP(lim.tensor, 0, [[1, 64], [64, 32]]), "ld_setup", writes=["lim1"])
            T.dma(dt1[:], AP(lstep.tensor, 0, [[0, 64], [1, 32]]), "ld_setup", writes=["dt1"])
            b1re = sb2("b1re", [64, 32, 16]); b1im = sb2("b1im", [64, 32, 16])
            T.dma(b1re[:], AP(bre.tensor, 0, [[16, 64], [1024, 32], [1, 16]]), "ld_setup", writes=["b1re"])
            T.dma(b1im[:], AP(bim.tensor, 0, [[16, 64], [1024, 32], [1, 16]]), "ld_setup", writes=["b1im"])
            cnre = sb2("cnre", [128, 4, 64]); cnim = sb2("cnim", [128, 4, 64])
            T.dma(cnre[:], AP(cre.tensor, 0, [[64, 128], [128 * 64, 4], [1, 64]]), "ld_setup", writes=["cnre"])
            T.dma(cnim[:], AP(cim.tensor, 0, [[64, 128], [128 * 64, 4], [1, 64]]), "ld_setup", writes=["cnim"])
            dup = sb2("dup", [64, 2, 128])
            T.dma(dup[:], cdup, "ld_setup", writes=["dup"])
            parm = sb2("parm", [64, 2, 512])
            T.dma(parm[:], AP(cpar.tensor, 0, [[0, 64], [512, 2], [1, 512]]), "ld_setup", writes=["parm"])
            wmask = sb2("wmask", [128, 2])
            T.dma(wmask[:], cwmask, "ld_setup", writes=["wmask"])
            for s in range(NSEQ_S):
                T.dma(h0s[:, s, 0, :], AP(sre.tensor, s * 2048, [[1, 128], [128, 16]]), "ld_setup", writes=["h0s"])
                T.dma(h0s[:, s, 1, :], AP(sim.tensor, s * 2048, [[1, 128], [128, 16]]), "ld_setup", writes=["h0s"])

            T.fence_group("ld_setup")
            cp("act", ident_b[:], ident_f[:], ["ident_f"], ["ident_b"])
            memset("pool", ones_b[:], 1.0, ["ones_b"])
            cast_engs = ["act", "dve", "pool"]
            ci = [0]

            def load_cast(dst_ap, src_ap, ncols, wkey, scale_ap=None):
                i = ci[0] % NXS
                e = cast_engs[ci[0] % 3]
                ci[0] += 1
                T.dma(xslot[i][:, 0:ncols], src_ap, f"ld_xs{i}", writes=[f"xslot{i}"])
                if scale_ap is None:
                    cp(e, dst_ap, xslot[i][:, 0:ncols], [f"xslot{i}"], [wkey])
                elif e == "act":
                    act(dst_ap, xslot[i][:, 0:ncols], AF.Copy, [f"xslot{i}", "ng"], [wkey], scale=scale_ap)
                else:
                    ts(e, dst_ap, xslot[i][:, 0:ncols], scale_ap, None, ALU.mult, ALU.bypass, [f"xslot{i}", "ng"], [wkey])

            for kt in range(8):
                for c in range(3):
                    load_cast(win_b[:, kt, c * 1024:(c + 1) * 1024], w_in[kt * 128:(kt + 1) * 128, c * 1024:(c + 1) * 1024],
                              1024, "win_b", scale_ap=ng[:, kt:kt + 1])
            for kt in range(8):
                load_cast(wout_b[:, kt, :], w_out[kt * 128:(kt + 1) * 128, :], 1024, "wout_b")
            for kt in range(4):
                load_cast(wglu_b[:, kt, :], w_glu[kt * 128:(kt + 1) * 128, :], 512, "wglu_b")


            def t64(name):
                return sb2(name, [64, 32])

            lrc = t64("lrc"); dtv = t64("dtv"); are = t64("are"); aim = t64("aim"); mag = t64("mag")
            th = t64("th"); th2 = t64("th2"); wS = t64("wS"); wC = t64("wC")
            cc = t64("cc"); ss_ = t64("ss_"); cs = t64("cs")
            RW = [K1]
            ts("dve", lrc[:], lre1[:], -1e-4, None, ALU.min, ALU.bypass, ["lre1"] + RW, RW)
            act(dtv[:], dt1[:], AF.Exp, ["dt1"] + RW, RW)
            tt("dve", are[:], lrc[:], dtv[:], ALU.mult, RW, RW)
            tt("dve", aim[:], lim1[:], dtv[:], ALU.mult, ["lim1"] + RW, RW)
            act(mag[:], are[:], AF.Exp, RW, RW)
            ts("dve", th[:], aim[:], 1.0 / 32.0, None, ALU.mult, ALU.bypass, RW, RW)
            tt("dve", th2[:], th[:], th[:], ALU.mult, RW, RW)
            a = [-1.0 / 6, 1.0 / 120, -1.0 / 5040, 1.0 / 362880]
            ts("dve", wS[:], th2[:], a[3], None, ALU.mult, ALU.bypass, RW, RW)
            for k in (2, 1, 0):
                ts("dve", wS[:], wS[:], a[k], None, ALU.add, ALU.bypass, RW, RW)
                tt("dve", wS[:], wS[:], th2[:], ALU.mult, RW, RW)
            ts("dve", wS[:], wS[:], 1.0, None, ALU.add, ALU.bypass, RW, RW)
            tt("dve", wS[:], wS[:], th[:], ALU.mult, RW, RW)
            b = [-0.5, 1.0 / 24, -1.0 / 720, 1.0 / 40320, -1.0 / 3628800]
            ts("dve", wC[:], th2[:], b[4], None, ALU.mult, ALU.bypass, RW, RW)
            for k in (3, 2, 1, 0):
                ts("dve", wC[:], wC[:], b[k], None, ALU.add, ALU.bypass, RW, RW)
                tt("dve", wC[:], wC[:], th2[:], ALU.mult, RW, RW)
            ts("dve", wC[:], wC[:], 1.0, None, ALU.add, ALU.bypass, RW, RW)
            for _ in range(5):
                tt("dve", cc[:], wC[:], wC[:], ALU.mult, RW, RW)
                tt("dve", ss_[:], wS[:], wS[:], ALU.mult, RW, RW)
                tt("dve", cs[:], wC[:], wS[:], ALU.mult, RW, RW)
                tt("dve", wC[:], cc[:], ss_[:], ALU.subtract, RW, RW)
                ts("dve", wS[:], cs[:], 2.0, None, ALU.mult, ALU.bypass, RW, RW)
            pwr = sb2("pwr", [64, L + 1, 32]); pwi = sb2("pwi", [64, L + 1, 32])
            memset("dve", pwr[:, 0, :], 1.0, RW)
            memset("dve", pwi[:, 0, :], 0.0, RW)
            tt("dve", pwr[:, 1, :], mag[:], wC[:], ALU.mult, RW, RW)
            tt("dve", pwi[:, 1, :], mag[:], wS[:], ALU.mult, RW, RW)
            tA = t64("tA"); tB = t64("tB")

            def cmul(o_re, o_im, a_re, a_im, b_re, b_im, t1_, t2_, xr=()):
                R_ = RW + list(xr)
                tt("dve", t1_, a_re, b_re, ALU.mult, R_, RW)
                tt("dve", t2_, a_im, b_im, ALU.mult, R_, RW)
                tt("dve", o_re, t1_, t2_, ALU.subtract, R_, RW)
                tt("dve", t1_, a_re, b_im, ALU.mult, R_, RW)
                tt("dve", t2_, a_im, b_re, ALU.mult, R_, RW)
                tt("dve", o_im, t1_, t2_, ALU.add, R_, RW)

            for k in range(1, L):
                cmul(pwr[:, k + 1, :], pwi[:, k + 1, :], pwr[:, k, :], pwi[:, k, :], pwr[:, 1, :], pwi[:, 1, :],
                     tA[:], tB[:])
            nre = t64("nre"); den = t64("den"); qre = t64("qre"); qim = t64("qim")
            ts("dve", nre[:], pwr[:, 1, :], -1.0, None, ALU.add, ALU.bypass, RW, RW)
            tt("dve", den[:], lrc[:], lrc[:], ALU.mult, RW, RW)
            tt("dve", tA[:], lim1[:], lim1[:], ALU.mult, RW, RW)
            tt("dve", den[:], den[:], tA[:], ALU.add, RW, RW)
            T.op("dve", lambda e: e.reciprocal(out=den[:], in_=den[:]), RW, RW)
            tt("dve", tA[:], nre[:], lrc[:], ALU.mult, RW, RW)
            tt("dve", tB[:], pwi[:, 1, :], lim1[:], ALU.mult, RW, RW)
            tt("dve", qre[:], tA[:], tB[:], ALU.add, RW, RW)
            tt("dve", qre[:], qre[:], den[:], ALU.mult, RW, RW)
            tt("dve", tA[:], pwi[:, 1, :], lrc[:], ALU.mult, RW, RW)
            tt("dve", tB[:], nre[:], lim1[:], ALU.mult, RW, RW)
            tt("dve", qim[:], tA[:], tB[:], ALU.subtract, RW, RW)
            tt("dve", qim[:], qim[:], den[:], ALU.mult, RW, RW)

            tap("lre1", lre1[:], [64, 32], ["lre1"]); tap("dt1", dt1[:], [64, 32], ["dt1"])
            tap("mag", mag[:], [64, 32], RW); tap("wC", wC[:], [64, 32], RW); tap("wS", wS[:], [64, 32], RW)
            tap("pwr", pwr[:, :, :].rearrange("p a b -> p (a b)"), [64, (L + 1) * 32], RW)
            tap("pwi", pwi[:, :, :].rearrange("p a b -> p (a b)"), [64, (L + 1) * 32], RW)
            tap("qre", qre[:], [64, 32], RW); tap("qim", qim[:], [64, 32], RW)

            def bc16(t, k=None):
                if k is None:
                    return AP(t, 0, [[32, 64], [1, 32], [0, 16]])
                return AP(t, k * 32, [[(L + 1) * 32, 64], [1, 32], [0, 16]])

            bbr = sb2("bbr", [64, 32, 16]); bbi = sb2("bbi", [64, 32, 16])
            u1 = sb2("u1", [64, 32, 16]); u2 = sb2("u2", [64, 32, 16])
            cmul(bbr[:], bbi[:], bc16(qre), bc16(qim), b1re[:], b1im[:], u1[:], u2[:], xr=["b1re", "b1im"])

            CTr = sb2("CTr", [64, 512]); CTi = sb2("CTi", [64, 512]); nCTi = sb2("nCTi", [64, 512])
            for (src, dst, key) in ((cnre, CTr, "cnre"), (cnim, CTi, "cnim")):
                bk = bank(False)
                for t in range(4):
                    mm(ps[bk][0:64, t * 128:(t + 1) * 128], src[:, t, :], ident_f[:, :], True, True,
                       [key, "ident_f"], [("ps", bk)])
                cp("dve", dst[:], ps[bk][0:64, :], [("ps", bk)], RW)
            ts("dve", nCTi[:], CTi[:], -1.0, None, ALU.mult, ALU.bypass, RW, RW)

            tap("bbr", bbr[:, :, :].rearrange("p a b -> p (a b)"), [64, 512], RW)
            tap("CTr", CTr[:], [64, 512], RW); tap("nCTi", nCTi[:], [64, 512], RW)
            Vre = sb2("Vre", [64, 512]); Vim = sb2("Vim", [64, 512])
            Vpr = sb2("Vpr", [64, 8, 128]); Vpi = sb2("Vpi", [64, 8, 128])
            memset("pool", Vpr[:], 0.0, ["Vp"])
            memset("pool", Vpi[:], 0.0, ["Vp"])
            V3r = Vre[:, :].rearrange("p (g c) -> p g c", c=16)
            V3i = Vim[:, :].rearrange("p (g c) -> p g c", c=16)
            dgr = AP(Vpr, 0, [[8 * 128, 64], [144, 8], [1, 16]])
            dgi = AP(Vpi, 0, [[8 * 128, 64], [144, 8], [1, 16]])
            for k in range(L):
                cmul(V3r, V3i, bc16(pwr, k), bc16(pwi, k), bbr[:], bbi[:], u1[:], u2[:])
                for t in range(4):
                    T.op("dve", lambda e: e.tensor_copy(
                        out=dgr, in_=Vre[:, t * 128:(t + 1) * 128].rearrange("p (g c) -> p g c", c=16)), RW, ["Vp"])
                    T.op("dve", lambda e: e.tensor_copy(
                        out=dgi, in_=Vim[:, t * 128:(t + 1) * 128].rearrange("p (g c) -> p g c", c=16)), RW, ["Vp"])
                    bk = bank(False)
                    for gl in range(8):
                        g = 8 * t + gl
                        mm(ps[bk][:, 16 * gl:16 * gl + 16], Vpr[:, gl, :], CTr[:, g * 16:(g + 1) * 16], True, False,
                           ["Vp"] + RW, [("ps", bk)])
                        mm(ps[bk][:, 16 * gl:16 * gl + 16], Vpi[:, gl, :], nCTi[:, g * 16:(g + 1) * 16], False, True,
                           ["Vp"] + RW, [("ps", bk)])
                    if k == 0:
                        stt(Kbd[:, t, k, :], ident_f[:, :], dvec[:, t:t + 1], ps[bk][:, 0:128], ALU.mult, ALU.add,
                            [("ps", bk), "ident_f", "dvec"], ["Kbd"])
                    else:
                        cp("dve", Kbd[:, t, k, :], ps[bk][:, 0:128], [("ps", bk)], ["Kbd"])
                s = L - 1 - k
                for ri, Vx in ((0, Vre), (1, Vim)):
                    bk = bank(False)
                    for t in range(4):
                        mm(ps[bk][:, t * 64:(t + 1) * 64], Vx[:, t * 128:(t + 1) * 128], ident_f[0:64, 0:64], True, True,
                           RW + ["ident_f"], [("ps", bk)])
                    for t in range(4):
                        tt("dve", Wt[:, t, s, ri, :].rearrange("p (m q) -> p m q", m=2),
                           AP(ps[bk], t * 64, [[512, 128], [0, 2], [1, 64]]),
                           AP(wmask, 0, [[2, 128], [1, 2], [0, 64]]), ALU.mult,
                           [("ps", bk), "wmask"], ["Wt"])
            EVr = sb2("EVr", [64, 512]); EVi = sb2("EVi", [64, 512])
            EVm = [sb2(f"EVm{m}", [64, 512]) for m in range(2)]
            E3r = EVr[:, :].rearrange("p (g c) -> p g c", c=16)
            E3i = EVi[:, :].rearrange("p (g c) -> p g c", c=16)
            CT3r = CTr[:, :].rearrange("p (g c) -> p g c", c=16)
            CT3i = CTi[:, :].rearrange("p (g c) -> p g c", c=16)
            for i in range(L):
                cmul(E3r, E3i, bc16(pwr, i + 1), bc16(pwi, i + 1), CT3r, CT3i, u1[:], u2[:])
                for ri, EV in ((0, EVr), (1, EVi)):
                    for m in range(2):
                        tt("dve", EVm[m][:], EV[:], parm[:, m, :], ALU.mult, RW + ["parm"], ["EVm"])
                    bk = bank(False)
                    mm(ps[bk][:, :], dup[:, 0, :], EVm[0][:], True, False, ["dup", "EVm"], [("ps", bk)])
                    mm(ps[bk][:, :], dup[:, 1, :], EVm[1][:], False, True, ["dup", "EVm"], [("ps", bk)])
                    T.op("act", lambda e, bk=bk, ri=ri, i=i: e.activation(
                        out=Et[:, :, i, ri, :], in_=ps[bk][:, :].rearrange("p (a b) -> p a b", b=32),
                        func=AF.Copy, scale=(1.0 if ri == 0 else -1.0)), [("ps", bk)], ["Et"])
            r1 = t64("r1"); uc = t64("uc"); us = t64("us")
            tt("dve", r1[:], mag[:], mag[:], ALU.mult, RW, RW)
            tt("dve", r1[:], r1[:], r1[:], ALU.mult, RW, RW)
            cp("dve", uc[:], wC[:], RW, RW)
            cp("dve", us[:], wS[:], RW, RW)
            for _ in range(2):
                tt("dve", cc[:], uc[:], uc[:], ALU.mult, RW, RW)
                tt("dve", ss_[:], us[:], us[:], ALU.mult, RW, RW)
                tt("dve", cs[:], uc[:], us[:], ALU.mult, RW, RW)
                tt("dve", uc[:], cc[:], ss_[:], ALU.subtract, RW, RW)
                ts("dve", us[:], cs[:], 2.0, None, ALU.mult, ALU.bypass, RW, RW)
            bk = bank(False)
            for i3, src in enumerate((r1, uc, us)):
                for m in range(2):
                    mm(ps[bk][:, i3 * 16:(i3 + 1) * 16], dup[:, m, :], AP(src, m, [[32, 64], [2, 16]]), m == 0, m == 1,
                       RW + ["dup"], [("ps", bk)])
            pwa = sb2("pwa", [128, 16]); pwb = sb2("pwb", [128, 16])
            x1 = sb2("x1", [128, 16, 32]); x2 = sb2("x2", [128, 16, 32])
            y1 = sb2("y1", [128, 16]); y2 = sb2("y2", [128, 16]); y3 = sb2("y3", [128, 16])
            AK = ["Atab"]
            cp("dve", rtab[:, :], ps[bk][:, 0:16], [("ps", bk)], AK)
            cp("dve", pwa[:, :], ps[bk][:, 16:32], [("ps", bk)], AK)
            cp("dve", pwb[:, :], ps[bk][:, 32:48], [("ps", bk)], AK)
            cp("dve", cosT[:, :, 0], pwa[:, :], AK, AK)
            cp("dve", sinT[:, :, 0], pwb[:, :], AK, AK)
            k = 1
            while k < NJ:
                pa = AP(pwa, 0, [[16, 128], [1, 16], [0, k]])
                pb = AP(pwb, 0, [[16, 128], [1, 16], [0, k]])
                tt("dve", x1[:, :, 0:k], cosT[:, :, 0:k], pa, ALU.mult, AK, AK)
                tt("dve", x2[:, :, 0:k], sinT[:, :, 0:k], pb, ALU.mult, AK, AK)
                tt("dve", cosT[:, :, k:2 * k], x1[:, :, 0:k], x2[:, :, 0:k], ALU.subtract, AK, AK)
                tt("dve", x1[:, :, 0:k], cosT[:, :, 0:k], pb, ALU.mult, AK, AK)
                tt("dve", x2[:, :, 0:k], sinT[:, :, 0:k], pa, ALU.mult, AK, AK)
                tt("dve", sinT[:, :, k:2 * k], x1[:, :, 0:k], x2[:, :, 0:k], ALU.add, AK, AK)
                tt("dve", y1[:], pwa[:], pwa[:], ALU.mult, AK, AK)
                tt("dve", y2[:], pwb[:], pwb[:], ALU.mult, AK, AK)
                tt("dve", y3[:], pwa[:], pwb[:], ALU.mult, AK, AK)
                tt("dve", pwa[:], y1[:], y2[:], ALU.subtract, AK, AK)
                ts("dve", pwb[:], y3[:], 2.0, None, ALU.mult, ALU.bypass, AK, AK)
                k *= 2
            memset("dve", Rt[:, :, :], 0.0, AK)
            cp("dve", Rt[:, :, 1:NJ], AP(rtab, 0, [[16, 128], [1, 16], [0, NJ - 1]]), AK, AK)
            for hh in range(2):
                sl = slice(hh * 32, (hh + 1) * 32)
                tt("dve", x1[:, :, :], cosT[:, :, sl], cosT[:, :, sl], ALU.mult, AK, AK)
                tt("dve", x2[:, :, :], sinT[:, :, sl], sinT[:, :, sl], ALU.mult, AK, AK)
                tt("dve", x1[:, :, :], x1[:, :, :], x2[:, :, :], ALU.add, AK, AK)
                ts("dve", x1[:, :, :], x1[:, :, :], -0.5, 1.5, ALU.mult, ALU.add, AK, AK)
                tt("dve", cosT[:, :, sl], cosT[:, :, sl], x1[:, :, :], ALU.mult, AK, AK)
                tt("dve", sinT[:, :, sl], sinT[:, :, sl], x1[:, :, :], ALU.mult, AK, AK)

        T.barrier()
        pinned.clear()
        if DBG.get("dump"):
            d_kbd = nc.dram_tensor("d_kbd", [128, 4 * L * 128], BF16, kind="ExternalOutput").ap()
            d_wt = nc.dram_tensor("d_wt", [128, 4 * L * 2 * 128], BF16, kind="ExternalOutput").ap()
            d_et = nc.dram_tensor("d_et", [128, 16 * L * 2 * 32], BF16, kind="ExternalOutput").ap()
            d_a = nc.dram_tensor("d_a", [128, 16 + 2 * 16 * NJ], F32, kind="ExternalOutput").ap()
            T.dma(d_kbd, Kbd[:, :, :, :].rearrange("p a b c -> p (a b c)"), "dbg", reads=["Kbd"])
            T.dma(d_wt, Wt[:, :, :, :, :].rearrange("p a b c d -> p (a b c d)"), "dbg", reads=["Wt"])
            T.dma(d_et, Et[:, :, :, :, :].rearrange("p a b c d -> p (a b c d)"), "dbg", reads=["Et"])
            T.dma(d_a[:, 0:16], rtab[:, :], "dbg", reads=["Atab"])
            T.dma(d_a[:, 16:16 + 16 * NJ], cosT[:, :, :].rearrange("p a b -> p (a b)"), "dbg", reads=["Atab"])
            T.dma(d_a[:, 16 + 16 * NJ:], sinT[:, :, :].rearrange("p a b -> p (a b)"), "dbg", reads=["Atab"])
        if DBG.get("setup_only"):
            T.finish("sp")
            build_nc.stats = (T.n_ops, T.n_waits, len(T.sems))
            return nc
        hnb = [sb(f"hnb{i}", [128, D], BF16) for i in range(1)]
        hnT = sb("hnT", [128, 8, NB], BF16)
        mixT = sb("mixT", [128, 8, NB], BF16)
        rot = [sb(f"rot{i}", [128, NT, 192], F32) for i in range(2)]
        rt1 = sb("rt1", [128, 512], F32)
        rt2 = sb("rt2", [128, 512], F32)
        qrot2 = [sb(f"qrot{i}", [128, NT, 512], BF16) for i in range(2)]
        krot2 = [sb(f"krot{i}", [128, NT, 512], BF16) for i in range(2)]
        ktd2 = [sb(f"ktd{i}", [128, NT, 512], BF16) for i in range(2)]
        vtok2 = [sb(f"vtok{i}", [128, NT, 512], BF16) for i in range(2)]
        qT = sb("qT", [128, 4, NB], BF16)
        kT = sb("kT", [128, 4, NB], BF16)
        qdT = sb("qdT", [128, 4, NB], BF16)
        sgret2 = [sb(f"sgret{i}", [128, 4, NB], BF16) for i in range(2)]
        uT2 = [sb(f"uT{i}", [128, 4, NB], BF16) for i in range(2)]
        sgssm2 = [sb(f"sgssm{i}", [128, 4, NB], BF16) for i in range(2)]
        c_sb = sb("c_sb", [128, 2, 16, NJ], F32)
        Hprev2 = [sb("Hprev0", [128, 2, 16, NJ], BF16)] * 2
        carry = sb("carry", [128, 2, 16], F32)
        ta = sb("ta", [128, 2, 4, NJ], F32)
        tb = sb("tb", [128, 2, 4, NJ], F32)
        hfin = sb("hfin", [128, NSEQ_S, 2, 16], F32)
        st3 = sb("st3", [128, 2, 16], F32)
        sTb = [sb(f"sTb{i}", [128, 256], BF16) for i in range(2)]
        Smaster = sb("Smaster", [128, 4, HD], F32)
        Sprev = [sb(f"Sprev{i}", [128, 4, HD], BF16) for i in range(2)]
        osq = sb("osq", [128, 4 * NB], BF16)
        rsd = sb("rsd", [128, 512], F32)
        ot = sb("ot", [128, 512], F32)
        zT = sb("zT", [128, 4, NB], BF16)
        g1 = [rsd[:, 0:NB]] * 2
        g2 = [ot[:, 0:NB]] * 2
        ssq = sb("ssq", [128, 8], F32)
        rstd = sb("rstd", [128, 8], F32)


        memset("pool", sTb[0][:], 0.0, ["sT0"])
        memset("pool", sTb[1][:], 0.0, ["sT1"])
        xs_ctr = [0]
        LQ = DBG.get("lq", "act")

        def next_xslot():
            i = xs_ctr[0] % NXS
            xs_ctr[0] += 1
            return i

        chunk_ctr = [0]
        blk_ctr = [0]

        def interleave2(g1, g2, n1, n2):
            c1 = c2 = 0
            d1 = d2 = False
            while not (d1 and d2):
                if d2 or (not d1 and c1 / n1 <= c2 / n2):
                    try:
                        next(g1); c1 += 1
                    except StopIteration:
                        d1 = True
                else:
                    try:
                        next(g2); c2 += 1
                    except StopIteration:
                        d2 = True
                yield

        def issue_x_load(cfg, t_, xi):
            XK = f"xslot{xi}"
            if cfg["sample"]:
                memset("pool", xslot[xi][:], 0.0, [XK])
                for half in range(2):
                    s_ = 2 * t_ + half
                    T.dma(xslot[xi][64 * half:64 * half + TS, :], xs[s_ * TS:(s_ + 1) * TS, :], f"ld_xs{xi}", writes=[XK])
            else:
                r0 = cfg["seq"] * SEQ + cfg["blk"] * NB
                T.dma(xslot[xi][:], xp[r0 + t_ * 128: r0 + (t_ + 1) * 128, :], f"ld_xs{xi}", writes=[XK])

        def issue_rot_load(cfg, rslot):
            RK = f"rot{rslot}"
            cfg["rslot"] = rslot
            if cfg["sample"]:
                for t_ in range(NT):
                    T.dma(rot[rslot][:, t_, :], crot_s, f"ld_rot{rslot}", writes=[RK])
            else:
                pos0 = cfg["blk"] * NB
                T.dma(rot[rslot][:, :, :], AP(crot_p.tensor, pos0 * 192, [[192, 128], [128 * 192, NT], [1, 192]]),
                      f"ld_rot{rslot}", writes=[RK])

        def _hdr(cfg, pb):
            sample = cfg["sample"]
            cf = 1 if sample else 0
            seq = cfg.get("seq", 0)
            blk = cfg.get("blk", 0)
            row0 = seq * SEQ + blk * NB
            gam = [1.0 - 2.0 ** (-5.0 - h) for h in range(4)]
            sdec = [g ** (16 if sample else 64) for g in gam]

            qrot, krot, ktd, vtok = qrot2[pb], krot2[pb], ktd2[pb], vtok2[pb]
            sgret, uT, sgssm, Hprev = sgret2[pb], uT2[pb], sgssm2[pb], Hprev2[pb]
            return locals()

        def phaseA(cfg, pb):
            L_ = _hdr(cfg, pb)
            sample, cf, seq, blk, row0 = L_['sample'], L_['cf'], L_['seq'], L_['blk'], L_['row0']
            qrot, krot, ktd, vtok = L_['qrot'], L_['krot'], L_['ktd'], L_['vtok']
            sgret, uT, sgssm, Hprev = L_['sgret'], L_['uT'], L_['sgssm'], L_['Hprev']
            rslot = cfg["rslot"]
            RK = f"rot{rslot}"

            for t_ in range(NT):
                xi = cfg["xslots"][t_]
                XK = f"xslot{xi}"
                hb = hnb[0]
                HK = "hnb0"
                act(hb[:], xslot[xi][:], AF.Square, [XK], [HK, "ssqA"], accum_out=ssq[:, t_:t_ + 1])
                act(ssq[:, t_:t_ + 1], ssq[:, t_:t_ + 1], AF.Ln, ["ssqA"], ["ssqA"], scale=1.0 / D, bias=EPS)
                act(rstd[:, t_:t_ + 1], ssq[:, t_:t_ + 1], AF.Exp, ["ssqA"], ["rstdA"], scale=-0.5)
                act(hb[:], xslot[xi][:], AF.Copy, [XK, "rstdA"], [HK], scale=rstd[:, t_:t_ + 1])
                bk = bank()
                for kt in range(8):
                    tr(psb[bk][:, kt * 128:(kt + 1) * 128], hb[:, kt * 128:(kt + 1) * 128], [HK, "ident_b"], [("ps", bk)])
                cp("act", hnT[:, :, t_ * 128:(t_ + 1) * 128], psb[bk][:, :].rearrange("p (k t) -> p k t", k=8),
                   [("ps", bk)], ["hnT"])
                rel(bk)
                yield

            if DBG.get('stop', 99) <= 1:
                return
            for t_ in range(NT):
                cosb = AP(rot[rslot], t_ * 192, [[NT * 192, 128], [0, 4], [0, 2], [1, 64]])
                sinb = AP(rot[rslot], t_ * 192 + 64, [[NT * 192, 128], [0, 4], [64, 2], [1, 64]])
                for (c0, dst, dkey) in ((0, qrot, f"qrot{pb}"), (512, krot, f"krot{pb}"), (1024, None, None)):
                    bb = bank()
                    for kt in range(8):
                        mm(ps[bb][:, :], hnT[:, kt, t_ * 128:(t_ + 1) * 128], win_b[:, kt, c0:c0 + 512], kt == 0, kt == 7,
                           ["hnT", "win_b"], [("ps", bb)])
                    yield
                    if dst is None:
                        cp("act", vtok[:, t_, :], ps[bb][:, :], [("ps", bb)], [f"vtok{pb}"])
                        rel(bb)
                        yield
                        continue
                    pv = AP(ps[bb], 0, [[512, 128], [128, 4], [64, 2], [1, 64]])
                    psw = AP(ps[bb], 64, [[512, 128], [128, 4], [-64, 2], [1, 64]])
                    tt("dve", rt1[:, :].rearrange("p (h a d) -> p h a d", h=4, a=2), pv, cosb, ALU.mult,
                       [("ps", bb), RK], ["rt1"])
                    tt("dve", rt2[:, :].rearrange("p (h a d) -> p h a d", h=4, a=2), psw, sinb, ALU.mult,
                       [("ps", bb), RK], ["rt2"])
                    rel(bb)
                    tt("dve", dst[:, t_, :], rt1[:, :], rt2[:, :], ALU.add, ["rt1", "rt2"], [dkey])
                    yield
                for h in range(4):
                    act(ktd[:, t_, h * 128:(h + 1) * 128], krot[:, t_, h * 128:(h + 1) * 128], AF.Copy,
                        [f"krot{pb}", "kdec"], [f"ktd{pb}"], scale=kdec[:, cf, h:h + 1])
                yield

            if DBG.get('stop', 99) <= 2:
                return
            for m2 in range(6):
                bk = bank()
                for half in range(2):
                    m = 2 * m2 + half
                    c0 = 1536 + m * 128
                    for kt in range(8):
                        mm(ps[bk][:, half * NB:(half + 1) * NB], win_b[:, kt, c0:c0 + 128], hnT[:, kt, :], kt == 0, kt == 7,
                           ["hnT", "win_b"], [("ps", bk)])
                    yield
                for half in range(2):
                    m = 2 * m2 + half
                    src = ps[bk][:, half * NB:(half + 1) * NB]
                    if m < 4:
                        act(sgret[:, m, :], src, AF.Silu, [("ps", bk)], [f"sgret{pb}"])
                    elif m < 8:
                        cp("act", uT[:, m - 4, :], src, [("ps", bk)], [f"uT{pb}"])
                    else:
                        act(sgssm[:, m - 8, :], src, AF.Silu, [("ps", bk)], [f"sgssm{pb}"])
                rel(bk)
                yield

            nxt2 = cfg.get("next2")
            if nxt2 is not None:
                issue_rot_load(nxt2, rslot)
            yield "SPLIT"

        def s5_front(cfg, pb):
            L_ = _hdr(cfg, pb)
            sample, blk = L_['sample'], L_['blk']
            uT = L_['uT']
            cfg["front_done"] = True
            c5 = c_sb[:, :, :, :].rearrange("p r (t q) j -> p r t q j", q=4)
            for q in range(4):
                bk = bank()
                for ri in range(2):
                    for t in range(4):
                        gi = ri * 4 + t
                        for s in range(L):
                            mm(ps[bk][:, gi * NJ:(gi + 1) * NJ], Wt[32 * q:32 * q + 32, t, s, ri, :],
                               uT[32 * q:32 * q + 32, t, :].rearrange("p (j s) -> p j s", s=L)[:, :, s],
                               s == 0, s == L - 1, [f"uT{pb}", "Wt"], [("ps", bk)], tile_position=(32 * q, 0))
                cp("dve", c5[:, :, :, q, :], ps[bk][:, :].rearrange("p (r t j) -> p r t j", r=2, t=4),
                   [("ps", bk)], ["c_sb"])
                rel(bk)
                yield
            if sample:
                return
            CS = 2 * 16 * NJ
            n = NJ
            if blk == 0:
                memset("pool", carry[:], 0.0, ["carry"])
            for ph in range(4):
                Ps = slice(4 * ph, 4 * ph + 4)
                cv = c_sb[:, :, Ps, :]
                cvsw = AP(c_sb, 16 * NJ + 4 * ph * NJ, [[CS, 128], [-16 * NJ, 2], [NJ, 4], [1, n]])
                cosb = AP(cosT, 4 * ph * NJ, [[16 * NJ, 128], [0, 2], [NJ, 4], [1, n]])
                sinb = AP(sinT, 4 * ph * NJ, [[16 * NJ, 128], [0, 2], [NJ, 4], [1, n]])
                tt("dve", ta[:, :, :, :], cv, cosb, ALU.mult, ["c_sb", "Atab"], ["ta"])
                tt("dve", tb[:, :, :, :], cvsw, sinb, ALU.mult, ["c_sb", "Atab"], ["tb"])
                tt("dve", c_sb[:, 0, Ps, :], ta[:, 0, :, :], tb[:, 0, :, :], ALU.add, ["ta", "tb"], ["c_sb"])
                tt("dve", c_sb[:, 1, Ps, :], ta[:, 1, :, :], tb[:, 1, :, :], ALU.subtract, ["ta", "tb"], ["c_sb"])
                yield
            tt("dve", st3[:, :, :], carry[:, :, :], AP(rtab, 0, [[16, 128], [0, 2], [1, 16]]), ALU.mult,
               ["carry", "Atab"], ["st3"])
            tt("dve", c_sb[:, :, :, 0], c_sb[:, :, :, 0], st3[:, :, :], ALU.add, ["c_sb", "st3"], ["c_sb"])
            yield

        def phaseB(cfg, pb):
            L_ = _hdr(cfg, pb)
            sample, cf, seq, blk, row0 = L_['sample'], L_['cf'], L_['seq'], L_['blk'], L_['row0']
            qrot, krot, ktd, vtok = L_['qrot'], L_['krot'], L_['ktd'], L_['vtok']
            sgret, uT, sgssm, Hprev = L_['sgret'], L_['uT'], L_['sgssm'], L_['Hprev']
            def gen_scan():
                if DBG.get('stop', 99) <= 3:
                    return
                if not cfg.get("front_done"):
                    yield from s5_front(cfg, pb)
                if DBG.get('stop', 99) <= 4:
                    return
                CS = 2 * 16 * NJ

                def seg_scan(j0, n, init, ikey, fin_j, fin_out, fkey):
                    for ph in range(4):
                        Ps = slice(4 * ph, 4 * ph + 4)
                        cv = c_sb[:, :, Ps, j0:j0 + n]
                        cvsw = AP(c_sb, 16 * NJ + 4 * ph * NJ + j0, [[CS, 128], [-16 * NJ, 2], [NJ, 4], [1, n]])
                        cosb = AP(cosT, 4 * ph * NJ, [[16 * NJ, 128], [0, 2], [NJ, 4], [1, n]])
                        sinb = AP(sinT, 4 * ph * NJ, [[16 * NJ, 128], [0, 2], [NJ, 4], [1, n]])
                        tav = ta[:, :, :, 0:n]
                        tbv = tb[:, :, :, 0:n]
                        tt("dve", tav, cv, cosb, ALU.mult, ["c_sb", "Atab"], ["ta"])
                        tt("dve", tbv, cvsw, sinb, ALU.mult, ["c_sb", "Atab"], ["tb"])
                        tt("dve", c_sb[:, 0, Ps, j0:j0 + n], ta[:, 0, :, 0:n], tb[:, 0, :, 0:n], ALU.add, ["ta", "tb"], ["c_sb"])
                        tt("dve", c_sb[:, 1, Ps, j0:j0 + n], ta[:, 1, :, 0:n], tb[:, 1, :, 0:n], ALU.subtract, ["ta", "tb"], ["c_sb"])
                        yield
                        for ri in range(2):
                            for P in range(4 * ph, 4 * ph + 4):
                                row = c_sb[:, ri, P, j0:j0 + n]
                                T.op("dve", lambda e, row=row, P=P, ri=ri: e.tensor_tensor_scan(
                                    out=row, data0=AP(rtab, P, [[16, 128], [0, n]]), data1=row,
                                    initial=init[:, ri, P:P + 1], op0=ALU.mult, op1=ALU.add),
                                    ["c_sb", "Atab", ikey], [("c_row", ri, P)])
                            yield
                        rows = [("c_row", ri, P) for ri in range(2) for P in range(4 * ph, 4 * ph + 4)]
                        tt("dve", tav, cv, cosb, ALU.mult, ["c_sb", "Atab"] + rows, ["ta", "c_sb"])
                        tt("dve", tbv, cvsw, sinb, ALU.mult, ["c_sb", "Atab"] + rows, ["tb", "c_sb"])
                        if n > 1:
                            tt("dve", Hprev[:, 0, Ps, j0 + 1:j0 + n], ta[:, 0, :, 0:n - 1], tb[:, 0, :, 0:n - 1], ALU.subtract,
                               ["ta", "tb"], ["Hprev"])
                            tt("dve", Hprev[:, 1, Ps, j0 + 1:j0 + n], ta[:, 1, :, 0:n - 1], tb[:, 1, :, 0:n - 1], ALU.add,
                               ["ta", "tb"], ["Hprev"])
                        cp("dve", Hprev[:, :, Ps, j0], init[:, :, Ps], [ikey], ["Hprev"])
                        tt("dve", fin_out[:, 0, Ps], ta[:, 0, :, fin_j], tb[:, 0, :, fin_j], ALU.subtract, ["ta", "tb"], [fkey])
                        tt("dve", fin_out[:, 1, Ps], ta[:, 1, :, fin_j], tb[:, 1, :, fin_j], ALU.add, ["ta", "tb"], [fkey])
                        yield

                def big_scan():
                    n = NJ
                    tcv = rsd[:, :].rearrange("p (a b c) -> p a b c", a=2, b=4)
                    tdv = ot[:, :].rearrange("p (a b c) -> p a b c", a=2, b=4)
                    cp("dve", Hprev[:, :, :, 0], carry[:, :, :], ["carry"], ["Hprev"])
                    for ri in range(2):
                        row = c_sb[:, ri, :, :].rearrange("p a b -> p (a b)")
                        T.op("dve", lambda e, row=row: e.tensor_tensor_scan(
                            out=row, data0=Rt[:, :, :].rearrange("p a b -> p (a b)"), data1=row,
                            initial=0.0, op0=ALU.mult, op1=ALU.add), ["c_sb", "Atab"], ["c_sb"])
                        yield
                    for ph in range(4):
                        Ps = slice(4 * ph, 4 * ph + 4)
                        cv = c_sb[:, :, Ps, :]
                        cvsw = AP(c_sb, 16 * NJ + 4 * ph * NJ, [[CS, 128], [-16 * NJ, 2], [NJ, 4], [1, n]])
                        cosb = AP(cosT, 4 * ph * NJ, [[16 * NJ, 128], [0, 2], [NJ, 4], [1, n]])
                        sinb = AP(sinT, 4 * ph * NJ, [[16 * NJ, 128], [0, 2], [NJ, 4], [1, n]])
                        tt("dve", tcv, cv, cosb, ALU.mult, ["c_sb", "Atab"], ["rsd"])
                        tt("dve", tdv, cvsw, sinb, ALU.mult, ["c_sb", "Atab"], ["ot"])
                        tt("dve", Hprev[:, 0, Ps, 1:n], tcv[:, 0, :, 0:n - 1], tdv[:, 0, :, 0:n - 1], ALU.subtract,
                           ["rsd", "ot"], ["Hprev"])
                        tt("dve", Hprev[:, 1, Ps, 1:n], tcv[:, 1, :, 0:n - 1], tdv[:, 1, :, 0:n - 1], ALU.add,
                           ["rsd", "ot"], ["Hprev"])
                        tt("dve", carry[:, 0, Ps], tcv[:, 0, :, n - 1], tdv[:, 0, :, n - 1], ALU.subtract, ["rsd", "ot"], ["carry"])
                        tt("dve", carry[:, 1, Ps], tcv[:, 1, :, n - 1], tdv[:, 1, :, n - 1], ALU.add, ["rsd", "ot"], ["carry"])
                        yield

                if sample:
                    for s_ in range(NSEQ_S):
                        yield from seg_scan(s_ * 16, 16, h0s[:, s_, :, :], "h0s", TS // L - 1, hfin[:, s_, :, :], "hfin")
                    for s_ in range(NSEQ_S):
                        T.dma(AP(hre_s.tensor, s_ * 2048, [[1, 128], [128, 16]]), hfin[:, s_, 0, :], f"st_hs{s_}", reads=["hfin"])
                        T.dma(AP(him_s.tensor, s_ * 2048, [[1, 128], [128, 16]]), hfin[:, s_, 1, :], f"st_hs{s_}b", reads=["hfin"])
                else:
                    yield from big_scan()
                    if blk == NBLK_SEQ - 1:
                        T.dma(AP(hre_p.tensor, seq * 2048, [[1, 128], [128, 16]]), carry[:, 0, :], "st_hp", reads=["carry"])
                        T.dma(AP(him_p.tensor, seq * 2048, [[1, 128], [128, 16]]), carry[:, 1, :], "st_hpb", reads=["carry"])


            def gen_ret_tr():
                for (src, skey, dst, key, eng) in ((qrot, f"qrot{pb}", qT, "qT", "act"), (krot, f"krot{pb}", kT, "kT", "dve")):
                    bk = bank()
                    for h in range(4):
                        for t_ in range(NT):
                            tr(psb[bk][:, h * NB + t_ * 128: h * NB + (t_ + 1) * 128], src[:, t_, h * 128:(h + 1) * 128],
                               [skey, "ident_b"], [("ps", bk)])
                    cp(eng, dst[:, :, :].rearrange("p h n -> p (h n)"), psb[bk][:, :], [("ps", bk)], [key])
                    rel(bk)
                    yield
                tt("dve", qdT[:, :, :].rearrange("p h (c l) -> p h c l", l=64),
                   qT[:, :, :].rearrange("p h (c l) -> p h c l", l=64),
                   AP(qdec, cf * 256, [[512, 128], [64, 4], [0, NC], [1, 64]]), ALU.mult, ["qT", "qdec"], ["qdT"])


            def gen_ret():
                if DBG.get('stop', 99) <= 5:
                    return
                if DBG.get('stop', 99) <= 6:
                    return
                if (not sample) and blk == 0:
                    par0 = chunk_ctr[0] % 2
                    memset("pool", Smaster[:], 0.0, ["Smaster"])
                    memset("pool", Sprev[par0][:], 0.0, [f"Sprev{par0}"])
                bo = [bank(), bank()]
                chunk_par = []
                for c in range(NC):
                    t_ = c // 2
                    base = 64 * (c % 2)
                    par = chunk_ctr[0] % 2
                    chunk_ctr[0] += 1
                    chunk_par.append(par)
                    if sample:
                        T.dma(Smaster[:, :, :], AP(sret.tensor, c * 4 * HD * HD, [[HD, 128], [HD * HD, 4], [1, HD]]),
                              "ld_S", writes=["Smaster"])
                        cp("act", Sprev[par][:], Smaster[:], ["Smaster"], [f"Sprev{par}"])
                    bs = bank()
                    for h in range(4):
                        mm(ps[bs][base:base + 64, h * 64:(h + 1) * 64], kT[:, h, c * 64:(c + 1) * 64], qT[:, h, c * 64:(c + 1) * 64],
                           True, True, ["kT", "qT"], [("ps", bs)])
                    sT = sTb[c % 2]
                    yield
                    tt("dve", sT[base:base + 64, :], ps[bs][base:base + 64, 0:256], maskT[base:base + 64, cf, :], ALU.mult,
                       [("ps", bs), "maskT"], [f"sT{c % 2}"])
                    rel(bs)
                    bkv = bank()
                    for h in range(4):
                        mm(ps[bkv][:, h * 128:(h + 1) * 128], ktd[base:base + 64, t_, h * 128:(h + 1) * 128],
                           vtok[base:base + 64, t_, h * 128:(h + 1) * 128], True, True, [f"ktd{pb}", f"vtok{pb}"], [("ps", bkv)])
                    yield
                    for h in range(4):
                        ob = bo[h // 2]
                        oc = (h % 2) * NB + c * 64
                        mm(ps[ob][:, oc:oc + 64], vtok[:, t_, h * 128:(h + 1) * 128],
                           sT[:, h * 64:(h + 1) * 64], True, False, [f"vtok{pb}", f"sT{c % 2}"], [("ps", ob)])
                        mm(ps[ob][:, oc:oc + 64], Sprev[par][:, h, :], qdT[:, h, c * 64:(c + 1) * 64], False, True,
                           [f"Sprev{par}", "qdT"], [("ps", ob)])
                    yield
                    for h in range(4):
                        stt(Smaster[:, h, :], Smaster[:, h, :], sdec_t[:, cf, h:h + 1], ps[bkv][:, h * 128:(h + 1) * 128],
                            ALU.mult, ALU.add, [("ps", bkv), "Smaster", "sdec_t"], ["Smaster"])
                    rel(bkv)
                    yield
                    npar = 1 - par
                    if sample:
                        T.dma(AP(rs.tensor, c * 4 * HD * HD, [[HD, 128], [HD * HD, 4], [1, HD]]), Smaster[:, :, :], "st_rs",
                              reads=["Smaster"])
                    else:
                        cp("act", Sprev[npar][:], Smaster[:], ["Smaster"], [f"Sprev{npar}"])
                        if blk == NBLK_SEQ - 1 and c == NC - 1:
                            T.dma(AP(rp.tensor, seq * 4 * HD * HD, [[HD, 128], [HD * HD, 4], [1, HD]]), Smaster[:, :, :],
                                  "st_rp", reads=["Smaster"])

                if DBG.get('stop', 99) <= 7:
                    return
                bn = [bank(), bank()]
                for i2 in range(2):
                    act(osq[:, i2 * 512:(i2 + 1) * 512], ps[bo[i2]][:, :], AF.Square, [("ps", bo[i2])], ["osq"])
                    mm(ps[bn[i2]][:, :], ones_b[:, :], osq[:, i2 * 512:(i2 + 1) * 512], True, True, ["osq", "ones_b"],
                       [("ps", bn[i2])])
                    yield
                for i2 in range(2):
                    act(rsd[:, :], ps[bn[i2]][:, :], AF.Ln, [("ps", bn[i2])], ["rsd"], scale=1.0 / HD, bias=EPS)
                    rel(bn[i2])
                    act(rsd[:, :], rsd[:, :], AF.Exp, ["rsd"], ["rsd"], scale=-0.5)
                    tt("dve", ot[:, :], ps[bo[i2]][:, :], rsd[:, :], ALU.mult, [("ps", bo[i2]), "rsd"], ["ot"])
                    rel(bo[i2])
                    for hh in range(2):
                        h = 2 * i2 + hh
                        stt(mixT[:, h, :], ot[:, hh * NB:(hh + 1) * NB], rng_[:, h:h + 1], sgret[:, h, :], ALU.mult, ALU.mult,
                            ["ot", "rng", f"sgret{pb}"], ["mixT"])
                    yield


            yield from gen_scan()
            yield from gen_ret_tr()
            yield "SPLIT"

            def gen_s5out():
                if DBG.get('stop', 99) <= 8:
                    return
                for t2 in range(2):
                    bk = bank()
                    for half in range(2):
                        t = 2 * t2 + half
                        yv = ps[bk][:, half * NB:(half + 1) * NB].rearrange("p (j s) -> p j s", s=L)
                        uv = uT[:, t, :].rearrange("p (j s) -> p j s", s=L)
                        for i in range(L):
                            for k in range(i + 1):
                                mm(yv[:, :, i], Kbd[:, t, k, :], uv[:, :, i - k], k == 0, False, ["Kbd", f"uT{pb}"], [("ps", bk)])
                            for q in range(4):
                                P = 4 * t + q
                                for ri in range(2):
                                    last = (ri == 1)
                                    mm(ps[bk][32 * q:32 * q + 32, half * NB:(half + 1) * NB].rearrange("p (j s) -> p j s", s=L)[:, :, i],
                                       Et[:, P, i, ri, :], Hprev[:, ri, P, :], False, last, ["Et", "Hprev"], [("ps", bk)],
                                       tile_position=(0, 32 * q))
                            yield
                    for half in range(2):
                        t = 2 * t2 + half
                        ysrc = ps[bk][:, half * NB:(half + 1) * NB]
                        ga = g1[half]
                        GK = "rsd"
                        act(ga[:], ysrc, AF.Square, [("ps", bk)], [GK])
                        ts("dve", ga[:], ga[:], 0.044715, 1.0, ALU.mult, ALU.add, [GK], [GK])
                        tt("dve", ga[:], ga[:], ysrc, ALU.mult, [GK, ("ps", bk)], [GK])
                        act(ga[:], ga[:], AF.Sigmoid, [GK], [GK], scale=1.5957691216057308)
                        tt("dve", zT[:, t, :], ga[:], ysrc, ALU.mult, [GK, ("ps", bk)], ["zT"])
                        if half == 1:
                            rel(bk)
                        yield

                if DBG.get('stop', 99) <= 9:
                    return
                for m2 in range(2):
                    bk = bank()
                    for half in range(2):
                        m = 2 * m2 + half
                        for kt in range(4):
                            mm(ps[bk][:, half * NB:(half + 1) * NB], wglu_b[:, kt, m * 128:(m + 1) * 128], zT[:, kt, :],
                               kt == 0, kt == 3, ["wglu_b", "zT"], [("ps", bk)])
                        yield
                    for half in range(2):
                        m = 2 * m2 + half
                        gb = g2[half]
                        GK = "ot"
                        act(gb[:], ps[bk][:, half * NB:(half + 1) * NB], AF.Sigmoid, [("ps", bk), "bglu"], [GK],
                            bias=bglu[:, m:m + 1])
                        if half == 1:
                            rel(bk)
                        tt("dve", gb[:], gb[:], zT[:, m, :], ALU.mult, [GK, "zT"], [GK])
                        tt("dve", mixT[:, 4 + m, :], gb[:], sgssm[:, m, :], ALU.mult, [GK, f"sgssm{pb}"], ["mixT"])
                        yield


            yield from interleave2(gen_ret(), gen_s5out(), 20.0, 32.0)

            if DBG.get('stop', 99) <= 10:
                return
            for t_ in range(NT):
                xi = cfg["xslots"][t_]
                XK = f"xslot{xi}"
                for half in range(2):
                    bk = bank()
                    for kt in range(8):
                        mm(ps[bk][:, :], mixT[:, kt, t_ * 128:(t_ + 1) * 128], wout_b[:, kt, half * 512:(half + 1) * 512],
                           kt == 0, kt == 7, ["mixT", "wout_b"], [("ps", bk)])
                    tt("dve", xslot[xi][:, half * 512:(half + 1) * 512], ps[bk][:, :], xslot[xi][:, half * 512:(half + 1) * 512],
                       ALU.add, [("ps", bk), XK], [XK])
                    rel(bk)
                    yield
                c_ = 4 + t_
                act(osq[:, :], xslot[xi][:], AF.Square, [XK], ["osq", "ssq"], accum_out=ssq[:, c_:c_ + 1])
                act(ssq[:, c_:c_ + 1], ssq[:, c_:c_ + 1], AF.Ln, ["ssq"], ["ssq"], scale=1.0 / D, bias=EPS)
                act(rstd[:, c_:c_ + 1], ssq[:, c_:c_ + 1], AF.Exp, ["ssq"], ["rstd"], scale=-0.5)
                stt(xslot[xi][:], xslot[xi][:], rstd[:, c_:c_ + 1], gfin[:], ALU.mult, ALU.mult, [XK, "rstd", "gfin"], [XK])
                if sample:
                    for half in range(2):
                        s_ = 2 * t_ + half
                        T.dma(ys[s_ * TS:(s_ + 1) * TS, :], xslot[xi][64 * half:64 * half + TS, :], f"st_y{xi}", reads=[XK])
                else:
                    T.dma(yp[row0 + t_ * 128: row0 + (t_ + 1) * 128, :], xslot[xi][:], f"st_y{xi}", reads=[XK])
                nxt2 = cfg.get("next2")
                if nxt2 is not None:
                    nxt2.setdefault("xslots", [None] * NT)[t_] = xi
                    issue_x_load(nxt2, t_, xi)
                yield


        blocks = []
        if DBG.get("sample", True):
            blocks.append(dict(sample=True))
        for seq in range(NSEQ_P):
            for blk in range(DBG.get("nblk", NBLK_SEQ)):
                blocks.append(dict(sample=False, seq=seq, blk=blk))
        for i, c_ in enumerate(blocks):
            if i + 2 < len(blocks):
                c_["next2"] = blocks[i + 2]
        for i, c_ in enumerate(blocks[:2]):
            c_["xslots"] = [2 * i, 2 * i + 1]
            for t_ in range(NT):
                issue_x_load(c_, t_, 2 * i + t_)
            issue_rot_load(c_, i)

        def drain(g):
            n = 0
            for _ in g:
                n += 1
            return n

        if DBG.get("nopipe"):
            for i, c_ in enumerate(blocks):
                drain(phaseA(c_, i % 2))
                drain(phaseB(c_, i % 2))
        else:
            est = {"a1": 39.0, "a2": 9.0, "b1": 12.0, "b2": 50.0}
            drain(phaseA(blocks[0], 0))
            pre_a = {}
            for i in range(1, len(blocks) + 1):
                gb = phaseB(blocks[i - 1], (i - 1) % 2)
                if i in pre_a:
                    ga = pre_a.pop(i)
                else:
                    ga = phaseA(blocks[i], i % 2) if i < len(blocks) else iter(())
                for stage in (1, 2):
                    if stage == 2 and i < len(blocks) and not blocks[i]["sample"] and not DBG.get("nofront"):
                        ga = s5_front(blocks[i], i % 2)
                    na, nb_ = est[f"a{stage}"], est[f"b{stage}"]
                    ca = cb = 0
                    da = db = False
                    while not (da and db):
                        if db or (not da and ca / na <= cb / nb_):
                            try:
                                r = next(ga)
                                if r == "SPLIT" and stage == 1:
                                    da = True
                                else:
                                    ca += 1
                            except StopIteration:
                                da = True
                        else:
                            try:
                                r = next(gb)
                                if r == "SPLIT" and stage == 1:
                                    db = True
                                else:
                                    cb += 1
                            except StopIteration:
                                db = True
                    if ca > 0:
                        est[f"a{stage}"] = float(ca)
                    if cb > 0:
                        est[f"b{stage}"] = float(cb)
                if i + 1 < len(blocks) and not DBG.get("nopre"):
                    gn = phaseA(blocks[i + 1], (i + 1) % 2)
                    for _ in range(NT):
                        next(gn)
                    pre_a[i + 1] = gn
        T.finish("sp")
        build_nc.stats = (T.n_ops, T.n_waits, len(T.sems))
    return nc


def _constants():
    c = {}
    c["cident"] = np.eye(128, dtype=np.float32)
    half = 64
    inv = 10000.0 ** (-np.arange(half, dtype=np.float64) / half)

    def rot_tab(pos):
        ang = pos[:, None].astype(np.float64) * inv[None, :]
        cos, sin = np.cos(ang), np.sin(ang)
        return np.concatenate([cos, -sin, sin], axis=1).astype(np.float32)

    c["crot_p"] = rot_tab(np.arange(SEQ))
    rs_ = np.zeros((128, 192), np.float32)
    tab = rot_tab(PAST + np.arange(TS))
    rs_[0:TS] = tab
    rs_[64:64 + TS] = tab
    c["crot_s"] = rs_
    gam = np.array([1.0 - 2.0 ** (-5.0 - h) for h in range(4)], dtype=np.float64)
    scale = HD ** -0.5
    cmask = np.zeros((2, 128, 4, 64), np.float64)
    cq = np.zeros((2, 4, 64), np.float64)
    ck = np.zeros((2, 128, 4), np.float64)
    for cfg, blk in ((0, 64), (1, TS)):
        for h in range(4):
            for m in range(blk):
                for l in range(m, blk):
                    cmask[cfg, m, h, l] = gam[h] ** (l - m) * scale
                    cmask[cfg, 64 + m, h, l] = gam[h] ** (l - m) * scale
            for l in range(blk):
                cq[cfg, h, l] = gam[h] ** (l + 1)
                ck[cfg, l, h] = gam[h] ** (blk - 1 - l) * scale
                ck[cfg, 64 + l, h] = gam[h] ** (blk - 1 - l) * scale
    c["cmask"] = cmask.reshape(256, 256).astype(np.float32)
    c["cqdec"] = cq.reshape(2, 256).astype(np.float32)
    c["ckdec"] = ck.reshape(256, 4).astype(np.float32)
    dup = np.zeros((64, 2, 2, 64), np.float32)
    for m in range(2):
        dup[np.arange(64), m, m, np.arange(64)] = 1.0
    c["cdup"] = dup.reshape(64, 256)
    par = np.zeros((2, 32, 16), np.float32)
    for m in range(2):
        par[m, m::2, :] = 1.0
    c["cpar"] = par.reshape(2, 512)
    wm = np.zeros((8, 16, 2), np.float32)
    for gl in range(8):
        wm[gl, :, gl % 2] = 1.0
    c["cwmask"] = wm.reshape(128, 2)
    c["csdec"] = np.array([[g ** 64 for g in gam] + [g ** TS for g in gam]], dtype=np.float32)
    return c


_NC_CACHE = {}


def kernel(x_prompt, x_sample, state_ret, state_ssm_re, state_ssm_im, norm_g, w_in, ret_norm_g,
           ssm_lambda_re, ssm_lambda_im, ssm_log_step, ssm_b_re, ssm_b_im, ssm_c_re, ssm_c_im,
           ssm_d, w_glu, b_glu, w_out, final_norm_g):
    f = lambda a: np.ascontiguousarray(np.asarray(a, dtype=np.float32))
    x_prompt = f(x_prompt); x_sample = f(x_sample)
    consts = _constants()
    shared = {
        "norm_g": f(norm_g).reshape(1, D), "w_in": f(w_in).reshape(D, 3072),
        "ret_norm_g": f(ret_norm_g).reshape(4, HD),
        "lre": f(ssm_lambda_re).reshape(32, 64), "lim": f(ssm_lambda_im).reshape(32, 64),
        "lstep": f(ssm_log_step).reshape(1, 32),
        "bre": f(ssm_b_re).reshape(2048, 16), "bim": f(ssm_b_im).reshape(2048, 16),
        "cre": f(ssm_c_re).reshape(512, 64), "cim": f(ssm_c_im).reshape(512, 64),
        "ssm_d": f(ssm_d).reshape(1, 512), "w_glu": f(w_glu).reshape(512, 512),
        "b_glu": f(b_glu).reshape(1, 512), "w_out": f(w_out).reshape(D, D),
        "fng": f(final_norm_g).reshape(1, D),
    }
    shared.update(consts)
    sr = f(state_ret)[0]; s_re = f(state_ssm_re)[0]; s_im = f(state_ssm_im)[0]
    in_maps = []
    for c in range(8):
        m = dict(shared)
        m["xp"] = x_prompt[c * NSEQ_P:(c + 1) * NSEQ_P].reshape(NSEQ_P * SEQ, D)
        m["xs"] = x_sample[c * NSEQ_S:(c + 1) * NSEQ_S].reshape(NSEQ_S * TS, D)
        m["sret"] = sr[c * NSEQ_S:(c + 1) * NSEQ_S].reshape(NSEQ_S * 4 * HD, HD)
        m["sre"] = s_re[c * NSEQ_S:(c + 1) * NSEQ_S].reshape(NSEQ_S, 2048)
        m["sim"] = s_im[c * NSEQ_S:(c + 1) * NSEQ_S].reshape(NSEQ_S, 2048)
        in_maps.append(m)
    if "nc" not in _NC_CACHE:
        _NC_CACHE["nc"] = build_nc()
    nc = _NC_CACHE["nc"]
    res = run_bass_kernel_spmd(nc, in_maps, core_ids=list(range(8)))
    R = res.results
    cat = lambda k: np.concatenate([np.asarray(r[k], dtype=np.float32) for r in R], axis=0)
    y_prompt = cat("yp").reshape(16, SEQ, D)
    y_sample = cat("ys").reshape(32, TS, D)
    ret_p = cat("rp").reshape(1, 16, 4, HD, HD)
    hre_p = cat("hre_p").reshape(1, 16, 32, 64)
    him_p = cat("him_p").reshape(1, 16, 32, 64)
    ret_s = cat("rs").reshape(1, 32, 4, HD, HD)
    hre_s = cat("hre_s").reshape(1, 32, 32, 64)
    him_s = cat("him_s").reshape(1, 32, 32, 64)
    return (y_prompt, y_sample, ret_p, hre_p, him_p, ret_s, hre_s, him_s)
```

```python
import math
from contextlib import ExitStack

import numpy as np
import concourse.bass as bass
import concourse.mybir as mybir
from concourse.bass_utils import run_bass_kernel_spmd

F32 = mybir.dt.float32
BF16 = mybir.dt.bfloat16
ALU = mybir.AluOpType
AF = mybir.ActivationFunctionType

D = 1024
SEQ = 4096
NSEQ_P = 2
NSEQ_S = 4
TS = 16
PAST = 2048
NB = 256
NT = 2
NBLK_SEQ = SEQ // NB
L = 4
NJ = NB // L
NC = NB // 64
EPS = 1e-6
HD = 128
STRICT = True
DBG = {}


class Trk:
    def __init__(self, nc, es):
        self.nc = nc
        self.es = es
        self.eng = {"pe": nc.tensor, "act": nc.scalar, "dve": nc.vector, "pool": nc.gpsimd, "sp": nc.sync}
        self.sems = {}
        self.cnt = {}
        for e in ("pe", "act", "dve", "pool"):
            self.sems[e] = es.enter_context(nc.semaphore("sem_" + e))
            self.cnt[e] = 0
        self.known = {e: {} for e in self.eng}
        self.last_w = {}
        self.readers = {}
        self.n_ops = 0
        self.n_waits = 0

    def _deps(self, reads, writes):
        deps = {}

        def add(s, v):
            if v > deps.get(s, 0):
                deps[s] = v

        for k in reads:
            lw = self.last_w.get(k)
            if lw:
                add(*lw)
        for k in writes:
            lw = self.last_w.get(k)
            if lw:
                add(*lw)
            for s, v in self.readers.get(k, {}).items():
                add(s, v)
        return deps

    def _wait(self, e, deps, own=None):
        for s, v in deps.items():
            if s == own and (own == "pe" or not STRICT):
                continue
            if self.known[e].get(s, 0) >= v:
                continue
            self.eng[e].wait_ge(self.sems[s], v)
            self.known[e][s] = v
            self.n_waits += 1

    def op(self, e, fn, reads=(), writes=()):
        deps = self._deps(reads, writes)
        self._wait(e, deps, own=e)
        ins = fn(self.eng[e])
        self.cnt[e] += 1
        n = self.cnt[e]
        ins.then_inc(self.sems[e], 1)
        for k in reads:
            self.readers.setdefault(k, {})[e] = n
        for k in writes:
            self.last_w[k] = (e, n)
            self.readers[k] = {}
        self.n_ops += 1
        return ins

    def dma(self, out, in_, sem, reads=(), writes=(), q="sp", **kw):
        if sem not in self.sems:
            self.sems[sem] = self.es.enter_context(self.nc.semaphore("d_" + sem))
            self.cnt[sem] = 0
        deps = self._deps(reads, writes)
        if sem == "ld_setup":
            deps.pop(sem, None)
        self._wait(q, deps, own=None)
        ins = self.eng[q].dma_start(out=out, in_=in_, **kw)
        self.cnt[sem] += 16
        n = self.cnt[sem]
        ins.then_inc(self.sems[sem], 16)
        for k in reads:
            self.readers.setdefault(k, {})[sem] = n
        for k in writes:
            self.last_w[k] = (sem, n)
            self.readers[k] = {}
        self.n_ops += 1
        return ins

    def fence_group(self, sem):
        tot = self.cnt[sem]
        for k, lw in list(self.last_w.items()):
            if lw[0] == sem:
                self.last_w[k] = (sem, tot)

    def barrier(self):
        for e in self.eng:
            for s_, c in self.cnt.items():
                if s_ == e or c == 0:
                    continue
                if self.known[e].get(s_, 0) < c:
                    self.eng[e].wait_ge(self.sems[s_], c)
                    self.known[e][s_] = c

    def finish(self, e="sp"):
        for s, c in self.cnt.items():
            if s in ("pe", "act", "dve", "pool"):
                continue
            if c > 0 and self.known[e].get(s, 0) < c:
                self.eng[e].wait_ge(self.sems[s], c)
                self.known[e][s] = c


def AP(t, off, dims):
    return bass.AP(t, off, [list(d) for d in dims])


def build_nc():
    nc = bass.Bass("TRN2", target_bir_lowering=False)

    def din(name, shape):
        return nc.dram_tensor(name, list(shape), F32, kind="ExternalInput").ap()

    def dout(name, shape):
        return nc.dram_tensor(name, list(shape), F32, kind="ExternalOutput").ap()

    xp = din("xp", [NSEQ_P * SEQ, D])
    xs = din("xs", [NSEQ_S * TS, D])
    sret = din("sret", [NSEQ_S * 4 * HD, HD])
    sre = din("sre", [NSEQ_S, 2048])
    sim = din("sim", [NSEQ_S, 2048])
    norm_g = din("norm_g", [1, D])
    w_in = din("w_in", [D, 3072])
    ret_norm_g = din("ret_norm_g", [4, HD])
    lre = din("lre", [32, 64])
    lim = din("lim", [32, 64])
    lstep = din("lstep", [1, 32])
    bre = din("bre", [2048, 16])
    bim = din("bim", [2048, 16])
    cre = din("cre", [512, 64])
    cim = din("cim", [512, 64])
    ssm_d = din("ssm_d", [1, 512])
    w_glu = din("w_glu", [512, 512])
    b_glu = din("b_glu", [1, 512])
    w_out = din("w_out", [D, D])
    fng = din("fng", [1, D])
    cident = din("cident", [128, 128])
    crot_p = din("crot_p", [SEQ, 192])
    crot_s = din("crot_s", [128, 192])
    cmask = din("cmask", [2 * 128, 256])
    cqdec = din("cqdec", [2, 256])
    ckdec = din("ckdec", [2 * 128, 4])
    cdup = din("cdup", [64, 256])
    cpar = din("cpar", [2, 512])
    cwmask = din("cwmask", [128, 2])
    csdec = din("csdec", [1, 8])

    yp = dout("yp", [NSEQ_P * SEQ, D])
    ys = dout("ys", [NSEQ_S * TS, D])
    rp = dout("rp", [NSEQ_P * 4 * HD, HD])
    hre_p = dout("hre_p", [NSEQ_P, 2048])
    him_p = dout("him_p", [NSEQ_P, 2048])
    rs = dout("rs", [NSEQ_S * 4 * HD, HD])
    hre_s = dout("hre_s", [NSEQ_S, 2048])
    him_s = dout("him_s", [NSEQ_S, 2048])

    with ExitStack() as es:
        es.enter_context(nc.allow_non_contiguous_dma(reason="small parameter layouts"))
        T = Trk(nc, es)

        def sb(name, shape, dt=F32):
            return es.enter_context(nc.sbuf_tensor(name, list(shape), dt))

        def tap(name, ap_, shape, keys, dt=F32):
            if not DBG.get("taps"):
                return
            d = nc.dram_tensor("tap_" + name, list(shape), dt, kind="ExternalOutput").ap()
            T.dma(d, ap_, "dbg", reads=keys)

        win_b = sb("win_b", [128, 8, 3072], BF16)
        wout_b = sb("wout_b", [128, 8, 1024], BF16)
        wglu_b = sb("wglu_b", [128, 4, 512], BF16)
        ident_f = sb("ident_f", [128, 128], F32)
        ident_b = sb("ident_b", [128, 128], BF16)
        ones_b = sb("ones_b", [128, 128], BF16)
        ng = sb("ng", [128, 8], F32)
        rng_ = sb("rng", [128, 4], F32)
        bglu = sb("bglu", [128, 4], F32)
        dvec = sb("dvec", [128, 4], F32)
        gfin = sb("gfin", [128, D], F32)
        maskT = sb("maskT", [128, 2, 256], F32)
        qdec = sb("qdec", [128, 2, 256], F32)
        kdec = sb("kdec", [128, 2, 4], F32)
        sdec_t = sb("sdec_t", [128, 2, 4], F32)
        gconst = sb("gconst", [128, 1], F32)
        Kbd = sb("Kbd", [128, 4, L, 128], BF16)
        Wt = sb("Wt", [128, 4, L, 2, 128], BF16)
        Et = sb("Et", [128, 16, L, 2, 32], BF16)
        cosT = sb("cosT", [128, 16, NJ], F32)
        sinT = sb("sinT", [128, 16, NJ], F32)
        rtab = sb("rtab", [128, 16], F32)
        Rt = sb("Rt", [128, 16, NJ], F32)
        h0s = sb("h0s", [128, NSEQ_S, 2, 16], F32)

        NXS = 4
        xslot = [sb(f"xslot{i}", [128, D], F32) for i in range(NXS)]

        ps = [es.enter_context(nc.psum_tensor(f"ps{i}", [128, 512], F32)) for i in range(8)]
        psb = [p.bitcast(BF16) for p in ps]
        bank_ctr = [0]

        pinned = set()

        def bank(pin=True):
            for _ in range(9):
                b = bank_ctr[0] % 8
                bank_ctr[0] += 1
                if b not in pinned:
                    break
            else:
                raise RuntimeError("out of PSUM banks")
            if pin:
                pinned.add(b)
            return b

        def rel(*bs):
            for b in bs:
                pinned.discard(b)

        def act(out, in_, func, reads, writes, **kw):
            return T.op("act", lambda e: e.activation(out=out, in_=in_, func=func, **kw), reads, writes)

        def tt(eng, out, in0, in1, op, reads, writes):
            return T.op(eng, lambda e: e.tensor_tensor(out=out, in0=in0, in1=in1, op=op), reads, writes)

        def ts(eng, out, in0, s1, s2, op0, op1, reads, writes):
            return T.op(eng, lambda e: e.tensor_scalar(out=out, in0=in0, scalar1=s1, scalar2=s2, op0=op0, op1=op1),
                        reads, writes)

        def stt(out, in0, scalar, in1, op0, op1, reads, writes):
            return T.op("dve", lambda e: e.scalar_tensor_tensor(out=out, in0=in0, scalar=scalar, in1=in1,
                                                                 op0=op0, op1=op1), reads, writes)

        def cp(eng, out, in_, reads, writes):
            if eng == "act":
                return act(out, in_, AF.Copy, reads, writes)
            return T.op(eng, lambda e: e.tensor_copy(out=out, in_=in_), reads, writes)

        def mm(out, lhsT, rhs, start, stop, reads, writes, **kw):
            return T.op("pe", lambda e: e.matmul(out, lhsT=lhsT, rhs=rhs, start=start, stop=stop, **kw), reads, writes)

        def tr(out, in_, reads, writes):
            return T.op("pe", lambda e: e.transpose(out=out, in_=in_, identity=ident_b[:]), reads, writes)

        def memset(eng, ap_, val, writes):
            return T.op(eng, lambda e: e.memset(ap_, val), (), writes)

        T.dma(ident_f[:], cident, "ld_setup", writes=["ident_f"])
        T.dma(ng[:], AP(norm_g.tensor, 0, [[1, 128], [128, 8]]), "ld_setup", writes=["ng"])
        T.dma(rng_[:], AP(ret_norm_g.tensor, 0, [[1, 128], [128, 4]]), "ld_setup", writes=["rng"])
        T.dma(bglu[:], AP(b_glu.tensor, 0, [[1, 128], [128, 4]]), "ld_setup", writes=["bglu"])
        T.dma(dvec[:], AP(ssm_d.tensor, 0, [[1, 128], [128, 4]]), "ld_setup", writes=["dvec"])
        T.dma(gfin[:], AP(fng.tensor, 0, [[0, 128], [1, D]]), "ld_setup", writes=["gfin"])
        T.dma(sdec_t[:, :, :], AP(csdec.tensor, 0, [[0, 128], [4, 2], [1, 4]]), "ld_setup", writes=["sdec_t"])
        for c in range(2):
            T.dma(maskT[:, c, :], cmask[c * 128:(c + 1) * 128, :], "ld_setup", writes=["maskT"])
            T.dma(qdec[:, c, :], AP(cqdec.tensor, c * 256, [[0, 128], [1, 256]]), "ld_setup", writes=["qdec"])
            T.dma(kdec[:, c, :], ckdec[c * 128:(c + 1) * 128, :], "ld_setup", writes=["kdec"])

        with ExitStack() as es2:
            def sb2(name, shape, dt=F32):
                return es2.enter_context(nc.sbuf_tensor(name, list(shape), dt))

            K1 = "s5"
            lre1 = sb2("lre1", [64, 32]); lim1 = sb2("lim1", [64, 32]); dt1 = sb2("dt1", [64, 32])
            T.dma(lre1[:], AP(lre.tensor, 0, [[1, 64], [64, 32]]), "ld_setup", writes=["lre1"])
            T.dma(lim1[:], AP(lim.tensor, 0, [[1, 64], [64, 32]]), "ld_setup", writes=["lim1"])
            T.dma(dt1[:], AP(lstep.tensor, 0, [[0, 64], [1, 32]]), "ld_setup", writes=["dt1"])
            b1re = sb2("b1re", [64, 32, 16]); b1im = sb2("b1im", [64, 32, 16])
            T.dma(b1re[:], AP(bre.tensor, 0, [[16, 64], [1024, 32], [1, 16]]), "ld_setup", writes=["b1re"])
            T.dma(b1im[:], AP(bim.tensor, 0, [[16, 64], [1024, 32], [1, 16]]), "ld_setup", writes=["b1im"])
            cnre = sb2("cnre", [128, 4, 64]); cnim = sb2("cnim", [128, 4, 64])
            T.dma(cnre[:], AP(cre.tensor, 0, [[64, 128], [128 * 64, 4], [1, 64]]), "ld_setup", writes=["cnre"])
            T.dma(cnim[:], AP(cim.tensor, 0, [[64, 128], [128 * 64, 4], [1, 64]]), "ld_setup", writes=["cnim"])
            dup = sb2("dup", [64, 2, 128])
            T.dma(dup[:], cdup, "ld_setup", writes=["dup"])
            parm = sb2("parm", [64, 2, 512])
            T.dma(parm[:], AP(cpar.tensor, 0, [[0, 64], [512, 2], [1, 512]]), "ld_setup", writes=["parm"])
            wmask = sb2("wmask", [128, 2])
            T.dma(wmask[:], cwmask, "ld_setup", writes=["wmask"])
            for s in range(NSEQ_S):
                T.dma(h0s[:, s, 0, :], AP(sre.tensor, s * 2048, [[1, 128], [128, 16]]), "ld_setup", writes=["h0s"])
                T.dma(h0s[:, s, 1, :], AP(sim.tensor, s * 2048, [[1, 128], [128, 16]]), "ld_setup", writes=["h0s"])

            T.fence_group("ld_setup")
            cp("act", ident_b[:], ident_f[:], ["ident_f"], ["ident_b"])
            memset("pool", ones_b[:], 1.0, ["ones_b"])
            cast_engs = ["act", "dve", "pool"]
            ci = [0]

            def load_cast(dst_ap, src_ap, ncols, wkey, scale_ap=None):
                i = ci[0] % NXS
                e = cast_engs[ci[0] % 3]
                ci[0] += 1
                T.dma(xslot[i][:, 0:ncols], src_ap, f"ld_xs{i}", writes=[f"xslot{i}"])
                if scale_ap is None:
                    cp(e, dst_ap, xslot[i][:, 0:ncols], [f"xslot{i}"], [wkey])
                elif e == "act":
                    act(dst_ap, xslot[i][:, 0:ncols], AF.Copy, [f"xslot{i}", "ng"], [wkey], scale=scale_ap)
                else:
                    ts(e, dst_ap, xslot[i][:, 0:ncols], scale_ap, None, ALU.mult, ALU.bypass, [f"xslot{i}", "ng"], [wkey])

            for kt in range(8):
                for c in range(3):
                    load_cast(win_b[:, kt, c * 1024:(c + 1) * 1024], w_in[kt * 128:(kt + 1) * 128, c * 1024:(c + 1) * 1024],
                              1024, "win_b", scale_ap=ng[:, kt:kt + 1])
            for kt in range(8):
                load_cast(wout_b[:, kt, :], w_out[kt * 128:(kt + 1) * 128, :], 1024, "wout_b")
            for kt in range(4):
                load_cast(wglu_b[:, kt, :], w_glu[kt * 128:(kt + 1) * 128, :], 512, "wglu_b")


            def t64(name):
                return sb2(name, [64, 32])

            lrc = t64("lrc"); dtv = t64("dtv"); are = t64("are"); aim = t64("aim"); mag = t64("mag")
            th = t64("th"); th2 = t64("th2"); wS = t64("wS"); wC = t64("wC")
            cc = t64("cc"); ss_ = t64("ss_"); cs = t64("cs")
            RW = [K1]
            ts("dve", lrc[:], lre1[:], -1e-4, None, ALU.min, ALU.bypass, ["lre1"] + RW, RW)
            act(dtv[:], dt1[:], AF.Exp, ["dt1"] + RW, RW)
            tt("dve", are[:], lrc[:], dtv[:], ALU.mult, RW, RW)
            tt("dve", aim[:], lim1[:], dtv[:], ALU.mult, ["lim1"] + RW, RW)
            act(mag[:], are[:], AF.Exp, RW, RW)
            ts("dve", th[:], aim[:], 1.0 / 32.0, None, ALU.mult, ALU.bypass, RW, RW)
            tt("dve", th2[:], th[:], th[:], ALU.mult, RW, RW)
            a = [-1.0 / 6, 1.0 / 120, -1.0 / 5040, 1.0 / 362880]
            ts("dve", wS[:], th2[:], a[3], None, ALU.mult, ALU.bypass, RW, RW)
            for k in (2, 1, 0):
                ts("dve", wS[:], wS[:], a[k], None, ALU.add, ALU.bypass, RW, RW)
                tt("dve", wS[:], wS[:], th2[:], ALU.mult, RW, RW)
            ts("dve", wS[:], wS[:], 1.0, None, ALU.add, ALU.bypass, RW, RW)
            tt("dve", wS[:], wS[:], th[:], ALU.mult, RW, RW)
            b = [-0.5, 1.0 / 24, -1.0 / 720, 1.0 / 40320, -1.0 / 3628800]
            ts("dve", wC[:], th2[:], b[4], None, ALU.mult, ALU.bypass, RW, RW)
            for k in (3, 2, 1, 0):
                ts("dve", wC[:], wC[:], b[k], None, ALU.add, ALU.bypass, RW, RW)
                tt("dve", wC[:], wC[:], th2[:], ALU.mult, RW, RW)
            ts("dve", wC[:], wC[:], 1.0, None, ALU.add, ALU.bypass, RW, RW)
            for _ in range(5):
                tt("dve", cc[:], wC[:], wC[:], ALU.mult, RW, RW)
                tt("dve", ss_[:], wS[:], wS[:], ALU.mult, RW, RW)
                tt("dve", cs[:], wC[:], wS[:], ALU.mult, RW, RW)
                tt("dve", wC[:], cc[:], ss_[:], ALU.subtract, RW, RW)
                ts("dve", wS[:], cs[:], 2.0, None, ALU.mult, ALU.bypass, RW, RW)
            pwr = sb2("pwr", [64, L + 1, 32]); pwi = sb2("pwi", [64, L + 1, 32])
            memset("dve", pwr[:, 0, :], 1.0, RW)
            memset("dve", pwi[:, 0, :], 0.0, RW)
            tt("dve", pwr[:, 1, :], mag[:], wC[:], ALU.mult, RW, RW)
            tt("dve", pwi[:, 1, :], mag[:], wS[:], ALU.mult, RW, RW)
            tA = t64("tA"); tB = t64("tB")

            def cmul(o_re, o_im, a_re, a_im, b_re, b_im, t1_, t2_, xr=()):
                R_ = RW + list(xr)
                tt("dve", t1_, a_re, b_re, ALU.mult, R_, RW)
                tt("dve", t2_, a_im, b_im, ALU.mult, R_, RW)
                tt("dve", o_re, t1_, t2_, ALU.subtract, R_, RW)
                tt("dve", t1_, a_re, b_im, ALU.mult, R_, RW)
                tt("dve", t2_, a_im, b_re, ALU.mult, R_, RW)
                tt("dve", o_im, t1_, t2_, ALU.add, R_, RW)

            for k in range(1, L):
                cmul(pwr[:, k + 1, :], pwi[:, k + 1, :], pwr[:, k, :], pwi[:, k, :], pwr[:, 1, :], pwi[:, 1, :],
                     tA[:], tB[:])
            nre = t64("nre"); den = t64("den"); qre = t64("qre"); qim = t64("qim")
            ts("dve", nre[:], pwr[:, 1, :], -1.0, None, ALU.add, ALU.bypass, RW, RW)
            tt("dve", den[:], lrc[:], lrc[:], ALU.mult, RW, RW)
            tt("dve", tA[:], lim1[:], lim1[:], ALU.mult, RW, RW)
            tt("dve", den[:], den[:], tA[:], ALU.add, RW, RW)
            T.op("dve", lambda e: e.reciprocal(out=den[:], in_=den[:]), RW, RW)
            tt("dve", tA[:], nre[:], lrc[:], ALU.mult, RW, RW)
            tt("dve", tB[:], pwi[:, 1, :], lim1[:], ALU.mult, RW, RW)
            tt("dve", qre[:], tA[:], tB[:], ALU.add, RW, RW)
            tt("dve", qre[:], qre[:], den[:], ALU.mult, RW, RW)
            tt("dve", tA[:], pwi[:, 1, :], lrc[:], ALU.mult, RW, RW)
            tt("dve", tB[:], nre[:], lim1[:], ALU.mult, RW, RW)
            tt("dve", qim[:], tA[:], tB[:], ALU.subtract, RW, RW)
            tt("dve", qim[:], qim[:], den[:], ALU.mult, RW, RW)

            tap("lre1", lre1[:], [64, 32], ["lre1"]); tap("dt1", dt1[:], [64, 32], ["dt1"])
            tap("mag", mag[:], [64, 32], RW); tap("wC", wC[:], [64, 32], RW); tap("wS", wS[:], [64, 32], RW)
            tap("pwr", pwr[:, :, :].rearrange("p a b -> p (a b)"), [64, (L + 1) * 32], RW)
            tap("pwi", pwi[:, :, :].rearrange("p a b -> p (a b)"), [64, (L + 1) * 32], RW)
            tap("qre", qre[:], [64, 32], RW); tap("qim", qim[:], [64, 32], RW)

            def bc16(t, k=None):
                if k is None:
                    return AP(t, 0, [[32, 64], [1, 32], [0, 16]])
                return AP(t, k * 32, [[(L + 1) * 32, 64], [1, 32], [0, 16]])

            bbr = sb2("bbr", [64, 32, 16]); bbi = sb2("bbi", [64, 32, 16])
            u1 = sb2("u1", [64, 32, 16]); u2 = sb2("u2", [64, 32, 16])
            cmul(bbr[:], bbi[:], bc16(qre), bc16(qim), b1re[:], b1im[:], u1[:], u2[:], xr=["b1re", "b1im"])

            CTr = sb2("CTr", [64, 512]); CTi = sb2("CTi", [64, 512]); nCTi = sb2("nCTi", [64, 512])
            for (src, dst, key) in ((cnre, CTr, "cnre"), (cnim, CTi, "cnim")):
                bk = bank(False)
                for t in range(4):
                    mm(ps[bk][0:64, t * 128:(t + 1) * 128], src[:, t, :], ident_f[:, :], True, True,
                       [key, "ident_f"], [("ps", bk)])
                cp("dve", dst[:], ps[bk][0:64, :], [("ps", bk)], RW)
            ts("dve", nCTi[:], CTi[:], -1.0, None, ALU.mult, ALU.bypass, RW, RW)

            tap("bbr", bbr[:, :, :].rearrange("p a b -> p (a b)"), [64, 512], RW)
            tap("CTr", CTr[:], [64, 512], RW); tap("nCTi", nCTi[:], [64, 512], RW)
            Vre = sb2("Vre", [64, 512]); Vim = sb2("Vim", [64, 512])
            Vpr = sb2("Vpr", [64, 8, 128]); Vpi = sb2("Vpi", [64, 8, 128])
            memset("pool", Vpr[:], 0.0, ["Vp"])
            memset("pool", Vpi[:], 0.0, ["Vp"])
            V3r = Vre[:, :].rearrange("p (g c) -> p g c", c=16)
            V3i = Vim[:, :].rearrange("p (g c) -> p g c", c=16)
            dgr = AP(Vpr, 0, [[8 * 128, 64], [144, 8], [1, 16]])
            dgi = AP(Vpi, 0, [[8 * 128, 64], [144, 8], [1, 16]])
            for k in range(L):
                cmul(V3r, V3i, bc16(pwr, k), bc16(pwi, k), bbr[:], bbi[:], u1[:], u2[:])
                for t in range(4):
                    T.op("dve", lambda e: e.tensor_copy(
                        out=dgr, in_=Vre[:, t * 128:(t + 1) * 128].rearrange("p (g c) -> p g c", c=16)), RW, ["Vp"])
                    T.op("dve", lambda e: e.tensor_copy(
                        out=dgi, in_=Vim[:, t * 128:(t + 1) * 128].rearrange("p (g c) -> p g c", c=16)), RW, ["Vp"])
                    bk = bank(False)
                    for gl in range(8):
                        g = 8 * t + gl
                        mm(ps[bk][:, 16 * gl:16 * gl + 16], Vpr[:, gl, :], CTr[:, g * 16:(g + 1) * 16], True, False,
                           ["Vp"] + RW, [("ps", bk)])
                        mm(ps[bk][:, 16 * gl:16 * gl + 16], Vpi[:, gl, :], nCTi[:, g * 16:(g + 1) * 16], False, True,
                           ["Vp"] + RW, [("ps", bk)])
                    if k == 0:
                        stt(Kbd[:, t, k, :], ident_f[:, :], dvec[:, t:t + 1], ps[bk][:, 0:128], ALU.mult, ALU.add,
                            [("ps", bk), "ident_f", "dvec"], ["Kbd"])
                    else:
                        cp("dve", Kbd[:, t, k, :], ps[bk][:, 0:128], [("ps", bk)], ["Kbd"])
                s = L - 1 - k
                for ri, Vx in ((0, Vre), (1, Vim)):
                    bk = bank(False)
                    for t in range(4):
                        mm(ps[bk][:, t * 64:(t + 1) * 64], Vx[:, t * 128:(t + 1) * 128], ident_f[0:64, 0:64], True, True,
                           RW + ["ident_f"], [("ps", bk)])
                    for t in range(4):
                        tt("dve", Wt[:, t, s, ri, :].rearrange("p (m q) -> p m q", m=2),
                           AP(ps[bk], t * 64, [[512, 128], [0, 2], [1, 64]]),
                           AP(wmask, 0, [[2, 128], [1, 2], [0, 64]]), ALU.mult,
                           [("ps", bk), "wmask"], ["Wt"])
            EVr = sb2("EVr", [64, 512]); EVi = sb2("EVi", [64, 512])
            EVm = [sb2(f"EVm{m}", [64, 512]) for m in range(2)]
            E3r = EVr[:, :].rearrange("p (g c) -> p g c", c=16)
            E3i = EVi[:, :].rearrange("p (g c) -> p g c", c=16)
            CT3r = CTr[:, :].rearrange("p (g c) -> p g c", c=16)
            CT3i = CTi[:, :].rearrange("p (g c) -> p g c", c=16)
            for i in range(L):
                cmul(E3r, E3i, bc16(pwr, i + 1), bc16(pwi, i + 1), CT3r, CT3i, u1[:], u2[:])
                for ri, EV in ((0, EVr), (1, EVi)):
                    for m in range(2):
                        tt("dve", EVm[m][:], EV[:], parm[:, m, :], ALU.mult, RW + ["parm"], ["EVm"])
                    bk = bank(False)
                    mm(ps[bk][:, :], dup[:, 0, :], EVm[0][:], True, False, ["dup", "EVm"], [("ps", bk)])
                    mm(ps[bk][:, :], dup[:, 1, :], EVm[1][:], False, True, ["dup", "EVm"], [("ps", bk)])
                    T.op("act", lambda e, bk=bk, ri=ri, i=i: e.activation(
                        out=Et[:, :, i, ri, :], in_=ps[bk][:, :].rearrange("p (a b) -> p a b", b=32),
                        func=AF.Copy, scale=(1.0 if ri == 0 else -1.0)), [("ps", bk)], ["Et"])
            r1 = t64("r1"); uc = t64("uc"); us = t64("us")
            tt("dve", r1[:], mag[:], mag[:], ALU.mult, RW, RW)
            tt("dve", r1[:], r1[:], r1[:], ALU.mult, RW, RW)
            cp("dve", uc[:], wC[:], RW, RW)
            cp("dve", us[:], wS[:], RW, RW)
            for _ in range(2):
                tt("dve", cc[:], uc[:], uc[:], ALU.mult, RW, RW)
                tt("dve", ss_[:], us[:], us[:], ALU.mult, RW, RW)
                tt("dve", cs[:], uc[:], us[:], ALU.mult, RW, RW)
                tt("dve", uc[:], cc[:], ss_[:], ALU.subtract, RW, RW)
                ts("dve", us[:], cs[:], 2.0, None, ALU.mult, ALU.bypass, RW, RW)
            bk = bank(False)
            for i3, src in enumerate((r1, uc, us)):
                for m in range(2):
                    mm(ps[bk][:, i3 * 16:(i3 + 1) * 16], dup[:, m, :], AP(src, m, [[32, 64], [2, 16]]), m == 0, m == 1,
                       RW + ["dup"], [("ps", bk)])
            pwa = sb2("pwa", [128, 16]); pwb = sb2("pwb", [128, 16])
            x1 = sb2("x1", [128, 16, 32]); x2 = sb2("x2", [128, 16, 32])
            y1 = sb2("y1", [128, 16]); y2 = sb2("y2", [128, 16]); y3 = sb2("y3", [128, 16])
            AK = ["Atab"]
            cp("dve", rtab[:, :], ps[bk][:, 0:16], [("ps", bk)], AK)
            cp("dve", pwa[:, :], ps[bk][:, 16:32], [("ps", bk)], AK)
            cp("dve", pwb[:, :], ps[bk][:, 32:48], [("ps", bk)], AK)
            cp("dve", cosT[:, :, 0], pwa[:, :], AK, AK)
            cp("dve", sinT[:, :, 0], pwb[:, :], AK, AK)
            k = 1
            while k < NJ:
                pa = AP(pwa, 0, [[16, 128], [1, 16], [0, k]])
                pb = AP(pwb, 0, [[16, 128], [1, 16], [0, k]])
                tt("dve", x1[:, :, 0:k], cosT[:, :, 0:k], pa, ALU.mult, AK, AK)
                tt("dve", x2[:, :, 0:k], sinT[:, :, 0:k], pb, ALU.mult, AK, AK)
                tt("dve", cosT[:, :, k:2 * k], x1[:, :, 0:k], x2[:, :, 0:k], ALU.subtract, AK, AK)
                tt("dve", x1[:, :, 0:k], cosT[:, :, 0:k], pb, ALU.mult, AK, AK)
                tt("dve", x2[:, :, 0:k], sinT[:, :, 0:k], pa, ALU.mult, AK, AK)
                tt("dve", sinT[:, :, k:2 * k], x1[:, :, 0:k], x2[:, :, 0:k], ALU.add, AK, AK)
                tt("dve", y1[:], pwa[:], pwa[:], ALU.mult, AK, AK)
                tt("dve", y2[:], pwb[:], pwb[:], ALU.mult, AK, AK)
                tt("dve", y3[:], pwa[:], pwb[:], ALU.mult, AK, AK)
                tt("dve", pwa[:], y1[:], y2[:], ALU.subtract, AK, AK)
                ts("dve", pwb[:], y3[:], 2.0, None, ALU.mult, ALU.bypass, AK, AK)
                k *= 2
            memset("dve", Rt[:, :, :], 0.0, AK)
            cp("dve", Rt[:, :, 1:NJ], AP(rtab, 0, [[16, 128], [1, 16], [0, NJ - 1]]), AK, AK)
            for hh in range(2):
                sl = slice(hh * 32, (hh + 1) * 32)
                tt("dve", x1[:, :, :], cosT[:, :, sl], cosT[:, :, sl], ALU.mult, AK, AK)
                tt("dve", x2[:, :, :], sinT[:, :, sl], sinT[:, :, sl], ALU.mult, AK, AK)
                tt("dve", x1[:, :, :], x1[:, :, :], x2[:, :, :], ALU.add, AK, AK)
                ts("dve", x1[:, :, :], x1[:, :, :], -0.5, 1.5, ALU.mult, ALU.add, AK, AK)
                tt("dve", cosT[:, :, sl], cosT[:, :, sl], x1[:, :, :], ALU.mult, AK, AK)
                tt("dve", sinT[:, :, sl], sinT[:, :, sl], x1[:, :, :], ALU.mult, AK, AK)

        T.barrier()
        pinned.clear()
        if DBG.get("dump"):
            d_kbd = nc.dram_tensor("d_kbd", [128, 4 * L * 128], BF16, kind="ExternalOutput").ap()
            d_wt = nc.dram_tensor("d_wt", [128, 4 * L * 2 * 128], BF16, kind="ExternalOutput").ap()
            d_et = nc.dram_tensor("d_et", [128, 16 * L * 2 * 32], BF16, kind="ExternalOutput").ap()
            d_a = nc.dram_tensor("d_a", [128, 16 + 2 * 16 * NJ], F32, kind="ExternalOutput").ap()
            T.dma(d_kbd, Kbd[:, :, :, :].rearrange("p a b c -> p (a b c)"), "dbg", reads=["Kbd"])
            T.dma(d_wt, Wt[:, :, :, :, :].rearrange("p a b c d -> p (a b c d)"), "dbg", reads=["Wt"])
            T.dma(d_et, Et[:, :, :, :, :].rearrange("p a b c d -> p (a b c d)"), "dbg", reads=["Et"])
            T.dma(d_a[:, 0:16], rtab[:, :], "dbg", reads=["Atab"])
            T.dma(d_a[:, 16:16 + 16 * NJ], cosT[:, :, :].rearrange("p a b -> p (a b)"), "dbg", reads=["Atab"])
            T.dma(d_a[:, 16 + 16 * NJ:], sinT[:, :, :].rearrange("p a b -> p (a b)"), "dbg", reads=["Atab"])
        if DBG.get("setup_only"):
            T.finish("sp")
            build_nc.stats = (T.n_ops, T.n_waits, len(T.sems))
            return nc
        hnb = [sb(f"hnb{i}", [128, D], BF16) for i in range(1)]
        hnT = sb("hnT", [128, 8, NB], BF16)
        mixT = sb("mixT", [128, 8, NB], BF16)
        rot = [sb(f"rot{i}", [128, NT, 192], F32) for i in range(2)]
        rt1 = sb("rt1", [128, 512], F32)
        rt2 = sb("rt2", [128, 512], F32)
        qrot2 = [sb(f"qrot{i}", [128, NT, 512], BF16) for i in range(2)]
        krot2 = [sb(f"krot{i}", [128, NT, 512], BF16) for i in range(2)]
        ktd2 = [sb(f"ktd{i}", [128, NT, 512], BF16) for i in range(2)]
        vtok2 = [sb(f"vtok{i}", [128, NT, 512], BF16) for i in range(2)]
        qT = sb("qT", [128, 4, NB], BF16)
        kT = sb("kT", [128, 4, NB], BF16)
        qdT = sb("qdT", [128, 4, NB], BF16)
        sgret2 = [sb(f"sgret{i}", [128, 4, NB], BF16) for i in range(2)]
        uT2 = [sb(f"uT{i}", [128, 4, NB], BF16) for i in range(2)]
        sgssm2 = [sb(f"sgssm{i}", [128, 4, NB], BF16) for i in range(2)]
        c_sb = sb("c_sb", [128, 2, 16, NJ], F32)
        Hprev2 = [sb("Hprev0", [128, 2, 16, NJ], BF16)] * 2
        carry = sb("carry", [128, 2, 16], F32)
        ta = sb("ta", [128, 2, 4, NJ], F32)
        tb = sb("tb", [128, 2, 4, NJ], F32)
        hfin = sb("hfin", [128, NSEQ_S, 2, 16], F32)
        st3 = sb("st3", [128, 2, 16], F32)
        sTb = [sb(f"sTb{i}", [128, 256], BF16) for i in range(2)]
        Smaster = sb("Smaster", [128, 4, HD], F32)
        Sprev = [sb(f"Sprev{i}", [128, 4, HD], BF16) for i in range(2)]
        osq = sb("osq", [128, 4 * NB], BF16)
        rsd = sb("rsd", [128, 512], F32)
        ot = sb("ot", [128, 512], F32)
        zT = sb("zT", [128, 4, NB], BF16)
        g1 = [rsd[:, 0:NB]] * 2
        g2 = [ot[:, 0:NB]] * 2
        ssq = sb("ssq", [128, 8], F32)
        rstd = sb("rstd", [128, 8], F32)


        memset("pool", sTb[0][:], 0.0, ["sT0"])
        memset("pool", sTb[1][:], 0.0, ["sT1"])
        memset("pool", gconst[:], 1.0 / 0.044715, ["gconst"])
        xs_ctr = [0]
        LQ = DBG.get("lq", "act")

        def next_xslot():
            i = xs_ctr[0] % NXS
            xs_ctr[0] += 1
            return i

        chunk_ctr = [0]
        blk_ctr = [0]

        def interleave2(g1, g2, n1, n2):
            c1 = c2 = 0
            d1 = d2 = False
            while not (d1 and d2):
                if d2 or (not d1 and c1 / n1 <= c2 / n2):
                    try:
                        next(g1); c1 += 1
                    except StopIteration:
                        d1 = True
                else:
                    try:
                        next(g2); c2 += 1
                    except StopIteration:
                        d2 = True
                yield

        def issue_x_load(cfg, t_, xi):
            XK = f"xslot{xi}"
            if cfg["sample"]:
                memset("pool", xslot[xi][:], 0.0, [XK])
                for half in range(2):
                    s_ = 2 * t_ + half
                    T.dma(xslot[xi][64 * half:64 * half + TS, :], xs[s_ * TS:(s_ + 1) * TS, :], f"ld_xs{xi}", writes=[XK])
            else:
                r0 = cfg["seq"] * SEQ + cfg["blk"] * NB
                T.dma(xslot[xi][:], xp[r0 + t_ * 128: r0 + (t_ + 1) * 128, :], f"ld_xs{xi}", writes=[XK])

        def issue_rot_load(cfg, rslot):
            RK = f"rot{rslot}"
            cfg["rslot"] = rslot
            if cfg["sample"]:
                for t_ in range(NT):
                    T.dma(rot[rslot][:, t_, :], crot_s, f"ld_rot{rslot}", writes=[RK])
            else:
                pos0 = cfg["blk"] * NB
                T.dma(rot[rslot][:, :, :], AP(crot_p.tensor, pos0 * 192, [[192, 128], [128 * 192, NT], [1, 192]]),
                      f"ld_rot{rslot}", writes=[RK])

        def _hdr(cfg, pb):
            sample = cfg["sample"]
            cf = 1 if sample else 0
            seq = cfg.get("seq", 0)
            blk = cfg.get("blk", 0)
            row0 = seq * SEQ + blk * NB
            gam = [1.0 - 2.0 ** (-5.0 - h) for h in range(4)]
            sdec = [g ** (16 if sample else 64) for g in gam]

            qrot, krot, ktd, vtok = qrot2[pb], krot2[pb], ktd2[pb], vtok2[pb]
            sgret, uT, sgssm, Hprev = sgret2[pb], uT2[pb], sgssm2[pb], Hprev2[pb]
            return locals()

        def phaseA(cfg, pb):
            L_ = _hdr(cfg, pb)
            sample, cf, seq, blk, row0 = L_['sample'], L_['cf'], L_['seq'], L_['blk'], L_['row0']
            qrot, krot, ktd, vtok = L_['qrot'], L_['krot'], L_['ktd'], L_['vtok']
            sgret, uT, sgssm, Hprev = L_['sgret'], L_['uT'], L_['sgssm'], L_['Hprev']
            rslot = cfg["rslot"]
            RK = f"rot{rslot}"

            for t_ in range(NT):
                xi = cfg["xslots"][t_]
                XK = f"xslot{xi}"
                hb = hnb[0]
                HK = "hnb0"
                act(hb[:], xslot[xi][:], AF.Square, [XK], [HK, "ssqA"], accum_out=ssq[:, t_:t_ + 1])
                act(ssq[:, t_:t_ + 1], ssq[:, t_:t_ + 1], AF.Ln, ["ssqA"], ["ssqA"], scale=1.0 / D, bias=EPS)
                act(rstd[:, t_:t_ + 1], ssq[:, t_:t_ + 1], AF.Exp, ["ssqA"], ["rstdA"], scale=-0.5)
                act(hb[:], xslot[xi][:], AF.Copy, [XK, "rstdA"], [HK], scale=rstd[:, t_:t_ + 1])
                bk = bank()
                for kt in range(8):
                    tr(psb[bk][:, kt * 128:(kt + 1) * 128], hb[:, kt * 128:(kt + 1) * 128], [HK, "ident_b"], [("ps", bk)])
                cp("act", hnT[:, :, t_ * 128:(t_ + 1) * 128], psb[bk][:, :].rearrange("p (k t) -> p k t", k=8),
                   [("ps", bk)], ["hnT"])
                rel(bk)
                yield

            if DBG.get('stop', 99) <= 1:
                return
            for t_ in range(NT):
                cosb = AP(rot[rslot], t_ * 192, [[NT * 192, 128], [0, 4], [0, 2], [1, 64]])
                sinb = AP(rot[rslot], t_ * 192 + 64, [[NT * 192, 128], [0, 4], [64, 2], [1, 64]])
                for (c0, dst, dkey) in ((0, qrot, f"qrot{pb}"), (512, krot, f"krot{pb}"), (1024, None, None)):
                    bb = bank()
                    for kt in range(8):
                        mm(ps[bb][:, :], hnT[:, kt, t_ * 128:(t_ + 1) * 128], win_b[:, kt, c0:c0 + 512], kt == 0, kt == 7,
                           ["hnT", "win_b"], [("ps", bb)])
                    yield
                    if dst is None:
                        cp("act", vtok[:, t_, :], ps[bb][:, :], [("ps", bb)], [f"vtok{pb}"])
                        rel(bb)
                        yield
                        continue
                    pv = AP(ps[bb], 0, [[512, 128], [128, 4], [64, 2], [1, 64]])
                    psw = AP(ps[bb], 64, [[512, 128], [128, 4], [-64, 2], [1, 64]])
                    tt("dve", rt1[:, :].rearrange("p (h a d) -> p h a d", h=4, a=2), pv, cosb, ALU.mult,
                       [("ps", bb), RK], ["rt1"])
                    tt("dve", rt2[:, :].rearrange("p (h a d) -> p h a d", h=4, a=2), psw, sinb, ALU.mult,
                       [("ps", bb), RK], ["rt2"])
                    rel(bb)
                    tt("dve", dst[:, t_, :], rt1[:, :], rt2[:, :], ALU.add, ["rt1", "rt2"], [dkey])
                    yield
                for h in range(4):
                    act(ktd[:, t_, h * 128:(h + 1) * 128], krot[:, t_, h * 128:(h + 1) * 128], AF.Copy,
                        [f"krot{pb}", "kdec"], [f"ktd{pb}"], scale=kdec[:, cf, h:h + 1])
                yield

            if DBG.get('stop', 99) <= 2:
                return
            for m2 in range(6):
                bk = bank()
                for half in range(2):
                    m = 2 * m2 + half
                    c0 = 1536 + m * 128
                    for kt in range(8):
                        mm(ps[bk][:, half * NB:(half + 1) * NB], win_b[:, kt, c0:c0 + 128], hnT[:, kt, :], kt == 0, kt == 7,
                           ["hnT", "win_b"], [("ps", bk)])
                    yield
                for half in range(2):
                    m = 2 * m2 + half
                    src = ps[bk][:, half * NB:(half + 1) * NB]
                    if m < 4:
                        act(sgret[:, m, :], src, AF.Silu, [("ps", bk)], [f"sgret{pb}"])
                    elif m < 8:
                        cp("act", uT[:, m - 4, :], src, [("ps", bk)], [f"uT{pb}"])
                    else:
                        act(sgssm[:, m - 8, :], src, AF.Silu, [("ps", bk)], [f"sgssm{pb}"])
                rel(bk)
                yield

            nxt2 = cfg.get("next2")
            if nxt2 is not None:
                issue_rot_load(nxt2, rslot)
            yield "SPLIT"

        def s5_front(cfg, pb):
            L_ = _hdr(cfg, pb)
            sample, blk = L_['sample'], L_['blk']
            uT = L_['uT']
            cfg["front_done"] = True
            c5 = c_sb[:, :, :, :].rearrange("p r (t q) j -> p r t q j", q=4)
            for q in range(4):
                bk = bank()
                for ri in range(2):
                    for t in range(4):
                        gi = ri * 4 + t
                        for s in range(L):
                            mm(ps[bk][:, gi * NJ:(gi + 1) * NJ], Wt[32 * q:32 * q + 32, t, s, ri, :],
                               uT[32 * q:32 * q + 32, t, :].rearrange("p (j s) -> p j s", s=L)[:, :, s],
                               s == 0, s == L - 1, [f"uT{pb}", "Wt"], [("ps", bk)], tile_position=(32 * q, 0))
                cp("dve", c5[:, :, :, q, :], ps[bk][:, :].rearrange("p (r t j) -> p r t j", r=2, t=4),
                   [("ps", bk)], ["c_sb"])
                rel(bk)
                yield
            if sample:
                return
            CS = 2 * 16 * NJ
            n = NJ
            if blk == 0:
                memset("pool", carry[:], 0.0, ["carry"])
            for ph in range(4):
                Ps = slice(4 * ph, 4 * ph + 4)
                cv = c_sb[:, :, Ps, :]
                cvsw = AP(c_sb, 16 * NJ + 4 * ph * NJ, [[CS, 128], [-16 * NJ, 2], [NJ, 4], [1, n]])
                cosb = AP(cosT, 4 * ph * NJ, [[16 * NJ, 128], [0, 2], [NJ, 4], [1, n]])
                sinb = AP(sinT, 4 * ph * NJ, [[16 * NJ, 128], [0, 2], [NJ, 4], [1, n]])
                tt("dve", ta[:, :, :, :], cv, cosb, ALU.mult, ["c_sb", "Atab"], ["ta"])
                tt("dve", tb[:, :, :, :], cvsw, sinb, ALU.mult, ["c_sb", "Atab"], ["tb"])
                tt("dve", c_sb[:, 0, Ps, :], ta[:, 0, :, :], tb[:, 0, :, :], ALU.add, ["ta", "tb"], ["c_sb"])
                tt("dve", c_sb[:, 1, Ps, :], ta[:, 1, :, :], tb[:, 1, :, :], ALU.subtract, ["ta", "tb"], ["c_sb"])
                yield
            tt("dve", st3[:, :, :], carry[:, :, :], AP(rtab, 0, [[16, 128], [0, 2], [1, 16]]), ALU.mult,
               ["carry", "Atab"], ["st3"])
            tt("dve", c_sb[:, :, :, 0], c_sb[:, :, :, 0], st3[:, :, :], ALU.add, ["c_sb", "st3"], ["c_sb"])
            yield

        def phaseB(cfg, pb):
            L_ = _hdr(cfg, pb)
            sample, cf, seq, blk, row0 = L_['sample'], L_['cf'], L_['seq'], L_['blk'], L_['row0']
            qrot, krot, ktd, vtok = L_['qrot'], L_['krot'], L_['ktd'], L_['vtok']
            sgret, uT, sgssm, Hprev = L_['sgret'], L_['uT'], L_['sgssm'], L_['Hprev']
            def gen_scan():
                if DBG.get('stop', 99) <= 3:
                    return
                if not cfg.get("front_done"):
                    yield from s5_front(cfg, pb)
                if DBG.get('stop', 99) <= 4:
                    return
                CS = 2 * 16 * NJ

                def seg_scan(j0, n, init, ikey, fin_j, fin_out, fkey):
                    for ph in range(4):
                        Ps = slice(4 * ph, 4 * ph + 4)
                        cv = c_sb[:, :, Ps, j0:j0 + n]
                        cvsw = AP(c_sb, 16 * NJ + 4 * ph * NJ + j0, [[CS, 128], [-16 * NJ, 2], [NJ, 4], [1, n]])
                        cosb = AP(cosT, 4 * ph * NJ, [[16 * NJ, 128], [0, 2], [NJ, 4], [1, n]])
                        sinb = AP(sinT, 4 * ph * NJ, [[16 * NJ, 128], [0, 2], [NJ, 4], [1, n]])
                        tav = ta[:, :, :, 0:n]
                        tbv = tb[:, :, :, 0:n]
                        tt("dve", tav, cv, cosb, ALU.mult, ["c_sb", "Atab"], ["ta"])
                        tt("dve", tbv, cvsw, sinb, ALU.mult, ["c_sb", "Atab"], ["tb"])
                        tt("dve", c_sb[:, 0, Ps, j0:j0 + n], ta[:, 0, :, 0:n], tb[:, 0, :, 0:n], ALU.add, ["ta", "tb"], ["c_sb"])
                        tt("dve", c_sb[:, 1, Ps, j0:j0 + n], ta[:, 1, :, 0:n], tb[:, 1, :, 0:n], ALU.subtract, ["ta", "tb"], ["c_sb"])
                        yield
                        for ri in range(2):
                            for P in range(4 * ph, 4 * ph + 4):
                                row = c_sb[:, ri, P, j0:j0 + n]
                                T.op("dve", lambda e, row=row, P=P, ri=ri: e.tensor_tensor_scan(
                                    out=row, data0=AP(rtab, P, [[16, 128], [0, n]]), data1=row,
                                    initial=init[:, ri, P:P + 1], op0=ALU.mult, op1=ALU.add),
                                    ["c_sb", "Atab", ikey], [("c_row", ri, P)])
                            yield
                        rows = [("c_row", ri, P) for ri in range(2) for P in range(4 * ph, 4 * ph + 4)]
                        tt("dve", tav, cv, cosb, ALU.mult, ["c_sb", "Atab"] + rows, ["ta", "c_sb"])
                        tt("dve", tbv, cvsw, sinb, ALU.mult, ["c_sb", "Atab"] + rows, ["tb", "c_sb"])
                        if n > 1:
                            tt("dve", Hprev[:, 0, Ps, j0 + 1:j0 + n], ta[:, 0, :, 0:n - 1], tb[:, 0, :, 0:n - 1], ALU.subtract,
                               ["ta", "tb"], ["Hprev"])
                            tt("dve", Hprev[:, 1, Ps, j0 + 1:j0 + n], ta[:, 1, :, 0:n - 1], tb[:, 1, :, 0:n - 1], ALU.add,
                               ["ta", "tb"], ["Hprev"])
                        cp("dve", Hprev[:, :, Ps, j0], init[:, :, Ps], [ikey], ["Hprev"])
                        tt("dve", fin_out[:, 0, Ps], ta[:, 0, :, fin_j], tb[:, 0, :, fin_j], ALU.subtract, ["ta", "tb"], [fkey])
                        tt("dve", fin_out[:, 1, Ps], ta[:, 1, :, fin_j], tb[:, 1, :, fin_j], ALU.add, ["ta", "tb"], [fkey])
                        yield

                def big_scan():
                    n = NJ
                    tcv = rsd[:, :].rearrange("p (a b c) -> p a b c", a=2, b=4)
                    tdv = ot[:, :].rearrange("p (a b c) -> p a b c", a=2, b=4)
                    cp("dve", Hprev[:, :, :, 0], carry[:, :, :], ["carry"], ["Hprev"])
                    for ri in range(2):
                        row = c_sb[:, ri, :, :].rearrange("p a b -> p (a b)")
                        T.op("dve", lambda e, row=row: e.tensor_tensor_scan(
                            out=row, data0=Rt[:, :, :].rearrange("p a b -> p (a b)"), data1=row,
                            initial=0.0, op0=ALU.mult, op1=ALU.add), ["c_sb", "Atab"], ["c_sb"])
                        yield
                    for ph in range(4):
                        Ps = slice(4 * ph, 4 * ph + 4)
                        cv = c_sb[:, :, Ps, :]
                        cvsw = AP(c_sb, 16 * NJ + 4 * ph * NJ, [[CS, 128], [-16 * NJ, 2], [NJ, 4], [1, n]])
                        cosb = AP(cosT, 4 * ph * NJ, [[16 * NJ, 128], [0, 2], [NJ, 4], [1, n]])
                        sinb = AP(sinT, 4 * ph * NJ, [[16 * NJ, 128], [0, 2], [NJ, 4], [1, n]])
                        tt("dve", tcv, cv, cosb, ALU.mult, ["c_sb", "Atab"], ["rsd"])
                        tt("dve", tdv, cvsw, sinb, ALU.mult, ["c_sb", "Atab"], ["ot"])
                        tt("dve", Hprev[:, 0, Ps, 1:n], tcv[:, 0, :, 0:n - 1], tdv[:, 0, :, 0:n - 1], ALU.subtract,
                           ["rsd", "ot"], ["Hprev"])
                        tt("dve", Hprev[:, 1, Ps, 1:n], tcv[:, 1, :, 0:n - 1], tdv[:, 1, :, 0:n - 1], ALU.add,
                           ["rsd", "ot"], ["Hprev"])
                        tt("dve", carry[:, 0, Ps], tcv[:, 0, :, n - 1], tdv[:, 0, :, n - 1], ALU.subtract, ["rsd", "ot"], ["carry"])
                        tt("dve", carry[:, 1, Ps], tcv[:, 1, :, n - 1], tdv[:, 1, :, n - 1], ALU.add, ["rsd", "ot"], ["carry"])
                        yield

                if sample:
                    for s_ in range(NSEQ_S):
                        yield from seg_scan(s_ * 16, 16, h0s[:, s_, :, :], "h0s", TS // L - 1, hfin[:, s_, :, :], "hfin")
                    for s_ in range(NSEQ_S):
                        T.dma(AP(hre_s.tensor, s_ * 2048, [[1, 128], [128, 16]]), hfin[:, s_, 0, :], f"st_hs{s_}", reads=["hfin"])
                        T.dma(AP(him_s.tensor, s_ * 2048, [[1, 128], [128, 16]]), hfin[:, s_, 1, :], f"st_hs{s_}b", reads=["hfin"])
                else:
                    yield from big_scan()
                    if blk == NBLK_SEQ - 1:
                        T.dma(AP(hre_p.tensor, seq * 2048, [[1, 128], [128, 16]]), carry[:, 0, :], "st_hp", reads=["carry"])
                        T.dma(AP(him_p.tensor, seq * 2048, [[1, 128], [128, 16]]), carry[:, 1, :], "st_hpb", reads=["carry"])


            def gen_ret_tr():
                for (src, skey, dst, key, eng) in ((qrot, f"qrot{pb}", qT, "qT", "act"), (krot, f"krot{pb}", kT, "kT", "dve")):
                    bk = bank()
                    for h in range(4):
                        for t_ in range(NT):
                            tr(psb[bk][:, h * NB + t_ * 128: h * NB + (t_ + 1) * 128], src[:, t_, h * 128:(h + 1) * 128],
                               [skey, "ident_b"], [("ps", bk)])
                    cp(eng, dst[:, :, :].rearrange("p h n -> p (h n)"), psb[bk][:, :], [("ps", bk)], [key])
                    rel(bk)
                    yield
                tt("dve", qdT[:, :, :].rearrange("p h (c l) -> p h c l", l=64),
                   qT[:, :, :].rearrange("p h (c l) -> p h c l", l=64),
                   AP(qdec, cf * 256, [[512, 128], [64, 4], [0, NC], [1, 64]]), ALU.mult, ["qT", "qdec"], ["qdT"])


            def gen_ret():
                if DBG.get('stop', 99) <= 5:
                    return
                if DBG.get('stop', 99) <= 6:
                    return
                if (not sample) and blk == 0:
                    par0 = chunk_ctr[0] % 2
                    memset("pool", Smaster[:], 0.0, ["Smaster"])
                    memset("pool", Sprev[par0][:], 0.0, [f"Sprev{par0}"])
                bo = [bank(), bank()]
                chunk_par = []
                for c in range(NC):
                    t_ = c // 2
                    base = 64 * (c % 2)
                    par = chunk_ctr[0] % 2
                    chunk_ctr[0] += 1
                    chunk_par.append(par)
                    if sample:
                        T.dma(Smaster[:, :, :], AP(sret.tensor, c * 4 * HD * HD, [[HD, 128], [HD * HD, 4], [1, HD]]),
                              "ld_S", writes=["Smaster"])
                        cp("act", Sprev[par][:], Smaster[:], ["Smaster"], [f"Sprev{par}"])
                    bs = bank()
                    for h in range(4):
                        mm(ps[bs][base:base + 64, h * 64:(h + 1) * 64], kT[:, h, c * 64:(c + 1) * 64], qT[:, h, c * 64:(c + 1) * 64],
                           True, True, ["kT", "qT"], [("ps", bs)])
                    sT = sTb[c % 2]
                    yield
                    tt("dve", sT[base:base + 64, :], ps[bs][base:base + 64, 0:256], maskT[base:base + 64, cf, :], ALU.mult,
                       [("ps", bs), "maskT"], [f"sT{c % 2}"])
                    rel(bs)
                    bkv = bank()
                    for h in range(4):
                        mm(ps[bkv][:, h * 128:(h + 1) * 128], ktd[base:base + 64, t_, h * 128:(h + 1) * 128],
                           vtok[base:base + 64, t_, h * 128:(h + 1) * 128], True, True, [f"ktd{pb}", f"vtok{pb}"], [("ps", bkv)])
                    yield
                    for h in range(4):
                        ob = bo[h // 2]
                        oc = (h % 2) * NB + c * 64
                        mm(ps[ob][:, oc:oc + 64], vtok[:, t_, h * 128:(h + 1) * 128],
                           sT[:, h * 64:(h + 1) * 64], True, False, [f"vtok{pb}", f"sT{c % 2}"], [("ps", ob)])
                        mm(ps[ob][:, oc:oc + 64], Sprev[par][:, h, :], qdT[:, h, c * 64:(c + 1) * 64], False, True,
                           [f"Sprev{par}", "qdT"], [("ps", ob)])
                    yield
                    for h in range(4):
                        stt(Smaster[:, h, :], Smaster[:, h, :], sdec_t[:, cf, h:h + 1], ps[bkv][:, h * 128:(h + 1) * 128],
                            ALU.mult, ALU.add, [("ps", bkv), "Smaster", "sdec_t"], ["Smaster"])
                    rel(bkv)
                    yield
                    npar = 1 - par
                    if sample:
                        T.dma(AP(rs.tensor, c * 4 * HD * HD, [[HD, 128], [HD * HD, 4], [1, HD]]), Smaster[:, :, :], "st_rs",
                              reads=["Smaster"])
                    else:
                        cp("act", Sprev[npar][:], Smaster[:], ["Smaster"], [f"Sprev{npar}"])
                        if blk == NBLK_SEQ - 1 and c == NC - 1:
                            T.dma(AP(rp.tensor, seq * 4 * HD * HD, [[HD, 128], [HD * HD, 4], [1, HD]]), Smaster[:, :, :],
                                  "st_rp", reads=["Smaster"])

                if DBG.get('stop', 99) <= 7:
                    return
                bn = [bank(), bank()]
                for i2 in range(2):
                    act(osq[:, i2 * 512:(i2 + 1) * 512], ps[bo[i2]][:, :], AF.Square, [("ps", bo[i2])], ["osq"])
                    mm(ps[bn[i2]][:, :], ones_b[:, :], osq[:, i2 * 512:(i2 + 1) * 512], True, True, ["osq", "ones_b"],
                       [("ps", bn[i2])])
                    yield
                for i2 in range(2):
                    act(rsd[:, :], ps[bn[i2]][:, :], AF.Ln, [("ps", bn[i2])], ["rsd"], scale=1.0 / HD, bias=EPS)
                    rel(bn[i2])
                    act(rsd[:, :], rsd[:, :], AF.Exp, ["rsd"], ["rsd"], scale=-0.5)
                    tt("dve", ot[:, :], ps[bo[i2]][:, :], rsd[:, :], ALU.mult, [("ps", bo[i2]), "rsd"], ["ot"])
                    rel(bo[i2])
                    for hh in range(2):
                        h = 2 * i2 + hh
                        stt(mixT[:, h, :], ot[:, hh * NB:(hh + 1) * NB], rng_[:, h:h + 1], sgret[:, h, :], ALU.mult, ALU.mult,
                            ["ot", "rng", f"sgret{pb}"], ["mixT"])
                    yield


            yield from gen_scan()
            yield from gen_ret_tr()
            yield "SPLIT"
            yield from gen_ret()

            if DBG.get('stop', 99) <= 8:
                return
            for t2 in range(2):
                bk = bank()
                for half in range(2):
                    t = 2 * t2 + half
                    yv = ps[bk][:, half * NB:(half + 1) * NB].rearrange("p (j s) -> p j s", s=L)
                    uv = uT[:, t, :].rearrange("p (j s) -> p j s", s=L)
                    for i in range(L):
                        for k in range(i + 1):
                            mm(yv[:, :, i], Kbd[:, t, k, :], uv[:, :, i - k], k == 0, False, ["Kbd", f"uT{pb}"], [("ps", bk)])
                        for q in range(4):
                            P = 4 * t + q
                            for ri in range(2):
                                last = (ri == 1)
                                mm(ps[bk][32 * q:32 * q + 32, half * NB:(half + 1) * NB].rearrange("p (j s) -> p j s", s=L)[:, :, i],
                                   Et[:, P, i, ri, :], Hprev[:, ri, P, :], False, last, ["Et", "Hprev"], [("ps", bk)],
                                   tile_position=(0, 32 * q))
                        yield
                for half in range(2):
                    t = 2 * t2 + half
                    ysrc = ps[bk][:, half * NB:(half + 1) * NB]
                    ga = g1[half]
                    GK = "rsd"
                    act(ga[:], ysrc, AF.Square, [("ps", bk)], [GK])
                    stt(ga[:], ga[:], gconst[:, 0:1], ysrc, ALU.add, ALU.mult, [GK, ("ps", bk), "gconst"], [GK])
                    act(ga[:], ga[:], AF.Sigmoid, [GK], [GK], scale=1.5957691216057308 * 0.044715)
                    tt("dve", zT[:, t, :], ga[:], ysrc, ALU.mult, [GK, ("ps", bk)], ["zT"])
                    if half == 1:
                        rel(bk)
                    yield

            if DBG.get('stop', 99) <= 9:
                return
            for m2 in range(2):
                bk = bank()
                for half in range(2):
                    m = 2 * m2 + half
                    for kt in range(4):
                        mm(ps[bk][:, half * NB:(half + 1) * NB], wglu_b[:, kt, m * 128:(m + 1) * 128], zT[:, kt, :],
                           kt == 0, kt == 3, ["wglu_b", "zT"], [("ps", bk)])
                    yield
                for half in range(2):
                    m = 2 * m2 + half
                    gb = g2[half]
                    GK = "ot"
                    act(gb[:], ps[bk][:, half * NB:(half + 1) * NB], AF.Sigmoid, [("ps", bk), "bglu"], [GK],
                        bias=bglu[:, m:m + 1])
                    if half == 1:
                        rel(bk)
                    tt("dve", gb[:], gb[:], zT[:, m, :], ALU.mult, [GK, "zT"], [GK])
                    tt("dve", mixT[:, 4 + m, :], gb[:], sgssm[:, m, :], ALU.mult, [GK, f"sgssm{pb}"], ["mixT"])
                    yield

            if DBG.get('stop', 99) <= 10:
                return
            for t_ in range(NT):
                xi = cfg["xslots"][t_]
                XK = f"xslot{xi}"
                for half in range(2):
                    bk = bank()
                    for kt in range(8):
                        mm(ps[bk][:, :], mixT[:, kt, t_ * 128:(t_ + 1) * 128], wout_b[:, kt, half * 512:(half + 1) * 512],
                           kt == 0, kt == 7, ["mixT", "wout_b"], [("ps", bk)])
                    tt("dve", xslot[xi][:, half * 512:(half + 1) * 512], ps[bk][:, :], xslot[xi][:, half * 512:(half + 1) * 512],
                       ALU.add, [("ps", bk), XK], [XK])
                    rel(bk)
                    yield
                c_ = 4 + t_
                act(osq[:, :], xslot[xi][:], AF.Square, [XK], ["osq", "ssq"], accum_out=ssq[:, c_:c_ + 1])
                act(ssq[:, c_:c_ + 1], ssq[:, c_:c_ + 1], AF.Ln, ["ssq"], ["ssq"], scale=1.0 / D, bias=EPS)
                act(rstd[:, c_:c_ + 1], ssq[:, c_:c_ + 1], AF.Exp, ["ssq"], ["rstd"], scale=-0.5)
                stt(xslot[xi][:], xslot[xi][:], rstd[:, c_:c_ + 1], gfin[:], ALU.mult, ALU.mult, [XK, "rstd", "gfin"], [XK])
                if sample:
                    for half in range(2):
                        s_ = 2 * t_ + half
                        T.dma(ys[s_ * TS:(s_ + 1) * TS, :], xslot[xi][64 * half:64 * half + TS, :], f"st_y{xi}", reads=[XK])
                else:
                    T.dma(yp[row0 + t_ * 128: row0 + (t_ + 1) * 128, :], xslot[xi][:], f"st_y{xi}", reads=[XK])
                nxt2 = cfg.get("next2")
                if nxt2 is not None:
                    nxt2.setdefault("xslots", [None] * NT)[t_] = xi
                    issue_x_load(nxt2, t_, xi)
                yield


        blocks = []
        if DBG.get("sample", True):
            blocks.append(dict(sample=True))
        for seq in range(NSEQ_P):
            for blk in range(DBG.get("nblk", NBLK_SEQ)):
                blocks.append(dict(sample=False, seq=seq, blk=blk))
        for i, c_ in enumerate(blocks):
            if i + 2 < len(blocks):
                c_["next2"] = blocks[i + 2]
        for i, c_ in enumerate(blocks[:2]):
            c_["xslots"] = [2 * i, 2 * i + 1]
            for t_ in range(NT):
                issue_x_load(c_, t_, 2 * i + t_)
            issue_rot_load(c_, i)

        def drain(g):
            n = 0
            for _ in g:
                n += 1
            return n

        if DBG.get("nopipe"):
            for i, c_ in enumerate(blocks):
                drain(phaseA(c_, i % 2))
                drain(phaseB(c_, i % 2))
        else:
            est = {"a1": 39.0, "a2": 9.0, "b1": 12.0, "b2": 50.0}
            drain(phaseA(blocks[0], 0))
            pre_a = {}
            for i in range(1, len(blocks) + 1):
                gb = phaseB(blocks[i - 1], (i - 1) % 2)
                if i in pre_a:
                    ga = pre_a.pop(i)
                else:
                    ga = phaseA(blocks[i], i % 2) if i < len(blocks) else iter(())
                for stage in (1, 2):
                    if stage == 2 and i < len(blocks) and not blocks[i]["sample"] and not DBG.get("nofront"):
                        ga = s5_front(blocks[i], i % 2)
                    na, nb_ = est[f"a{stage}"], est[f"b{stage}"]
                    ca = cb = 0
                    da = db = False
                    while not (da and db):
                        if db or (not da and ca / na <= cb / nb_):
                            try:
                                r = next(ga)
                                if r == "SPLIT" and stage == 1:
                                    da = True
                                else:
                                    ca += 1
                            except StopIteration:
                                da = True
                        else:
                            try:
                                r = next(gb)
                                if r == "SPLIT" and stage == 1:
                                    db = True
                                else:
                                    cb += 1
                            except StopIteration:
                                db = True
                    if ca > 0:
                        est[f"a{stage}"] = float(ca)
                    if cb > 0:
                        est[f"b{stage}"] = float(cb)
                if i + 1 < len(blocks) and not DBG.get("nopre"):
                    gn = phaseA(blocks[i + 1], (i + 1) % 2)
                    for _ in range(NT):
                        next(gn)
                    pre_a[i + 1] = gn
        T.finish("sp")
        build_nc.stats = (T.n_ops, T.n_waits, len(T.sems))
    return nc


def _constants():
    c = {}
    c["cident"] = np.eye(128, dtype=np.float32)
    half = 64
    inv = 10000.0 ** (-np.arange(half, dtype=np.float64) / half)

    def rot_tab(pos):
        ang = pos[:, None].astype(np.float64) * inv[None, :]
        cos, sin = np.cos(ang), np.sin(ang)
        return np.concatenate([cos, -sin, sin], axis=1).astype(np.float32)

    c["crot_p"] = rot_tab(np.arange(SEQ))
    rs_ = np.zeros((128, 192), np.float32)
    tab = rot_tab(PAST + np.arange(TS))
    rs_[0:TS] = tab
    rs_[64:64 + TS] = tab
    c["crot_s"] = rs_
    gam = np.array([1.0 - 2.0 ** (-5.0 - h) for h in range(4)], dtype=np.float64)
    scale = HD ** -0.5
    cmask = np.zeros((2, 128, 4, 64), np.float64)
    cq = np.zeros((2, 4, 64), np.float64)
    ck = np.zeros((2, 128, 4), np.float64)
    for cfg, blk in ((0, 64), (1, TS)):
        for h in range(4):
            for m in range(blk):
                for l in range(m, blk):
                    cmask[cfg, m, h, l] = gam[h] ** (l - m) * scale
                    cmask[cfg, 64 + m, h, l] = gam[h] ** (l - m) * scale
            for l in range(blk):
                cq[cfg, h, l] = gam[h] ** (l + 1)
                ck[cfg, l, h] = gam[h] ** (blk - 1 - l) * scale
                ck[cfg, 64 + l, h] = gam[h] ** (blk - 1 - l) * scale
    c["cmask"] = cmask.reshape(256, 256).astype(np.float32)
    c["cqdec"] = cq.reshape(2, 256).astype(np.float32)
    c["ckdec"] = ck.reshape(256, 4).astype(np.float32)
    dup = np.zeros((64, 2, 2, 64), np.float32)
    for m in range(2):
        dup[np.arange(64), m, m, np.arange(64)] = 1.0
    c["cdup"] = dup.reshape(64, 256)
    par = np.zeros((2, 32, 16), np.float32)
    for m in range(2):
        par[m, m::2, :] = 1.0
    c["cpar"] = par.reshape(2, 512)
    wm = np.zeros((8, 16, 2), np.float32)
    for gl in range(8):
        wm[gl, :, gl % 2] = 1.0
    c["cwmask"] = wm.reshape(128, 2)
    c["csdec"] = np.array([[g ** 64 for g in gam] + [g ** TS for g in gam]], dtype=np.float32)
    return c


_NC_CACHE = {}


def kernel(x_prompt, x_sample, state_ret, state_ssm_re, state_ssm_im, norm_g, w_in, ret_norm_g,
           ssm_lambda_re, ssm_lambda_im, ssm_log_step, ssm_b_re, ssm_b_im, ssm_c_re, ssm_c_im,
           ssm_d, w_glu, b_glu, w_out, final_norm_g):
    f = lambda a: np.ascontiguousarray(np.asarray(a, dtype=np.float32))
    x_prompt = f(x_prompt); x_sample = f(x_sample)
    consts = _constants()
    shared = {
        "norm_g": f(norm_g).reshape(1, D), "w_in": f(w_in).reshape(D, 3072),
        "ret_norm_g": f(ret_norm_g).reshape(4, HD),
        "lre": f(ssm_lambda_re).reshape(32, 64), "lim": f(ssm_lambda_im).reshape(32, 64),
        "lstep": f(ssm_log_step).reshape(1, 32),
        "bre": f(ssm_b_re).reshape(2048, 16), "bim": f(ssm_b_im).reshape(2048, 16),
        "cre": f(ssm_c_re).reshape(512, 64), "cim": f(ssm_c_im).reshape(512, 64),
        "ssm_d": f(ssm_d).reshape(1, 512), "w_glu": f(w_glu).reshape(512, 512),
        "b_glu": f(b_glu).reshape(1, 512), "w_out": f(w_out).reshape(D, D),
        "fng": f(final_norm_g).reshape(1, D),
    }
    shared.update(consts)
    sr = f(state_ret)[0]; s_re = f(state_ssm_re)[0]; s_im = f(state_ssm_im)[0]
    in_maps = []
    for c in range(8):
        m = dict(shared)
        m["xp"] = x_prompt[c * NSEQ_P:(c + 1) * NSEQ_P].reshape(NSEQ_P * SEQ, D)
        m["xs"] = x_sample[c * NSEQ_S:(c + 1) * NSEQ_S].reshape(NSEQ_S * TS, D)
        m["sret"] = sr[c * NSEQ_S:(c + 1) * NSEQ_S].reshape(NSEQ_S * 4 * HD, HD)
        m["sre"] = s_re[c * NSEQ_S:(c + 1) * NSEQ_S].reshape(NSEQ_S, 2048)
        m["sim"] = s_im[c * NSEQ_S:(c + 1) * NSEQ_S].reshape(NSEQ_S, 2048)
        in_maps.append(m)
    if "nc" not in _NC_CACHE:
        _NC_CACHE["nc"] = build_nc()
    nc = _NC_CACHE["nc"]
    res = run_bass_kernel_spmd(nc, in_maps, core_ids=list(range(8)))
    R = res.results
    cat = lambda k: np.concatenate([np.asarray(r[k], dtype=np.float32) for r in R], axis=0)
    y_prompt = cat("yp").reshape(16, SEQ, D)
    y_sample = cat("ys").reshape(32, TS, D)
    ret_p = cat("rp").reshape(1, 16, 4, HD, HD)
    hre_p = cat("hre_p").reshape(1, 16, 32, 64)
    him_p = cat("him_p").reshape(1, 16, 32, 64)
    ret_s = cat("rs").reshape(1, 32, 4, HD, HD)
    hre_s = cat("hre_s").reshape(1, 32, 32, 64)
    him_s = cat("him_s").reshape(1, 32, 32, 64)
    return (y_prompt, y_sample, ret_p, hre_p, him_p, ret_s, hre_s, him_s)
```

```python
import math
from contextlib import ExitStack

import numpy as np
import concourse.bass as bass
import concourse.mybir as mybir
from concourse.bass_utils import run_bass_kernel_spmd

F32 = mybir.dt.float32
BF16 = mybir.dt.bfloat16
ALU = mybir.AluOpType
AF = mybir.ActivationFunctionType

D = 1024
SEQ = 4096
NSEQ_P = 2
NSEQ_S = 4
TS = 16
PAST = 2048
NB = 256
NT = 2
NBLK_SEQ = SEQ // NB
L = 4
NJ = NB // L
NC = NB // 64
EPS = 1e-6
HD = 128
STRICT = True
DBG = {}


class Trk:
    def __init__(self, nc, es):
        self.nc = nc
        self.es = es
        self.eng = {"pe": nc.tensor, "act": nc.scalar, "dve": nc.vector, "pool": nc.gpsimd, "sp": nc.sync}
        self.sems = {}
        self.cnt = {}
        for e in ("pe", "act", "dve", "pool"):
            self.sems[e] = es.enter_context(nc.semaphore("sem_" + e))
            self.cnt[e] = 0
        self.known = {e: {} for e in self.eng}
        self.last_w = {}
        self.readers = {}
        self.n_ops = 0
        self.n_waits = 0

    def _deps(self, reads, writes):
        deps = {}

        def add(s, v):
            if v > deps.get(s, 0):
                deps[s] = v

        for k in reads:
            lw = self.last_w.get(k)
            if lw:
                add(*lw)
        for k in writes:
            lw = self.last_w.get(k)
            if lw:
                add(*lw)
            for s, v in self.readers.get(k, {}).items():
                add(s, v)
        return deps

    def _wait(self, e, deps, own=None):
        for s, v in deps.items():
            if s == own and (own == "pe" or not STRICT):
                continue
            if self.known[e].get(s, 0) >= v:
                continue
            self.eng[e].wait_ge(self.sems[s], v)
            self.known[e][s] = v
            self.n_waits += 1

    def op(self, e, fn, reads=(), writes=()):
        deps = self._deps(reads, writes)
        self._wait(e, deps, own=e)
        ins = fn(self.eng[e])
        self.cnt[e] += 1
        n = self.cnt[e]
        ins.then_inc(self.sems[e], 1)
        for k in reads:
            self.readers.setdefault(k, {})[e] = n
        for k in writes:
            self.last_w[k] = (e, n)
            self.readers[k] = {}
        self.n_ops += 1
        return ins

    def dma(self, out, in_, sem, reads=(), writes=(), q="sp", **kw):
        if sem not in self.sems:
            self.sems[sem] = self.es.enter_context(self.nc.semaphore("d_" + sem))
            self.cnt[sem] = 0
        deps = self._deps(reads, writes)
        if sem == "ld_setup":
            deps.pop(sem, None)
        self._wait(q, deps, own=None)
        ins = self.eng[q].dma_start(out=out, in_=in_, **kw)
        self.cnt[sem] += 16
        n = self.cnt[sem]
        ins.then_inc(self.sems[sem], 16)
        for k in reads:
            self.readers.setdefault(k, {})[sem] = n
        for k in writes:
            self.last_w[k] = (sem, n)
            self.readers[k] = {}
        self.n_ops += 1
        return ins

    def fence_group(self, sem):
        tot = self.cnt[sem]
        for k, lw in list(self.last_w.items()):
            if lw[0] == sem:
                self.last_w[k] = (sem, tot)

    def barrier(self):
        for e in self.eng:
            for s_, c in self.cnt.items():
                if s_ == e or c == 0:
                    continue
                if self.known[e].get(s_, 0) < c:
                    self.eng[e].wait_ge(self.sems[s_], c)
                    self.known[e][s_] = c

    def finish(self, e="sp"):
        for s, c in self.cnt.items():
            if s in ("pe", "act", "dve", "pool"):
                continue
            if c > 0 and self.known[e].get(s, 0) < c:
                self.eng[e].wait_ge(self.sems[s], c)
                self.known[e][s] = c


def AP(t, off, dims):
    return bass.AP(t, off, [list(d) for d in dims])


def build_nc():
    nc = bass.Bass("TRN2", target_bir_lowering=False)

    def din(name, shape):
        return nc.dram_tensor(name, list(shape), F32, kind="ExternalInput").ap()

    def dout(name, shape):
        return nc.dram_tensor(name, list(shape), F32, kind="ExternalOutput").ap()

    xp = din("xp", [NSEQ_P * SEQ, D])
    xs = din("xs", [NSEQ_S * TS, D])
    sret = din("sret", [NSEQ_S * 4 * HD, HD])
    sre = din("sre", [NSEQ_S, 2048])
    sim = din("sim", [NSEQ_S, 2048])
    norm_g = din("norm_g", [1, D])
    w_in = din("w_in", [D, 3072])
    ret_norm_g = din("ret_norm_g", [4, HD])
    lre = din("lre", [32, 64])
    lim = din("lim", [32, 64])
    lstep = din("lstep", [1, 32])
    bre = din("bre", [2048, 16])
    bim = din("bim", [2048, 16])
    cre = din("cre", [512, 64])
    cim = din("cim", [512, 64])
    ssm_d = din("ssm_d", [1, 512])
    w_glu = din("w_glu", [512, 512])
    b_glu = din("b_glu", [1, 512])
    w_out = din("w_out", [D, D])
    fng = din("fng", [1, D])
    cident = din("cident", [128, 128])
    crot_p = din("crot_p", [SEQ, 192])
    crot_s = din("crot_s", [128, 192])
    cmask = din("cmask", [2 * 128, 256])
    cqdec = din("cqdec", [2, 256])
    ckdec = din("ckdec", [2 * 128, 4])
    cdup = din("cdup", [64, 256])
    cpar = din("cpar", [2, 512])
    cwmask = din("cwmask", [128, 2])
    csdec = din("csdec", [1, 8])

    yp = dout("yp", [NSEQ_P * SEQ, D])
    ys = dout("ys", [NSEQ_S * TS, D])
    rp = dout("rp", [NSEQ_P * 4 * HD, HD])
    hre_p = dout("hre_p", [NSEQ_P, 2048])
    him_p = dout("him_p", [NSEQ_P, 2048])
    rs = dout("rs", [NSEQ_S * 4 * HD, HD])
    hre_s = dout("hre_s", [NSEQ_S, 2048])
    him_s = dout("him_s", [NSEQ_S, 2048])

    with ExitStack() as es:
        es.enter_context(nc.allow_non_contiguous_dma(reason="small parameter layouts"))
        T = Trk(nc, es)

        def sb(name, shape, dt=F32):
            return es.enter_context(nc.sbuf_tensor(name, list(shape), dt))

        def tap(name, ap_, shape, keys, dt=F32):
            if not DBG.get("taps"):
                return
            d = nc.dram_tensor("tap_" + name, list(shape), dt, kind="ExternalOutput").ap()
            T.dma(d, ap_, "dbg", reads=keys)

        win_b = sb("win_b", [128, 8, 3072], BF16)
        wout_b = sb("wout_b", [128, 8, 1024], BF16)
        wglu_b = sb("wglu_b", [128, 4, 512], BF16)
        ident_f = sb("ident_f", [128, 128], F32)
        ident_b = sb("ident_b", [128, 128], BF16)
        ones_b = sb("ones_b", [128, 128], BF16)
        ng = sb("ng", [128, 8], F32)
        rng_ = sb("rng", [128, 4], F32)
        bglu = sb("bglu", [128, 4], F32)
        dvec = sb("dvec", [128, 4], F32)
        gfin = sb("gfin", [128, D], F32)
        maskT = sb("maskT", [128, 2, 256], F32)
        qdec = sb("qdec", [128, 2, 256], F32)
        kdec = sb("kdec", [128, 2, 4], F32)
        sdec_t = sb("sdec_t", [128, 2, 4], F32)
        Kbd = sb("Kbd", [128, 4, L, 128], BF16)
        Wt = sb("Wt", [128, 4, L, 2, 128], BF16)
        Et = sb("Et", [128, 16, L, 2, 32], BF16)
        cosT = sb("cosT", [128, 16, NJ], F32)
        sinT = sb("sinT", [128, 16, NJ], F32)
        rtab = sb("rtab", [128, 16], F32)
        Rt = sb("Rt", [128, 16, NJ], F32)
        h0s = sb("h0s", [128, NSEQ_S, 2, 16], F32)

        NXS = 4
        xslot = [sb(f"xslot{i}", [128, D], F32) for i in range(NXS)]

        ps = [es.enter_context(nc.psum_tensor(f"ps{i}", [128, 512], F32)) for i in range(8)]
        psb = [p.bitcast(BF16) for p in ps]
        bank_ctr = [0]

        pinned = set()

        def bank(pin=True):
            for _ in range(9):
                b = bank_ctr[0] % 8
                bank_ctr[0] += 1
                if b not in pinned:
                    break
            else:
                raise RuntimeError("out of PSUM banks")
            if pin:
                pinned.add(b)
            return b

        def rel(*bs):
            for b in bs:
                pinned.discard(b)

        def act(out, in_, func, reads, writes, **kw):
            return T.op("act", lambda e: e.activation(out=out, in_=in_, func=func, **kw), reads, writes)

        def tt(eng, out, in0, in1, op, reads, writes):
            return T.op(eng, lambda e: e.tensor_tensor(out=out, in0=in0, in1=in1, op=op), reads, writes)

        def ts(eng, out, in0, s1, s2, op0, op1, reads, writes):
            return T.op(eng, lambda e: e.tensor_scalar(out=out, in0=in0, scalar1=s1, scalar2=s2, op0=op0, op1=op1),
                        reads, writes)

        def stt(out, in0, scalar, in1, op0, op1, reads, writes):
            return T.op("dve", lambda e: e.scalar_tensor_tensor(out=out, in0=in0, scalar=scalar, in1=in1,
                                                                 op0=op0, op1=op1), reads, writes)

        def cp(eng, out, in_, reads, writes):
            if eng == "act":
                return act(out, in_, AF.Copy, reads, writes)
            return T.op(eng, lambda e: e.tensor_copy(out=out, in_=in_), reads, writes)

        def mm(out, lhsT, rhs, start, stop, reads, writes, **kw):
            return T.op("pe", lambda e: e.matmul(out, lhsT=lhsT, rhs=rhs, start=start, stop=stop, **kw), reads, writes)

        def tr(out, in_, reads, writes):
            return T.op("pe", lambda e: e.transpose(out=out, in_=in_, identity=ident_b[:]), reads, writes)

        def memset(eng, ap_, val, writes):
            return T.op(eng, lambda e: e.memset(ap_, val), (), writes)

        T.dma(ident_f[:], cident, "ld_setup", writes=["ident_f"])
        T.dma(ng[:], AP(norm_g.tensor, 0, [[1, 128], [128, 8]]), "ld_setup", writes=["ng"])
        T.dma(rng_[:], AP(ret_norm_g.tensor, 0, [[1, 128], [128, 4]]), "ld_setup", writes=["rng"])
        T.dma(bglu[:], AP(b_glu.tensor, 0, [[1, 128], [128, 4]]), "ld_setup", writes=["bglu"])
        T.dma(dvec[:], AP(ssm_d.tensor, 0, [[1, 128], [128, 4]]), "ld_setup", writes=["dvec"])
        T.dma(gfin[:], AP(fng.tensor, 0, [[0, 128], [1, D]]), "ld_setup", writes=["gfin"])
        T.dma(sdec_t[:, :, :], AP(csdec.tensor, 0, [[0, 128], [4, 2], [1, 4]]), "ld_setup", writes=["sdec_t"])
        for c in range(2):
            T.dma(maskT[:, c, :], cmask[c * 128:(c + 1) * 128, :], "ld_setup", writes=["maskT"])
            T.dma(qdec[:, c, :], AP(cqdec.tensor, c * 256, [[0, 128], [1, 256]]), "ld_setup", writes=["qdec"])
            T.dma(kdec[:, c, :], ckdec[c * 128:(c + 1) * 128, :], "ld_setup", writes=["kdec"])

        with ExitStack() as es2:
            def sb2(name, shape, dt=F32):
                return es2.enter_context(nc.sbuf_tensor(name, list(shape), dt))

            K1 = "s5"
            lre1 = sb2("lre1", [64, 32]); lim1 = sb2("lim1", [64, 32]); dt1 = sb2("dt1", [64, 32])
            T.dma(lre1[:], AP(lre.tensor, 0, [[1, 64], [64, 32]]), "ld_setup", writes=["lre1"])
            T.dma(lim1[:], AP(lim.tensor, 0, [[1, 64], [64, 32]]), "ld_setup", writes=["lim1"])
            T.dma(dt1[:], AP(lstep.tensor, 0, [[0, 64], [1, 32]]), "ld_setup", writes=["dt1"])
            b1re = sb2("b1re", [64, 32, 16]); b1im = sb2("b1im", [64, 32, 16])
            T.dma(b1re[:], AP(bre.tensor, 0, [[16, 64], [1024, 32], [1, 16]]), "ld_setup", writes=["b1re"])
            T.dma(b1im[:], AP(bim.tensor, 0, [[16, 64], [1024, 32], [1, 16]]), "ld_setup", writes=["b1im"])
            cnre = sb2("cnre", [128, 4, 64]); cnim = sb2("cnim", [128, 4, 64])
            T.dma(cnre[:], AP(cre.tensor, 0, [[64, 128], [128 * 64, 4], [1, 64]]), "ld_setup", writes=["cnre"])
            T.dma(cnim[:], AP(cim.tensor, 0, [[64, 128], [128 * 64, 4], [1, 64]]), "ld_setup", writes=["cnim"])
            dup = sb2("dup", [64, 2, 128])
            T.dma(dup[:], cdup, "ld_setup", writes=["dup"])
            parm = sb2("parm", [64, 2, 512])
            T.dma(parm[:], AP(cpar.tensor, 0, [[0, 64], [512, 2], [1, 512]]), "ld_setup", writes=["parm"])
            wmask = sb2("wmask", [128, 2])
            T.dma(wmask[:], cwmask, "ld_setup", writes=["wmask"])
            for s in range(NSEQ_S):
                T.dma(h0s[:, s, 0, :], AP(sre.tensor, s * 2048, [[1, 128], [128, 16]]), "ld_setup", writes=["h0s"])
                T.dma(h0s[:, s, 1, :], AP(sim.tensor, s * 2048, [[1, 128], [128, 16]]), "ld_setup", writes=["h0s"])

            T.fence_group("ld_setup")
            cp("act", ident_b[:], ident_f[:], ["ident_f"], ["ident_b"])
            memset("pool", ones_b[:], 1.0, ["ones_b"])
            cast_engs = ["act", "dve", "pool"]
            ci = [0]

            def load_cast(dst_ap, src_ap, ncols, wkey, scale_ap=None):
                i = ci[0] % NXS
                e = cast_engs[ci[0] % 3]
                ci[0] += 1
                T.dma(xslot[i][:, 0:ncols], src_ap, f"ld_xs{i}", writes=[f"xslot{i}"])
                if scale_ap is None:
                    cp(e, dst_ap, xslot[i][:, 0:ncols], [f"xslot{i}"], [wkey])
                elif e == "act":
                    act(dst_ap, xslot[i][:, 0:ncols], AF.Copy, [f"xslot{i}", "ng"], [wkey], scale=scale_ap)
                else:
                    ts(e, dst_ap, xslot[i][:, 0:ncols], scale_ap, None, ALU.mult, ALU.bypass, [f"xslot{i}", "ng"], [wkey])

            for kt in range(8):
                for c in range(3):
                    load_cast(win_b[:, kt, c * 1024:(c + 1) * 1024], w_in[kt * 128:(kt + 1) * 128, c * 1024:(c + 1) * 1024],
                              1024, "win_b", scale_ap=ng[:, kt:kt + 1])
            for kt in range(8):
                load_cast(wout_b[:, kt, :], w_out[kt * 128:(kt + 1) * 128, :], 1024, "wout_b")
            for kt in range(4):
                load_cast(wglu_b[:, kt, :], w_glu[kt * 128:(kt + 1) * 128, :], 512, "wglu_b")


            def t64(name):
                return sb2(name, [64, 32])

            lrc = t64("lrc"); dtv = t64("dtv"); are = t64("are"); aim = t64("aim"); mag = t64("mag")
            th = t64("th"); th2 = t64("th2"); wS = t64("wS"); wC = t64("wC")
            cc = t64("cc"); ss_ = t64("ss_"); cs = t64("cs")
            RW = [K1]
            ts("dve", lrc[:], lre1[:], -1e-4, None, ALU.min, ALU.bypass, ["lre1"] + RW, RW)
            act(dtv[:], dt1[:], AF.Exp, ["dt1"] + RW, RW)
            tt("dve", are[:], lrc[:], dtv[:], ALU.mult, RW, RW)
            tt("dve", aim[:], lim1[:], dtv[:], ALU.mult, ["lim1"] + RW, RW)
            act(mag[:], are[:], AF.Exp, RW, RW)
            ts("dve", th[:], aim[:], 1.0 / 32.0, None, ALU.mult, ALU.bypass, RW, RW)
            tt("dve", th2[:], th[:], th[:], ALU.mult, RW, RW)
            a = [-1.0 / 6, 1.0 / 120, -1.0 / 5040, 1.0 / 362880]
            ts("dve", wS[:], th2[:], a[3], None, ALU.mult, ALU.bypass, RW, RW)
            for k in (2, 1, 0):
                ts("dve", wS[:], wS[:], a[k], None, ALU.add, ALU.bypass, RW, RW)
                tt("dve", wS[:], wS[:], th2[:], ALU.mult, RW, RW)
            ts("dve", wS[:], wS[:], 1.0, None, ALU.add, ALU.bypass, RW, RW)
            tt("dve", wS[:], wS[:], th[:], ALU.mult, RW, RW)
            b = [-0.5, 1.0 / 24, -1.0 / 720, 1.0 / 40320, -1.0 / 3628800]
            ts("dve", wC[:], th2[:], b[4], None, ALU.mult, ALU.bypass, RW, RW)
            for k in (3, 2, 1, 0):
                ts("dve", wC[:], wC[:], b[k], None, ALU.add, ALU.bypass, RW, RW)
                tt("dve", wC[:], wC[:], th2[:], ALU.mult, RW, RW)
            ts("dve", wC[:], wC[:], 1.0, None, ALU.add, ALU.bypass, RW, RW)
            for _ in range(5):
                tt("dve", cc[:], wC[:], wC[:], ALU.mult, RW, RW)
                tt("dve", ss_[:], wS[:], wS[:], ALU.mult, RW, RW)
                tt("dve", cs[:], wC[:], wS[:], ALU.mult, RW, RW)
                tt("dve", wC[:], cc[:], ss_[:], ALU.subtract, RW, RW)
                ts("dve", wS[:], cs[:], 2.0, None, ALU.mult, ALU.bypass, RW, RW)
            pwr = sb2("pwr", [64, L + 1, 32]); pwi = sb2("pwi", [64, L + 1, 32])
            memset("dve", pwr[:, 0, :], 1.0, RW)
            memset("dve", pwi[:, 0, :], 0.0, RW)
            tt("dve", pwr[:, 1, :], mag[:], wC[:], ALU.mult, RW, RW)
            tt("dve", pwi[:, 1, :], mag[:], wS[:], ALU.mult, RW, RW)
            tA = t64("tA"); tB = t64("tB")

            def cmul(o_re, o_im, a_re, a_im, b_re, b_im, t1_, t2_, xr=()):
                R_ = RW + list(xr)
                tt("dve", t1_, a_re, b_re, ALU.mult, R_, RW)
                tt("dve", t2_, a_im, b_im, ALU.mult, R_, RW)
                tt("dve", o_re, t1_, t2_, ALU.subtract, R_, RW)
                tt("dve", t1_, a_re, b_im, ALU.mult, R_, RW)
                tt("dve", t2_, a_im, b_re, ALU.mult, R_, RW)
                tt("dve", o_im, t1_, t2_, ALU.add, R_, RW)

            for k in range(1, L):
                cmul(pwr[:, k + 1, :], pwi[:, k + 1, :], pwr[:, k, :], pwi[:, k, :], pwr[:, 1, :], pwi[:, 1, :],
                     tA[:], tB[:])
            nre = t64("nre"); den = t64("den"); qre = t64("qre"); qim = t64("qim")
            ts("dve", nre[:], pwr[:, 1, :], -1.0, None, ALU.add, ALU.bypass, RW, RW)
            tt("dve", den[:], lrc[:], lrc[:], ALU.mult, RW, RW)
            tt("dve", tA[:], lim1[:], lim1[:], ALU.mult, RW, RW)
            tt("dve", den[:], den[:], tA[:], ALU.add, RW, RW)
            T.op("dve", lambda e: e.reciprocal(out=den[:], in_=den[:]), RW, RW)
            tt("dve", tA[:], nre[:], lrc[:], ALU.mult, RW, RW)
            tt("dve", tB[:], pwi[:, 1, :], lim1[:], ALU.mult, RW, RW)
            tt("dve", qre[:], tA[:], tB[:], ALU.add, RW, RW)
            tt("dve", qre[:], qre[:], den[:], ALU.mult, RW, RW)
            tt("dve", tA[:], pwi[:, 1, :], lrc[:], ALU.mult, RW, RW)
            tt("dve", tB[:], nre[:], lim1[:], ALU.mult, RW, RW)
            tt("dve", qim[:], tA[:], tB[:], ALU.subtract, RW, RW)
            tt("dve", qim[:], qim[:], den[:], ALU.mult, RW, RW)

            tap("lre1", lre1[:], [64, 32], ["lre1"]); tap("dt1", dt1[:], [64, 32], ["dt1"])
            tap("mag", mag[:], [64, 32], RW); tap("wC", wC[:], [64, 32], RW); tap("wS", wS[:], [64, 32], RW)
            tap("pwr", pwr[:, :, :].rearrange("p a b -> p (a b)"), [64, (L + 1) * 32], RW)
            tap("pwi", pwi[:, :, :].rearrange("p a b -> p (a b)"), [64, (L + 1) * 32], RW)
            tap("qre", qre[:], [64, 32], RW); tap("qim", qim[:], [64, 32], RW)

            def bc16(t, k=None):
                if k is None:
                    return AP(t, 0, [[32, 64], [1, 32], [0, 16]])
                return AP(t, k * 32, [[(L + 1) * 32, 64], [1, 32], [0, 16]])

            bbr = sb2("bbr", [64, 32, 16]); bbi = sb2("bbi", [64, 32, 16])
            u1 = sb2("u1", [64, 32, 16]); u2 = sb2("u2", [64, 32, 16])
            cmul(bbr[:], bbi[:], bc16(qre), bc16(qim), b1re[:], b1im[:], u1[:], u2[:], xr=["b1re", "b1im"])

            CTr = sb2("CTr", [64, 512]); CTi = sb2("CTi", [64, 512]); nCTi = sb2("nCTi", [64, 512])
            for (src, dst, key) in ((cnre, CTr, "cnre"), (cnim, CTi, "cnim")):
                bk = bank(False)
                for t in range(4):
                    mm(ps[bk][0:64, t * 128:(t + 1) * 128], src[:, t, :], ident_f[:, :], True, True,
                       [key, "ident_f"], [("ps", bk)])
                cp("dve", dst[:], ps[bk][0:64, :], [("ps", bk)], RW)
            ts("dve", nCTi[:], CTi[:], -1.0, None, ALU.mult, ALU.bypass, RW, RW)

            tap("bbr", bbr[:, :, :].rearrange("p a b -> p (a b)"), [64, 512], RW)
            tap("CTr", CTr[:], [64, 512], RW); tap("nCTi", nCTi[:], [64, 512], RW)
            Vre = sb2("Vre", [64, 512]); Vim = sb2("Vim", [64, 512])
            Vpr = sb2("Vpr", [64, 8, 128]); Vpi = sb2("Vpi", [64, 8, 128])
            memset("pool", Vpr[:], 0.0, ["Vp"])
            memset("pool", Vpi[:], 0.0, ["Vp"])
            V3r = Vre[:, :].rearrange("p (g c) -> p g c", c=16)
            V3i = Vim[:, :].rearrange("p (g c) -> p g c", c=16)
            dgr = AP(Vpr, 0, [[8 * 128, 64], [144, 8], [1, 16]])
            dgi = AP(Vpi, 0, [[8 * 128, 64], [144, 8], [1, 16]])
            for k in range(L):
                cmul(V3r, V3i, bc16(pwr, k), bc16(pwi, k), bbr[:], bbi[:], u1[:], u2[:])
                for t in range(4):
                    T.op("dve", lambda e: e.tensor_copy(
                        out=dgr, in_=Vre[:, t * 128:(t + 1) * 128].rearrange("p (g c) -> p g c", c=16)), RW, ["Vp"])
                    T.op("dve", lambda e: e.tensor_copy(
                        out=dgi, in_=Vim[:, t * 128:(t + 1) * 128].rearrange("p (g c) -> p g c", c=16)), RW, ["Vp"])
                    bk = bank(False)
                    for gl in range(8):
                        g = 8 * t + gl
                        mm(ps[bk][:, 16 * gl:16 * gl + 16], Vpr[:, gl, :], CTr[:, g * 16:(g + 1) * 16], True, False,
                           ["Vp"] + RW, [("ps", bk)])
                        mm(ps[bk][:, 16 * gl:16 * gl + 16], Vpi[:, gl, :], nCTi[:, g * 16:(g + 1) * 16], False, True,
                           ["Vp"] + RW, [("ps", bk)])
                    if k == 0:
                        stt(Kbd[:, t, k, :], ident_f[:, :], dvec[:, t:t + 1], ps[bk][:, 0:128], ALU.mult, ALU.add,
                            [("ps", bk), "ident_f", "dvec"], ["Kbd"])
                    else:
                        cp("dve", Kbd[:, t, k, :], ps[bk][:, 0:128], [("ps", bk)], ["Kbd"])
                s = L - 1 - k
                for ri, Vx in ((0, Vre), (1, Vim)):
                    bk = bank(False)
                    for t in range(4):
                        mm(ps[bk][:, t * 64:(t + 1) * 64], Vx[:, t * 128:(t + 1) * 128], ident_f[0:64, 0:64], True, True,
                           RW + ["ident_f"], [("ps", bk)])
                    for t in range(4):
                        tt("dve", Wt[:, t, s, ri, :].rearrange("p (m q) -> p m q", m=2),
                           AP(ps[bk], t * 64, [[512, 128], [0, 2], [1, 64]]),
                           AP(wmask, 0, [[2, 128], [1, 2], [0, 64]]), ALU.mult,
                           [("ps", bk), "wmask"], ["Wt"])
            EVr = sb2("EVr", [64, 512]); EVi = sb2("EVi", [64, 512])
            EVm = [sb2(f"EVm{m}", [64, 512]) for m in range(2)]
            E3r = EVr[:, :].rearrange("p (g c) -> p g c", c=16)
            E3i = EVi[:, :].rearrange("p (g c) -> p g c", c=16)
            CT3r = CTr[:, :].rearrange("p (g c) -> p g c", c=16)
            CT3i = CTi[:, :].rearrange("p (g c) -> p g c", c=16)
            for i in range(L):
                cmul(E3r, E3i, bc16(pwr, i + 1), bc16(pwi, i + 1), CT3r, CT3i, u1[:], u2[:])
                for ri, EV in ((0, EVr), (1, EVi)):
                    for m in range(2):
                        tt("dve", EVm[m][:], EV[:], parm[:, m, :], ALU.mult, RW + ["parm"], ["EVm"])
                    bk = bank(False)
                    mm(ps[bk][:, :], dup[:, 0, :], EVm[0][:], True, False, ["dup", "EVm"], [("ps", bk)])
                    mm(ps[bk][:, :], dup[:, 1, :], EVm[1][:], False, True, ["dup", "EVm"], [("ps", bk)])
                    T.op("act", lambda e, bk=bk, ri=ri, i=i: e.activation(
                        out=Et[:, :, i, ri, :], in_=ps[bk][:, :].rearrange("p (a b) -> p a b", b=32),
                        func=AF.Copy, scale=(1.0 if ri == 0 else -1.0)), [("ps", bk)], ["Et"])
            r1 = t64("r1"); uc = t64("uc"); us = t64("us")
            tt("dve", r1[:], mag[:], mag[:], ALU.mult, RW, RW)
            tt("dve", r1[:], r1[:], r1[:], ALU.mult, RW, RW)
            cp("dve", uc[:], wC[:], RW, RW)
            cp("dve", us[:], wS[:], RW, RW)
            for _ in range(2):
                tt("dve", cc[:], uc[:], uc[:], ALU.mult, RW, RW)
                tt("dve", ss_[:], us[:], us[:], ALU.mult, RW, RW)
                tt("dve", cs[:], uc[:], us[:], ALU.mult, RW, RW)
                tt("dve", uc[:], cc[:], ss_[:], ALU.subtract, RW, RW)
                ts("dve", us[:], cs[:], 2.0, None, ALU.mult, ALU.bypass, RW, RW)
            bk = bank(False)
            for i3, src in enumerate((r1, uc, us)):
                for m in range(2):
                    mm(ps[bk][:, i3 * 16:(i3 + 1) * 16], dup[:, m, :], AP(src, m, [[32, 64], [2, 16]]), m == 0, m == 1,
                       RW + ["dup"], [("ps", bk)])
            pwa = sb2("pwa", [128, 16]); pwb = sb2("pwb", [128, 16])
            x1 = sb2("x1", [128, 16, 32]); x2 = sb2("x2", [128, 16, 32])
            y1 = sb2("y1", [128, 16]); y2 = sb2("y2", [128, 16]); y3 = sb2("y3", [128, 16])
            AK = ["Atab"]
            cp("dve", rtab[:, :], ps[bk][:, 0:16], [("ps", bk)], AK)
            cp("dve", pwa[:, :], ps[bk][:, 16:32], [("ps", bk)], AK)
            cp("dve", pwb[:, :], ps[bk][:, 32:48], [("ps", bk)], AK)
            cp("dve", cosT[:, :, 0], pwa[:, :], AK, AK)
            cp("dve", sinT[:, :, 0], pwb[:, :], AK, AK)
            k = 1
            while k < NJ:
                pa = AP(pwa, 0, [[16, 128], [1, 16], [0, k]])
                pb = AP(pwb, 0, [[16, 128], [1, 16], [0, k]])
                tt("dve", x1[:, :, 0:k], cosT[:, :, 0:k], pa, ALU.mult, AK, AK)
                tt("dve", x2[:, :, 0:k], sinT[:, :, 0:k], pb, ALU.mult, AK, AK)
                tt("dve", cosT[:, :, k:2 * k], x1[:, :, 0:k], x2[:, :, 0:k], ALU.subtract, AK, AK)
                tt("dve", x1[:, :, 0:k], cosT[:, :, 0:k], pb, ALU.mult, AK, AK)
                tt("dve", x2[:, :, 0:k], sinT[:, :, 0:k], pa, ALU.mult, AK, AK)
                tt("dve", sinT[:, :, k:2 * k], x1[:, :, 0:k], x2[:, :, 0:k], ALU.add, AK, AK)
                tt("dve", y1[:], pwa[:], pwa[:], ALU.mult, AK, AK)
                tt("dve", y2[:], pwb[:], pwb[:], ALU.mult, AK, AK)
                tt("dve", y3[:], pwa[:], pwb[:], ALU.mult, AK, AK)
                tt("dve", pwa[:], y1[:], y2[:], ALU.subtract, AK, AK)
                ts("dve", pwb[:], y3[:], 2.0, None, ALU.mult, ALU.bypass, AK, AK)
                k *= 2
            memset("dve", Rt[:, :, :], 0.0, AK)
            cp("dve", Rt[:, :, 1:NJ], AP(rtab, 0, [[16, 128], [1, 16], [0, NJ - 1]]), AK, AK)
            for hh in range(2):
                sl = slice(hh * 32, (hh + 1) * 32)
                tt("dve", x1[:, :, :], cosT[:, :, sl], cosT[:, :, sl], ALU.mult, AK, AK)
                tt("dve", x2[:, :, :], sinT[:, :, sl], sinT[:, :, sl], ALU.mult, AK, AK)
                tt("dve", x1[:, :, :], x1[:, :, :], x2[:, :, :], ALU.add, AK, AK)
                ts("dve", x1[:, :, :], x1[:, :, :], -0.5, 1.5, ALU.mult, ALU.add, AK, AK)
                tt("dve", cosT[:, :, sl], cosT[:, :, sl], x1[:, :, :], ALU.mult, AK, AK)
                tt("dve", sinT[:, :, sl], sinT[:, :, sl], x1[:, :, :], ALU.mult, AK, AK)

        T.barrier()
        pinned.clear()
        if DBG.get("dump"):
            d_kbd = nc.dram_tensor("d_kbd", [128, 4 * L * 128], BF16, kind="ExternalOutput").ap()
            d_wt = nc.dram_tensor("d_wt", [128, 4 * L * 2 * 128], BF16, kind="ExternalOutput").ap()
            d_et = nc.dram_tensor("d_et", [128, 16 * L * 2 * 32], BF16, kind="ExternalOutput").ap()
            d_a = nc.dram_tensor("d_a", [128, 16 + 2 * 16 * NJ], F32, kind="ExternalOutput").ap()
            T.dma(d_kbd, Kbd[:, :, :, :].rearrange("p a b c -> p (a b c)"), "dbg", reads=["Kbd"])
            T.dma(d_wt, Wt[:, :, :, :, :].rearrange("p a b c d -> p (a b c d)"), "dbg", reads=["Wt"])
            T.dma(d_et, Et[:, :, :, :, :].rearrange("p a b c d -> p (a b c d)"), "dbg", reads=["Et"])
            T.dma(d_a[:, 0:16], rtab[:, :], "dbg", reads=["Atab"])
            T.dma(d_a[:, 16:16 + 16 * NJ], cosT[:, :, :].rearrange("p a b -> p (a b)"), "dbg", reads=["Atab"])
            T.dma(d_a[:, 16 + 16 * NJ:], sinT[:, :, :].rearrange("p a b -> p (a b)"), "dbg", reads=["Atab"])
        if DBG.get("setup_only"):
            T.finish("sp")
            build_nc.stats = (T.n_ops, T.n_waits, len(T.sems))
            return nc
        hnb = [sb(f"hnb{i}", [128, D], BF16) for i in range(1)]
        hnT = sb("hnT", [128, 8, NB], BF16)
        mixT = sb("mixT", [128, 8, NB], BF16)
        rot = [sb(f"rot{i}", [128, NT, 192], F32) for i in range(2)]
        rt1 = sb("rt1", [128, 512], F32)
        rt2 = sb("rt2", [128, 512], F32)
        qrot2 = [sb(f"qrot{i}", [128, NT, 512], BF16) for i in range(2)]
        krot2 = [sb(f"krot{i}", [128, NT, 512], BF16) for i in range(2)]
        ktd2 = [sb(f"ktd{i}", [128, NT, 512], BF16) for i in range(2)]
        vtok2 = [sb(f"vtok{i}", [128, NT, 512], BF16) for i in range(2)]
        qT = sb("qT", [128, 4, NB], BF16)
        kT = sb("kT", [128, 4, NB], BF16)
        qdT = sb("qdT", [128, 4, NB], BF16)
        sgret2 = [sb(f"sgret{i}", [128, 4, NB], BF16) for i in range(2)]
        uT2 = [sb(f"uT{i}", [128, 4, NB], BF16) for i in range(2)]
        sgssm2 = [sb(f"sgssm{i}", [128, 4, NB], BF16) for i in range(2)]
        c_sb = sb("c_sb", [128, 2, 16, NJ], F32)
        Hprev2 = [sb("Hprev0", [128, 2, 16, NJ], BF16)] * 2
        carry = sb("carry", [128, 2, 16], F32)
        ta = sb("ta", [128, 2, 4, NJ], F32)
        tb = sb("tb", [128, 2, 4, NJ], F32)
        hfin = sb("hfin", [128, NSEQ_S, 2, 16], F32)
        st3 = sb("st3", [128, 2, 16], F32)
        sTb = [sb(f"sTb{i}", [128, 256], BF16) for i in range(2)]
        Smaster = sb("Smaster", [128, 4, HD], F32)
        Sprev = [sb(f"Sprev{i}", [128, 4, HD], BF16) for i in range(2)]
        osq = sb("osq", [128, 4 * NB], BF16)
        rsd = sb("rsd", [128, 512], F32)
        ot = sb("ot", [128, 512], F32)
        zT = sb("zT", [128, 4, NB], BF16)
        g1 = [rsd[:, 0:NB]] * 2
        g2 = [ot[:, 0:NB]] * 2
        ssq = sb("ssq", [128, 8], F32)
        rstd = sb("rstd", [128, 8], F32)


        memset("pool", sTb[0][:], 0.0, ["sT0"])
        memset("pool", sTb[1][:], 0.0, ["sT1"])
        xs_ctr = [0]
        LQ = DBG.get("lq", "act")

        def next_xslot():
            i = xs_ctr[0] % NXS
            xs_ctr[0] += 1
            return i

        chunk_ctr = [0]
        blk_ctr = [0]

        def interleave2(g1, g2, n1, n2):
            c1 = c2 = 0
            d1 = d2 = False
            while not (d1 and d2):
                if d2 or (not d1 and c1 / n1 <= c2 / n2):
                    try:
                        next(g1); c1 += 1
                    except StopIteration:
                        d1 = True
                else:
                    try:
                        next(g2); c2 += 1
                    except StopIteration:
                        d2 = True
                yield

        def issue_x_load(cfg, t_, xi):
            XK = f"xslot{xi}"
            if cfg["sample"]:
                memset("pool", xslot[xi][:], 0.0, [XK])
                for half in range(2):
                    s_ = 2 * t_ + half
                    T.dma(xslot[xi][64 * half:64 * half + TS, :], xs[s_ * TS:(s_ + 1) * TS, :], f"ld_xs{xi}", writes=[XK])
            else:
                r0 = cfg["seq"] * SEQ + cfg["blk"] * NB
                T.dma(xslot[xi][:], xp[r0 + t_ * 128: r0 + (t_ + 1) * 128, :], f"ld_xs{xi}", writes=[XK])

        def issue_rot_load(cfg, rslot):
            RK = f"rot{rslot}"
            cfg["rslot"] = rslot
            if cfg["sample"]:
                for t_ in range(NT):
                    T.dma(rot[rslot][:, t_, :], crot_s, f"ld_rot{rslot}", writes=[RK])
            else:
                pos0 = cfg["blk"] * NB
                T.dma(rot[rslot][:, :, :], AP(crot_p.tensor, pos0 * 192, [[192, 128], [128 * 192, NT], [1, 192]]),
                      f"ld_rot{rslot}", writes=[RK])

        def _hdr(cfg, pb):
            sample = cfg["sample"]
            cf = 1 if sample else 0
            seq = cfg.get("seq", 0)
            blk = cfg.get("blk", 0)
            row0 = seq * SEQ + blk * NB
            gam = [1.0 - 2.0 ** (-5.0 - h) for h in range(4)]
            sdec = [g ** (16 if sample else 64) for g in gam]

            qrot, krot, ktd, vtok = qrot2[pb], krot2[pb], ktd2[pb], vtok2[pb]
            sgret, uT, sgssm, Hprev = sgret2[pb], uT2[pb], sgssm2[pb], Hprev2[pb]
            return locals()

        def phaseA(cfg, pb):
            L_ = _hdr(cfg, pb)
            sample, cf, seq, blk, row0 = L_['sample'], L_['cf'], L_['seq'], L_['blk'], L_['row0']
            qrot, krot, ktd, vtok = L_['qrot'], L_['krot'], L_['ktd'], L_['vtok']
            sgret, uT, sgssm, Hprev = L_['sgret'], L_['uT'], L_['sgssm'], L_['Hprev']
            rslot = cfg["rslot"]
            RK = f"rot{rslot}"

            for t_ in range(NT):
                xi = cfg["xslots"][t_]
                XK = f"xslot{xi}"
                hb = hnb[0]
                HK = "hnb0"
                act(hb[:], xslot[xi][:], AF.Square, [XK], [HK, "ssqA"], accum_out=ssq[:, t_:t_ + 1])
                act(ssq[:, t_:t_ + 1], ssq[:, t_:t_ + 1], AF.Ln, ["ssqA"], ["ssqA"], scale=1.0 / D, bias=EPS)
                act(rstd[:, t_:t_ + 1], ssq[:, t_:t_ + 1], AF.Exp, ["ssqA"], ["rstdA"], scale=-0.5)
                act(hb[:], xslot[xi][:], AF.Copy, [XK, "rstdA"], [HK], scale=rstd[:, t_:t_ + 1])
                bk = bank()
                for kt in range(8):
                    tr(psb[bk][:, kt * 128:(kt + 1) * 128], hb[:, kt * 128:(kt + 1) * 128], [HK, "ident_b"], [("ps", bk)])
                cp("act", hnT[:, :, t_ * 128:(t_ + 1) * 128], psb[bk][:, :].rearrange("p (k t) -> p k t", k=8),
                   [("ps", bk)], ["hnT"])
                rel(bk)
                yield

            if DBG.get('stop', 99) <= 1:
                return
            for t_ in range(NT):
                cosb = AP(rot[rslot], t_ * 192, [[NT * 192, 128], [0, 4], [0, 2], [1, 64]])
                sinb = AP(rot[rslot], t_ * 192 + 64, [[NT * 192, 128], [0, 4], [64, 2], [1, 64]])
                for (c0, dst, dkey) in ((0, qrot, f"qrot{pb}"), (512, krot, f"krot{pb}"), (1024, None, None)):
                    bb = bank()
                    for kt in range(8):
                        mm(ps[bb][:, :], hnT[:, kt, t_ * 128:(t_ + 1) * 128], win_b[:, kt, c0:c0 + 512], kt == 0, kt == 7,
                           ["hnT", "win_b"], [("ps", bb)])
                    yield
                    if dst is None:
                        cp("act", vtok[:, t_, :], ps[bb][:, :], [("ps", bb)], [f"vtok{pb}"])
                        rel(bb)
                        yield
                        continue
                    pv = AP(ps[bb], 0, [[512, 128], [128, 4], [64, 2], [1, 64]])
                    psw = AP(ps[bb], 64, [[512, 128], [128, 4], [-64, 2], [1, 64]])
                    tt("dve", rt1[:, :].rearrange("p (h a d) -> p h a d", h=4, a=2), pv, cosb, ALU.mult,
                       [("ps", bb), RK], ["rt1"])
                    tt("dve", rt2[:, :].rearrange("p (h a d) -> p h a d", h=4, a=2), psw, sinb, ALU.mult,
                       [("ps", bb), RK], ["rt2"])
                    rel(bb)
                    tt("dve", dst[:, t_, :], rt1[:, :], rt2[:, :], ALU.add, ["rt1", "rt2"], [dkey])
                    yield
                for h in range(4):
                    act(ktd[:, t_, h * 128:(h + 1) * 128], krot[:, t_, h * 128:(h + 1) * 128], AF.Copy,
                        [f"krot{pb}", "kdec"], [f"ktd{pb}"], scale=kdec[:, cf, h:h + 1])
                yield

            if DBG.get('stop', 99) <= 2:
                return
            for m2 in range(6):
                bk = bank()
                for half in range(2):
                    m = 2 * m2 + half
                    c0 = 1536 + m * 128
                    for kt in range(8):
                        mm(ps[bk][:, half * NB:(half + 1) * NB], win_b[:, kt, c0:c0 + 128], hnT[:, kt, :], kt == 0, kt == 7,
                           ["hnT", "win_b"], [("ps", bk)])
                    yield
                for half in range(2):
                    m = 2 * m2 + half
                    src = ps[bk][:, half * NB:(half + 1) * NB]
                    if m < 4:
                        act(sgret[:, m, :], src, AF.Silu, [("ps", bk)], [f"sgret{pb}"])
                    elif m < 8:
                        cp("act", uT[:, m - 4, :], src, [("ps", bk)], [f"uT{pb}"])
                    else:
                        act(sgssm[:, m - 8, :], src, AF.Silu, [("ps", bk)], [f"sgssm{pb}"])
                rel(bk)
                yield

            nxt2 = cfg.get("next2")
            if nxt2 is not None:
                issue_rot_load(nxt2, rslot)
            yield "SPLIT"

        def s5_front(cfg, pb):
            L_ = _hdr(cfg, pb)
            sample, blk = L_['sample'], L_['blk']
            uT = L_['uT']
            cfg["front_done"] = True
            c5 = c_sb[:, :, :, :].rearrange("p r (t q) j -> p r t q j", q=4)
            for q in range(4):
                bk = bank()
                for ri in range(2):
                    for t in range(4):
                        gi = ri * 4 + t
                        for s in range(L):
                            mm(ps[bk][:, gi * NJ:(gi + 1) * NJ], Wt[32 * q:32 * q + 32, t, s, ri, :],
                               uT[32 * q:32 * q + 32, t, :].rearrange("p (j s) -> p j s", s=L)[:, :, s],
                               s == 0, s == L - 1, [f"uT{pb}", "Wt"], [("ps", bk)], tile_position=(32 * q, 0))
                cp("dve", c5[:, :, :, q, :], ps[bk][:, :].rearrange("p (r t j) -> p r t j", r=2, t=4),
                   [("ps", bk)], ["c_sb"])
                rel(bk)
                yield
            if sample:
                return
            CS = 2 * 16 * NJ
            n = NJ
            if blk == 0:
                memset("pool", carry[:], 0.0, ["carry"])
            for ph in range(4):
                Ps = slice(4 * ph, 4 * ph + 4)
                cv = c_sb[:, :, Ps, :]
                cvsw = AP(c_sb, 16 * NJ + 4 * ph * NJ, [[CS, 128], [-16 * NJ, 2], [NJ, 4], [1, n]])
                cosb = AP(cosT, 4 * ph * NJ, [[16 * NJ, 128], [0, 2], [NJ, 4], [1, n]])
                sinb = AP(sinT, 4 * ph * NJ, [[16 * NJ, 128], [0, 2], [NJ, 4], [1, n]])
                tt("dve", ta[:, :, :, :], cv, cosb, ALU.mult, ["c_sb", "Atab"], ["ta"])
                tt("dve", tb[:, :, :, :], cvsw, sinb, ALU.mult, ["c_sb", "Atab"], ["tb"])
                tt("dve", c_sb[:, 0, Ps, :], ta[:, 0, :, :], tb[:, 0, :, :], ALU.add, ["ta", "tb"], ["c_sb"])
                tt("dve", c_sb[:, 1, Ps, :], ta[:, 1, :, :], tb[:, 1, :, :], ALU.subtract, ["ta", "tb"], ["c_sb"])
                yield
            tt("dve", st3[:, :, :], carry[:, :, :], AP(rtab, 0, [[16, 128], [0, 2], [1, 16]]), ALU.mult,
               ["carry", "Atab"], ["st3"])
            tt("dve", c_sb[:, :, :, 0], c_sb[:, :, :, 0], st3[:, :, :], ALU.add, ["c_sb", "st3"], ["c_sb"])
            yield

        def phaseB(cfg, pb):
            L_ = _hdr(cfg, pb)
            sample, cf, seq, blk, row0 = L_['sample'], L_['cf'], L_['seq'], L_['blk'], L_['row0']
            qrot, krot, ktd, vtok = L_['qrot'], L_['krot'], L_['ktd'], L_['vtok']
            sgret, uT, sgssm, Hprev = L_['sgret'], L_['uT'], L_['sgssm'], L_['Hprev']
            def gen_scan():
                if DBG.get('stop', 99) <= 3:
                    return
                if not cfg.get("front_done"):
                    yield from s5_front(cfg, pb)
                if DBG.get('stop', 99) <= 4:
                    return
                CS = 2 * 16 * NJ

                def seg_scan(j0, n, init, ikey, fin_j, fin_out, fkey):
                    for ph in range(4):
                        Ps = slice(4 * ph, 4 * ph + 4)
                        cv = c_sb[:, :, Ps, j0:j0 + n]
                        cvsw = AP(c_sb, 16 * NJ + 4 * ph * NJ + j0, [[CS, 128], [-16 * NJ, 2], [NJ, 4], [1, n]])
                        cosb = AP(cosT, 4 * ph * NJ, [[16 * NJ, 128], [0, 2], [NJ, 4], [1, n]])
                        sinb = AP(sinT, 4 * ph * NJ, [[16 * NJ, 128], [0, 2], [NJ, 4], [1, n]])
                        tav = ta[:, :, :, 0:n]
                        tbv = tb[:, :, :, 0:n]
                        tt("dve", tav, cv, cosb, ALU.mult, ["c_sb", "Atab"], ["ta"])
                        tt("dve", tbv, cvsw, sinb, ALU.mult, ["c_sb", "Atab"], ["tb"])
                        tt("dve", c_sb[:, 0, Ps, j0:j0 + n], ta[:, 0, :, 0:n], tb[:, 0, :, 0:n], ALU.add, ["ta", "tb"], ["c_sb"])
                        tt("dve", c_sb[:, 1, Ps, j0:j0 + n], ta[:, 1, :, 0:n], tb[:, 1, :, 0:n], ALU.subtract, ["ta", "tb"], ["c_sb"])
                        yield
                        for ri in range(2):
                            for P in range(4 * ph, 4 * ph + 4):
                                row = c_sb[:, ri, P, j0:j0 + n]
                                T.op("dve", lambda e, row=row, P=P, ri=ri: e.tensor_tensor_scan(
                                    out=row, data0=AP(rtab, P, [[16, 128], [0, n]]), data1=row,
                                    initial=init[:, ri, P:P + 1], op0=ALU.mult, op1=ALU.add),
                                    ["c_sb", "Atab", ikey], [("c_row", ri, P)])
                            yield
                        rows = [("c_row", ri, P) for ri in range(2) for P in range(4 * ph, 4 * ph + 4)]
                        tt("dve", tav, cv, cosb, ALU.mult, ["c_sb", "Atab"] + rows, ["ta", "c_sb"])
                        tt("dve", tbv, cvsw, sinb, ALU.mult, ["c_sb", "Atab"] + rows, ["tb", "c_sb"])
                        if n > 1:
                            tt("dve", Hprev[:, 0, Ps, j0 + 1:j0 + n], ta[:, 0, :, 0:n - 1], tb[:, 0, :, 0:n - 1], ALU.subtract,
                               ["ta", "tb"], ["Hprev"])
                            tt("dve", Hprev[:, 1, Ps, j0 + 1:j0 + n], ta[:, 1, :, 0:n - 1], tb[:, 1, :, 0:n - 1], ALU.add,
                               ["ta", "tb"], ["Hprev"])
                        cp("dve", Hprev[:, :, Ps, j0], init[:, :, Ps], [ikey], ["Hprev"])
                        tt("dve", fin_out[:, 0, Ps], ta[:, 0, :, fin_j], tb[:, 0, :, fin_j], ALU.subtract, ["ta", "tb"], [fkey])
                        tt("dve", fin_out[:, 1, Ps], ta[:, 1, :, fin_j], tb[:, 1, :, fin_j], ALU.add, ["ta", "tb"], [fkey])
                        yield

                def big_scan():
                    n = NJ
                    tcv = rsd[:, :].rearrange("p (a b c) -> p a b c", a=2, b=4)
                    tdv = ot[:, :].rearrange("p (a b c) -> p a b c", a=2, b=4)
                    cp("dve", Hprev[:, :, :, 0], carry[:, :, :], ["carry"], ["Hprev"])
                    for ri in range(2):
                        row = c_sb[:, ri, :, :].rearrange("p a b -> p (a b)")
                        T.op("dve", lambda e, row=row: e.tensor_tensor_scan(
                            out=row, data0=Rt[:, :, :].rearrange("p a b -> p (a b)"), data1=row,
                            initial=0.0, op0=ALU.mult, op1=ALU.add), ["c_sb", "Atab"], ["c_sb"])
                        yield
                    for ph in range(4):
                        Ps = slice(4 * ph, 4 * ph + 4)
                        cv = c_sb[:, :, Ps, :]
                        cvsw = AP(c_sb, 16 * NJ + 4 * ph * NJ, [[CS, 128], [-16 * NJ, 2], [NJ, 4], [1, n]])
                        cosb = AP(cosT, 4 * ph * NJ, [[16 * NJ, 128], [0, 2], [NJ, 4], [1, n]])
                        sinb = AP(sinT, 4 * ph * NJ, [[16 * NJ, 128], [0, 2], [NJ, 4], [1, n]])
                        tt("dve", tcv, cv, cosb, ALU.mult, ["c_sb", "Atab"], ["rsd"])
                        tt("dve", tdv, cvsw, sinb, ALU.mult, ["c_sb", "Atab"], ["ot"])
                        tt("dve", Hprev[:, 0, Ps, 1:n], tcv[:, 0, :, 0:n - 1], tdv[:, 0, :, 0:n - 1], ALU.subtract,
                           ["rsd", "ot"], ["Hprev"])
                        tt("dve", Hprev[:, 1, Ps, 1:n], tcv[:, 1, :, 0:n - 1], tdv[:, 1, :, 0:n - 1], ALU.add,
                           ["rsd", "ot"], ["Hprev"])
                        tt("dve", carry[:, 0, Ps], tcv[:, 0, :, n - 1], tdv[:, 0, :, n - 1], ALU.subtract, ["rsd", "ot"], ["carry"])
                        tt("dve", carry[:, 1, Ps], tcv[:, 1, :, n - 1], tdv[:, 1, :, n - 1], ALU.add, ["rsd", "ot"], ["carry"])
                        yield

                if sample:
                    for s_ in range(NSEQ_S):
                        yield from seg_scan(s_ * 16, 16, h0s[:, s_, :, :], "h0s", TS // L - 1, hfin[:, s_, :, :], "hfin")
                    for s_ in range(NSEQ_S):
                        T.dma(AP(hre_s.tensor, s_ * 2048, [[1, 128], [128, 16]]), hfin[:, s_, 0, :], f"st_hs{s_}", reads=["hfin"])
                        T.dma(AP(him_s.tensor, s_ * 2048, [[1, 128], [128, 16]]), hfin[:, s_, 1, :], f"st_hs{s_}b", reads=["hfin"])
                else:
                    yield from big_scan()
                    if blk == NBLK_SEQ - 1:
                        T.dma(AP(hre_p.tensor, seq * 2048, [[1, 128], [128, 16]]), carry[:, 0, :], "st_hp", reads=["carry"])
                        T.dma(AP(him_p.tensor, seq * 2048, [[1, 128], [128, 16]]), carry[:, 1, :], "st_hpb", reads=["carry"])


            def gen_ret_tr():
                for (src, skey, dst, key, eng) in ((qrot, f"qrot{pb}", qT, "qT", "act"), (krot, f"krot{pb}", kT, "kT", "dve")):
                    bk = bank()
                    for h in range(4):
                        for t_ in range(NT):
                            tr(psb[bk][:, h * NB + t_ * 128: h * NB + (t_ + 1) * 128], src[:, t_, h * 128:(h + 1) * 128],
                               [skey, "ident_b"], [("ps", bk)])
                    cp(eng, dst[:, :, :].rearrange("p h n -> p (h n)"), psb[bk][:, :], [("ps", bk)], [key])
                    rel(bk)
                    yield
                tt("dve", qdT[:, :, :].rearrange("p h (c l) -> p h c l", l=64),
                   qT[:, :, :].rearrange("p h (c l) -> p h c l", l=64),
                   AP(qdec, cf * 256, [[512, 128], [64, 4], [0, NC], [1, 64]]), ALU.mult, ["qT", "qdec"], ["qdT"])


            def gen_ret():
                if DBG.get('stop', 99) <= 5:
                    return
                if DBG.get('stop', 99) <= 6:
                    return
                if (not sample) and blk == 0:
                    par0 = chunk_ctr[0] % 2
                    memset("pool", Smaster[:], 0.0, ["Smaster"])
                    memset("pool", Sprev[par0][:], 0.0, [f"Sprev{par0}"])
                bo = [bank(), bank()]
                chunk_par = []
                for c in range(NC):
                    t_ = c // 2
                    base = 64 * (c % 2)
                    par = chunk_ctr[0] % 2
                    chunk_ctr[0] += 1
                    chunk_par.append(par)
                    if sample:
                        T.dma(Smaster[:, :, :], AP(sret.tensor, c * 4 * HD * HD, [[HD, 128], [HD * HD, 4], [1, HD]]),
                              "ld_S", writes=["Smaster"])
                        cp("act", Sprev[par][:], Smaster[:], ["Smaster"], [f"Sprev{par}"])
                    bs = bank()
                    for h in range(4):
                        mm(ps[bs][base:base + 64, h * 64:(h + 1) * 64], kT[:, h, c * 64:(c + 1) * 64], qT[:, h, c * 64:(c + 1) * 64],
                           True, True, ["kT", "qT"], [("ps", bs)])
                    sT = sTb[c % 2]
                    yield
                    tt("dve", sT[base:base + 64, :], ps[bs][base:base + 64, 0:256], maskT[base:base + 64, cf, :], ALU.mult,
                       [("ps", bs), "maskT"], [f"sT{c % 2}"])
                    rel(bs)
                    bkv = bank()
                    for h in range(4):
                        mm(ps[bkv][:, h * 128:(h + 1) * 128], ktd[base:base + 64, t_, h * 128:(h + 1) * 128],
                           vtok[base:base + 64, t_, h * 128:(h + 1) * 128], True, True, [f"ktd{pb}", f"vtok{pb}"], [("ps", bkv)])
                    yield
                    for h in range(4):
                        ob = bo[h // 2]
                        oc = (h % 2) * NB + c * 64
                        mm(ps[ob][:, oc:oc + 64], vtok[:, t_, h * 128:(h + 1) * 128],
                           sT[:, h * 64:(h + 1) * 64], True, False, [f"vtok{pb}", f"sT{c % 2}"], [("ps", ob)])
                        mm(ps[ob][:, oc:oc + 64], Sprev[par][:, h, :], qdT[:, h, c * 64:(c + 1) * 64], False, True,
                           [f"Sprev{par}", "qdT"], [("ps", ob)])
                    yield
                    for h in range(4):
                        stt(Smaster[:, h, :], Smaster[:, h, :], sdec_t[:, cf, h:h + 1], ps[bkv][:, h * 128:(h + 1) * 128],
                            ALU.mult, ALU.add, [("ps", bkv), "Smaster", "sdec_t"], ["Smaster"])
                    rel(bkv)
                    yield
                    npar = 1 - par
                    if sample:
                        T.dma(AP(rs.tensor, c * 4 * HD * HD, [[HD, 128], [HD * HD, 4], [1, HD]]), Smaster[:, :, :], "st_rs",
                              reads=["Smaster"])
                    else:
                        cp("act", Sprev[npar][:], Smaster[:], ["Smaster"], [f"Sprev{npar}"])
                        if blk == NBLK_SEQ - 1 and c == NC - 1:
                            T.dma(AP(rp.tensor, seq * 4 * HD * HD, [[HD, 128], [HD * HD, 4], [1, HD]]), Smaster[:, :, :],
                                  "st_rp", reads=["Smaster"])

                if DBG.get('stop', 99) <= 7:
                    return
                bn = [bank(), bank()]
                for i2 in range(2):
                    act(osq[:, i2 * 512:(i2 + 1) * 512], ps[bo[i2]][:, :], AF.Square, [("ps", bo[i2])], ["osq"])
                    mm(ps[bn[i2]][:, :], ones_b[:, :], osq[:, i2 * 512:(i2 + 1) * 512], True, True, ["osq", "ones_b"],
                       [("ps", bn[i2])])
                    yield
                for i2 in range(2):
                    act(rsd[:, :], ps[bn[i2]][:, :], AF.Ln, [("ps", bn[i2])], ["rsd"], scale=1.0 / HD, bias=EPS)
                    rel(bn[i2])
                    act(rsd[:, :], rsd[:, :], AF.Exp, ["rsd"], ["rsd"], scale=-0.5)
                    tt("dve", ot[:, :], ps[bo[i2]][:, :], rsd[:, :], ALU.mult, [("ps", bo[i2]), "rsd"], ["ot"])
                    rel(bo[i2])
                    for hh in range(2):
                        h = 2 * i2 + hh
                        stt(mixT[:, h, :], ot[:, hh * NB:(hh + 1) * NB], rng_[:, h:h + 1], sgret[:, h, :], ALU.mult, ALU.mult,
                            ["ot", "rng", f"sgret{pb}"], [f"mixT{h}"])
                    yield


            yield from gen_scan()
            yield from gen_ret_tr()
            yield "SPLIT"
            yield from gen_ret()

            if DBG.get('stop', 99) <= 8:
                return
            for t2 in range(2):
                bk = bank()
                for half in range(2):
                    t = 2 * t2 + half
                    yv = ps[bk][:, half * NB:(half + 1) * NB].rearrange("p (j s) -> p j s", s=L)
                    uv = uT[:, t, :].rearrange("p (j s) -> p j s", s=L)
                    for i in range(L):
                        for k in range(i + 1):
                            mm(yv[:, :, i], Kbd[:, t, k, :], uv[:, :, i - k], k == 0, False, ["Kbd", f"uT{pb}"], [("ps", bk)])
                        for q in range(4):
                            P = 4 * t + q
                            for ri in range(2):
                                last = (ri == 1)
                                mm(ps[bk][32 * q:32 * q + 32, half * NB:(half + 1) * NB].rearrange("p (j s) -> p j s", s=L)[:, :, i],
                                   Et[:, P, i, ri, :], Hprev[:, ri, P, :], False, last, ["Et", "Hprev"], [("ps", bk)],
                                   tile_position=(0, 32 * q))
                        yield
                for half in range(2):
                    t = 2 * t2 + half
                    ysrc = ps[bk][:, half * NB:(half + 1) * NB]
                    act(zT[:, t, :], ysrc, AF.Gelu_apprx_tanh, [("ps", bk)], ["zT"])
                    if half == 1:
                        rel(bk)
                    yield

            if DBG.get('stop', 99) <= 9:
                return
            for m2 in range(2):
                bk = bank()
                for half in range(2):
                    m = 2 * m2 + half
                    for kt in range(4):
                        mm(ps[bk][:, half * NB:(half + 1) * NB], wglu_b[:, kt, m * 128:(m + 1) * 128], zT[:, kt, :],
                           kt == 0, kt == 3, ["wglu_b", "zT"], [("ps", bk)])
                    yield
                for half in range(2):
                    m = 2 * m2 + half
                    gb = g2[half]
                    GK = "ot"
                    act(gb[:], ps[bk][:, half * NB:(half + 1) * NB], AF.Sigmoid, [("ps", bk), "bglu"], [GK],
                        bias=bglu[:, m:m + 1])
                    if half == 1:
                        rel(bk)
                    tt("dve", gb[:], gb[:], zT[:, m, :], ALU.mult, [GK, "zT"], [GK])
                    tt("dve", mixT[:, 4 + m, :], gb[:], sgssm[:, m, :], ALU.mult, [GK, f"sgssm{pb}"], [f"mixT{4 + m}"])
                    yield

            if DBG.get('stop', 99) <= 10:
                return
            for t_ in range(NT):
                xi = cfg["xslots"][t_]
                XK = f"xslot{xi}"
                for half in range(2):
                    bk = bank()
                    for kt in range(8):
                        mm(ps[bk][:, :], mixT[:, kt, t_ * 128:(t_ + 1) * 128], wout_b[:, kt, half * 512:(half + 1) * 512],
                           kt == 0, kt == 7, [f"mixT{kt}", "wout_b"], [("ps", bk)])
                    tt("dve", xslot[xi][:, half * 512:(half + 1) * 512], ps[bk][:, :], xslot[xi][:, half * 512:(half + 1) * 512],
                       ALU.add, [("ps", bk), XK], [XK])
                    rel(bk)
                    yield
                c_ = 4 + t_
                act(osq[:, :], xslot[xi][:], AF.Square, [XK], ["osq", "ssq"], accum_out=ssq[:, c_:c_ + 1])
                act(ssq[:, c_:c_ + 1], ssq[:, c_:c_ + 1], AF.Ln, ["ssq"], ["ssq"], scale=1.0 / D, bias=EPS)
                act(rstd[:, c_:c_ + 1], ssq[:, c_:c_ + 1], AF.Exp, ["ssq"], ["rstd"], scale=-0.5)
                stt(xslot[xi][:], xslot[xi][:], rstd[:, c_:c_ + 1], gfin[:], ALU.mult, ALU.mult, [XK, "rstd", "gfin"], [XK])
                if sample:
                    for half in range(2):
                        s_ = 2 * t_ + half
                        T.dma(ys[s_ * TS:(s_ + 1) * TS, :], xslot[xi][64 * half:64 * half + TS, :], f"st_y{xi}", reads=[XK])
                else:
                    T.dma(yp[row0 + t_ * 128: row0 + (t_ + 1) * 128, :], xslot[xi][:], f"st_y{xi}", reads=[XK])
                nxt2 = cfg.get("next2")
                if nxt2 is not None:
                    nxt2.setdefault("xslots", [None] * NT)[t_] = xi
                    issue_x_load(nxt2, t_, xi)
                yield


        blocks = []
        if DBG.get("sample", True):
            blocks.append(dict(sample=True))
        for seq in range(NSEQ_P):
            for blk in range(DBG.get("nblk", NBLK_SEQ)):
                blocks.append(dict(sample=False, seq=seq, blk=blk))
        for i, c_ in enumerate(blocks):
            if i + 2 < len(blocks):
                c_["next2"] = blocks[i + 2]
        for i, c_ in enumerate(blocks[:2]):
            c_["xslots"] = [2 * i, 2 * i + 1]
            for t_ in range(NT):
                issue_x_load(c_, t_, 2 * i + t_)
            issue_rot_load(c_, i)

        def drain(g):
            n = 0
            for _ in g:
                n += 1
            return n

        if DBG.get("nopipe"):
            for i, c_ in enumerate(blocks):
                drain(phaseA(c_, i % 2))
                drain(phaseB(c_, i % 2))
        else:
            est = {"a1": 39.0, "a2": 9.0, "b1": 12.0, "b2": 50.0}
            drain(phaseA(blocks[0], 0))
            pre_a = {}
            for i in range(1, len(blocks) + 1):
                gb = phaseB(blocks[i - 1], (i - 1) % 2)
                if i in pre_a:
                    ga = pre_a.pop(i)
                else:
                    ga = phaseA(blocks[i], i % 2) if i < len(blocks) else iter(())
                for stage in (1, 2):
                    if stage == 2 and i < len(blocks) and not blocks[i]["sample"] and not DBG.get("nofront"):
                        ga = s5_front(blocks[i], i % 2)
                    na, nb_ = est[f"a{stage}"], est[f"b{stage}"]
                    ca = cb = 0
                    da = db = False
                    while not (da and db):
                        if db or (not da and ca / na <= cb / nb_):
                            try:
                                r = next(ga)
                                if r == "SPLIT" and stage == 1:
                                    da = True
                                else:
                                    ca += 1
                            except StopIteration:
                                da = True
                        else:
                            try:
                                r = next(gb)
                                if r == "SPLIT" and stage == 1:
                                    db = True
                                else:
                                    cb += 1
                            except StopIteration:
                                db = True
                    if ca > 0:
                        est[f"a{stage}"] = float(ca)
                    if cb > 0:
                        est[f"b{stage}"] = float(cb)
                if i + 1 < len(blocks) and not DBG.get("nopre"):
                    gn = phaseA(blocks[i + 1], (i + 1) % 2)
                    for _ in range(NT):
                        next(gn)
                    pre_a[i + 1] = gn
        T.finish("sp")
        build_nc.stats = (T.n_ops, T.n_waits, len(T.sems))
    return nc


def _constants():
    c = {}
    c["cident"] = np.eye(128, dtype=np.float32)
    half = 64
    inv = 10000.0 ** (-np.arange(half, dtype=np.float64) / half)

    def rot_tab(pos):
        ang = pos[:, None].astype(np.float64) * inv[None, :]
        cos, sin = np.cos(ang), np.sin(ang)
        return np.concatenate([cos, -sin, sin], axis=1).astype(np.float32)

    c["crot_p"] = rot_tab(np.arange(SEQ))
    rs_ = np.zeros((128, 192), np.float32)
    tab = rot_tab(PAST + np.arange(TS))
    rs_[0:TS] = tab
    rs_[64:64 + TS] = tab
    c["crot_s"] = rs_
    gam = np.array([1.0 - 2.0 ** (-5.0 - h) for h in range(4)], dtype=np.float64)
    scale = HD ** -0.5
    cmask = np.zeros((2, 128, 4, 64), np.float64)
    cq = np.zeros((2, 4, 64), np.float64)
    ck = np.zeros((2, 128, 4), np.float64)
    for cfg, blk in ((0, 64), (1, TS)):
        for h in range(4):
            for m in range(blk):
                for l in range(m, blk):
                    cmask[cfg, m, h, l] = gam[h] ** (l - m) * scale
                    cmask[cfg, 64 + m, h, l] = gam[h] ** (l - m) * scale
            for l in range(blk):
                cq[cfg, h, l] = gam[h] ** (l + 1)
                ck[cfg, l, h] = gam[h] ** (blk - 1 - l) * scale
                ck[cfg, 64 + l, h] = gam[h] ** (blk - 1 - l) * scale
    c["cmask"] = cmask.reshape(256, 256).astype(np.float32)
    c["cqdec"] = cq.reshape(2, 256).astype(np.float32)
    c["ckdec"] = ck.reshape(256, 4).astype(np.float32)
    dup = np.zeros((64, 2, 2, 64), np.float32)
    for m in range(2):
        dup[np.arange(64), m, m, np.arange(64)] = 1.0
    c["cdup"] = dup.reshape(64, 256)
    par = np.zeros((2, 32, 16), np.float32)
    for m in range(2):
        par[m, m::2, :] = 1.0
    c["cpar"] = par.reshape(2, 512)
    wm = np.zeros((8, 16, 2), np.float32)
    for gl in range(8):
        wm[gl, :, gl % 2] = 1.0
    c["cwmask"] = wm.reshape(128, 2)
    c["csdec"] = np.array([[g ** 64 for g in gam] + [g ** TS for g in gam]], dtype=np.float32)
    return c


_NC_CACHE = {}


def kernel(x_prompt, x_sample, state_ret, state_ssm_re, state_ssm_im, norm_g, w_in, ret_norm_g,
           ssm_lambda_re, ssm_lambda_im, ssm_log_step, ssm_b_re, ssm_b_im, ssm_c_re, ssm_c_im,
           ssm_d, w_glu, b_glu, w_out, final_norm_g):
    f = lambda a: np.ascontiguousarray(np.asarray(a, dtype=np.float32))
    x_prompt = f(x_prompt); x_sample = f(x_sample)
    consts = _constants()
    shared = {
        "norm_g": f(norm_g).reshape(1, D), "w_in": f(w_in).reshape(D, 3072),
        "ret_norm_g": f(ret_norm_g).reshape(4, HD),
        "lre": f(ssm_lambda_re).reshape(32, 64), "lim": f(ssm_lambda_im).reshape(32, 64),
        "lstep": f(ssm_log_step).reshape(1, 32),
        "bre": f(ssm_b_re).reshape(2048, 16), "bim": f(ssm_b_im).reshape(2048, 16),
        "cre": f(ssm_c_re).reshape(512, 64), "cim": f(ssm_c_im).reshape(512, 64),
        "ssm_d": f(ssm_d).reshape(1, 512), "w_glu": f(w_glu).reshape(512, 512),
        "b_glu": f(b_glu).reshape(1, 512), "w_out": f(w_out).reshape(D, D),
        "fng": f(final_norm_g).reshape(1, D),
    }
    shared.update(consts)
    sr = f(state_ret)[0]; s_re = f(state_ssm_re)[0]; s_im = f(state_ssm_im)[0]
    in_maps = []
    for c in range(8):
        m = dict(shared)
        m["xp"] = x_prompt[c * NSEQ_P:(c + 1) * NSEQ_P].reshape(NSEQ_P * SEQ, D)
        m["xs"] = x_sample[c * NSEQ_S:(c + 1) * NSEQ_S].reshape(NSEQ_S * TS, D)
        m["sret"] = sr[c * NSEQ_S:(c + 1) * NSEQ_S].reshape(NSEQ_S * 4 * HD, HD)
        m["sre"] = s_re[c * NSEQ_S:(c + 1) * NSEQ_S].reshape(NSEQ_S, 2048)
        m["sim"] = s_im[c * NSEQ_S:(c + 1) * NSEQ_S].reshape(NSEQ_S, 2048)
        in_maps.append(m)
    if "nc" not in _NC_CACHE:
        _NC_CACHE["nc"] = build_nc()
    nc = _NC_CACHE["nc"]
    res = run_bass_kernel_spmd(nc, in_maps, core_ids=list(range(8)))
    R = res.results
    cat = lambda k: np.concatenate([np.asarray(r[k], dtype=np.float32) for r in R], axis=0)
    y_prompt = cat("yp").reshape(16, SEQ, D)
    y_sample = cat("ys").reshape(32, TS, D)
    ret_p = cat("rp").reshape(1, 16, 4, HD, HD)
    hre_p = cat("hre_p").reshape(1, 16, 32, 64)
    him_p = cat("him_p").reshape(1, 16, 32, 64)
    ret_s = cat("rs").reshape(1, 32, 4, HD, HD)
    hre_s = cat("hre_s").reshape(1, 32, 32, 64)
    him_s = cat("him_s").reshape(1, 32, 32, 64)
    return (y_prompt, y_sample, ret_p, hre_p, him_p, ret_s, hre_s, him_s)
```

```python
import math
from contextlib import ExitStack

import numpy as np
import concourse.bass as bass
import concourse.mybir as mybir
from concourse.bass_utils import run_bass_kernel_spmd

F32 = mybir.dt.float32
BF16 = mybir.dt.bfloat16
ALU = mybir.AluOpType
AF = mybir.ActivationFunctionType

D = 1024
SEQ = 4096
NSEQ_P = 2
NSEQ_S = 4
TS = 16
PAST = 2048
NB = 256
NT = 2
NBLK_SEQ = SEQ // NB
L = 4
NJ = NB // L
NC = NB // 64
EPS = 1e-6
HD = 128
STRICT = True
DBG = {}


class Trk:
    def __init__(self, nc, es):
        self.nc = nc
        self.es = es
        self.eng = {"pe": nc.tensor, "act": nc.scalar, "dve": nc.vector, "pool": nc.gpsimd, "sp": nc.sync}
        self.sems = {}
        self.cnt = {}
        for e in ("pe", "act", "dve", "pool"):
            self.sems[e] = es.enter_context(nc.semaphore("sem_" + e))
            self.cnt[e] = 0
        self.known = {e: {} for e in self.eng}
        self.last_w = {}
        self.readers = {}
        self.n_ops = 0
        self.n_waits = 0

    def _deps(self, reads, writes):
        deps = {}

        def add(s, v):
            if v > deps.get(s, 0):
                deps[s] = v

        for k in reads:
            lw = self.last_w.get(k)
            if lw:
                add(*lw)
        for k in writes:
            lw = self.last_w.get(k)
            if lw:
                add(*lw)
            for s, v in self.readers.get(k, {}).items():
                add(s, v)
        return deps

    def _wait(self, e, deps, own=None):
        for s, v in deps.items():
            if s == own and (own == "pe" or not STRICT):
                continue
            if self.known[e].get(s, 0) >= v:
                continue
            self.eng[e].wait_ge(self.sems[s], v)
            self.known[e][s] = v
            self.n_waits += 1

    def op(self, e, fn, reads=(), writes=()):
        deps = self._deps(reads, writes)
        self._wait(e, deps, own=e)
        ins = fn(self.eng[e])
        self.cnt[e] += 1
        n = self.cnt[e]
        ins.then_inc(self.sems[e], 1)
        for k in reads:
            self.readers.setdefault(k, {})[e] = n
        for k in writes:
            self.last_w[k] = (e, n)
            self.readers[k] = {}
        self.n_ops += 1
        return ins

    def dma(self, out, in_, sem, reads=(), writes=(), q="sp", **kw):
        if sem not in self.sems:
            self.sems[sem] = self.es.enter_context(self.nc.semaphore("d_" + sem))
            self.cnt[sem] = 0
        deps = self._deps(reads, writes)
        if sem == "ld_setup":
            deps.pop(sem, None)
        self._wait(q, deps, own=None)
        ins = self.eng[q].dma_start(out=out, in_=in_, **kw)
        self.cnt[sem] += 16
        n = self.cnt[sem]
        ins.then_inc(self.sems[sem], 16)
        for k in reads:
            self.readers.setdefault(k, {})[sem] = n
        for k in writes:
            self.last_w[k] = (sem, n)
            self.readers[k] = {}
        self.n_ops += 1
        return ins

    def fence_group(self, sem):
        tot = self.cnt[sem]
        for k, lw in list(self.last_w.items()):
            if lw[0] == sem:
                self.last_w[k] = (sem, tot)

    def barrier(self):
        for e in self.eng:
            for s_, c in self.cnt.items():
                if s_ == e or c == 0:
                    continue
                if self.known[e].get(s_, 0) < c:
                    self.eng[e].wait_ge(self.sems[s_], c)
                    self.known[e][s_] = c

    def finish(self, e="sp"):
        for s, c in self.cnt.items():
            if s in ("pe", "act", "dve", "pool"):
                continue
            if c > 0 and self.known[e].get(s, 0) < c:
                self.eng[e].wait_ge(self.sems[s], c)
                self.known[e][s] = c


def AP(t, off, dims):
    return bass.AP(t, off, [list(d) for d in dims])


def build_nc():
    nc = bass.Bass("TRN2", target_bir_lowering=False)

    def din(name, shape):
        return nc.dram_tensor(name, list(shape), F32, kind="ExternalInput").ap()

    def dout(name, shape):
        return nc.dram_tensor(name, list(shape), F32, kind="ExternalOutput").ap()

    xp = din("xp", [NSEQ_P * SEQ, D])
    xs = din("xs", [NSEQ_S * TS, D])
    sret = din("sret", [NSEQ_S * 4 * HD, HD])
    sre = din("sre", [NSEQ_S, 2048])
    sim = din("sim", [NSEQ_S, 2048])
    norm_g = din("norm_g", [1, D])
    w_in = din("w_in", [D, 3072])
    ret_norm_g = din("ret_norm_g", [4, HD])
    lre = din("lre", [32, 64])
    lim = din("lim", [32, 64])
    lstep = din("lstep", [1, 32])
    bre = din("bre", [2048, 16])
    bim = din("bim", [2048, 16])
    cre = din("cre", [512, 64])
    cim = din("cim", [512, 64])
    ssm_d = din("ssm_d", [1, 512])
    w_glu = din("w_glu", [512, 512])
    b_glu = din("b_glu", [1, 512])
    w_out = din("w_out", [D, D])
    fng = din("fng", [1, D])
    cident = din("cident", [128, 128])
    crot_p = din("crot_p", [SEQ, 192])
    crot_s = din("crot_s", [128, 192])
    cmask = din("cmask", [2 * 128, 256])
    cqdec = din("cqdec", [2, 256])
    ckdec = din("ckdec", [2 * 128, 4])
    cdup = din("cdup", [64, 256])
    cpar = din("cpar", [2, 512])
    cwmask = din("cwmask", [128, 2])
    csdec = din("csdec", [1, 8])

    yp = dout("yp", [NSEQ_P * SEQ, D])
    ys = dout("ys", [NSEQ_S * TS, D])
    rp = dout("rp", [NSEQ_P * 4 * HD, HD])
    hre_p = dout("hre_p", [NSEQ_P, 2048])
    him_p = dout("him_p", [NSEQ_P, 2048])
    rs = dout("rs", [NSEQ_S * 4 * HD, HD])
    hre_s = dout("hre_s", [NSEQ_S, 2048])
    him_s = dout("him_s", [NSEQ_S, 2048])

    with ExitStack() as es:
        es.enter_context(nc.allow_non_contiguous_dma(reason="small parameter layouts"))
        T = Trk(nc, es)

        def sb(name, shape, dt=F32):
            return es.enter_context(nc.sbuf_tensor(name, list(shape), dt))

        def tap(name, ap_, shape, keys, dt=F32):
            if not DBG.get("taps"):
                return
            d = nc.dram_tensor("tap_" + name, list(shape), dt, kind="ExternalOutput").ap()
            T.dma(d, ap_, "dbg", reads=keys)

        win_b = sb("win_b", [128, 8, 3072], BF16)
        wout_b = sb("wout_b", [128, 8, 1024], BF16)
        wglu_b = sb("wglu_b", [128, 4, 512], BF16)
        ident_f = sb("ident_f", [128, 128], F32)
        ident_b = sb("ident_b", [128, 128], BF16)
        ones_b = sb("ones_b", [128, 128], BF16)
        ng = sb("ng", [128, 8], F32)
        rng_ = sb("rng", [128, 4], F32)
        bglu = sb("bglu", [128, 4], F32)
        dvec = sb("dvec", [128, 4], F32)
        gfin = sb("gfin", [128, D], F32)
        maskT = sb("maskT", [128, 2, 256], F32)
        qdec = sb("qdec", [128, 2, 256], F32)
        kdec = sb("kdec", [128, 2, 4], F32)
        sdec_t = sb("sdec_t", [128, 2, 4], F32)
        Kbd = sb("Kbd", [128, 4, L, 128], BF16)
        Wt = sb("Wt", [128, 4, L, 2, 128], BF16)
        Et = sb("Et", [128, 16, L, 2, 32], BF16)
        cosT = sb("cosT", [128, 16, NJ], F32)
        sinT = sb("sinT", [128, 16, NJ], F32)
        rtab = sb("rtab", [128, 16], F32)
        Rt = sb("Rt", [128, 16, NJ], F32)
        h0s = sb("h0s", [128, NSEQ_S, 2, 16], F32)

        NXS = 4
        xslot = [sb(f"xslot{i}", [128, D], F32) for i in range(NXS)]

        ps = [es.enter_context(nc.psum_tensor(f"ps{i}", [128, 512], F32)) for i in range(8)]
        psb = [p.bitcast(BF16) for p in ps]
        bank_ctr = [0]

        pinned = set()

        def bank(pin=True):
            for _ in range(9):
                b = bank_ctr[0] % 8
                bank_ctr[0] += 1
                if b not in pinned:
                    break
            else:
                raise RuntimeError("out of PSUM banks")
            if pin:
                pinned.add(b)
            return b

        def rel(*bs):
            for b in bs:
                pinned.discard(b)

        def act(out, in_, func, reads, writes, **kw):
            return T.op("act", lambda e: e.activation(out=out, in_=in_, func=func, **kw), reads, writes)

        def tt(eng, out, in0, in1, op, reads, writes):
            return T.op(eng, lambda e: e.tensor_tensor(out=out, in0=in0, in1=in1, op=op), reads, writes)

        def ts(eng, out, in0, s1, s2, op0, op1, reads, writes):
            return T.op(eng, lambda e: e.tensor_scalar(out=out, in0=in0, scalar1=s1, scalar2=s2, op0=op0, op1=op1),
                        reads, writes)

        def stt(out, in0, scalar, in1, op0, op1, reads, writes):
            return T.op("dve", lambda e: e.scalar_tensor_tensor(out=out, in0=in0, scalar=scalar, in1=in1,
                                                                 op0=op0, op1=op1), reads, writes)

        def cp(eng, out, in_, reads, writes):
            if eng == "act":
                return act(out, in_, AF.Copy, reads, writes)
            return T.op(eng, lambda e: e.tensor_copy(out=out, in_=in_), reads, writes)

        def mm(out, lhsT, rhs, start, stop, reads, writes, **kw):
            return T.op("pe", lambda e: e.matmul(out, lhsT=lhsT, rhs=rhs, start=start, stop=stop, **kw), reads, writes)

        def tr(out, in_, reads, writes):
            return T.op("pe", lambda e: e.transpose(out=out, in_=in_, identity=ident_b[:]), reads, writes)

        def memset(eng, ap_, val, writes):
            return T.op(eng, lambda e: e.memset(ap_, val), (), writes)

        T.dma(ident_f[:], cident, "ld_setup", writes=["ident_f"])
        T.dma(ng[:], AP(norm_g.tensor, 0, [[1, 128], [128, 8]]), "ld_setup", writes=["ng"])
        T.dma(rng_[:], AP(ret_norm_g.tensor, 0, [[1, 128], [128, 4]]), "ld_setup", writes=["rng"])
        T.dma(bglu[:], AP(b_glu.tensor, 0, [[1, 128], [128, 4]]), "ld_setup", writes=["bglu"])
        T.dma(dvec[:], AP(ssm_d.tensor, 0, [[1, 128], [128, 4]]), "ld_setup", writes=["dvec"])
        T.dma(gfin[:], AP(fng.tensor, 0, [[0, 128], [1, D]]), "ld_setup", writes=["gfin"])
        T.dma(sdec_t[:, :, :], AP(csdec.tensor, 0, [[0, 128], [4, 2], [1, 4]]), "ld_setup", writes=["sdec_t"])
        for c in range(2):
            T.dma(maskT[:, c, :], cmask[c * 128:(c + 1) * 128, :], "ld_setup", writes=["maskT"])
            T.dma(qdec[:, c, :], AP(cqdec.tensor, c * 256, [[0, 128], [1, 256]]), "ld_setup", writes=["qdec"])
            T.dma(kdec[:, c, :], ckdec[c * 128:(c + 1) * 128, :], "ld_setup", writes=["kdec"])

        with ExitStack() as es2:
            def sb2(name, shape, dt=F32):
                return es2.enter_context(nc.sbuf_tensor(name, list(shape), dt))

            K1 = "s5"
            lre1 = sb2("lre1", [64, 32]); lim1 = sb2("lim1", [64, 32]); dt1 = sb2("dt1", [64, 32])
            T.dma(lre1[:], AP(lre.tensor, 0, [[1, 64], [64, 32]]), "ld_setup", writes=["lre1"])
            T.dma(lim1[:], AP(lim.tensor, 0, [[1, 64], [64, 32]]), "ld_setup", writes=["lim1"])
            T.dma(dt1[:], AP(lstep.tensor, 0, [[0, 64], [1, 32]]), "ld_setup", writes=["dt1"])
            b1re = sb2("b1re", [64, 32, 16]); b1im = sb2("b1im", [64, 32, 16])
            T.dma(b1re[:], AP(bre.tensor, 0, [[16, 64], [1024, 32], [1, 16]]), "ld_setup", writes=["b1re"])
            T.dma(b1im[:], AP(bim.tensor, 0, [[16, 64], [1024, 32], [1, 16]]), "ld_setup", writes=["b1im"])
            cnre = sb2("cnre", [128, 4, 64]); cnim = sb2("cnim", [128, 4, 64])
            T.dma(cnre[:], AP(cre.tensor, 0, [[64, 128], [128 * 64, 4], [1, 64]]), "ld_setup", writes=["cnre"])
            T.dma(cnim[:], AP(cim.tensor, 0, [[64, 128], [128 * 64, 4], [1, 64]]), "ld_setup", writes=["cnim"])
            dup = sb2("dup", [64, 2, 128])
            T.dma(dup[:], cdup, "ld_setup", writes=["dup"])
            parm = sb2("parm", [64, 2, 512])
            T.dma(parm[:], AP(cpar.tensor, 0, [[0, 64], [512, 2], [1, 512]]), "ld_setup", writes=["parm"])
            wmask = sb2("wmask", [128, 2])
            T.dma(wmask[:], cwmask, "ld_setup", writes=["wmask"])
            for s in range(NSEQ_S):
                T.dma(h0s[:, s, 0, :], AP(sre.tensor, s * 2048, [[1, 128], [128, 16]]), "ld_setup", writes=["h0s"])
                T.dma(h0s[:, s, 1, :], AP(sim.tensor, s * 2048, [[1, 128], [128, 16]]), "ld_setup", writes=["h0s"])

            T.fence_group("ld_setup")
            cp("act", ident_b[:], ident_f[:], ["ident_f"], ["ident_b"])
            memset("pool", ones_b[:], 1.0, ["ones_b"])
            cast_engs = ["act", "dve", "pool"]
            ci = [0]

            def load_cast(dst_ap, src_ap, ncols, wkey, scale_ap=None):
                i = ci[0] % NXS
                e = cast_engs[ci[0] % 3]
                ci[0] += 1
                T.dma(xslot[i][:, 0:ncols], src_ap, f"ld_xs{i}", writes=[f"xslot{i}"])
                if scale_ap is None:
                    cp(e, dst_ap, xslot[i][:, 0:ncols], [f"xslot{i}"], [wkey])
                elif e == "act":
                    act(dst_ap, xslot[i][:, 0:ncols], AF.Copy, [f"xslot{i}", "ng"], [wkey], scale=scale_ap)
                else:
                    ts(e, dst_ap, xslot[i][:, 0:ncols], scale_ap, None, ALU.mult, ALU.bypass, [f"xslot{i}", "ng"], [wkey])

            for kt in range(8):
                for c in range(3):
                    load_cast(win_b[:, kt, c * 1024:(c + 1) * 1024], w_in[kt * 128:(kt + 1) * 128, c * 1024:(c + 1) * 1024],
                              1024, "win_b", scale_ap=ng[:, kt:kt + 1])
            for kt in range(8):
                load_cast(wout_b[:, kt, :], w_out[kt * 128:(kt + 1) * 128, :], 1024, "wout_b")
            for kt in range(4):
                load_cast(wglu_b[:, kt, :], w_glu[kt * 128:(kt + 1) * 128, :], 512, "wglu_b")


            def t64(name):
                return sb2(name, [64, 32])

            lrc = t64("lrc"); dtv = t64("dtv"); are = t64("are"); aim = t64("aim"); mag = t64("mag")
            th = t64("th"); th2 = t64("th2"); wS = t64("wS"); wC = t64("wC")
            cc = t64("cc"); ss_ = t64("ss_"); cs = t64("cs")
            RW = [K1]
            ts("dve", lrc[:], lre1[:], -1e-4, None, ALU.min, ALU.bypass, ["lre1"] + RW, RW)
            act(dtv[:], dt1[:], AF.Exp, ["dt1"] + RW, RW)
            tt("dve", are[:], lrc[:], dtv[:], ALU.mult, RW, RW)
            tt("dve", aim[:], lim1[:], dtv[:], ALU.mult, ["lim1"] + RW, RW)
            act(mag[:], are[:], AF.Exp, RW, RW)
            ts("dve", th[:], aim[:], 1.0 / 32.0, None, ALU.mult, ALU.bypass, RW, RW)
            tt("dve", th2[:], th[:], th[:], ALU.mult, RW, RW)
            a = [-1.0 / 6, 1.0 / 120, -1.0 / 5040, 1.0 / 362880]
            ts("dve", wS[:], th2[:], a[3], None, ALU.mult, ALU.bypass, RW, RW)
            for k in (2, 1, 0):
                ts("dve", wS[:], wS[:], a[k], None, ALU.add, ALU.bypass, RW, RW)
                tt("dve", wS[:], wS[:], th2[:], ALU.mult, RW, RW)
            ts("dve", wS[:], wS[:], 1.0, None, ALU.add, ALU.bypass, RW, RW)
            tt("dve", wS[:], wS[:], th[:], ALU.mult, RW, RW)
            b = [-0.5, 1.0 / 24, -1.0 / 720, 1.0 / 40320, -1.0 / 3628800]
            ts("dve", wC[:], th2[:], b[4], None, ALU.mult, ALU.bypass, RW, RW)
            for k in (3, 2, 1, 0):
                ts("dve", wC[:], wC[:], b[k], None, ALU.add, ALU.bypass, RW, RW)
                tt("dve", wC[:], wC[:], th2[:], ALU.mult, RW, RW)
            ts("dve", wC[:], wC[:], 1.0, None, ALU.add, ALU.bypass, RW, RW)
            for _ in range(5):
                tt("dve", cc[:], wC[:], wC[:], ALU.mult, RW, RW)
                tt("dve", ss_[:], wS[:], wS[:], ALU.mult, RW, RW)
                tt("dve", cs[:], wC[:], wS[:], ALU.mult, RW, RW)
                tt("dve", wC[:], cc[:], ss_[:], ALU.subtract, RW, RW)
                ts("dve", wS[:], cs[:], 2.0, None, ALU.mult, ALU.bypass, RW, RW)
            pwr = sb2("pwr", [64, L + 1, 32]); pwi = sb2("pwi", [64, L + 1, 32])
            memset("dve", pwr[:, 0, :], 1.0, RW)
            memset("dve", pwi[:, 0, :], 0.0, RW)
            tt("dve", pwr[:, 1, :], mag[:], wC[:], ALU.mult, RW, RW)
            tt("dve", pwi[:, 1, :], mag[:], wS[:], ALU.mult, RW, RW)
            tA = t64("tA"); tB = t64("tB")

            def cmul(o_re, o_im, a_re, a_im, b_re, b_im, t1_, t2_, xr=()):
                R_ = RW + list(xr)
                tt("dve", t1_, a_re, b_re, ALU.mult, R_, RW)
                tt("dve", t2_, a_im, b_im, ALU.mult, R_, RW)
                tt("dve", o_re, t1_, t2_, ALU.subtract, R_, RW)
                tt("dve", t1_, a_re, b_im, ALU.mult, R_, RW)
                tt("dve", t2_, a_im, b_re, ALU.mult, R_, RW)
                tt("dve", o_im, t1_, t2_, ALU.add, R_, RW)

            for k in range(1, L):
                cmul(pwr[:, k + 1, :], pwi[:, k + 1, :], pwr[:, k, :], pwi[:, k, :], pwr[:, 1, :], pwi[:, 1, :],
                     tA[:], tB[:])
            nre = t64("nre"); den = t64("den"); qre = t64("qre"); qim = t64("qim")
            ts("dve", nre[:], pwr[:, 1, :], -1.0, None, ALU.add, ALU.bypass, RW, RW)
            tt("dve", den[:], lrc[:], lrc[:], ALU.mult, RW, RW)
            tt("dve", tA[:], lim1[:], lim1[:], ALU.mult, RW, RW)
            tt("dve", den[:], den[:], tA[:], ALU.add, RW, RW)
            T.op("dve", lambda e: e.reciprocal(out=den[:], in_=den[:]), RW, RW)
            tt("dve", tA[:], nre[:], lrc[:], ALU.mult, RW, RW)
            tt("dve", tB[:], pwi[:, 1, :], lim1[:], ALU.mult, RW, RW)
            tt("dve", qre[:], tA[:], tB[:], ALU.add, RW, RW)
            tt("dve", qre[:], qre[:], den[:], ALU.mult, RW, RW)
            tt("dve", tA[:], pwi[:, 1, :], lrc[:], ALU.mult, RW, RW)
            tt("dve", tB[:], nre[:], lim1[:], ALU.mult, RW, RW)
            tt("dve", qim[:], tA[:], tB[:], ALU.subtract, RW, RW)
            tt("dve", qim[:], qim[:], den[:], ALU.mult, RW, RW)

            tap("lre1", lre1[:], [64, 32], ["lre1"]); tap("dt1", dt1[:], [64, 32], ["dt1"])
            tap("mag", mag[:], [64, 32], RW); tap("wC", wC[:], [64, 32], RW); tap("wS", wS[:], [64, 32], RW)
            tap("pwr", pwr[:, :, :].rearrange("p a b -> p (a b)"), [64, (L + 1) * 32], RW)
            tap("pwi", pwi[:, :, :].rearrange("p a b -> p (a b)"), [64, (L + 1) * 32], RW)
            tap("qre", qre[:], [64, 32], RW); tap("qim", qim[:], [64, 32], RW)

            def bc16(t, k=None):
                if k is None:
                    return AP(t, 0, [[32, 64], [1, 32], [0, 16]])
                return AP(t, k * 32, [[(L + 1) * 32, 64], [1, 32], [0, 16]])

            bbr = sb2("bbr", [64, 32, 16]); bbi = sb2("bbi", [64, 32, 16])
            u1 = sb2("u1", [64, 32, 16]); u2 = sb2("u2", [64, 32, 16])
            cmul(bbr[:], bbi[:], bc16(qre), bc16(qim), b1re[:], b1im[:], u1[:], u2[:], xr=["b1re", "b1im"])

            CTr = sb2("CTr", [64, 512]); CTi = sb2("CTi", [64, 512]); nCTi = sb2("nCTi", [64, 512])
            for (src, dst, key) in ((cnre, CTr, "cnre"), (cnim, CTi, "cnim")):
                bk = bank(False)
                for t in range(4):
                    mm(ps[bk][0:64, t * 128:(t + 1) * 128], src[:, t, :], ident_f[:, :], True, True,
                       [key, "ident_f"], [("ps", bk)])
                cp("dve", dst[:], ps[bk][0:64, :], [("ps", bk)], RW)
            ts("dve", nCTi[:], CTi[:], -1.0, None, ALU.mult, ALU.bypass, RW, RW)

            tap("bbr", bbr[:, :, :].rearrange("p a b -> p (a b)"), [64, 512], RW)
            tap("CTr", CTr[:], [64, 512], RW); tap("nCTi", nCTi[:], [64, 512], RW)
            Vre = sb2("Vre", [64, 512]); Vim = sb2("Vim", [64, 512])
            Vpr = sb2("Vpr", [64, 8, 128]); Vpi = sb2("Vpi", [64, 8, 128])
            memset("pool", Vpr[:], 0.0, ["Vp"])
            memset("pool", Vpi[:], 0.0, ["Vp"])
            V3r = Vre[:, :].rearrange("p (g c) -> p g c", c=16)
            V3i = Vim[:, :].rearrange("p (g c) -> p g c", c=16)
            dgr = AP(Vpr, 0, [[8 * 128, 64], [144, 8], [1, 16]])
            dgi = AP(Vpi, 0, [[8 * 128, 64], [144, 8], [1, 16]])
            for k in range(L):
                cmul(V3r, V3i, bc16(pwr, k), bc16(pwi, k), bbr[:], bbi[:], u1[:], u2[:])
                for t in range(4):
                    T.op("dve", lambda e: e.tensor_copy(
                        out=dgr, in_=Vre[:, t * 128:(t + 1) * 128].rearrange("p (g c) -> p g c", c=16)), RW, ["Vp"])
                    T.op("dve", lambda e: e.tensor_copy(
                        out=dgi, in_=Vim[:, t * 128:(t + 1) * 128].rearrange("p (g c) -> p g c", c=16)), RW, ["Vp"])
                    bk = bank(False)
                    for gl in range(8):
                        g = 8 * t + gl
                        mm(ps[bk][:, 16 * gl:16 * gl + 16], Vpr[:, gl, :], CTr[:, g * 16:(g + 1) * 16], True, False,
                           ["Vp"] + RW, [("ps", bk)])
                        mm(ps[bk][:, 16 * gl:16 * gl + 16], Vpi[:, gl, :], nCTi[:, g * 16:(g + 1) * 16], False, True,
                           ["Vp"] + RW, [("ps", bk)])
                    if k == 0:
                        stt(Kbd[:, t, k, :], ident_f[:, :], dvec[:, t:t + 1], ps[bk][:, 0:128], ALU.mult, ALU.add,
                            [("ps", bk), "ident_f", "dvec"], ["Kbd"])
                    else:
                        cp("dve", Kbd[:, t, k, :], ps[bk][:, 0:128], [("ps", bk)], ["Kbd"])
                s = L - 1 - k
                for ri, Vx in ((0, Vre), (1, Vim)):
                    bk = bank(False)
                    for t in range(4):
                        mm(ps[bk][:, t * 64:(t + 1) * 64], Vx[:, t * 128:(t + 1) * 128], ident_f[0:64, 0:64], True, True,
                           RW + ["ident_f"], [("ps", bk)])
                    for t in range(4):
                        tt("dve", Wt[:, t, s, ri, :].rearrange("p (m q) -> p m q", m=2),
                           AP(ps[bk], t * 64, [[512, 128], [0, 2], [1, 64]]),
                           AP(wmask, 0, [[2, 128], [1, 2], [0, 64]]), ALU.mult,
                           [("ps", bk), "wmask"], ["Wt"])
            EVr = sb2("EVr", [64, 512]); EVi = sb2("EVi", [64, 512])
            EVm = [sb2(f"EVm{m}", [64, 512]) for m in range(2)]
            E3r = EVr[:, :].rearrange("p (g c) -> p g c", c=16)
            E3i = EVi[:, :].rearrange("p (g c) -> p g c", c=16)
            CT3r = CTr[:, :].rearrange("p (g c) -> p g c", c=16)
            CT3i = CTi[:, :].rearrange("p (g c) -> p g c", c=16)
            for i in range(L):
                cmul(E3r, E3i, bc16(pwr, i + 1), bc16(pwi, i + 1), CT3r, CT3i, u1[:], u2[:])
                for ri, EV in ((0, EVr), (1, EVi)):
                    for m in range(2):
                        tt("dve", EVm[m][:], EV[:], parm[:, m, :], ALU.mult, RW + ["parm"], ["EVm"])
                    bk = bank(False)
                    mm(ps[bk][:, :], dup[:, 0, :], EVm[0][:], True, False, ["dup", "EVm"], [("ps", bk)])
                    mm(ps[bk][:, :], dup[:, 1, :], EVm[1][:], False, True, ["dup", "EVm"], [("ps", bk)])
                    T.op("act", lambda e, bk=bk, ri=ri, i=i: e.activation(
                        out=Et[:, :, i, ri, :], in_=ps[bk][:, :].rearrange("p (a b) -> p a b", b=32),
                        func=AF.Copy, scale=(1.0 if ri == 0 else -1.0)), [("ps", bk)], ["Et"])
            r1 = t64("r1"); uc = t64("uc"); us = t64("us")
            tt("dve", r1[:], mag[:], mag[:], ALU.mult, RW, RW)
            tt("dve", r1[:], r1[:], r1[:], ALU.mult, RW, RW)
            cp("dve", uc[:], wC[:], RW, RW)
            cp("dve", us[:], wS[:], RW, RW)
            for _ in range(2):
                tt("dve", cc[:], uc[:], uc[:], ALU.mult, RW, RW)
                tt("dve", ss_[:], us[:], us[:], ALU.mult, RW, RW)
                tt("dve", cs[:], uc[:], us[:], ALU.mult, RW, RW)
                tt("dve", uc[:], cc[:], ss_[:], ALU.subtract, RW, RW)
                ts("dve", us[:], cs[:], 2.0, None, ALU.mult, ALU.bypass, RW, RW)
            bk = bank(False)
            for i3, src in enumerate((r1, uc, us)):
                for m in range(2):
                    mm(ps[bk][:, i3 * 16:(i3 + 1) * 16], dup[:, m, :], AP(src, m, [[32, 64], [2, 16]]), m == 0, m == 1,
                       RW + ["dup"], [("ps", bk)])
            pwa = sb2("pwa", [128, 16]); pwb = sb2("pwb", [128, 16])
            x1 = sb2("x1", [128, 16, 32]); x2 = sb2("x2", [128, 16, 32])
            y1 = sb2("y1", [128, 16]); y2 = sb2("y2", [128, 16]); y3 = sb2("y3", [128, 16])
            AK = ["Atab"]
            cp("dve", rtab[:, :], ps[bk][:, 0:16], [("ps", bk)], AK)
            cp("dve", pwa[:, :], ps[bk][:, 16:32], [("ps", bk)], AK)
            cp("dve", pwb[:, :], ps[bk][:, 32:48], [("ps", bk)], AK)
            cp("dve", cosT[:, :, 0], pwa[:, :], AK, AK)
            cp("dve", sinT[:, :, 0], pwb[:, :], AK, AK)
            k = 1
            while k < NJ:
                pa = AP(pwa, 0, [[16, 128], [1, 16], [0, k]])
                pb = AP(pwb, 0, [[16, 128], [1, 16], [0, k]])
                tt("dve", x1[:, :, 0:k], cosT[:, :, 0:k], pa, ALU.mult, AK, AK)
                tt("dve", x2[:, :, 0:k], sinT[:, :, 0:k], pb, ALU.mult, AK, AK)
                tt("dve", cosT[:, :, k:2 * k], x1[:, :, 0:k], x2[:, :, 0:k], ALU.subtract, AK, AK)
                tt("dve", x1[:, :, 0:k], cosT[:, :, 0:k], pb, ALU.mult, AK, AK)
                tt("dve", x2[:, :, 0:k], sinT[:, :, 0:k], pa, ALU.mult, AK, AK)
                tt("dve", sinT[:, :, k:2 * k], x1[:, :, 0:k], x2[:, :, 0:k], ALU.add, AK, AK)
                tt("dve", y1[:], pwa[:], pwa[:], ALU.mult, AK, AK)
                tt("dve", y2[:], pwb[:], pwb[:], ALU.mult, AK, AK)
                tt("dve", y3[:], pwa[:], pwb[:], ALU.mult, AK, AK)
                tt("dve", pwa[:], y1[:], y2[:], ALU.subtract, AK, AK)
                ts("dve", pwb[:], y3[:], 2.0, None, ALU.mult, ALU.bypass, AK, AK)
                k *= 2
            memset("dve", Rt[:, :, :], 0.0, AK)
            cp("dve", Rt[:, :, 1:NJ], AP(rtab, 0, [[16, 128], [1, 16], [0, NJ - 1]]), AK, AK)
            for hh in range(2):
                sl = slice(hh * 32, (hh + 1) * 32)
                tt("dve", x1[:, :, :], cosT[:, :, sl], cosT[:, :, sl], ALU.mult, AK, AK)
                tt("dve", x2[:, :, :], sinT[:, :, sl], sinT[:, :, sl], ALU.mult, AK, AK)
                tt("dve", x1[:, :, :], x1[:, :, :], x2[:, :, :], ALU.add, AK, AK)
                ts("dve", x1[:, :, :], x1[:, :, :], -0.5, 1.5, ALU.mult, ALU.add, AK, AK)
                tt("dve", cosT[:, :, sl], cosT[:, :, sl], x1[:, :, :], ALU.mult, AK, AK)
                tt("dve", sinT[:, :, sl], sinT[:, :, sl], x1[:, :, :], ALU.mult, AK, AK)

        T.barrier()
        pinned.clear()
        if DBG.get("dump"):
            d_kbd = nc.dram_tensor("d_kbd", [128, 4 * L * 128], BF16, kind="ExternalOutput").ap()
            d_wt = nc.dram_tensor("d_wt", [128, 4 * L * 2 * 128], BF16, kind="ExternalOutput").ap()
            d_et = nc.dram_tensor("d_et", [128, 16 * L * 2 * 32], BF16, kind="ExternalOutput").ap()
            d_a = nc.dram_tensor("d_a", [128, 16 + 2 * 16 * NJ], F32, kind="ExternalOutput").ap()
            T.dma(d_kbd, Kbd[:, :, :, :].rearrange("p a b c -> p (a b c)"), "dbg", reads=["Kbd"])
            T.dma(d_wt, Wt[:, :, :, :, :].rearrange("p a b c d -> p (a b c d)"), "dbg", reads=["Wt"])
            T.dma(d_et, Et[:, :, :, :, :].rearrange("p a b c d -> p (a b c d)"), "dbg", reads=["Et"])
            T.dma(d_a[:, 0:16], rtab[:, :], "dbg", reads=["Atab"])
            T.dma(d_a[:, 16:16 + 16 * NJ], cosT[:, :, :].rearrange("p a b -> p (a b)"), "dbg", reads=["Atab"])
            T.dma(d_a[:, 16 + 16 * NJ:], sinT[:, :, :].rearrange("p a b -> p (a b)"), "dbg", reads=["Atab"])
        if DBG.get("setup_only"):
            T.finish("sp")
            build_nc.stats = (T.n_ops, T.n_waits, len(T.sems))
            return nc
        hnb = [sb(f"hnb{i}", [128, D], BF16) for i in range(1)]
        hnT = sb("hnT", [128, 8, NB], BF16)
        mixT = sb("mixT", [128, 8, NB], BF16)
        rot = [sb(f"rot{i}", [128, NT, 192], F32) for i in range(2)]
        rt1 = sb("rt1", [128, 512], F32)
        rt2 = sb("rt2", [128, 512], F32)
        qrot2 = [sb(f"qrot{i}", [128, NT, 512], BF16) for i in range(2)]
        krot2 = [sb(f"krot{i}", [128, NT, 512], BF16) for i in range(2)]
        ktd2 = [sb(f"ktd{i}", [128, NT, 512], BF16) for i in range(2)]
        vtok2 = [sb(f"vtok{i}", [128, NT, 512], BF16) for i in range(2)]
        qT = sb("qT", [128, 4, NB], BF16)
        kT = sb("kT", [128, 4, NB], BF16)
        qdT = sb("qdT", [128, 4, NB], BF16)
        sgret2 = [sb(f"sgret{i}", [128, 4, NB], BF16) for i in range(2)]
        uT2 = [sb(f"uT{i}", [128, 4, NB], BF16) for i in range(2)]
        sgssm2 = [sb(f"sgssm{i}", [128, 4, NB], BF16) for i in range(2)]
        c_sb = sb("c_sb", [128, 2, 16, NJ], F32)
        Hprev2 = [sb("Hprev0", [128, 2, 16, NJ], BF16)] * 2
        carry = sb("carry", [128, 2, 16], F32)
        ta = sb("ta", [128, 2, 4, NJ], F32)
        tb = sb("tb", [128, 2, 4, NJ], F32)
        hfin = sb("hfin", [128, NSEQ_S, 2, 16], F32)
        st3 = sb("st3", [128, 2, 16], F32)
        sTb = [sb(f"sTb{i}", [128, 256], BF16) for i in range(2)]
        Smaster = sb("Smaster", [128, 4, HD], F32)
        Sprev = [sb(f"Sprev{i}", [128, 4, HD], BF16) for i in range(2)]
        osq = sb("osq", [128, 4 * NB], BF16)
        rsd = sb("rsd", [128, 512], F32)
        ot = sb("ot", [128, 512], F32)
        zT = sb("zT", [128, 4, NB], BF16)
        g1 = [rsd[:, 0:NB]] * 2
        g2 = [ot[:, 0:NB]] * 2
        ssq = sb("ssq", [128, 8], F32)
        rstd = sb("rstd", [128, 8], F32)


        memset("pool", sTb[0][:], 0.0, ["sT0"])
        memset("pool", sTb[1][:], 0.0, ["sT1"])
        xs_ctr = [0]
        LQ = DBG.get("lq", "act")

        def next_xslot():
            i = xs_ctr[0] % NXS
            xs_ctr[0] += 1
            return i

        chunk_ctr = [0]
        blk_ctr = [0]

        def interleave2(g1, g2, n1, n2):
            c1 = c2 = 0
            d1 = d2 = False
            while not (d1 and d2):
                if d2 or (not d1 and c1 / n1 <= c2 / n2):
                    try:
                        next(g1); c1 += 1
                    except StopIteration:
                        d1 = True
                else:
                    try:
                        next(g2); c2 += 1
                    except StopIteration:
                        d2 = True
                yield

        def issue_x_load(cfg, t_, xi):
            XK = f"xslot{xi}"
            if cfg["sample"]:
                memset("pool", xslot[xi][:], 0.0, [XK])
                for half in range(2):
                    s_ = 2 * t_ + half
                    T.dma(xslot[xi][64 * half:64 * half + TS, :], xs[s_ * TS:(s_ + 1) * TS, :], f"ld_xs{xi}", writes=[XK])
            else:
                r0 = cfg["seq"] * SEQ + cfg["blk"] * NB
                T.dma(xslot[xi][:], xp[r0 + t_ * 128: r0 + (t_ + 1) * 128, :], f"ld_xs{xi}", writes=[XK])

        def issue_rot_load(cfg, rslot):
            RK = f"rot{rslot}"
            cfg["rslot"] = rslot
            if cfg["sample"]:
                for t_ in range(NT):
                    T.dma(rot[rslot][:, t_, :], crot_s, f"ld_rot{rslot}", writes=[RK])
            else:
                pos0 = cfg["blk"] * NB
                T.dma(rot[rslot][:, :, :], AP(crot_p.tensor, pos0 * 192, [[192, 128], [128 * 192, NT], [1, 192]]),
                      f"ld_rot{rslot}", writes=[RK])

        def _hdr(cfg, pb):
            sample = cfg["sample"]
            cf = 1 if sample else 0
            seq = cfg.get("seq", 0)
            blk = cfg.get("blk", 0)
            row0 = seq * SEQ + blk * NB
            gam = [1.0 - 2.0 ** (-5.0 - h) for h in range(4)]
            sdec = [g ** (16 if sample else 64) for g in gam]

            qrot, krot, ktd, vtok = qrot2[pb], krot2[pb], ktd2[pb], vtok2[pb]
            sgret, uT, sgssm, Hprev = sgret2[pb], uT2[pb], sgssm2[pb], Hprev2[pb]
            return locals()

        def phaseA(cfg, pb):
            L_ = _hdr(cfg, pb)
            sample, cf, seq, blk, row0 = L_['sample'], L_['cf'], L_['seq'], L_['blk'], L_['row0']
            qrot, krot, ktd, vtok = L_['qrot'], L_['krot'], L_['ktd'], L_['vtok']
            sgret, uT, sgssm, Hprev = L_['sgret'], L_['uT'], L_['sgssm'], L_['Hprev']
            rslot = cfg["rslot"]
            RK = f"rot{rslot}"

            for t_ in range(NT):
                xi = cfg["xslots"][t_]
                XK = f"xslot{xi}"
                hb = hnb[0]
                HK = "hnb0"
                act(hb[:], xslot[xi][:], AF.Square, [XK], [HK, "ssqA"], accum_out=ssq[:, t_:t_ + 1])
                act(ssq[:, t_:t_ + 1], ssq[:, t_:t_ + 1], AF.Ln, ["ssqA"], ["ssqA"], scale=1.0 / D, bias=EPS)
                act(rstd[:, t_:t_ + 1], ssq[:, t_:t_ + 1], AF.Exp, ["ssqA"], ["rstdA"], scale=-0.5)
                act(hb[:], xslot[xi][:], AF.Copy, [XK, "rstdA"], [HK], scale=rstd[:, t_:t_ + 1])
                bk = bank()
                for kt in range(8):
                    tr(psb[bk][:, kt * 128:(kt + 1) * 128], hb[:, kt * 128:(kt + 1) * 128], [HK, "ident_b"], [("ps", bk)])
                cp("act", hnT[:, :, t_ * 128:(t_ + 1) * 128], psb[bk][:, :].rearrange("p (k t) -> p k t", k=8),
                   [("ps", bk)], [f"hnT{t_}"])
                rel(bk)
                yield

            if DBG.get('stop', 99) <= 1:
                return
            for t_ in range(NT):
                cosb = AP(rot[rslot], t_ * 192, [[NT * 192, 128], [0, 4], [0, 2], [1, 64]])
                sinb = AP(rot[rslot], t_ * 192 + 64, [[NT * 192, 128], [0, 4], [64, 2], [1, 64]])
                for (c0, dst, dkey) in ((0, qrot, f"qrot{pb}"), (512, krot, f"krot{pb}"), (1024, None, None)):
                    bb = bank()
                    for kt in range(8):
                        mm(ps[bb][:, :], hnT[:, kt, t_ * 128:(t_ + 1) * 128], win_b[:, kt, c0:c0 + 512], kt == 0, kt == 7,
                           [f"hnT{t_}", "win_b"], [("ps", bb)])
                    yield
                    if dst is None:
                        cp("act", vtok[:, t_, :], ps[bb][:, :], [("ps", bb)], [f"vtok{pb}"])
                        rel(bb)
                        yield
                        continue
                    pv = AP(ps[bb], 0, [[512, 128], [128, 4], [64, 2], [1, 64]])
                    psw = AP(ps[bb], 64, [[512, 128], [128, 4], [-64, 2], [1, 64]])
                    tt("dve", rt1[:, :].rearrange("p (h a d) -> p h a d", h=4, a=2), pv, cosb, ALU.mult,
                       [("ps", bb), RK], ["rt1"])
                    tt("dve", rt2[:, :].rearrange("p (h a d) -> p h a d", h=4, a=2), psw, sinb, ALU.mult,
                       [("ps", bb), RK], ["rt2"])
                    rel(bb)
                    tt("dve", dst[:, t_, :], rt1[:, :], rt2[:, :], ALU.add, ["rt1", "rt2"], [dkey])
                    yield
                for h in range(4):
                    act(ktd[:, t_, h * 128:(h + 1) * 128], krot[:, t_, h * 128:(h + 1) * 128], AF.Copy,
                        [f"krot{pb}", "kdec"], [f"ktd{pb}"], scale=kdec[:, cf, h:h + 1])
                yield

            if DBG.get('stop', 99) <= 2:
                return
            for m2 in range(6):
                bk = bank()
                for half in range(2):
                    m = 2 * m2 + half
                    c0 = 1536 + m * 128
                    for kt in range(8):
                        mm(ps[bk][:, half * NB:(half + 1) * NB], win_b[:, kt, c0:c0 + 128], hnT[:, kt, :], kt == 0, kt == 7,
                           ["hnT0", "hnT1", "win_b"], [("ps", bk)])
                    yield
                for half in range(2):
                    m = 2 * m2 + half
                    src = ps[bk][:, half * NB:(half + 1) * NB]
                    if m < 4:
                        act(sgret[:, m, :], src, AF.Silu, [("ps", bk)], [f"sgret{pb}"])
                    elif m < 8:
                        cp("act", uT[:, m - 4, :], src, [("ps", bk)], [f"uT{pb}"])
                    else:
                        act(sgssm[:, m - 8, :], src, AF.Silu, [("ps", bk)], [f"sgssm{pb}"])
                rel(bk)
                yield

            nxt2 = cfg.get("next2")
            if nxt2 is not None:
                issue_rot_load(nxt2, rslot)
            yield "SPLIT"

        def s5_front(cfg, pb):
            L_ = _hdr(cfg, pb)
            sample, blk = L_['sample'], L_['blk']
            uT = L_['uT']
            cfg["front_done"] = True
            c5 = c_sb[:, :, :, :].rearrange("p r (t q) j -> p r t q j", q=4)
            for q in range(4):
                bk = bank()
                for ri in range(2):
                    for t in range(4):
                        gi = ri * 4 + t
                        for s in range(L):
                            mm(ps[bk][:, gi * NJ:(gi + 1) * NJ], Wt[32 * q:32 * q + 32, t, s, ri, :],
                               uT[32 * q:32 * q + 32, t, :].rearrange("p (j s) -> p j s", s=L)[:, :, s],
                               s == 0, s == L - 1, [f"uT{pb}", "Wt"], [("ps", bk)], tile_position=(32 * q, 0))
                cp("dve", c5[:, :, :, q, :], ps[bk][:, :].rearrange("p (r t j) -> p r t j", r=2, t=4),
                   [("ps", bk)], ["c_sb"])
                rel(bk)
                yield
            if sample:
                return
            CS = 2 * 16 * NJ
            n = NJ
            if blk == 0:
                memset("pool", carry[:], 0.0, ["carry"])
            for ph in range(4):
                Ps = slice(4 * ph, 4 * ph + 4)
                cv = c_sb[:, :, Ps, :]
                cvsw = AP(c_sb, 16 * NJ + 4 * ph * NJ, [[CS, 128], [-16 * NJ, 2], [NJ, 4], [1, n]])
                cosb = AP(cosT, 4 * ph * NJ, [[16 * NJ, 128], [0, 2], [NJ, 4], [1, n]])
                sinb = AP(sinT, 4 * ph * NJ, [[16 * NJ, 128], [0, 2], [NJ, 4], [1, n]])
                tt("dve", ta[:, :, :, :], cv, cosb, ALU.mult, ["c_sb", "Atab"], ["ta"])
                tt("dve", tb[:, :, :, :], cvsw, sinb, ALU.mult, ["c_sb", "Atab"], ["tb"])
                tt("dve", c_sb[:, 0, Ps, :], ta[:, 0, :, :], tb[:, 0, :, :], ALU.add, ["ta", "tb"], ["c_sb"])
                tt("dve", c_sb[:, 1, Ps, :], ta[:, 1, :, :], tb[:, 1, :, :], ALU.subtract, ["ta", "tb"], ["c_sb"])
                yield
            tt("dve", st3[:, :, :], carry[:, :, :], AP(rtab, 0, [[16, 128], [0, 2], [1, 16]]), ALU.mult,
               ["carry", "Atab"], ["st3"])
            tt("dve", c_sb[:, :, :, 0], c_sb[:, :, :, 0], st3[:, :, :], ALU.add, ["c_sb", "st3"], ["c_sb"])
            yield

        def phaseB(cfg, pb):
            L_ = _hdr(cfg, pb)
            sample, cf, seq, blk, row0 = L_['sample'], L_['cf'], L_['seq'], L_['blk'], L_['row0']
            qrot, krot, ktd, vtok = L_['qrot'], L_['krot'], L_['ktd'], L_['vtok']
            sgret, uT, sgssm, Hprev = L_['sgret'], L_['uT'], L_['sgssm'], L_['Hprev']
            def gen_scan():
                if DBG.get('stop', 99) <= 3:
                    return
                if not cfg.get("front_done"):
                    yield from s5_front(cfg, pb)
                if DBG.get('stop', 99) <= 4:
                    return
                CS = 2 * 16 * NJ

                def seg_scan(j0, n, init, ikey, fin_j, fin_out, fkey):
                    for ph in range(4):
                        Ps = slice(4 * ph, 4 * ph + 4)
                        cv = c_sb[:, :, Ps, j0:j0 + n]
                        cvsw = AP(c_sb, 16 * NJ + 4 * ph * NJ + j0, [[CS, 128], [-16 * NJ, 2], [NJ, 4], [1, n]])
                        cosb = AP(cosT, 4 * ph * NJ, [[16 * NJ, 128], [0, 2], [NJ, 4], [1, n]])
                        sinb = AP(sinT, 4 * ph * NJ, [[16 * NJ, 128], [0, 2], [NJ, 4], [1, n]])
                        tav = ta[:, :, :, 0:n]
                        tbv = tb[:, :, :, 0:n]
                        tt("dve", tav, cv, cosb, ALU.mult, ["c_sb", "Atab"], ["ta"])
                        tt("dve", tbv, cvsw, sinb, ALU.mult, ["c_sb", "Atab"], ["tb"])
                        tt("dve", c_sb[:, 0, Ps, j0:j0 + n], ta[:, 0, :, 0:n], tb[:, 0, :, 0:n], ALU.add, ["ta", "tb"], ["c_sb"])
                        tt("dve", c_sb[:, 1, Ps, j0:j0 + n], ta[:, 1, :, 0:n], tb[:, 1, :, 0:n], ALU.subtract, ["ta", "tb"], ["c_sb"])
                        yield
                        for ri in range(2):
                            for P in range(4 * ph, 4 * ph + 4):
                                row = c_sb[:, ri, P, j0:j0 + n]
                                T.op("dve", lambda e, row=row, P=P, ri=ri: e.tensor_tensor_scan(
                                    out=row, data0=AP(rtab, P, [[16, 128], [0, n]]), data1=row,
                                    initial=init[:, ri, P:P + 1], op0=ALU.mult, op1=ALU.add),
                                    ["c_sb", "Atab", ikey], [("c_row", ri, P)])
                            yield
                        rows = [("c_row", ri, P) for ri in range(2) for P in range(4 * ph, 4 * ph + 4)]
                        tt("dve", tav, cv, cosb, ALU.mult, ["c_sb", "Atab"] + rows, ["ta", "c_sb"])
                        tt("dve", tbv, cvsw, sinb, ALU.mult, ["c_sb", "Atab"] + rows, ["tb", "c_sb"])
                        if n > 1:
                            tt("dve", Hprev[:, 0, Ps, j0 + 1:j0 + n], ta[:, 0, :, 0:n - 1], tb[:, 0, :, 0:n - 1], ALU.subtract,
                               ["ta", "tb"], ["Hprev"])
                            tt("dve", Hprev[:, 1, Ps, j0 + 1:j0 + n], ta[:, 1, :, 0:n - 1], tb[:, 1, :, 0:n - 1], ALU.add,
                               ["ta", "tb"], ["Hprev"])
                        cp("dve", Hprev[:, :, Ps, j0], init[:, :, Ps], [ikey], ["Hprev"])
                        tt("dve", fin_out[:, 0, Ps], ta[:, 0, :, fin_j], tb[:, 0, :, fin_j], ALU.subtract, ["ta", "tb"], [fkey])
                        tt("dve", fin_out[:, 1, Ps], ta[:, 1, :, fin_j], tb[:, 1, :, fin_j], ALU.add, ["ta", "tb"], [fkey])
                        yield

                def big_scan():
                    n = NJ
                    tcv = rsd[:, :].rearrange("p (a b c) -> p a b c", a=2, b=4)
                    tdv = ot[:, :].rearrange("p (a b c) -> p a b c", a=2, b=4)
                    cp("dve", Hprev[:, :, :, 0], carry[:, :, :], ["carry"], ["Hprev"])
                    for ri in range(2):
                        row = c_sb[:, ri, :, :].rearrange("p a b -> p (a b)")
                        T.op("dve", lambda e, row=row: e.tensor_tensor_scan(
                            out=row, data0=Rt[:, :, :].rearrange("p a b -> p (a b)"), data1=row,
                            initial=0.0, op0=ALU.mult, op1=ALU.add), ["c_sb", "Atab"], ["c_sb"])
                        yield
                    for ph in range(4):
                        Ps = slice(4 * ph, 4 * ph + 4)
                        cv = c_sb[:, :, Ps, :]
                        cvsw = AP(c_sb, 16 * NJ + 4 * ph * NJ, [[CS, 128], [-16 * NJ, 2], [NJ, 4], [1, n]])
                        cosb = AP(cosT, 4 * ph * NJ, [[16 * NJ, 128], [0, 2], [NJ, 4], [1, n]])
                        sinb = AP(sinT, 4 * ph * NJ, [[16 * NJ, 128], [0, 2], [NJ, 4], [1, n]])
                        tt("dve", tcv, cv, cosb, ALU.mult, ["c_sb", "Atab"], ["rsd"])
                        tt("dve", tdv, cvsw, sinb, ALU.mult, ["c_sb", "Atab"], ["ot"])
                        tt("dve", Hprev[:, 0, Ps, 1:n], tcv[:, 0, :, 0:n - 1], tdv[:, 0, :, 0:n - 1], ALU.subtract,
                           ["rsd", "ot"], ["Hprev"])
                        tt("dve", Hprev[:, 1, Ps, 1:n], tcv[:, 1, :, 0:n - 1], tdv[:, 1, :, 0:n - 1], ALU.add,
                           ["rsd", "ot"], ["Hprev"])
                        tt("dve", carry[:, 0, Ps], tcv[:, 0, :, n - 1], tdv[:, 0, :, n - 1], ALU.subtract, ["rsd", "ot"], ["carry"])
                        tt("dve", carry[:, 1, Ps], tcv[:, 1, :, n - 1], tdv[:, 1, :, n - 1], ALU.add, ["rsd", "ot"], ["carry"])
                        yield

                if sample:
                    for s_ in range(NSEQ_S):
                        yield from seg_scan(s_ * 16, 16, h0s[:, s_, :, :], "h0s", TS // L - 1, hfin[:, s_, :, :], "hfin")
                    for s_ in range(NSEQ_S):
                        T.dma(AP(hre_s.tensor, s_ * 2048, [[1, 128], [128, 16]]), hfin[:, s_, 0, :], f"st_hs{s_}", reads=["hfin"])
                        T.dma(AP(him_s.tensor, s_ * 2048, [[1, 128], [128, 16]]), hfin[:, s_, 1, :], f"st_hs{s_}b", reads=["hfin"])
                else:
                    yield from big_scan()
                    if blk == NBLK_SEQ - 1:
                        T.dma(AP(hre_p.tensor, seq * 2048, [[1, 128], [128, 16]]), carry[:, 0, :], "st_hp", reads=["carry"])
                        T.dma(AP(him_p.tensor, seq * 2048, [[1, 128], [128, 16]]), carry[:, 1, :], "st_hpb", reads=["carry"])


            def gen_ret_tr():
                for (src, skey, dst, key, eng) in ((qrot, f"qrot{pb}", qT, "qT", "act"), (krot, f"krot{pb}", kT, "kT", "dve")):
                    bk = bank()
                    for h in range(4):
                        for t_ in range(NT):
                            tr(psb[bk][:, h * NB + t_ * 128: h * NB + (t_ + 1) * 128], src[:, t_, h * 128:(h + 1) * 128],
                               [skey, "ident_b"], [("ps", bk)])
                    cp(eng, dst[:, :, :].rearrange("p h n -> p (h n)"), psb[bk][:, :], [("ps", bk)], [key])
                    rel(bk)
                    yield
                tt("dve", qdT[:, :, :].rearrange("p h (c l) -> p h c l", l=64),
                   qT[:, :, :].rearrange("p h (c l) -> p h c l", l=64),
                   AP(qdec, cf * 256, [[512, 128], [64, 4], [0, NC], [1, 64]]), ALU.mult, ["qT", "qdec"], ["qdT"])


            def gen_ret():
                if DBG.get('stop', 99) <= 5:
                    return
                if DBG.get('stop', 99) <= 6:
                    return
                if (not sample) and blk == 0:
                    par0 = chunk_ctr[0] % 2
                    memset("pool", Smaster[:], 0.0, ["Smaster"])
                    memset("pool", Sprev[par0][:], 0.0, [f"Sprev{par0}"])
                bo = [bank(), bank()]
                chunk_par = []
                for c in range(NC):
                    t_ = c // 2
                    base = 64 * (c % 2)
                    par = chunk_ctr[0] % 2
                    chunk_ctr[0] += 1
                    chunk_par.append(par)
                    if sample:
                        T.dma(Smaster[:, :, :], AP(sret.tensor, c * 4 * HD * HD, [[HD, 128], [HD * HD, 4], [1, HD]]),
                              "ld_S", writes=["Smaster"])
                        cp("act", Sprev[par][:], Smaster[:], ["Smaster"], [f"Sprev{par}"])
                    bs = bank()
                    for h in range(4):
                        mm(ps[bs][base:base + 64, h * 64:(h + 1) * 64], kT[:, h, c * 64:(c + 1) * 64], qT[:, h, c * 64:(c + 1) * 64],
                           True, True, ["kT", "qT"], [("ps", bs)])
                    sT = sTb[c % 2]
                    yield
                    tt("dve", sT[base:base + 64, :], ps[bs][base:base + 64, 0:256], maskT[base:base + 64, cf, :], ALU.mult,
                       [("ps", bs), "maskT"], [f"sT{c % 2}"])
                    rel(bs)
                    bkv = bank()
                    for h in range(4):
                        mm(ps[bkv][:, h * 128:(h + 1) * 128], ktd[base:base + 64, t_, h * 128:(h + 1) * 128],
                           vtok[base:base + 64, t_, h * 128:(h + 1) * 128], True, True, [f"ktd{pb}", f"vtok{pb}"], [("ps", bkv)])
                    yield
                    for h in range(4):
                        ob = bo[h // 2]
                        oc = (h % 2) * NB + c * 64
                        mm(ps[ob][:, oc:oc + 64], vtok[:, t_, h * 128:(h + 1) * 128],
                           sT[:, h * 64:(h + 1) * 64], True, False, [f"vtok{pb}", f"sT{c % 2}"], [("ps", ob)])
                        mm(ps[ob][:, oc:oc + 64], Sprev[par][:, h, :], qdT[:, h, c * 64:(c + 1) * 64], False, True,
                           [f"Sprev{par}", "qdT"], [("ps", ob)])
                    yield
                    for h in range(4):
                        stt(Smaster[:, h, :], Smaster[:, h, :], sdec_t[:, cf, h:h + 1], ps[bkv][:, h * 128:(h + 1) * 128],
                            ALU.mult, ALU.add, [("ps", bkv), "Smaster", "sdec_t"], ["Smaster"])
                    rel(bkv)
                    yield
                    npar = 1 - par
                    if sample:
                        T.dma(AP(rs.tensor, c * 4 * HD * HD, [[HD, 128], [HD * HD, 4], [1, HD]]), Smaster[:, :, :], "st_rs",
                              reads=["Smaster"])
                    else:
                        cp("act", Sprev[npar][:], Smaster[:], ["Smaster"], [f"Sprev{npar}"])
                        if blk == NBLK_SEQ - 1 and c == NC - 1:
                            T.dma(AP(rp.tensor, seq * 4 * HD * HD, [[HD, 128], [HD * HD, 4], [1, HD]]), Smaster[:, :, :],
                                  "st_rp", reads=["Smaster"])

                if DBG.get('stop', 99) <= 7:
                    return
                bn = [bank(), bank()]
                for i2 in range(2):
                    act(osq[:, i2 * 512:(i2 + 1) * 512], ps[bo[i2]][:, :], AF.Square, [("ps", bo[i2])], ["osq"])
                    mm(ps[bn[i2]][:, :], ones_b[:, :], osq[:, i2 * 512:(i2 + 1) * 512], True, True, ["osq", "ones_b"],
                       [("ps", bn[i2])])
                    yield
                for i2 in range(2):
                    act(rsd[:, :], ps[bn[i2]][:, :], AF.Ln, [("ps", bn[i2])], ["rsd"], scale=1.0 / HD, bias=EPS)
                    rel(bn[i2])
                    act(rsd[:, :], rsd[:, :], AF.Exp, ["rsd"], ["rsd"], scale=-0.5)
                    tt("dve", ot[:, :], ps[bo[i2]][:, :], rsd[:, :], ALU.mult, [("ps", bo[i2]), "rsd"], ["ot"])
                    rel(bo[i2])
                    for hh in range(2):
                        h = 2 * i2 + hh
                        stt(mixT[:, h, :], ot[:, hh * NB:(hh + 1) * NB], rng_[:, h:h + 1], sgret[:, h, :], ALU.mult, ALU.mult,
                            ["ot", "rng", f"sgret{pb}"], [f"mixT{h}"])
                    yield


            yield from gen_scan()
            yield from gen_ret_tr()
            yield "SPLIT"
            yield from gen_ret()

            if DBG.get('stop', 99) <= 8:
                return
            for t2 in range(2):
                bk = bank()
                for half in range(2):
                    t = 2 * t2 + half
                    yv = ps[bk][:, half * NB:(half + 1) * NB].rearrange("p (j s) -> p j s", s=L)
                    uv = uT[:, t, :].rearrange("p (j s) -> p j s", s=L)
                    for i in range(L):
                        for k in range(i + 1):
                            mm(yv[:, :, i], Kbd[:, t, k, :], uv[:, :, i - k], k == 0, False, ["Kbd", f"uT{pb}"], [("ps", bk)])
                        for q in range(4):
                            P = 4 * t + q
                            for ri in range(2):
                                last = (ri == 1)
                                mm(ps[bk][32 * q:32 * q + 32, half * NB:(half + 1) * NB].rearrange("p (j s) -> p j s", s=L)[:, :, i],
                                   Et[:, P, i, ri, :], Hprev[:, ri, P, :], False, last, ["Et", "Hprev"], [("ps", bk)],
                                   tile_position=(0, 32 * q))
                        yield
                for half in range(2):
                    t = 2 * t2 + half
                    ysrc = ps[bk][:, half * NB:(half + 1) * NB]
                    act(zT[:, t, :], ysrc, AF.Gelu_apprx_tanh, [("ps", bk)], ["zT"])
                    if half == 1:
                        rel(bk)
                    yield

            if DBG.get('stop', 99) <= 9:
                return
            for m2 in range(2):
                bk = bank()
                for half in range(2):
                    m = 2 * m2 + half
                    for kt in range(4):
                        mm(ps[bk][:, half * NB:(half + 1) * NB], wglu_b[:, kt, m * 128:(m + 1) * 128], zT[:, kt, :],
                           kt == 0, kt == 3, ["wglu_b", "zT"], [("ps", bk)])
                    yield
                for half in range(2):
                    m = 2 * m2 + half
                    gb = g2[half]
                    GK = "ot"
                    act(gb[:], ps[bk][:, half * NB:(half + 1) * NB], AF.Sigmoid, [("ps", bk), "bglu"], [GK],
                        bias=bglu[:, m:m + 1])
                    if half == 1:
                        rel(bk)
                    tt("dve", gb[:], gb[:], zT[:, m, :], ALU.mult, [GK, "zT"], [GK])
                    tt("dve", mixT[:, 4 + m, :], gb[:], sgssm[:, m, :], ALU.mult, [GK, f"sgssm{pb}"], [f"mixT{4 + m}"])
                    yield

            if DBG.get('stop', 99) <= 10:
                return
            for t_ in range(NT):
                xi = cfg["xslots"][t_]
                XK = f"xslot{xi}"
                for half in range(2):
                    bk = bank()
                    for kt in range(8):
                        mm(ps[bk][:, :], mixT[:, kt, t_ * 128:(t_ + 1) * 128], wout_b[:, kt, half * 512:(half + 1) * 512],
                           kt == 0, kt == 7, [f"mixT{kt}", "wout_b"], [("ps", bk)])
                    tt("dve", xslot[xi][:, half * 512:(half + 1) * 512], ps[bk][:, :], xslot[xi][:, half * 512:(half + 1) * 512],
                       ALU.add, [("ps", bk), XK], [XK])
                    rel(bk)
                    yield
                c_ = 4 + t_
                act(osq[:, :], xslot[xi][:], AF.Square, [XK], ["osq", "ssq"], accum_out=ssq[:, c_:c_ + 1])
                act(ssq[:, c_:c_ + 1], ssq[:, c_:c_ + 1], AF.Ln, ["ssq"], ["ssq"], scale=1.0 / D, bias=EPS)
                act(rstd[:, c_:c_ + 1], ssq[:, c_:c_ + 1], AF.Exp, ["ssq"], ["rstd"], scale=-0.5)
                stt(xslot[xi][:], xslot[xi][:], rstd[:, c_:c_ + 1], gfin[:], ALU.mult, ALU.mult, [XK, "rstd", "gfin"], [XK])
                if sample:
                    for half in range(2):
                        s_ = 2 * t_ + half
                        T.dma(ys[s_ * TS:(s_ + 1) * TS, :], xslot[xi][64 * half:64 * half + TS, :], f"st_y{xi}", reads=[XK])
                else:
                    T.dma(yp[row0 + t_ * 128: row0 + (t_ + 1) * 128, :], xslot[xi][:], f"st_y{xi}", reads=[XK])
                nxt2 = cfg.get("next2")
                if nxt2 is not None:
                    nxt2.setdefault("xslots", [None] * NT)[t_] = xi
                    issue_x_load(nxt2, t_, xi)
                yield


        blocks = []
        if DBG.get("sample", True):
            blocks.append(dict(sample=True))
        for seq in range(NSEQ_P):
            for blk in range(DBG.get("nblk", NBLK_SEQ)):
                blocks.append(dict(sample=False, seq=seq, blk=blk))
        for i, c_ in enumerate(blocks):
            if i + 2 < len(blocks):
                c_["next2"] = blocks[i + 2]
        for i, c_ in enumerate(blocks[:2]):
            c_["xslots"] = [2 * i, 2 * i + 1]
            for t_ in range(NT):
                issue_x_load(c_, t_, 2 * i + t_)
            issue_rot_load(c_, i)

        def drain(g):
            n = 0
            for _ in g:
                n += 1
            return n

        if DBG.get("nopipe"):
            for i, c_ in enumerate(blocks):
                drain(phaseA(c_, i % 2))
                drain(phaseB(c_, i % 2))
        else:
            est = {"a1": 39.0, "a2": 9.0, "b1": 12.0, "b2": 50.0}
            drain(phaseA(blocks[0], 0))
            pre_a = {}
            for i in range(1, len(blocks) + 1):
                gb = phaseB(blocks[i - 1], (i - 1) % 2)
                if i in pre_a:
                    ga = pre_a.pop(i)
                else:
                    ga = phaseA(blocks[i], i % 2) if i < len(blocks) else iter(())
                for stage in (1, 2):
                    if stage == 2 and i < len(blocks) and not blocks[i]["sample"] and not DBG.get("nofront"):
                        ga = s5_front(blocks[i], i % 2)
                    na, nb_ = est[f"a{stage}"], est[f"b{stage}"]
                    ca = cb = 0
                    da = db = False
                    while not (da and db):
                        if db or (not da and ca / na <= cb / nb_):
                            try:
                                r = next(ga)
                                if r == "SPLIT" and stage == 1:
                                    da = True
                                else:
                                    ca += 1
                            except StopIteration:
                                da = True
                        else:
                            try:
                                r = next(gb)
                                if r == "SPLIT" and stage == 1:
                                    db = True
                                else:
                                    cb += 1
                            except StopIteration:
                                db = True
                    if ca > 0:
                        est[f"a{stage}"] = float(ca)
                    if cb > 0:
                        est[f"b{stage}"] = float(cb)
                if i + 1 < len(blocks) and not DBG.get("nopre"):
                    gn = phaseA(blocks[i + 1], (i + 1) % 2)
                    for _ in range(NT):
                        next(gn)
                    pre_a[i + 1] = gn
        T.finish("sp")
        build_nc.stats = (T.n_ops, T.n_waits, len(T.sems))
    return nc


def _constants():
    c = {}
    c["cident"] = np.eye(128, dtype=np.float32)
    half = 64
    inv = 10000.0 ** (-np.arange(half, dtype=np.float64) / half)

    def rot_tab(pos):
        ang = pos[:, None].astype(np.float64) * inv[None, :]
        cos, sin = np.cos(ang), np.sin(ang)
        return np.concatenate([cos, -sin, sin], axis=1).astype(np.float32)

    c["crot_p"] = rot_tab(np.arange(SEQ))
    rs_ = np.zeros((128, 192), np.float32)
    tab = rot_tab(PAST + np.arange(TS))
    rs_[0:TS] = tab
    rs_[64:64 + TS] = tab
    c["crot_s"] = rs_
    gam = np.array([1.0 - 2.0 ** (-5.0 - h) for h in range(4)], dtype=np.float64)
    scale = HD ** -0.5
    cmask = np.zeros((2, 128, 4, 64), np.float64)
    cq = np.zeros((2, 4, 64), np.float64)
    ck = np.zeros((2, 128, 4), np.float64)
    for cfg, blk in ((0, 64), (1, TS)):
        for h in range(4):
            for m in range(blk):
                for l in range(m, blk):
                    cmask[cfg, m, h, l] = gam[h] ** (l - m) * scale
                    cmask[cfg, 64 + m, h, l] = gam[h] ** (l - m) * scale
            for l in range(blk):
                cq[cfg, h, l] = gam[h] ** (l + 1)
                ck[cfg, l, h] = gam[h] ** (blk - 1 - l) * scale
                ck[cfg, 64 + l, h] = gam[h] ** (blk - 1 - l) * scale
    c["cmask"] = cmask.reshape(256, 256).astype(np.float32)
    c["cqdec"] = cq.reshape(2, 256).astype(np.float32)
    c["ckdec"] = ck.reshape(256, 4).astype(np.float32)
    dup = np.zeros((64, 2, 2, 64), np.float32)
    for m in range(2):
        dup[np.arange(64), m, m, np.arange(64)] = 1.0
    c["cdup"] = dup.reshape(64, 256)
    par = np.zeros((2, 32, 16), np.float32)
    for m in range(2):
        par[m, m::2, :] = 1.0
    c["cpar"] = par.reshape(2, 512)
    wm = np.zeros((8, 16, 2), np.float32)
    for gl in range(8):
        wm[gl, :, gl % 2] = 1.0
    c["cwmask"] = wm.reshape(128, 2)
    c["csdec"] = np.array([[g ** 64 for g in gam] + [g ** TS for g in gam]], dtype=np.float32)
    return c


_NC_CACHE = {}


def kernel(x_prompt, x_sample, state_ret, state_ssm_re, state_ssm_im, norm_g, w_in, ret_norm_g,
           ssm_lambda_re, ssm_lambda_im, ssm_log_step, ssm_b_re, ssm_b_im, ssm_c_re, ssm_c_im,
           ssm_d, w_glu, b_glu, w_out, final_norm_g):
    f = lambda a: np.ascontiguousarray(np.asarray(a, dtype=np.float32))
    x_prompt = f(x_prompt); x_sample = f(x_sample)
    consts = _constants()
    shared = {
        "norm_g": f(norm_g).reshape(1, D), "w_in": f(w_in).reshape(D, 3072),
        "ret_norm_g": f(ret_norm_g).reshape(4, HD),
        "lre": f(ssm_lambda_re).reshape(32, 64), "lim": f(ssm_lambda_im).reshape(32, 64),
        "lstep": f(ssm_log_step).reshape(1, 32),
        "bre": f(ssm_b_re).reshape(2048, 16), "bim": f(ssm_b_im).reshape(2048, 16),
        "cre": f(ssm_c_re).reshape(512, 64), "cim": f(ssm_c_im).reshape(512, 64),
        "ssm_d": f(ssm_d).reshape(1, 512), "w_glu": f(w_glu).reshape(512, 512),
        "b_glu": f(b_glu).reshape(1, 512), "w_out": f(w_out).reshape(D, D),
        "fng": f(final_norm_g).reshape(1, D),
    }
    shared.update(consts)
    sr = f(state_ret)[0]; s_re = f(state_ssm_re)[0]; s_im = f(state_ssm_im)[0]
    in_maps = []
    for c in range(8):
        m = dict(shared)
        m["xp"] = x_prompt[c * NSEQ_P:(c + 1) * NSEQ_P].reshape(NSEQ_P * SEQ, D)
        m["xs"] = x_sample[c * NSEQ_S:(c + 1) * NSEQ_S].reshape(NSEQ_S * TS, D)
        m["sret"] = sr[c * NSEQ_S:(c + 1) * NSEQ_S].reshape(NSEQ_S * 4 * HD, HD)
        m["sre"] = s_re[c * NSEQ_S:(c + 1) * NSEQ_S].reshape(NSEQ_S, 2048)
        m["sim"] = s_im[c * NSEQ_S:(c + 1) * NSEQ_S].reshape(NSEQ_S, 2048)
        in_maps.append(m)
    if "nc" not in _NC_CACHE:
        _NC_CACHE["nc"] = build_nc()
    nc = _NC_CACHE["nc"]
    res = run_bass_kernel_spmd(nc, in_maps, core_ids=list(range(8)))
    R = res.results
    cat = lambda k: np.concatenate([np.asarray(r[k], dtype=np.float32) for r in R], axis=0)
    y_prompt = cat("yp").reshape(16, SEQ, D)
    y_sample = cat("ys").reshape(32, TS, D)
    ret_p = cat("rp").reshape(1, 16, 4, HD, HD)
    hre_p = cat("hre_p").reshape(1, 16, 32, 64)
    him_p = cat("him_p").reshape(1, 16, 32, 64)
    ret_s = cat("rs").reshape(1, 32, 4, HD, HD)
    hre_s = cat("hre_s").reshape(1, 32, 32, 64)
    him_s = cat("him_s").reshape(1, 32, 32, 64)
    return (y_prompt, y_sample, ret_p, hre_p, him_p, ret_s, hre_s, him_s)
```

```python
import math
from contextlib import ExitStack

import numpy as np
import concourse.bass as bass
import concourse.mybir as mybir
from concourse.bass_utils import run_bass_kernel_spmd

F32 = mybir.dt.float32
BF16 = mybir.dt.bfloat16
ALU = mybir.AluOpType
AF = mybir.ActivationFunctionType

D = 1024
SEQ = 4096
NSEQ_P = 2
NSEQ_S = 4
TS = 16
PAST = 2048
NB = 256
NT = 2
NBLK_SEQ = SEQ // NB
L = 4
NJ = NB // L
NC = NB // 64
EPS = 1e-6
HD = 128
STRICT = True
DBG = {}


class Trk:
    def __init__(self, nc, es):
        self.nc = nc
        self.es = es
        self.eng = {"pe": nc.tensor, "act": nc.scalar, "dve": nc.vector, "pool": nc.gpsimd, "sp": nc.sync}
        self.sems = {}
        self.cnt = {}
        for e in ("pe", "act", "dve", "pool"):
            self.sems[e] = es.enter_context(nc.semaphore("sem_" + e))
            self.cnt[e] = 0
        self.known = {e: {} for e in self.eng}
        self.last_w = {}
        self.readers = {}
        self.n_ops = 0
        self.n_waits = 0

    def _deps(self, reads, writes):
        deps = {}

        def add(s, v):
            if v > deps.get(s, 0):
                deps[s] = v

        for k in reads:
            lw = self.last_w.get(k)
            if lw:
                add(*lw)
        for k in writes:
            lw = self.last_w.get(k)
            if lw:
                add(*lw)
            for s, v in self.readers.get(k, {}).items():
                add(s, v)
        return deps

    def _wait(self, e, deps, own=None):
        for s, v in deps.items():
            if s == own and (own == "pe" or not STRICT):
                continue
            if self.known[e].get(s, 0) >= v:
                continue
            self.eng[e].wait_ge(self.sems[s], v)
            self.known[e][s] = v
            self.n_waits += 1

    def op(self, e, fn, reads=(), writes=()):
        deps = self._deps(reads, writes)
        self._wait(e, deps, own=e)
        ins = fn(self.eng[e])
        self.cnt[e] += 1
        n = self.cnt[e]
        ins.then_inc(self.sems[e], 1)
        for k in reads:
            self.readers.setdefault(k, {})[e] = n
        for k in writes:
            self.last_w[k] = (e, n)
            self.readers[k] = {}
        self.n_ops += 1
        return ins

    def dma(self, out, in_, sem, reads=(), writes=(), q="sp", **kw):
        if sem not in self.sems:
            self.sems[sem] = self.es.enter_context(self.nc.semaphore("d_" + sem))
            self.cnt[sem] = 0
        deps = self._deps(reads, writes)
        if sem == "ld_setup":
            deps.pop(sem, None)
        self._wait(q, deps, own=None)
        ins = self.eng[q].dma_start(out=out, in_=in_, **kw)
        self.cnt[sem] += 16
        n = self.cnt[sem]
        ins.then_inc(self.sems[sem], 16)
        for k in reads:
            self.readers.setdefault(k, {})[sem] = n
        for k in writes:
            self.last_w[k] = (sem, n)
            self.readers[k] = {}
        self.n_ops += 1
        return ins

    def fence_group(self, sem):
        tot = self.cnt[sem]
        for k, lw in list(self.last_w.items()):
            if lw[0] == sem:
                self.last_w[k] = (sem, tot)

    def barrier(self):
        for e in self.eng:
            for s_, c in self.cnt.items():
                if s_ == e or c == 0:
                    continue
                if self.known[e].get(s_, 0) < c:
                    self.eng[e].wait_ge(self.sems[s_], c)
                    self.known[e][s_] = c

    def finish(self, e="sp"):
        for s, c in self.cnt.items():
            if s in ("pe", "act", "dve", "pool"):
                continue
            if c > 0 and self.known[e].get(s, 0) < c:
                self.eng[e].wait_ge(self.sems[s], c)
                self.known[e][s] = c


def AP(t, off, dims):
    return bass.AP(t, off, [list(d) for d in dims])


def build_nc():
    nc = bass.Bass("TRN2", target_bir_lowering=False)

    def din(name, shape):
        return nc.dram_tensor(name, list(shape), F32, kind="ExternalInput").ap()

    def dout(name, shape):
        return nc.dram_tensor(name, list(shape), F32, kind="ExternalOutput").ap()

    xp = din("xp", [NSEQ_P * SEQ, D])
    xs = din("xs", [NSEQ_S * TS, D])
    sret = din("sret", [NSEQ_S * 4 * HD, HD])
    sre = din("sre", [NSEQ_S, 2048])
    sim = din("sim", [NSEQ_S, 2048])
    norm_g = din("norm_g", [1, D])
    w_in = din("w_in", [D, 3072])
    ret_norm_g = din("ret_norm_g", [4, HD])
    lre = din("lre", [32, 64])
    lim = din("lim", [32, 64])
    lstep = din("lstep", [1, 32])
    bre = din("bre", [2048, 16])
    bim = din("bim", [2048, 16])
    cre = din("cre", [512, 64])
    cim = din("cim", [512, 64])
    ssm_d = din("ssm_d", [1, 512])
    w_glu = din("w_glu", [512, 512])
    b_glu = din("b_glu", [1, 512])
    w_out = din("w_out", [D, D])
    fng = din("fng", [1, D])
    cident = din("cident", [128, 128])
    crot_p = din("crot_p", [SEQ, 192])
    crot_s = din("crot_s", [128, 192])
    cmask = din("cmask", [2 * 128, 256])
    cqdec = din("cqdec", [2, 256])
    ckdec = din("ckdec", [2 * 128, 4])
    cdup = din("cdup", [64, 256])
    cpar = din("cpar", [2, 512])
    cwmask = din("cwmask", [128, 2])
    csdec = din("csdec", [1, 8])

    yp = dout("yp", [NSEQ_P * SEQ, D])
    ys = dout("ys", [NSEQ_S * TS, D])
    rp = dout("rp", [NSEQ_P * 4 * HD, HD])
    hre_p = dout("hre_p", [NSEQ_P, 2048])
    him_p = dout("him_p", [NSEQ_P, 2048])
    rs = dout("rs", [NSEQ_S * 4 * HD, HD])
    hre_s = dout("hre_s", [NSEQ_S, 2048])
    him_s = dout("him_s", [NSEQ_S, 2048])

    with ExitStack() as es:
        es.enter_context(nc.allow_non_contiguous_dma(reason="small parameter layouts"))
        T = Trk(nc, es)

        def sb(name, shape, dt=F32):
            return es.enter_context(nc.sbuf_tensor(name, list(shape), dt))

        def tap(name, ap_, shape, keys, dt=F32):
            if not DBG.get("taps"):
                return
            d = nc.dram_tensor("tap_" + name, list(shape), dt, kind="ExternalOutput").ap()
            T.dma(d, ap_, "dbg", reads=keys)

        win_b = sb("win_b", [128, 8, 3072], BF16)
        wout_b = sb("wout_b", [128, 8, 1024], BF16)
        wglu_b = sb("wglu_b", [128, 4, 512], BF16)
        ident_f = sb("ident_f", [128, 128], F32)
        ident_b = sb("ident_b", [128, 128], BF16)
        ones_b = sb("ones_b", [128, 128], BF16)
        ng = sb("ng", [128, 8], F32)
        rng_ = sb("rng", [128, 4], F32)
        bglu = sb("bglu", [128, 4], F32)
        dvec = sb("dvec", [128, 4], F32)
        gfin = sb("gfin", [128, D], F32)
        maskT = sb("maskT", [128, 2, 256], F32)
        qdec = sb("qdec", [128, 2, 256], F32)
        kdec = sb("kdec", [128, 2, 4], F32)
        sdec_t = sb("sdec_t", [128, 2, 4], F32)
        Kbd = sb("Kbd", [128, 4, L, 128], BF16)
        Wt = sb("Wt", [128, 4, L, 2, 128], BF16)
        Et = sb("Et", [128, 16, L, 2, 32], BF16)
        cosT = sb("cosT", [128, 16, NJ], F32)
        sinT = sb("sinT", [128, 16, NJ], F32)
        rtab = sb("rtab", [128, 16], F32)
        Rt = sb("Rt", [128, 16, NJ], F32)
        h0s = sb("h0s", [128, NSEQ_S, 2, 16], F32)

        NXS = 4
        xslot = [sb(f"xslot{i}", [128, D], F32) for i in range(NXS)]

        ps = [es.enter_context(nc.psum_tensor(f"ps{i}", [128, 512], F32)) for i in range(8)]
        psb = [p.bitcast(BF16) for p in ps]
        bank_ctr = [0]

        pinned = set()

        def bank(pin=True):
            for _ in range(9):
                b = bank_ctr[0] % 8
                bank_ctr[0] += 1
                if b not in pinned:
                    break
            else:
                raise RuntimeError("out of PSUM banks")
            if pin:
                pinned.add(b)
            return b

        def rel(*bs):
            for b in bs:
                pinned.discard(b)

        def act(out, in_, func, reads, writes, **kw):
            return T.op("act", lambda e: e.activation(out=out, in_=in_, func=func, **kw), reads, writes)

        def tt(eng, out, in0, in1, op, reads, writes):
            return T.op(eng, lambda e: e.tensor_tensor(out=out, in0=in0, in1=in1, op=op), reads, writes)

        def ts(eng, out, in0, s1, s2, op0, op1, reads, writes):
            return T.op(eng, lambda e: e.tensor_scalar(out=out, in0=in0, scalar1=s1, scalar2=s2, op0=op0, op1=op1),
                        reads, writes)

        def stt(out, in0, scalar, in1, op0, op1, reads, writes):
            return T.op("dve", lambda e: e.scalar_tensor_tensor(out=out, in0=in0, scalar=scalar, in1=in1,
                                                                 op0=op0, op1=op1), reads, writes)

        def cp(eng, out, in_, reads, writes):
            if eng == "act":
                return act(out, in_, AF.Copy, reads, writes)
            return T.op(eng, lambda e: e.tensor_copy(out=out, in_=in_), reads, writes)

        def mm(out, lhsT, rhs, start, stop, reads, writes, **kw):
            return T.op("pe", lambda e: e.matmul(out, lhsT=lhsT, rhs=rhs, start=start, stop=stop, **kw), reads, writes)

        def tr(out, in_, reads, writes):
            return T.op("pe", lambda e: e.transpose(out=out, in_=in_, identity=ident_b[:]), reads, writes)

        def memset(eng, ap_, val, writes):
            return T.op(eng, lambda e: e.memset(ap_, val), (), writes)

        T.dma(ident_f[:], cident, "ld_setup", writes=["ident_f"])
        T.dma(ng[:], AP(norm_g.tensor, 0, [[1, 128], [128, 8]]), "ld_setup", writes=["ng"])
        T.dma(rng_[:], AP(ret_norm_g.tensor, 0, [[1, 128], [128, 4]]), "ld_setup", writes=["rng"])
        T.dma(bglu[:], AP(b_glu.tensor, 0, [[1, 128], [128, 4]]), "ld_setup", writes=["bglu"])
        T.dma(dvec[:], AP(ssm_d.tensor, 0, [[1, 128], [128, 4]]), "ld_setup", writes=["dvec"])
        T.dma(gfin[:], AP(fng.tensor, 0, [[0, 128], [1, D]]), "ld_setup", writes=["gfin"])
        T.dma(sdec_t[:, :, :], AP(csdec.tensor, 0, [[0, 128], [4, 2], [1, 4]]), "ld_setup", writes=["sdec_t"])
        for c in range(2):
            T.dma(maskT[:, c, :], cmask[c * 128:(c + 1) * 128, :], "ld_setup", writes=["maskT"])
            T.dma(qdec[:, c, :], AP(cqdec.tensor, c * 256, [[0, 128], [1, 256]]), "ld_setup", writes=["qdec"])
            T.dma(kdec[:, c, :], ckdec[c * 128:(c + 1) * 128, :], "ld_setup", writes=["kdec"])

        with ExitStack() as es2:
            def sb2(name, shape, dt=F32):
                return es2.enter_context(nc.sbuf_tensor(name, list(shape), dt))

            K1 = "s5"
            lre1 = sb2("lre1", [64, 32]); lim1 = sb2("lim1", [64, 32]); dt1 = sb2("dt1", [64, 32])
            T.dma(lre1[:], AP(lre.tensor, 0, [[1, 64], [64, 32]]), "ld_setup", writes=["lre1"])
            T.dma(lim1[:], AP(lim.tensor, 0, [[1, 64], [64, 32]]), "ld_setup", writes=["lim1"])
            T.dma(dt1[:], AP(lstep.tensor, 0, [[0, 64], [1, 32]]), "ld_setup", writes=["dt1"])
            b1re = sb2("b1re", [64, 32, 16]); b1im = sb2("b1im", [64, 32, 16])
            T.dma(b1re[:], AP(bre.tensor, 0, [[16, 64], [1024, 32], [1, 16]]), "ld_setup", writes=["b1re"])
            T.dma(b1im[:], AP(bim.tensor, 0, [[16, 64], [1024, 32], [1, 16]]), "ld_setup", writes=["b1im"])
            cnre = sb2("cnre", [128, 4, 64]); cnim = sb2("cnim", [128, 4, 64])
            T.dma(cnre[:], AP(cre.tensor, 0, [[64, 128], [128 * 64, 4], [1, 64]]), "ld_setup", writes=["cnre"])
            T.dma(cnim[:], AP(cim.tensor, 0, [[64, 128], [128 * 64, 4], [1, 64]]), "ld_setup", writes=["cnim"])
            dup = sb2("dup", [64, 2, 128])
            T.dma(dup[:], cdup, "ld_setup", writes=["dup"])
            parm = sb2("parm", [64, 2, 512])
            T.dma(parm[:], AP(cpar.tensor, 0, [[0, 64], [512, 2], [1, 512]]), "ld_setup", writes=["parm"])
            wmask = sb2("wmask", [128, 2])
            T.dma(wmask[:], cwmask, "ld_setup", writes=["wmask"])
            for s in range(NSEQ_S):
                T.dma(h0s[:, s, 0, :], AP(sre.tensor, s * 2048, [[1, 128], [128, 16]]), "ld_setup", writes=["h0s"])
                T.dma(h0s[:, s, 1, :], AP(sim.tensor, s * 2048, [[1, 128], [128, 16]]), "ld_setup", writes=["h0s"])

            T.fence_group("ld_setup")
            cp("act", ident_b[:], ident_f[:], ["ident_f"], ["ident_b"])
            memset("pool", ones_b[:], 1.0, ["ones_b"])
            cast_engs = ["act", "dve", "pool"]
            ci = [0]

            def load_cast(dst_ap, src_ap, ncols, wkey, scale_ap=None):
                i = ci[0] % NXS
                e = cast_engs[ci[0] % 3]
                ci[0] += 1
                T.dma(xslot[i][:, 0:ncols], src_ap, f"ld_xs{i}", writes=[f"xslot{i}"])
                if scale_ap is None:
                    cp(e, dst_ap, xslot[i][:, 0:ncols], [f"xslot{i}"], [wkey])
                elif e == "act":
                    act(dst_ap, xslot[i][:, 0:ncols], AF.Copy, [f"xslot{i}", "ng"], [wkey], scale=scale_ap)
                else:
                    ts(e, dst_ap, xslot[i][:, 0:ncols], scale_ap, None, ALU.mult, ALU.bypass, [f"xslot{i}", "ng"], [wkey])

            for kt in range(8):
                for c in range(3):
                    load_cast(win_b[:, kt, c * 1024:(c + 1) * 1024], w_in[kt * 128:(kt + 1) * 128, c * 1024:(c + 1) * 1024],
                              1024, "win_b", scale_ap=ng[:, kt:kt + 1])
            for kt in range(8):
                load_cast(wout_b[:, kt, :], w_out[kt * 128:(kt + 1) * 128, :], 1024, "wout_b")
            for kt in range(4):
                load_cast(wglu_b[:, kt, :], w_glu[kt * 128:(kt + 1) * 128, :], 512, "wglu_b")


            def t64(name):
                return sb2(name, [64, 32])

            lrc = t64("lrc"); dtv = t64("dtv"); are = t64("are"); aim = t64("aim"); mag = t64("mag")
            th = t64("th"); th2 = t64("th2"); wS = t64("wS"); wC = t64("wC")
            cc = t64("cc"); ss_ = t64("ss_"); cs = t64("cs")
            RW = [K1]
            ts("dve", lrc[:], lre1[:], -1e-4, None, ALU.min, ALU.bypass, ["lre1"] + RW, RW)
            act(dtv[:], dt1[:], AF.Exp, ["dt1"] + RW, RW)
            tt("dve", are[:], lrc[:], dtv[:], ALU.mult, RW, RW)
            tt("dve", aim[:], lim1[:], dtv[:], ALU.mult, ["lim1"] + RW, RW)
            act(mag[:], are[:], AF.Exp, RW, RW)
            ts("dve", th[:], aim[:], 1.0 / 32.0, None, ALU.mult, ALU.bypass, RW, RW)
            tt("dve", th2[:], th[:], th[:], ALU.mult, RW, RW)
            a = [-1.0 / 6, 1.0 / 120, -1.0 / 5040, 1.0 / 362880]
            ts("dve", wS[:], th2[:], a[3], None, ALU.mult, ALU.bypass, RW, RW)
            for k in (2, 1, 0):
                ts("dve", wS[:], wS[:], a[k], None, ALU.add, ALU.bypass, RW, RW)
                tt("dve", wS[:], wS[:], th2[:], ALU.mult, RW, RW)
            ts("dve", wS[:], wS[:], 1.0, None, ALU.add, ALU.bypass, RW, RW)
            tt("dve", wS[:], wS[:], th[:], ALU.mult, RW, RW)
            b = [-0.5, 1.0 / 24, -1.0 / 720, 1.0 / 40320, -1.0 / 3628800]
            ts("dve", wC[:], th2[:], b[4], None, ALU.mult, ALU.bypass, RW, RW)
            for k in (3, 2, 1, 0):
                ts("dve", wC[:], wC[:], b[k], None, ALU.add, ALU.bypass, RW, RW)
                tt("dve", wC[:], wC[:], th2[:], ALU.mult, RW, RW)
            ts("dve", wC[:], wC[:], 1.0, None, ALU.add, ALU.bypass, RW, RW)
            for _ in range(5):
                tt("dve", cc[:], wC[:], wC[:], ALU.mult, RW, RW)
                tt("dve", ss_[:], wS[:], wS[:], ALU.mult, RW, RW)
                tt("dve", cs[:], wC[:], wS[:], ALU.mult, RW, RW)
                tt("dve", wC[:], cc[:], ss_[:], ALU.subtract, RW, RW)
                ts("dve", wS[:], cs[:], 2.0, None, ALU.mult, ALU.bypass, RW, RW)
            pwr = sb2("pwr", [64, L + 1, 32]); pwi = sb2("pwi", [64, L + 1, 32])
            memset("dve", pwr[:, 0, :], 1.0, RW)
            memset("dve", pwi[:, 0, :], 0.0, RW)
            tt("dve", pwr[:, 1, :], mag[:], wC[:], ALU.mult, RW, RW)
            tt("dve", pwi[:, 1, :], mag[:], wS[:], ALU.mult, RW, RW)
            tA = t64("tA"); tB = t64("tB")

            def cmul(o_re, o_im, a_re, a_im, b_re, b_im, t1_, t2_, xr=()):
                R_ = RW + list(xr)
                tt("dve", t1_, a_re, b_re, ALU.mult, R_, RW)
                tt("dve", t2_, a_im, b_im, ALU.mult, R_, RW)
                tt("dve", o_re, t1_, t2_, ALU.subtract, R_, RW)
                tt("dve", t1_, a_re, b_im, ALU.mult, R_, RW)
                tt("dve", t2_, a_im, b_re, ALU.mult, R_, RW)
                tt("dve", o_im, t1_, t2_, ALU.add, R_, RW)

            for k in range(1, L):
                cmul(pwr[:, k + 1, :], pwi[:, k + 1, :], pwr[:, k, :], pwi[:, k, :], pwr[:, 1, :], pwi[:, 1, :],
                     tA[:], tB[:])
            nre = t64("nre"); den = t64("den"); qre = t64("qre"); qim = t64("qim")
            ts("dve", nre[:], pwr[:, 1, :], -1.0, None, ALU.add, ALU.bypass, RW, RW)
            tt("dve", den[:], lrc[:], lrc[:], ALU.mult, RW, RW)
            tt("dve", tA[:], lim1[:], lim1[:], ALU.mult, RW, RW)
            tt("dve", den[:], den[:], tA[:], ALU.add, RW, RW)
            T.op("dve", lambda e: e.reciprocal(out=den[:], in_=den[:]), RW, RW)
            tt("dve", tA[:], nre[:], lrc[:], ALU.mult, RW, RW)
            tt("dve", tB[:], pwi[:, 1, :], lim1[:], ALU.mult, RW, RW)
            tt("dve", qre[:], tA[:], tB[:], ALU.add, RW, RW)
            tt("dve", qre[:], qre[:], den[:], ALU.mult, RW, RW)
            tt("dve", tA[:], pwi[:, 1, :], lrc[:], ALU.mult, RW, RW)
            tt("dve", tB[:], nre[:], lim1[:], ALU.mult, RW, RW)
            tt("dve", qim[:], tA[:], tB[:], ALU.subtract, RW, RW)
            tt("dve", qim[:], qim[:], den[:], ALU.mult, RW, RW)

            tap("lre1", lre1[:], [64, 32], ["lre1"]); tap("dt1", dt1[:], [64, 32], ["dt1"])
            tap("mag", mag[:], [64, 32], RW); tap("wC", wC[:], [64, 32], RW); tap("wS", wS[:], [64, 32], RW)
            tap("pwr", pwr[:, :, :].rearrange("p a b -> p (a b)"), [64, (L + 1) * 32], RW)
            tap("pwi", pwi[:, :, :].rearrange("p a b -> p (a b)"), [64, (L + 1) * 32], RW)
            tap("qre", qre[:], [64, 32], RW); tap("qim", qim[:], [64, 32], RW)

            def bc16(t, k=None):
                if k is None:
                    return AP(t, 0, [[32, 64], [1, 32], [0, 16]])
                return AP(t, k * 32, [[(L + 1) * 32, 64], [1, 32], [0, 16]])

            bbr = sb2("bbr", [64, 32, 16]); bbi = sb2("bbi", [64, 32, 16])
            u1 = sb2("u1", [64, 32, 16]); u2 = sb2("u2", [64, 32, 16])
            cmul(bbr[:], bbi[:], bc16(qre), bc16(qim), b1re[:], b1im[:], u1[:], u2[:], xr=["b1re", "b1im"])

            CTr = sb2("CTr", [64, 512]); CTi = sb2("CTi", [64, 512]); nCTi = sb2("nCTi", [64, 512])
            for (src, dst, key) in ((cnre, CTr, "cnre"), (cnim, CTi, "cnim")):
                bk = bank(False)
                for t in range(4):
                    mm(ps[bk][0:64, t * 128:(t + 1) * 128], src[:, t, :], ident_f[:, :], True, True,
                       [key, "ident_f"], [("ps", bk)])
                cp("dve", dst[:], ps[bk][0:64, :], [("ps", bk)], RW)
            ts("dve", nCTi[:], CTi[:], -1.0, None, ALU.mult, ALU.bypass, RW, RW)

            tap("bbr", bbr[:, :, :].rearrange("p a b -> p (a b)"), [64, 512], RW)
            tap("CTr", CTr[:], [64, 512], RW); tap("nCTi", nCTi[:], [64, 512], RW)
            Vre = sb2("Vre", [64, 512]); Vim = sb2("Vim", [64, 512])
            Vpr = sb2("Vpr", [64, 8, 128]); Vpi = sb2("Vpi", [64, 8, 128])
            memset("pool", Vpr[:], 0.0, ["Vp"])
            memset("pool", Vpi[:], 0.0, ["Vp"])
            V3r = Vre[:, :].rearrange("p (g c) -> p g c", c=16)
            V3i = Vim[:, :].rearrange("p (g c) -> p g c", c=16)
            dgr = AP(Vpr, 0, [[8 * 128, 64], [144, 8], [1, 16]])
            dgi = AP(Vpi, 0, [[8 * 128, 64], [144, 8], [1, 16]])
            for k in range(L):
                cmul(V3r, V3i, bc16(pwr, k), bc16(pwi, k), bbr[:], bbi[:], u1[:], u2[:])
                for t in range(4):
                    T.op("dve", lambda e: e.tensor_copy(
                        out=dgr, in_=Vre[:, t * 128:(t + 1) * 128].rearrange("p (g c) -> p g c", c=16)), RW, ["Vp"])
                    T.op("dve", lambda e: e.tensor_copy(
                        out=dgi, in_=Vim[:, t * 128:(t + 1) * 128].rearrange("p (g c) -> p g c", c=16)), RW, ["Vp"])
                    bk = bank(False)
                    for gl in range(8):
                        g = 8 * t + gl
                        mm(ps[bk][:, 16 * gl:16 * gl + 16], Vpr[:, gl, :], CTr[:, g * 16:(g + 1) * 16], True, False,
                           ["Vp"] + RW, [("ps", bk)])
                        mm(ps[bk][:, 16 * gl:16 * gl + 16], Vpi[:, gl, :], nCTi[:, g * 16:(g + 1) * 16], False, True,
                           ["Vp"] + RW, [("ps", bk)])
                    if k == 0:
                        stt(Kbd[:, t, k, :], ident_f[:, :], dvec[:, t:t + 1], ps[bk][:, 0:128], ALU.mult, ALU.add,
                            [("ps", bk), "ident_f", "dvec"], ["Kbd"])
                    else:
                        cp("dve", Kbd[:, t, k, :], ps[bk][:, 0:128], [("ps", bk)], ["Kbd"])
                s = L - 1 - k
                for ri, Vx in ((0, Vre), (1, Vim)):
                    bk = bank(False)
                    for t in range(4):
                        mm(ps[bk][:, t * 64:(t + 1) * 64], Vx[:, t * 128:(t + 1) * 128], ident_f[0:64, 0:64], True, True,
                           RW + ["ident_f"], [("ps", bk)])
                    for t in range(4):
                        tt("dve", Wt[:, t, s, ri, :].rearrange("p (m q) -> p m q", m=2),
                           AP(ps[bk], t * 64, [[512, 128], [0, 2], [1, 64]]),
                           AP(wmask, 0, [[2, 128], [1, 2], [0, 64]]), ALU.mult,
                           [("ps", bk), "wmask"], ["Wt"])
            EVr = sb2("EVr", [64, 512]); EVi = sb2("EVi", [64, 512])
            EVm = [sb2(f"EVm{m}", [64, 512]) for m in range(2)]
            E3r = EVr[:, :].rearrange("p (g c) -> p g c", c=16)
            E3i = EVi[:, :].rearrange("p (g c) -> p g c", c=16)
            CT3r = CTr[:, :].rearrange("p (g c) -> p g c", c=16)
            CT3i = CTi[:, :].rearrange("p (g c) -> p g c", c=16)
            for i in range(L):
                cmul(E3r, E3i, bc16(pwr, i + 1), bc16(pwi, i + 1), CT3r, CT3i, u1[:], u2[:])
                for ri, EV in ((0, EVr), (1, EVi)):
                    for m in range(2):
                        tt("dve", EVm[m][:], EV[:], parm[:, m, :], ALU.mult, RW + ["parm"], ["EVm"])
                    bk = bank(False)
                    mm(ps[bk][:, :], dup[:, 0, :], EVm[0][:], True, False, ["dup", "EVm"], [("ps", bk)])
                    mm(ps[bk][:, :], dup[:, 1, :], EVm[1][:], False, True, ["dup", "EVm"], [("ps", bk)])
                    T.op("act", lambda e, bk=bk, ri=ri, i=i: e.activation(
                        out=Et[:, :, i, ri, :], in_=ps[bk][:, :].rearrange("p (a b) -> p a b", b=32),
                        func=AF.Copy, scale=(1.0 if ri == 0 else -1.0)), [("ps", bk)], ["Et"])
            r1 = t64("r1"); uc = t64("uc"); us = t64("us")
            tt("dve", r1[:], mag[:], mag[:], ALU.mult, RW, RW)
            tt("dve", r1[:], r1[:], r1[:], ALU.mult, RW, RW)
            cp("dve", uc[:], wC[:], RW, RW)
            cp("dve", us[:], wS[:], RW, RW)
            for _ in range(2):
                tt("dve", cc[:], uc[:], uc[:], ALU.mult, RW, RW)
                tt("dve", ss_[:], us[:], us[:], ALU.mult, RW, RW)
                tt("dve", cs[:], uc[:], us[:], ALU.mult, RW, RW)
                tt("dve", uc[:], cc[:], ss_[:], ALU.subtract, RW, RW)
                ts("dve", us[:], cs[:], 2.0, None, ALU.mult, ALU.bypass, RW, RW)
            bk = bank(False)
            for i3, src in enumerate((r1, uc, us)):
                for m in range(2):
                    mm(ps[bk][:, i3 * 16:(i3 + 1) * 16], dup[:, m, :], AP(src, m, [[32, 64], [2, 16]]), m == 0, m == 1,
                       RW + ["dup"], [("ps", bk)])
            pwa = sb2("pwa", [128, 16]); pwb = sb2("pwb", [128, 16])
            x1 = sb2("x1", [128, 16, 32]); x2 = sb2("x2", [128, 16, 32])
            y1 = sb2("y1", [128, 16]); y2 = sb2("y2", [128, 16]); y3 = sb2("y3", [128, 16])
            AK = ["Atab"]
            cp("dve", rtab[:, :], ps[bk][:, 0:16], [("ps", bk)], AK)
            cp("dve", pwa[:, :], ps[bk][:, 16:32], [("ps", bk)], AK)
            cp("dve", pwb[:, :], ps[bk][:, 32:48], [("ps", bk)], AK)
            cp("dve", cosT[:, :, 0], pwa[:, :], AK, AK)
            cp("dve", sinT[:, :, 0], pwb[:, :], AK, AK)
            k = 1
            while k < NJ:
                pa = AP(pwa, 0, [[16, 128], [1, 16], [0, k]])
                pb = AP(pwb, 0, [[16, 128], [1, 16], [0, k]])
                tt("dve", x1[:, :, 0:k], cosT[:, :, 0:k], pa, ALU.mult, AK, AK)
                tt("dve", x2[:, :, 0:k], sinT[:, :, 0:k], pb, ALU.mult, AK, AK)
                tt("dve", cosT[:, :, k:2 * k], x1[:, :, 0:k], x2[:, :, 0:k], ALU.subtract, AK, AK)
                tt("dve", x1[:, :, 0:k], cosT[:, :, 0:k], pb, ALU.mult, AK, AK)
                tt("dve", x2[:, :, 0:k], sinT[:, :, 0:k], pa, ALU.mult, AK, AK)
                tt("dve", sinT[:, :, k:2 * k], x1[:, :, 0:k], x2[:, :, 0:k], ALU.add, AK, AK)
                tt("dve", y1[:], pwa[:], pwa[:], ALU.mult, AK, AK)
                tt("dve", y2[:], pwb[:], pwb[:], ALU.mult, AK, AK)
                tt("dve", y3[:], pwa[:], pwb[:], ALU.mult, AK, AK)
                tt("dve", pwa[:], y1[:], y2[:], ALU.subtract, AK, AK)
                ts("dve", pwb[:], y3[:], 2.0, None, ALU.mult, ALU.bypass, AK, AK)
                k *= 2
            memset("dve", Rt[:, :, :], 0.0, AK)
            cp("dve", Rt[:, :, 1:NJ], AP(rtab, 0, [[16, 128], [1, 16], [0, NJ - 1]]), AK, AK)
            for hh in range(2):
                sl = slice(hh * 32, (hh + 1) * 32)
                tt("dve", x1[:, :, :], cosT[:, :, sl], cosT[:, :, sl], ALU.mult, AK, AK)
                tt("dve", x2[:, :, :], sinT[:, :, sl], sinT[:, :, sl], ALU.mult, AK, AK)
                tt("dve", x1[:, :, :], x1[:, :, :], x2[:, :, :], ALU.add, AK, AK)
                ts("dve", x1[:, :, :], x1[:, :, :], -0.5, 1.5, ALU.mult, ALU.add, AK, AK)
                tt("dve", cosT[:, :, sl], cosT[:, :, sl], x1[:, :, :], ALU.mult, AK, AK)
                tt("dve", sinT[:, :, sl], sinT[:, :, sl], x1[:, :, :], ALU.mult, AK, AK)

        T.barrier()
        pinned.clear()
        if DBG.get("dump"):
            d_kbd = nc.dram_tensor("d_kbd", [128, 4 * L * 128], BF16, kind="ExternalOutput").ap()
            d_wt = nc.dram_tensor("d_wt", [128, 4 * L * 2 * 128], BF16, kind="ExternalOutput").ap()
            d_et = nc.dram_tensor("d_et", [128, 16 * L * 2 * 32], BF16, kind="ExternalOutput").ap()
            d_a = nc.dram_tensor("d_a", [128, 16 + 2 * 16 * NJ], F32, kind="ExternalOutput").ap()
            T.dma(d_kbd, Kbd[:, :, :, :].rearrange("p a b c -> p (a b c)"), "dbg", reads=["Kbd"])
            T.dma(d_wt, Wt[:, :, :, :, :].rearrange("p a b c d -> p (a b c d)"), "dbg", reads=["Wt"])
            T.dma(d_et, Et[:, :, :, :, :].rearrange("p a b c d -> p (a b c d)"), "dbg", reads=["Et"])
            T.dma(d_a[:, 0:16], rtab[:, :], "dbg", reads=["Atab"])
            T.dma(d_a[:, 16:16 + 16 * NJ], cosT[:, :, :].rearrange("p a b -> p (a b)"), "dbg", reads=["Atab"])
            T.dma(d_a[:, 16 + 16 * NJ:], sinT[:, :, :].rearrange("p a b -> p (a b)"), "dbg", reads=["Atab"])
        if DBG.get("setup_only"):
            T.finish("sp")
            build_nc.stats = (T.n_ops, T.n_waits, len(T.sems))
            return nc
        hnb = [sb(f"hnb{i}", [128, D], BF16) for i in range(1)]
        hnT = sb("hnT", [128, 8, NB], BF16)
        mixT = sb("mixT", [128, 8, NB], BF16)
        rot = [sb(f"rot{i}", [128, NT, 192], F32) for i in range(2)]
        rt1 = sb("rt1", [128, 512], F32)
        rt2 = sb("rt2", [128, 512], F32)
        qrot2 = [sb(f"qrot{i}", [128, NT, 512], BF16) for i in range(2)]
        krot2 = [sb(f"krot{i}", [128, NT, 512], BF16) for i in range(2)]
        ktd2 = [sb(f"ktd{i}", [128, NT, 512], BF16) for i in range(2)]
        vtok2 = [sb(f"vtok{i}", [128, NT, 512], BF16) for i in range(2)]
        qT = sb("qT", [128, 4, NB], BF16)
        kT = sb("kT", [128, 4, NB], BF16)
        qdT = sb("qdT", [128, 4, NB], BF16)
        sgret2 = [sb(f"sgret{i}", [128, 4, NB], BF16) for i in range(2)]
        uT2 = [sb(f"uT{i}", [128, 4, NB], BF16) for i in range(2)]
        sgssm2 = [sb(f"sgssm{i}", [128, 4, NB], BF16) for i in range(2)]
        c_sb = sb("c_sb", [128, 2, 16, NJ], F32)
        Hprev2 = [sb("Hprev0", [128, 2, 16, NJ], BF16)] * 2
        carry = sb("carry", [128, 2, 16], F32)
        ta = sb("ta", [128, 2, 4, NJ], F32)
        tb = sb("tb", [128, 2, 4, NJ], F32)
        hfin = sb("hfin", [128, NSEQ_S, 2, 16], F32)
        st3 = sb("st3", [128, 2, 16], F32)
        sTb = [sb(f"sTb{i}", [128, 256], BF16) for i in range(2)]
        Smaster = sb("Smaster", [128, 4, HD], F32)
        Sprev = [sb(f"Sprev{i}", [128, 4, HD], BF16) for i in range(2)]
        osq = sb("osq", [128, 4 * NB], BF16)
        rsd = sb("rsd", [128, 512], F32)
        ot = sb("ot", [128, 512], F32)
        zT = sb("zT", [128, 4, NB], BF16)
        g1 = [rsd[:, 0:NB]] * 2
        g2 = [ot[:, 0:NB]] * 2
        ssq = sb("ssq", [128, 8], F32)
        rstd = sb("rstd", [128, 8], F32)


        memset("pool", sTb[0][:], 0.0, ["sT0"])
        memset("pool", sTb[1][:], 0.0, ["sT1"])
        xs_ctr = [0]
        LQ = DBG.get("lq", "act")

        def next_xslot():
            i = xs_ctr[0] % NXS
            xs_ctr[0] += 1
            return i

        chunk_ctr = [0]
        blk_ctr = [0]

        def interleave2(g1, g2, n1, n2):
            c1 = c2 = 0
            d1 = d2 = False
            while not (d1 and d2):
                if d2 or (not d1 and c1 / n1 <= c2 / n2):
                    try:
                        next(g1); c1 += 1
                    except StopIteration:
                        d1 = True
                else:
                    try:
                        next(g2); c2 += 1
                    except StopIteration:
                        d2 = True
                yield

        def issue_x_load(cfg, t_, xi):
            XK = f"xslot{xi}"
            if cfg["sample"]:
                memset("pool", xslot[xi][:], 0.0, [XK])
                for half in range(2):
                    s_ = 2 * t_ + half
                    T.dma(xslot[xi][64 * half:64 * half + TS, :], xs[s_ * TS:(s_ + 1) * TS, :], f"ld_xs{xi}", writes=[XK])
            else:
                r0 = cfg["seq"] * SEQ + cfg["blk"] * NB
                T.dma(xslot[xi][:], xp[r0 + t_ * 128: r0 + (t_ + 1) * 128, :], f"ld_xs{xi}", writes=[XK])

        def issue_rot_load(cfg, rslot):
            RK = f"rot{rslot}"
            cfg["rslot"] = rslot
            if cfg["sample"]:
                for t_ in range(NT):
                    T.dma(rot[rslot][:, t_, :], crot_s, f"ld_rot{rslot}", writes=[RK])
            else:
                pos0 = cfg["blk"] * NB
                T.dma(rot[rslot][:, :, :], AP(crot_p.tensor, pos0 * 192, [[192, 128], [128 * 192, NT], [1, 192]]),
                      f"ld_rot{rslot}", writes=[RK])

        def _hdr(cfg, pb):
            sample = cfg["sample"]
            cf = 1 if sample else 0
            seq = cfg.get("seq", 0)
            blk = cfg.get("blk", 0)
            row0 = seq * SEQ + blk * NB
            gam = [1.0 - 2.0 ** (-5.0 - h) for h in range(4)]
            sdec = [g ** (16 if sample else 64) for g in gam]

            qrot, krot, ktd, vtok = qrot2[pb], krot2[pb], ktd2[pb], vtok2[pb]
            sgret, uT, sgssm, Hprev = sgret2[pb], uT2[pb], sgssm2[pb], Hprev2[pb]
            return locals()

        def phaseA(cfg, pb):
            L_ = _hdr(cfg, pb)
            sample, cf, seq, blk, row0 = L_['sample'], L_['cf'], L_['seq'], L_['blk'], L_['row0']
            qrot, krot, ktd, vtok = L_['qrot'], L_['krot'], L_['ktd'], L_['vtok']
            sgret, uT, sgssm, Hprev = L_['sgret'], L_['uT'], L_['sgssm'], L_['Hprev']
            rslot = cfg["rslot"]
            RK = f"rot{rslot}"

            for t_ in range(NT):
                xi = cfg["xslots"][t_]
                XK = f"xslot{xi}"
                hb = hnb[0]
                HK = "hnb0"
                act(hb[:], xslot[xi][:], AF.Square, [XK], [HK, "ssqA"], accum_out=ssq[:, t_:t_ + 1])
                act(ssq[:, t_:t_ + 1], ssq[:, t_:t_ + 1], AF.Ln, ["ssqA"], ["ssqA"], scale=1.0 / D, bias=EPS)
                act(rstd[:, t_:t_ + 1], ssq[:, t_:t_ + 1], AF.Exp, ["ssqA"], ["rstdA"], scale=-0.5)
                act(hb[:], xslot[xi][:], AF.Copy, [XK, "rstdA"], [HK], scale=rstd[:, t_:t_ + 1])
                bk = bank()
                for kt in range(8):
                    tr(psb[bk][:, kt * 128:(kt + 1) * 128], hb[:, kt * 128:(kt + 1) * 128], [HK, "ident_b"], [("ps", bk)])
                cp("act", hnT[:, :, t_ * 128:(t_ + 1) * 128], psb[bk][:, :].rearrange("p (k t) -> p k t", k=8),
                   [("ps", bk)], [f"hnT{t_}"])
                rel(bk)
                yield

            if DBG.get('stop', 99) <= 1:
                return
            for t_ in range(NT):
                cosb = AP(rot[rslot], t_ * 192, [[NT * 192, 128], [0, 4], [0, 2], [1, 64]])
                sinb = AP(rot[rslot], t_ * 192 + 64, [[NT * 192, 128], [0, 4], [64, 2], [1, 64]])
                for (c0, dst, dkey) in ((0, qrot, f"qrot{pb}"), (512, krot, f"krot{pb}"), (1024, None, None)):
                    bb = bank()
                    for kt in range(8):
                        mm(ps[bb][:, :], hnT[:, kt, t_ * 128:(t_ + 1) * 128], win_b[:, kt, c0:c0 + 512], kt == 0, kt == 7,
                           [f"hnT{t_}", "win_b"], [("ps", bb)])
                    yield
                    if dst is None:
                        cp("act", vtok[:, t_, :], ps[bb][:, :], [("ps", bb)], [f"vtok{pb}"])
                        rel(bb)
                        yield
                        continue
                    pv = AP(ps[bb], 0, [[512, 128], [128, 4], [64, 2], [1, 64]])
                    psw = AP(ps[bb], 64, [[512, 128], [128, 4], [-64, 2], [1, 64]])
                    tt("dve", rt1[:, :].rearrange("p (h a d) -> p h a d", h=4, a=2), pv, cosb, ALU.mult,
                       [("ps", bb), RK], ["rt1"])
                    tt("dve", rt2[:, :].rearrange("p (h a d) -> p h a d", h=4, a=2), psw, sinb, ALU.mult,
                       [("ps", bb), RK], ["rt2"])
                    rel(bb)
                    tt("dve", dst[:, t_, :], rt1[:, :], rt2[:, :], ALU.add, ["rt1", "rt2"], [dkey])
                    yield
                for h in range(4):
                    act(ktd[:, t_, h * 128:(h + 1) * 128], krot[:, t_, h * 128:(h + 1) * 128], AF.Copy,
                        [f"krot{pb}", "kdec"], [f"ktd{pb}"], scale=kdec[:, cf, h:h + 1])
                yield

            if DBG.get('stop', 99) <= 2:
                return
            for m2 in range(6):
                bk = bank()
                for half in range(2):
                    m = 2 * m2 + half
                    c0 = 1536 + m * 128
                    for kt in range(8):
                        mm(ps[bk][:, half * NB:(half + 1) * NB], win_b[:, kt, c0:c0 + 128], hnT[:, kt, :], kt == 0, kt == 7,
                           ["hnT0", "hnT1", "win_b"], [("ps", bk)])
                    yield
                for half in range(2):
                    m = 2 * m2 + half
                    src = ps[bk][:, half * NB:(half + 1) * NB]
                    if m < 4:
                        act(sgret[:, m, :], src, AF.Silu, [("ps", bk)], [f"sgret{pb}"])
                    elif m < 8:
                        cp("act", uT[:, m - 4, :], src, [("ps", bk)], [f"uT{pb}"])
                    else:
                        act(sgssm[:, m - 8, :], src, AF.Silu, [("ps", bk)], [f"sgssm{pb}"])
                rel(bk)
                yield

            nxt2 = cfg.get("next2")
            if nxt2 is not None:
                issue_rot_load(nxt2, rslot)
            yield "SPLIT"

        def s5_front(cfg, pb):
            L_ = _hdr(cfg, pb)
            sample, blk = L_['sample'], L_['blk']
            uT = L_['uT']
            cfg["front_done"] = True
            c5 = c_sb[:, :, :, :].rearrange("p r (t q) j -> p r t q j", q=4)
            for q in range(4):
                bk = bank()
                for ri in range(2):
                    for t in range(4):
                        gi = ri * 4 + t
                        for s in range(L):
                            mm(ps[bk][:, gi * NJ:(gi + 1) * NJ], Wt[32 * q:32 * q + 32, t, s, ri, :],
                               uT[32 * q:32 * q + 32, t, :].rearrange("p (j s) -> p j s", s=L)[:, :, s],
                               s == 0, s == L - 1, [f"uT{pb}", "Wt"], [("ps", bk)], tile_position=(32 * q, 0))
                cp("dve", c5[:, :, :, q, :], ps[bk][:, :].rearrange("p (r t j) -> p r t j", r=2, t=4),
                   [("ps", bk)], ["c_sb"])
                rel(bk)
                yield
            if sample:
                return
            CS = 2 * 16 * NJ
            n = NJ
            if blk == 0:
                memset("pool", carry[:], 0.0, ["carry"])
            for ph in range(4):
                Ps = slice(4 * ph, 4 * ph + 4)
                cv = c_sb[:, :, Ps, :]
                cvsw = AP(c_sb, 16 * NJ + 4 * ph * NJ, [[CS, 128], [-16 * NJ, 2], [NJ, 4], [1, n]])
                cosb = AP(cosT, 4 * ph * NJ, [[16 * NJ, 128], [0, 2], [NJ, 4], [1, n]])
                sinb = AP(sinT, 4 * ph * NJ, [[16 * NJ, 128], [0, 2], [NJ, 4], [1, n]])
                tt("dve", ta[:, :, :, :], cv, cosb, ALU.mult, ["c_sb", "Atab"], ["ta"])
                tt("dve", tb[:, :, :, :], cvsw, sinb, ALU.mult, ["c_sb", "Atab"], ["tb"])
                tt("dve", c_sb[:, 0, Ps, :], ta[:, 0, :, :], tb[:, 0, :, :], ALU.add, ["ta", "tb"], ["c_sb"])
                tt("dve", c_sb[:, 1, Ps, :], ta[:, 1, :, :], tb[:, 1, :, :], ALU.subtract, ["ta", "tb"], ["c_sb"])
                yield
            tt("dve", st3[:, :, :], carry[:, :, :], AP(rtab, 0, [[16, 128], [0, 2], [1, 16]]), ALU.mult,
               ["carry", "Atab"], ["st3"])
            tt("dve", c_sb[:, :, :, 0], c_sb[:, :, :, 0], st3[:, :, :], ALU.add, ["c_sb", "st3"], ["c_sb"])
            yield

        def phaseB(cfg, pb):
            L_ = _hdr(cfg, pb)
            sample, cf, seq, blk, row0 = L_['sample'], L_['cf'], L_['seq'], L_['blk'], L_['row0']
            qrot, krot, ktd, vtok = L_['qrot'], L_['krot'], L_['ktd'], L_['vtok']
            sgret, uT, sgssm, Hprev = L_['sgret'], L_['uT'], L_['sgssm'], L_['Hprev']
            def gen_scan():
                if DBG.get('stop', 99) <= 3:
                    return
                if not cfg.get("front_done"):
                    yield from s5_front(cfg, pb)
                if DBG.get('stop', 99) <= 4:
                    return
                CS = 2 * 16 * NJ

                def seg_scan(j0, n, init, ikey, fin_j, fin_out, fkey):
                    for ph in range(4):
                        Ps = slice(4 * ph, 4 * ph + 4)
                        cv = c_sb[:, :, Ps, j0:j0 + n]
                        cvsw = AP(c_sb, 16 * NJ + 4 * ph * NJ + j0, [[CS, 128], [-16 * NJ, 2], [NJ, 4], [1, n]])
                        cosb = AP(cosT, 4 * ph * NJ, [[16 * NJ, 128], [0, 2], [NJ, 4], [1, n]])
                        sinb = AP(sinT, 4 * ph * NJ, [[16 * NJ, 128], [0, 2], [NJ, 4], [1, n]])
                        tav = ta[:, :, :, 0:n]
                        tbv = tb[:, :, :, 0:n]
                        tt("dve", tav, cv, cosb, ALU.mult, ["c_sb", "Atab"], ["ta"])
                        tt("dve", tbv, cvsw, sinb, ALU.mult, ["c_sb", "Atab"], ["tb"])
                        tt("dve", c_sb[:, 0, Ps, j0:j0 + n], ta[:, 0, :, 0:n], tb[:, 0, :, 0:n], ALU.add, ["ta", "tb"], ["c_sb"])
                        tt("dve", c_sb[:, 1, Ps, j0:j0 + n], ta[:, 1, :, 0:n], tb[:, 1, :, 0:n], ALU.subtract, ["ta", "tb"], ["c_sb"])
                        yield
                        for ri in range(2):
                            for P in range(4 * ph, 4 * ph + 4):
                                row = c_sb[:, ri, P, j0:j0 + n]
                                T.op("dve", lambda e, row=row, P=P, ri=ri: e.tensor_tensor_scan(
                                    out=row, data0=AP(rtab, P, [[16, 128], [0, n]]), data1=row,
                                    initial=init[:, ri, P:P + 1], op0=ALU.mult, op1=ALU.add),
                                    ["c_sb", "Atab", ikey], [("c_row", ri, P)])
                            yield
                        rows = [("c_row", ri, P) for ri in range(2) for P in range(4 * ph, 4 * ph + 4)]
                        tt("dve", tav, cv, cosb, ALU.mult, ["c_sb", "Atab"] + rows, ["ta", "c_sb"])
                        tt("dve", tbv, cvsw, sinb, ALU.mult, ["c_sb", "Atab"] + rows, ["tb", "c_sb"])
                        if n > 1:
                            tt("dve", Hprev[:, 0, Ps, j0 + 1:j0 + n], ta[:, 0, :, 0:n - 1], tb[:, 0, :, 0:n - 1], ALU.subtract,
                               ["ta", "tb"], ["Hprev"])
                            tt("dve", Hprev[:, 1, Ps, j0 + 1:j0 + n], ta[:, 1, :, 0:n - 1], tb[:, 1, :, 0:n - 1], ALU.add,
                               ["ta", "tb"], ["Hprev"])
                        cp("dve", Hprev[:, :, Ps, j0], init[:, :, Ps], [ikey], ["Hprev"])
                        tt("dve", fin_out[:, 0, Ps], ta[:, 0, :, fin_j], tb[:, 0, :, fin_j], ALU.subtract, ["ta", "tb"], [fkey])
                        tt("dve", fin_out[:, 1, Ps], ta[:, 1, :, fin_j], tb[:, 1, :, fin_j], ALU.add, ["ta", "tb"], [fkey])
                        yield

                def big_scan():
                    n = NJ
                    tcv = rsd[:, :].rearrange("p (a b c) -> p a b c", a=2, b=4)
                    tdv = ot[:, :].rearrange("p (a b c) -> p a b c", a=2, b=4)
                    cp("dve", Hprev[:, :, :, 0], carry[:, :, :], ["carry"], ["Hprev"])
                    for ri in range(2):
                        row = c_sb[:, ri, :, :].rearrange("p a b -> p (a b)")
                        T.op("dve", lambda e, row=row: e.tensor_tensor_scan(
                            out=row, data0=Rt[:, :, :].rearrange("p a b -> p (a b)"), data1=row,
                            initial=0.0, op0=ALU.mult, op1=ALU.add), ["c_sb", "Atab"], ["c_sb"])
                        yield
                    for ph in range(4):
                        Ps = slice(4 * ph, 4 * ph + 4)
                        cv = c_sb[:, :, Ps, :]
                        cvsw = AP(c_sb, 16 * NJ + 4 * ph * NJ, [[CS, 128], [-16 * NJ, 2], [NJ, 4], [1, n]])
                        cosb = AP(cosT, 4 * ph * NJ, [[16 * NJ, 128], [0, 2], [NJ, 4], [1, n]])
                        sinb = AP(sinT, 4 * ph * NJ, [[16 * NJ, 128], [0, 2], [NJ, 4], [1, n]])
                        tt("dve", tcv, cv, cosb, ALU.mult, ["c_sb", "Atab"], ["rsd"])
                        tt("dve", tdv, cvsw, sinb, ALU.mult, ["c_sb", "Atab"], ["ot"])
                        tt("dve", Hprev[:, 0, Ps, 1:n], tcv[:, 0, :, 0:n - 1], tdv[:, 0, :, 0:n - 1], ALU.subtract,
                           ["rsd", "ot"], ["Hprev"])
                        tt("dve", Hprev[:, 1, Ps, 1:n], tcv[:, 1, :, 0:n - 1], tdv[:, 1, :, 0:n - 1], ALU.add,
                           ["rsd", "ot"], ["Hprev"])
                        tt("dve", carry[:, 0, Ps], tcv[:, 0, :, n - 1], tdv[:, 0, :, n - 1], ALU.subtract, ["rsd", "ot"], ["carry"])
                        tt("dve", carry[:, 1, Ps], tcv[:, 1, :, n - 1], tdv[:, 1, :, n - 1], ALU.add, ["rsd", "ot"], ["carry"])
                        yield

                if sample:
                    for s_ in range(NSEQ_S):
                        yield from seg_scan(s_ * 16, 16, h0s[:, s_, :, :], "h0s", TS // L - 1, hfin[:, s_, :, :], "hfin")
                    for s_ in range(NSEQ_S):
                        T.dma(AP(hre_s.tensor, s_ * 2048, [[1, 128], [128, 16]]), hfin[:, s_, 0, :], f"st_hs{s_}", reads=["hfin"])
                        T.dma(AP(him_s.tensor, s_ * 2048, [[1, 128], [128, 16]]), hfin[:, s_, 1, :], f"st_hs{s_}b", reads=["hfin"])
                else:
                    yield from big_scan()
                    if blk == NBLK_SEQ - 1:
                        T.dma(AP(hre_p.tensor, seq * 2048, [[1, 128], [128, 16]]), carry[:, 0, :], "st_hp", reads=["carry"])
                        T.dma(AP(him_p.tensor, seq * 2048, [[1, 128], [128, 16]]), carry[:, 1, :], "st_hpb", reads=["carry"])


            def gen_ret_tr():
                for (src, skey, dst, key, eng) in ((qrot, f"qrot{pb}", qT, "qT", "act"), (krot, f"krot{pb}", kT, "kT", "dve")):
                    bk = bank()
                    for h in range(4):
                        for t_ in range(NT):
                            tr(psb[bk][:, h * NB + t_ * 128: h * NB + (t_ + 1) * 128], src[:, t_, h * 128:(h + 1) * 128],
                               [skey, "ident_b"], [("ps", bk)])
                    cp(eng, dst[:, :, :].rearrange("p h n -> p (h n)"), psb[bk][:, :], [("ps", bk)], [key])
                    rel(bk)
                    yield
                tt("dve", qdT[:, :, :].rearrange("p h (c l) -> p h c l", l=64),
                   qT[:, :, :].rearrange("p h (c l) -> p h c l", l=64),
                   AP(qdec, cf * 256, [[512, 128], [64, 4], [0, NC], [1, 64]]), ALU.mult, ["qT", "qdec"], ["qdT"])


            def gen_ret():
                if DBG.get('stop', 99) <= 5:
                    return
                if DBG.get('stop', 99) <= 6:
                    return
                if (not sample) and blk == 0:
                    par0 = chunk_ctr[0] % 2
                    memset("pool", Smaster[:], 0.0, ["Smaster0", "Smaster1", "Smaster2", "Smaster3"])
                    memset("pool", Sprev[par0][:], 0.0, [f"Sprev{par0}"])
                bo = [bank(), bank()]
                chunk_par = []
                for c in range(NC):
                    t_ = c // 2
                    base = 64 * (c % 2)
                    par = chunk_ctr[0] % 2
                    chunk_ctr[0] += 1
                    chunk_par.append(par)
                    if sample:
                        T.dma(Smaster[:, :, :], AP(sret.tensor, c * 4 * HD * HD, [[HD, 128], [HD * HD, 4], [1, HD]]),
                              "ld_S", writes=["Smaster0", "Smaster1", "Smaster2", "Smaster3"])
                        cp("act", Sprev[par][:], Smaster[:], ["Smaster0", "Smaster1", "Smaster2", "Smaster3"], [f"Sprev{par}"])
                    bs = bank()
                    for h in range(4):
                        mm(ps[bs][base:base + 64, h * 64:(h + 1) * 64], kT[:, h, c * 64:(c + 1) * 64], qT[:, h, c * 64:(c + 1) * 64],
                           True, True, ["kT", "qT"], [("ps", bs)])
                    sT = sTb[c % 2]
                    yield
                    tt("dve", sT[base:base + 64, :], ps[bs][base:base + 64, 0:256], maskT[base:base + 64, cf, :], ALU.mult,
                       [("ps", bs), "maskT"], [f"sT{c % 2}"])
                    rel(bs)
                    bkv = bank()
                    for h in range(4):
                        mm(ps[bkv][:, h * 128:(h + 1) * 128], ktd[base:base + 64, t_, h * 128:(h + 1) * 128],
                           vtok[base:base + 64, t_, h * 128:(h + 1) * 128], True, True, [f"ktd{pb}", f"vtok{pb}"], [("ps", bkv)])
                    yield
                    for h in range(4):
                        ob = bo[h // 2]
                        oc = (h % 2) * NB + c * 64
                        mm(ps[ob][:, oc:oc + 64], vtok[:, t_, h * 128:(h + 1) * 128],
                           sT[:, h * 64:(h + 1) * 64], True, False, [f"vtok{pb}", f"sT{c % 2}"], [("ps", ob)])
                        mm(ps[ob][:, oc:oc + 64], Sprev[par][:, h, :], qdT[:, h, c * 64:(c + 1) * 64], False, True,
                           [f"Sprev{par}", "qdT"], [("ps", ob)])
                    yield
                    for h in range(4):
                        stt(Smaster[:, h, :], Smaster[:, h, :], sdec_t[:, cf, h:h + 1], ps[bkv][:, h * 128:(h + 1) * 128],
                            ALU.mult, ALU.add, [("ps", bkv), f"Smaster{h}", "sdec_t"], [f"Smaster{h}"])
                    rel(bkv)
                    yield
                    npar = 1 - par
                    if sample:
                        T.dma(AP(rs.tensor, c * 4 * HD * HD, [[HD, 128], [HD * HD, 4], [1, HD]]), Smaster[:, :, :], "st_rs",
                              reads=["Smaster0", "Smaster1", "Smaster2", "Smaster3"])
                    else:
                        cp("act", Sprev[npar][:], Smaster[:], ["Smaster0", "Smaster1", "Smaster2", "Smaster3"], [f"Sprev{npar}"])
                        if blk == NBLK_SEQ - 1 and c == NC - 1:
                            T.dma(AP(rp.tensor, seq * 4 * HD * HD, [[HD, 128], [HD * HD, 4], [1, HD]]), Smaster[:, :, :],
                                  "st_rp", reads=["Smaster0", "Smaster1", "Smaster2", "Smaster3"])

                if DBG.get('stop', 99) <= 7:
                    return
                bn = [bank(), bank()]
                for i2 in range(2):
                    act(osq[:, i2 * 512:(i2 + 1) * 512], ps[bo[i2]][:, :], AF.Square, [("ps", bo[i2])], ["osq"])
                    mm(ps[bn[i2]][:, :], ones_b[:, :], osq[:, i2 * 512:(i2 + 1) * 512], True, True, ["osq", "ones_b"],
                       [("ps", bn[i2])])
                    yield
                for i2 in range(2):
                    act(rsd[:, :], ps[bn[i2]][:, :], AF.Ln, [("ps", bn[i2])], ["rsd"], scale=1.0 / HD, bias=EPS)
                    rel(bn[i2])
                    act(rsd[:, :], rsd[:, :], AF.Exp, ["rsd"], ["rsd"], scale=-0.5)
                    tt("dve", ot[:, :], ps[bo[i2]][:, :], rsd[:, :], ALU.mult, [("ps", bo[i2]), "rsd"], ["ot"])
                    rel(bo[i2])
                    for hh in range(2):
                        h = 2 * i2 + hh
                        stt(mixT[:, h, :], ot[:, hh * NB:(hh + 1) * NB], rng_[:, h:h + 1], sgret[:, h, :], ALU.mult, ALU.mult,
                            ["ot", "rng", f"sgret{pb}"], [f"mixT{h}"])
                    yield


            yield from gen_scan()
            yield from gen_ret_tr()
            yield "SPLIT"
            yield from gen_ret()

            if DBG.get('stop', 99) <= 8:
                return
            for t2 in range(2):
                bk = bank()
                for half in range(2):
                    t = 2 * t2 + half
                    yv = ps[bk][:, half * NB:(half + 1) * NB].rearrange("p (j s) -> p j s", s=L)
                    uv = uT[:, t, :].rearrange("p (j s) -> p j s", s=L)
                    for i in range(L):
                        for k in range(i + 1):
                            mm(yv[:, :, i], Kbd[:, t, k, :], uv[:, :, i - k], k == 0, False, ["Kbd", f"uT{pb}"], [("ps", bk)])
                        for q in range(4):
                            P = 4 * t + q
                            for ri in range(2):
                                last = (ri == 1)
                                mm(ps[bk][32 * q:32 * q + 32, half * NB:(half + 1) * NB].rearrange("p (j s) -> p j s", s=L)[:, :, i],
                                   Et[:, P, i, ri, :], Hprev[:, ri, P, :], False, last, ["Et", "Hprev"], [("ps", bk)],
                                   tile_position=(0, 32 * q))
                        yield
                for half in range(2):
                    t = 2 * t2 + half
                    ysrc = ps[bk][:, half * NB:(half + 1) * NB]
                    act(zT[:, t, :], ysrc, AF.Gelu_apprx_tanh, [("ps", bk)], ["zT"])
                    if half == 1:
                        rel(bk)
                    yield

            if DBG.get('stop', 99) <= 9:
                return
            for m2 in range(2):
                bk = bank()
                for half in range(2):
                    m = 2 * m2 + half
                    for kt in range(4):
                        mm(ps[bk][:, half * NB:(half + 1) * NB], wglu_b[:, kt, m * 128:(m + 1) * 128], zT[:, kt, :],
                           kt == 0, kt == 3, ["wglu_b", "zT"], [("ps", bk)])
                    yield
                for half in range(2):
                    m = 2 * m2 + half
                    gb = g2[half]
                    GK = "ot"
                    act(gb[:], ps[bk][:, half * NB:(half + 1) * NB], AF.Sigmoid, [("ps", bk), "bglu"], [GK],
                        bias=bglu[:, m:m + 1])
                    if half == 1:
                        rel(bk)
                    tt("dve", gb[:], gb[:], zT[:, m, :], ALU.mult, [GK, "zT"], [GK])
                    tt("dve", mixT[:, 4 + m, :], gb[:], sgssm[:, m, :], ALU.mult, [GK, f"sgssm{pb}"], [f"mixT{4 + m}"])
                    yield

            if DBG.get('stop', 99) <= 10:
                return
            for t_ in range(NT):
                xi = cfg["xslots"][t_]
                XK = f"xslot{xi}"
                for half in range(2):
                    bk = bank()
                    for kt in range(8):
                        mm(ps[bk][:, :], mixT[:, kt, t_ * 128:(t_ + 1) * 128], wout_b[:, kt, half * 512:(half + 1) * 512],
                           kt == 0, kt == 7, [f"mixT{kt}", "wout_b"], [("ps", bk)])
                    tt("dve", xslot[xi][:, half * 512:(half + 1) * 512], ps[bk][:, :], xslot[xi][:, half * 512:(half + 1) * 512],
                       ALU.add, [("ps", bk), XK], [XK])
                    rel(bk)
                    yield
                c_ = 4 + t_
                act(osq[:, :], xslot[xi][:], AF.Square, [XK], ["osq", "ssq"], accum_out=ssq[:, c_:c_ + 1])
                act(ssq[:, c_:c_ + 1], ssq[:, c_:c_ + 1], AF.Ln, ["ssq"], ["ssq"], scale=1.0 / D, bias=EPS)
                act(rstd[:, c_:c_ + 1], ssq[:, c_:c_ + 1], AF.Exp, ["ssq"], ["rstd"], scale=-0.5)
                stt(xslot[xi][:], xslot[xi][:], rstd[:, c_:c_ + 1], gfin[:], ALU.mult, ALU.mult, [XK, "rstd", "gfin"], [XK])
                if sample:
                    for half in range(2):
                        s_ = 2 * t_ + half
                        T.dma(ys[s_ * TS:(s_ + 1) * TS, :], xslot[xi][64 * half:64 * half + TS, :], f"st_y{xi}", reads=[XK])
                else:
                    T.dma(yp[row0 + t_ * 128: row0 + (t_ + 1) * 128, :], xslot[xi][:], f"st_y{xi}", reads=[XK])
                nxt2 = cfg.get("next2")
                if nxt2 is not None:
                    nxt2.setdefault("xslots", [None] * NT)[t_] = xi
                    issue_x_load(nxt2, t_, xi)
                yield


        blocks = []
        if DBG.get("sample", True):
            blocks.append(dict(sample=True))
        for seq in range(NSEQ_P):
            for blk in range(DBG.get("nblk", NBLK_SEQ)):
                blocks.append(dict(sample=False, seq=seq, blk=blk))
        for i, c_ in enumerate(blocks):
            if i + 2 < len(blocks):
                c_["next2"] = blocks[i + 2]
        for i, c_ in enumerate(blocks[:2]):
            c_["xslots"] = [2 * i, 2 * i + 1]
            for t_ in range(NT):
                issue_x_load(c_, t_, 2 * i + t_)
            issue_rot_load(c_, i)

        def drain(g):
            n = 0
            for _ in g:
                n += 1
            return n

        if DBG.get("nopipe"):
            for i, c_ in enumerate(blocks):
                drain(phaseA(c_, i % 2))
                drain(phaseB(c_, i % 2))
        else:
            est = {"a1": 39.0, "a2": 9.0, "b1": 12.0, "b2": 50.0}
            drain(phaseA(blocks[0], 0))
            pre_a = {}
            for i in range(1, len(blocks) + 1):
                gb = phaseB(blocks[i - 1], (i - 1) % 2)
                if i in pre_a:
                    ga = pre_a.pop(i)
                else:
                    ga = phaseA(blocks[i], i % 2) if i < len(blocks) else iter(())
                for stage in (1, 2):
                    if stage == 2 and i < len(blocks) and not blocks[i]["sample"] and not DBG.get("nofront"):
                        ga = s5_front(blocks[i], i % 2)
                    na, nb_ = est[f"a{stage}"], est[f"b{stage}"]
                    ca = cb = 0
                    da = db = False
                    while not (da and db):
                        if db or (not da and ca / na <= cb / nb_):
                            try:
                                r = next(ga)
                                if r == "SPLIT" and stage == 1:
                                    da = True
                                else:
                                    ca += 1
                            except StopIteration:
                                da = True
                        else:
                            try:
                                r = next(gb)
                                if r == "SPLIT" and stage == 1:
                                    db = True
                                else:
                                    cb += 1
                            except StopIteration:
                                db = True
                    if ca > 0:
                        est[f"a{stage}"] = float(ca)
                    if cb > 0:
                        est[f"b{stage}"] = float(cb)
                if i + 1 < len(blocks) and not DBG.get("nopre"):
                    gn = phaseA(blocks[i + 1], (i + 1) % 2)
                    for _ in range(NT):
                        next(gn)
                    pre_a[i + 1] = gn
        T.finish("sp")
        build_nc.stats = (T.n_ops, T.n_waits, len(T.sems))
    return nc


def _constants():
    c = {}
    c["cident"] = np.eye(128, dtype=np.float32)
    half = 64
    inv = 10000.0 ** (-np.arange(half, dtype=np.float64) / half)

    def rot_tab(pos):
        ang = pos[:, None].astype(np.float64) * inv[None, :]
        cos, sin = np.cos(ang), np.sin(ang)
        return np.concatenate([cos, -sin, sin], axis=1).astype(np.float32)

    c["crot_p"] = rot_tab(np.arange(SEQ))
    rs_ = np.zeros((128, 192), np.float32)
    tab = rot_tab(PAST + np.arange(TS))
    rs_[0:TS] = tab
    rs_[64:64 + TS] = tab
    c["crot_s"] = rs_
    gam = np.array([1.0 - 2.0 ** (-5.0 - h) for h in range(4)], dtype=np.float64)
    scale = HD ** -0.5
    cmask = np.zeros((2, 128, 4, 64), np.float64)
    cq = np.zeros((2, 4, 64), np.float64)
    ck = np.zeros((2, 128, 4), np.float64)
    for cfg, blk in ((0, 64), (1, TS)):
        for h in range(4):
            for m in range(blk):
                for l in range(m, blk):
                    cmask[cfg, m, h, l] = gam[h] ** (l - m) * scale
                    cmask[cfg, 64 + m, h, l] = gam[h] ** (l - m) * scale
            for l in range(blk):
                cq[cfg, h, l] = gam[h] ** (l + 1)
                ck[cfg, l, h] = gam[h] ** (blk - 1 - l) * scale
                ck[cfg, 64 + l, h] = gam[h] ** (blk - 1 - l) * scale
    c["cmask"] = cmask.reshape(256, 256).astype(np.float32)
    c["cqdec"] = cq.reshape(2, 256).astype(np.float32)
    c["ckdec"] = ck.reshape(256, 4).astype(np.float32)
    dup = np.zeros((64, 2, 2, 64), np.float32)
    for m in range(2):
        dup[np.arange(64), m, m, np.arange(64)] = 1.0
    c["cdup"] = dup.reshape(64, 256)
    par = np.zeros((2, 32, 16), np.float32)
    for m in range(2):
        par[m, m::2, :] = 1.0
    c["cpar"] = par.reshape(2, 512)
    wm = np.zeros((8, 16, 2), np.float32)
    for gl in range(8):
        wm[gl, :, gl % 2] = 1.0
    c["cwmask"] = wm.reshape(128, 2)
    c["csdec"] = np.array([[g ** 64 for g in gam] + [g ** TS for g in gam]], dtype=np.float32)
    return c


_NC_CACHE = {}


def kernel(x_prompt, x_sample, state_ret, state_ssm_re, state_ssm_im, norm_g, w_in, ret_norm_g,
           ssm_lambda_re, ssm_lambda_im, ssm_log_step, ssm_b_re, ssm_b_im, ssm_c_re, ssm_c_im,
           ssm_d, w_glu, b_glu, w_out, final_norm_g):
    f = lambda a: np.ascontiguousarray(np.asarray(a, dtype=np.float32))
    x_prompt = f(x_prompt); x_sample = f(x_sample)
    consts = _constants()
    shared = {
        "norm_g": f(norm_g).reshape(1, D), "w_in": f(w_in).reshape(D, 3072),
        "ret_norm_g": f(ret_norm_g).reshape(4, HD),
        "lre": f(ssm_lambda_re).reshape(32, 64), "lim": f(ssm_lambda_im).reshape(32, 64),
        "lstep": f(ssm_log_step).reshape(1, 32),
        "bre": f(ssm_b_re).reshape(2048, 16), "bim": f(ssm_b_im).reshape(2048, 16),
        "cre": f(ssm_c_re).reshape(512, 64), "cim": f(ssm_c_im).reshape(512, 64),
        "ssm_d": f(ssm_d).reshape(1, 512), "w_glu": f(w_glu).reshape(512, 512),
        "b_glu": f(b_glu).reshape(1, 512), "w_out": f(w_out).reshape(D, D),
        "fng": f(final_norm_g).reshape(1, D),
    }
    shared.update(consts)
    sr = f(state_ret)[0]; s_re = f(state_ssm_re)[0]; s_im = f(state_ssm_im)[0]
    in_maps = []
    for c in range(8):
        m = dict(shared)
        m["xp"] = x_prompt[c * NSEQ_P:(c + 1) * NSEQ_P].reshape(NSEQ_P * SEQ, D)
        m["xs"] = x_sample[c * NSEQ_S:(c + 1) * NSEQ_S].reshape(NSEQ_S * TS, D)
        m["sret"] = sr[c * NSEQ_S:(c + 1) * NSEQ_S].reshape(NSEQ_S * 4 * HD, HD)
        m["sre"] = s_re[c * NSEQ_S:(c + 1) * NSEQ_S].reshape(NSEQ_S, 2048)
        m["sim"] = s_im[c * NSEQ_S:(c + 1) * NSEQ_S].reshape(NSEQ_S, 2048)
        in_maps.append(m)
    if "nc" not in _NC_CACHE:
        _NC_CACHE["nc"] = build_nc()
    nc = _NC_CACHE["nc"]
    res = run_bass_kernel_spmd(nc, in_maps, core_ids=list(range(8)))
    R = res.results
    cat = lambda k: np.concatenate([np.asarray(r[k], dtype=np.float32) for r in R], axis=0)
    y_prompt = cat("yp").reshape(16, SEQ, D)
    y_sample = cat("ys").reshape(32, TS, D)
    ret_p = cat("rp").reshape(1, 16, 4, HD, HD)
    hre_p = cat("hre_p").reshape(1, 16, 32, 64)
    him_p = cat("him_p").reshape(1, 16, 32, 64)
    ret_s = cat("rs").reshape(1, 32, 4, HD, HD)
    hre_s = cat("hre_s").reshape(1, 32, 32, 64)
    him_s = cat("him_s").reshape(1, 32, 32, 64)
    return (y_prompt, y_sample, ret_p, hre_p, him_p, ret_s, hre_s, him_s)
```

```python
import math
from contextlib import ExitStack

import numpy as np
import concourse.bass as bass
import concourse.mybir as mybir
from concourse.bass_utils import run_bass_kernel_spmd

F32 = mybir.dt.float32
BF16 = mybir.dt.bfloat16
ALU = mybir.AluOpType
AF = mybir.ActivationFunctionType

D = 1024
SEQ = 4096
NSEQ_P = 2
NSEQ_S = 4
TS = 16
PAST = 2048
NB = 256
NT = 2
NBLK_SEQ = SEQ // NB
L = 4
NJ = NB // L
NC = NB // 64
EPS = 1e-6
HD = 128
STRICT = True
DBG = {}


class Trk:
    def __init__(self, nc, es):
        self.nc = nc
        self.es = es
        self.eng = {"pe": nc.tensor, "act": nc.scalar, "dve": nc.vector, "pool": nc.gpsimd, "sp": nc.sync}
        self.sems = {}
        self.cnt = {}
        for e in ("pe", "act", "dve", "pool"):
            self.sems[e] = es.enter_context(nc.semaphore("sem_" + e))
            self.cnt[e] = 0
        self.known = {e: {} for e in self.eng}
        self.last_w = {}
        self.readers = {}
        self.n_ops = 0
        self.n_waits = 0

    def _deps(self, reads, writes):
        deps = {}

        def add(s, v):
            if v > deps.get(s, 0):
                deps[s] = v

        for k in reads:
            lw = self.last_w.get(k)
            if lw:
                add(*lw)
        for k in writes:
            lw = self.last_w.get(k)
            if lw:
                add(*lw)
            for s, v in self.readers.get(k, {}).items():
                add(s, v)
        return deps

    def _wait(self, e, deps, own=None):
        for s, v in deps.items():
            if s == own and (own == "pe" or not STRICT):
                continue
            if self.known[e].get(s, 0) >= v:
                continue
            self.eng[e].wait_ge(self.sems[s], v)
            self.known[e][s] = v
            self.n_waits += 1

    def op(self, e, fn, reads=(), writes=()):
        deps = self._deps(reads, writes)
        self._wait(e, deps, own=e)
        ins = fn(self.eng[e])
        self.cnt[e] += 1
        n = self.cnt[e]
        ins.then_inc(self.sems[e], 1)
        for k in reads:
            self.readers.setdefault(k, {})[e] = n
        for k in writes:
            self.last_w[k] = (e, n)
            self.readers[k] = {}
        self.n_ops += 1
        return ins

    def dma(self, out, in_, sem, reads=(), writes=(), q="sp", **kw):
        if sem not in self.sems:
            self.sems[sem] = self.es.enter_context(self.nc.semaphore("d_" + sem))
            self.cnt[sem] = 0
        deps = self._deps(reads, writes)
        if sem == "ld_setup":
            deps.pop(sem, None)
        self._wait(q, deps, own=None)
        ins = self.eng[q].dma_start(out=out, in_=in_, **kw)
        self.cnt[sem] += 16
        n = self.cnt[sem]
        ins.then_inc(self.sems[sem], 16)
        for k in reads:
            self.readers.setdefault(k, {})[sem] = n
        for k in writes:
            self.last_w[k] = (sem, n)
            self.readers[k] = {}
        self.n_ops += 1
        return ins

    def fence_group(self, sem):
        tot = self.cnt[sem]
        for k, lw in list(self.last_w.items()):
            if lw[0] == sem:
                self.last_w[k] = (sem, tot)

    def barrier(self):
        for e in self.eng:
            for s_, c in self.cnt.items():
                if s_ == e or c == 0:
                    continue
                if self.known[e].get(s_, 0) < c:
                    self.eng[e].wait_ge(self.sems[s_], c)
                    self.known[e][s_] = c

    def finish(self, e="sp"):
        for s, c in self.cnt.items():
            if s in ("pe", "act", "dve", "pool"):
                continue
            if c > 0 and self.known[e].get(s, 0) < c:
                self.eng[e].wait_ge(self.sems[s], c)
                self.known[e][s] = c


def AP(t, off, dims):
    return bass.AP(t, off, [list(d) for d in dims])


def build_nc():
    nc = bass.Bass("TRN2", target_bir_lowering=False)

    def din(name, shape):
        return nc.dram_tensor(name, list(shape), F32, kind="ExternalInput").ap()

    def dout(name, shape):
        return nc.dram_tensor(name, list(shape), F32, kind="ExternalOutput").ap()

    xp = din("xp", [NSEQ_P * SEQ, D])
    xs = din("xs", [NSEQ_S * TS, D])
    sret = din("sret", [NSEQ_S * 4 * HD, HD])
    sre = din("sre", [NSEQ_S, 2048])
    sim = din("sim", [NSEQ_S, 2048])
    norm_g = din("norm_g", [1, D])
    w_in = din("w_in", [D, 3072])
    ret_norm_g = din("ret_norm_g", [4, HD])
    lre = din("lre", [32, 64])
    lim = din("lim", [32, 64])
    lstep = din("lstep", [1, 32])
    bre = din("bre", [2048, 16])
    bim = din("bim", [2048, 16])
    cre = din("cre", [512, 64])
    cim = din("cim", [512, 64])
    ssm_d = din("ssm_d", [1, 512])
    w_glu = din("w_glu", [512, 512])
    b_glu = din("b_glu", [1, 512])
    w_out = din("w_out", [D, D])
    fng = din("fng", [1, D])
    cident = din("cident", [128, 128])
    crot_p = din("crot_p", [SEQ, 192])
    crot_s = din("crot_s", [128, 192])
    cmask = din("cmask", [2 * 128, 256])
    cqdec = din("cqdec", [2, 256])
    ckdec = din("ckdec", [2 * 128, 4])
    cdup = din("cdup", [64, 256])
    cpar = din("cpar", [2, 512])
    cwmask = din("cwmask", [128, 2])
    csdec = din("csdec", [1, 8])

    yp = dout("yp", [NSEQ_P * SEQ, D])
    ys = dout("ys", [NSEQ_S * TS, D])
    rp = dout("rp", [NSEQ_P * 4 * HD, HD])
    hre_p = dout("hre_p", [NSEQ_P, 2048])
    him_p = dout("him_p", [NSEQ_P, 2048])
    rs = dout("rs", [NSEQ_S * 4 * HD, HD])
    hre_s = dout("hre_s", [NSEQ_S, 2048])
    him_s = dout("him_s", [NSEQ_S, 2048])

    with ExitStack() as es:
        es.enter_context(nc.allow_non_contiguous_dma(reason="small parameter layouts"))
        T = Trk(nc, es)

        def sb(name, shape, dt=F32):
            return es.enter_context(nc.sbuf_tensor(name, list(shape), dt))

        def tap(name, ap_, shape, keys, dt=F32):
            if not DBG.get("taps"):
                return
            d = nc.dram_tensor("tap_" + name, list(shape), dt, kind="ExternalOutput").ap()
            T.dma(d, ap_, "dbg", reads=keys)

        win_b = sb("win_b", [128, 8, 3072], BF16)
        wout_b = sb("wout_b", [128, 8, 1024], BF16)
        wglu_b = sb("wglu_b", [128, 4, 512], BF16)
        ident_f = sb("ident_f", [128, 128], F32)
        ident_b = sb("ident_b", [128, 128], BF16)
        ones_b = sb("ones_b", [128, 128], BF16)
        ng = sb("ng", [128, 8], F32)
        rng_ = sb("rng", [128, 4], F32)
        bglu = sb("bglu", [128, 4], F32)
        dvec = sb("dvec", [128, 4], F32)
        gfin = sb("gfin", [128, D], F32)
        maskT = sb("maskT", [128, 2, 256], F32)
        qdec = sb("qdec", [128, 2, 256], F32)
        kdec = sb("kdec", [128, 2, 4], F32)
        sdec_t = sb("sdec_t", [128, 2, 4], F32)
        Kbd = sb("Kbd", [128, 4, L, 128], BF16)
        Wt = sb("Wt", [128, 4, L, 2, 128], BF16)
        Et = sb("Et", [128, 16, L, 2, 32], BF16)
        cosT = sb("cosT", [128, 16, NJ], F32)
        sinT = sb("sinT", [128, 16, NJ], F32)
        rtab = sb("rtab", [128, 16], F32)
        Rt = sb("Rt", [128, 16, NJ], F32)
        h0s = sb("h0s", [128, NSEQ_S, 2, 16], F32)

        NXS = 4
        xslot = [sb(f"xslot{i}", [128, D], F32) for i in range(NXS)]

        ps = [es.enter_context(nc.psum_tensor(f"ps{i}", [128, 512], F32)) for i in range(8)]
        psb = [p.bitcast(BF16) for p in ps]
        bank_ctr = [0]

        pinned = set()

        def bank(pin=True):
            for _ in range(9):
                b = bank_ctr[0] % 8
                bank_ctr[0] += 1
                if b not in pinned:
                    break
            else:
                raise RuntimeError("out of PSUM banks")
            if pin:
                pinned.add(b)
            return b

        def rel(*bs):
            for b in bs:
                pinned.discard(b)

        def act(out, in_, func, reads, writes, **kw):
            return T.op("act", lambda e: e.activation(out=out, in_=in_, func=func, **kw), reads, writes)

        def tt(eng, out, in0, in1, op, reads, writes):
            return T.op(eng, lambda e: e.tensor_tensor(out=out, in0=in0, in1=in1, op=op), reads, writes)

        def ts(eng, out, in0, s1, s2, op0, op1, reads, writes):
            return T.op(eng, lambda e: e.tensor_scalar(out=out, in0=in0, scalar1=s1, scalar2=s2, op0=op0, op1=op1),
                        reads, writes)

        def stt(out, in0, scalar, in1, op0, op1, reads, writes):
            return T.op("dve", lambda e: e.scalar_tensor_tensor(out=out, in0=in0, scalar=scalar, in1=in1,
                                                                 op0=op0, op1=op1), reads, writes)

        def cp(eng, out, in_, reads, writes):
            if eng == "act":
                return act(out, in_, AF.Copy, reads, writes)
            return T.op(eng, lambda e: e.tensor_copy(out=out, in_=in_), reads, writes)

        def mm(out, lhsT, rhs, start, stop, reads, writes, **kw):
            return T.op("pe", lambda e: e.matmul(out, lhsT=lhsT, rhs=rhs, start=start, stop=stop, **kw), reads, writes)

        def tr(out, in_, reads, writes):
            return T.op("pe", lambda e: e.transpose(out=out, in_=in_, identity=ident_b[:]), reads, writes)

        def memset(eng, ap_, val, writes):
            return T.op(eng, lambda e: e.memset(ap_, val), (), writes)

        T.dma(ident_f[:], cident, "ld_setup", writes=["ident_f"])
        T.dma(ng[:], AP(norm_g.tensor, 0, [[1, 128], [128, 8]]), "ld_setup", writes=["ng"])
        T.dma(rng_[:], AP(ret_norm_g.tensor, 0, [[1, 128], [128, 4]]), "ld_setup", writes=["rng"])
        T.dma(bglu[:], AP(b_glu.tensor, 0, [[1, 128], [128, 4]]), "ld_setup", writes=["bglu"])
        T.dma(dvec[:], AP(ssm_d.tensor, 0, [[1, 128], [128, 4]]), "ld_setup", writes=["dvec"])
        T.dma(gfin[:], AP(fng.tensor, 0, [[0, 128], [1, D]]), "ld_setup", writes=["gfin"])
        T.dma(sdec_t[:, :, :], AP(csdec.tensor, 0, [[0, 128], [4, 2], [1, 4]]), "ld_setup", writes=["sdec_t"])
        for c in range(2):
            T.dma(maskT[:, c, :], cmask[c * 128:(c + 1) * 128, :], "ld_setup", writes=["maskT"])
            T.dma(qdec[:, c, :], AP(cqdec.tensor, c * 256, [[0, 128], [1, 256]]), "ld_setup", writes=["qdec"])
            T.dma(kdec[:, c, :], ckdec[c * 128:(c + 1) * 128, :], "ld_setup", writes=["kdec"])

        with ExitStack() as es2:
            def sb2(name, shape, dt=F32):
                return es2.enter_context(nc.sbuf_tensor(name, list(shape), dt))

            K1 = "s5"
            lre1 = sb2("lre1", [64, 32]); lim1 = sb2("lim1", [64, 32]); dt1 = sb2("dt1", [64, 32])
            T.dma(lre1[:], AP(lre.tensor, 0, [[1, 64], [64, 32]]), "ld_setup", writes=["lre1"])
            T.dma(lim1[:], AP(lim.tensor, 0, [[1, 64], [64, 32]]), "ld_setup", writes=["lim1"])
            T.dma(dt1[:], AP(lstep.tensor, 0, [[0, 64], [1, 32]]), "ld_setup", writes=["dt1"])
            b1re = sb2("b1re", [64, 32, 16]); b1im = sb2("b1im", [64, 32, 16])
            T.dma(b1re[:], AP(bre.tensor, 0, [[16, 64], [1024, 32], [1, 16]]), "ld_setup", writes=["b1re"])
            T.dma(b1im[:], AP(bim.tensor, 0, [[16, 64], [1024, 32], [1, 16]]), "ld_setup", writes=["b1im"])
            cnre = sb2("cnre", [128, 4, 64]); cnim = sb2("cnim", [128, 4, 64])
            T.dma(cnre[:], AP(cre.tensor, 0, [[64, 128], [128 * 64, 4], [1, 64]]), "ld_setup", writes=["cnre"])
            T.dma(cnim[:], AP(cim.tensor, 0, [[64, 128], [128 * 64, 4], [1, 64]]), "ld_setup", writes=["cnim"])
            dup = sb2("dup", [64, 2, 128])
            T.dma(dup[:], cdup, "ld_setup", writes=["dup"])
            parm = sb2("parm", [64, 2, 512])
            T.dma(parm[:], AP(cpar.tensor, 0, [[0, 64], [512, 2], [1, 512]]), "ld_setup", writes=["parm"])
            wmask = sb2("wmask", [128, 2])
            T.dma(wmask[:], cwmask, "ld_setup", writes=["wmask"])
            for s in range(NSEQ_S):
                T.dma(h0s[:, s, 0, :], AP(sre.tensor, s * 2048, [[1, 128], [128, 16]]), "ld_setup", writes=["h0s"])
                T.dma(h0s[:, s, 1, :], AP(sim.tensor, s * 2048, [[1, 128], [128, 16]]), "ld_setup", writes=["h0s"])

            T.fence_group("ld_setup")
            cp("act", ident_b[:], ident_f[:], ["ident_f"], ["ident_b"])
            memset("pool", ones_b[:], 1.0, ["ones_b"])
            cast_engs = ["act", "dve", "pool"]
            ci = [0]

            def load_cast(dst_ap, src_ap, ncols, wkey, scale_ap=None):
                i = ci[0] % NXS
                e = cast_engs[ci[0] % 3]
                ci[0] += 1
                T.dma(xslot[i][:, 0:ncols], src_ap, f"ld_xs{i}", writes=[f"xslot{i}"])
                if scale_ap is None:
                    cp(e, dst_ap, xslot[i][:, 0:ncols], [f"xslot{i}"], [wkey])
                elif e == "act":
                    act(dst_ap, xslot[i][:, 0:ncols], AF.Copy, [f"xslot{i}", "ng"], [wkey], scale=scale_ap)
                else:
                    ts(e, dst_ap, xslot[i][:, 0:ncols], scale_ap, None, ALU.mult, ALU.bypass, [f"xslot{i}", "ng"], [wkey])

            for kt in range(8):
                for c in range(3):
                    load_cast(win_b[:, kt, c * 1024:(c + 1) * 1024], w_in[kt * 128:(kt + 1) * 128, c * 1024:(c + 1) * 1024],
                              1024, "win_b", scale_ap=ng[:, kt:kt + 1])
            for kt in range(8):
                load_cast(wout_b[:, kt, :], w_out[kt * 128:(kt + 1) * 128, :], 1024, "wout_b")
            for kt in range(4):
                load_cast(wglu_b[:, kt, :], w_glu[kt * 128:(kt + 1) * 128, :], 512, "wglu_b")


            def t64(name):
                return sb2(name, [64, 32])

            lrc = t64("lrc"); dtv = t64("dtv"); are = t64("are"); aim = t64("aim"); mag = t64("mag")
            th = t64("th"); th2 = t64("th2"); wS = t64("wS"); wC = t64("wC")
            cc = t64("cc"); ss_ = t64("ss_"); cs = t64("cs")
            RW = [K1]
            ts("dve", lrc[:], lre1[:], -1e-4, None, ALU.min, ALU.bypass, ["lre1"] + RW, RW)
            act(dtv[:], dt1[:], AF.Exp, ["dt1"] + RW, RW)
            tt("dve", are[:], lrc[:], dtv[:], ALU.mult, RW, RW)
            tt("dve", aim[:], lim1[:], dtv[:], ALU.mult, ["lim1"] + RW, RW)
            act(mag[:], are[:], AF.Exp, RW, RW)
            ts("dve", th[:], aim[:], 1.0 / 32.0, None, ALU.mult, ALU.bypass, RW, RW)
            tt("dve", th2[:], th[:], th[:], ALU.mult, RW, RW)
            a = [-1.0 / 6, 1.0 / 120, -1.0 / 5040, 1.0 / 362880]
            ts("dve", wS[:], th2[:], a[3], None, ALU.mult, ALU.bypass, RW, RW)
            for k in (2, 1, 0):
                ts("dve", wS[:], wS[:], a[k], None, ALU.add, ALU.bypass, RW, RW)
                tt("dve", wS[:], wS[:], th2[:], ALU.mult, RW, RW)
            ts("dve", wS[:], wS[:], 1.0, None, ALU.add, ALU.bypass, RW, RW)
            tt("dve", wS[:], wS[:], th[:], ALU.mult, RW, RW)
            b = [-0.5, 1.0 / 24, -1.0 / 720, 1.0 / 40320, -1.0 / 3628800]
            ts("dve", wC[:], th2[:], b[4], None, ALU.mult, ALU.bypass, RW, RW)
            for k in (3, 2, 1, 0):
                ts("dve", wC[:], wC[:], b[k], None, ALU.add, ALU.bypass, RW, RW)
                tt("dve", wC[:], wC[:], th2[:], ALU.mult, RW, RW)
            ts("dve", wC[:], wC[:], 1.0, None, ALU.add, ALU.bypass, RW, RW)
            for _ in range(5):
                tt("dve", cc[:], wC[:], wC[:], ALU.mult, RW, RW)
                tt("dve", ss_[:], wS[:], wS[:], ALU.mult, RW, RW)
                tt("dve", cs[:], wC[:], wS[:], ALU.mult, RW, RW)
                tt("dve", wC[:], cc[:], ss_[:], ALU.subtract, RW, RW)
                ts("dve", wS[:], cs[:], 2.0, None, ALU.mult, ALU.bypass, RW, RW)
            pwr = sb2("pwr", [64, L + 1, 32]); pwi = sb2("pwi", [64, L + 1, 32])
            memset("dve", pwr[:, 0, :], 1.0, RW)
            memset("dve", pwi[:, 0, :], 0.0, RW)
            tt("dve", pwr[:, 1, :], mag[:], wC[:], ALU.mult, RW, RW)
            tt("dve", pwi[:, 1, :], mag[:], wS[:], ALU.mult, RW, RW)
            tA = t64("tA"); tB = t64("tB")

            def cmul(o_re, o_im, a_re, a_im, b_re, b_im, t1_, t2_, xr=()):
                R_ = RW + list(xr)
                tt("dve", t1_, a_re, b_re, ALU.mult, R_, RW)
                tt("dve", t2_, a_im, b_im, ALU.mult, R_, RW)
                tt("dve", o_re, t1_, t2_, ALU.subtract, R_, RW)
                tt("dve", t1_, a_re, b_im, ALU.mult, R_, RW)
                tt("dve", t2_, a_im, b_re, ALU.mult, R_, RW)
                tt("dve", o_im, t1_, t2_, ALU.add, R_, RW)

            for k in range(1, L):
                cmul(pwr[:, k + 1, :], pwi[:, k + 1, :], pwr[:, k, :], pwi[:, k, :], pwr[:, 1, :], pwi[:, 1, :],
                     tA[:], tB[:])
            nre = t64("nre"); den = t64("den"); qre = t64("qre"); qim = t64("qim")
            ts("dve", nre[:], pwr[:, 1, :], -1.0, None, ALU.add, ALU.bypass, RW, RW)
            tt("dve", den[:], lrc[:], lrc[:], ALU.mult, RW, RW)
            tt("dve", tA[:], lim1[:], lim1[:], ALU.mult, RW, RW)
            tt("dve", den[:], den[:], tA[:], ALU.add, RW, RW)
            T.op("dve", lambda e: e.reciprocal(out=den[:], in_=den[:]), RW, RW)
            tt("dve", tA[:], nre[:], lrc[:], ALU.mult, RW, RW)
            tt("dve", tB[:], pwi[:, 1, :], lim1[:], ALU.mult, RW, RW)
            tt("dve", qre[:], tA[:], tB[:], ALU.add, RW, RW)
            tt("dve", qre[:], qre[:], den[:], ALU.mult, RW, RW)
            tt("dve", tA[:], pwi[:, 1, :], lrc[:], ALU.mult, RW, RW)
            tt("dve", tB[:], nre[:], lim1[:], ALU.mult, RW, RW)
            tt("dve", qim[:], tA[:], tB[:], ALU.subtract, RW, RW)
            tt("dve", qim[:], qim[:], den[:], ALU.mult, RW, RW)

            tap("lre1", lre1[:], [64, 32], ["lre1"]); tap("dt1", dt1[:], [64, 32], ["dt1"])
            tap("mag", mag[:], [64, 32], RW); tap("wC", wC[:], [64, 32], RW); tap("wS", wS[:], [64, 32], RW)
            tap("pwr", pwr[:, :, :].rearrange("p a b -> p (a b)"), [64, (L + 1) * 32], RW)
            tap("pwi", pwi[:, :, :].rearrange("p a b -> p (a b)"), [64, (L + 1) * 32], RW)
            tap("qre", qre[:], [64, 32], RW); tap("qim", qim[:], [64, 32], RW)

            def bc16(t, k=None):
                if k is None:
                    return AP(t, 0, [[32, 64], [1, 32], [0, 16]])
                return AP(t, k * 32, [[(L + 1) * 32, 64], [1, 32], [0, 16]])

            bbr = sb2("bbr", [64, 32, 16]); bbi = sb2("bbi", [64, 32, 16])
            u1 = sb2("u1", [64, 32, 16]); u2 = sb2("u2", [64, 32, 16])
            cmul(bbr[:], bbi[:], bc16(qre), bc16(qim), b1re[:], b1im[:], u1[:], u2[:], xr=["b1re", "b1im"])

            CTr = sb2("CTr", [64, 512]); CTi = sb2("CTi", [64, 512]); nCTi = sb2("nCTi", [64, 512])
            for (src, dst, key) in ((cnre, CTr, "cnre"), (cnim, CTi, "cnim")):
                bk = bank(False)
                for t in range(4):
                    mm(ps[bk][0:64, t * 128:(t + 1) * 128], src[:, t, :], ident_f[:, :], True, True,
                       [key, "ident_f"], [("ps", bk)])
                cp("dve", dst[:], ps[bk][0:64, :], [("ps", bk)], RW)
            ts("dve", nCTi[:], CTi[:], -1.0, None, ALU.mult, ALU.bypass, RW, RW)

            tap("bbr", bbr[:, :, :].rearrange("p a b -> p (a b)"), [64, 512], RW)
            tap("CTr", CTr[:], [64, 512], RW); tap("nCTi", nCTi[:], [64, 512], RW)
            Vre = sb2("Vre", [64, 512]); Vim = sb2("Vim", [64, 512])
            Vpr = sb2("Vpr", [64, 8, 128]); Vpi = sb2("Vpi", [64, 8, 128])
            memset("pool", Vpr[:], 0.0, ["Vp"])
            memset("pool", Vpi[:], 0.0, ["Vp"])
            V3r = Vre[:, :].rearrange("p (g c) -> p g c", c=16)
            V3i = Vim[:, :].rearrange("p (g c) -> p g c", c=16)
            dgr = AP(Vpr, 0, [[8 * 128, 64], [144, 8], [1, 16]])
            dgi = AP(Vpi, 0, [[8 * 128, 64], [144, 8], [1, 16]])
            for k in range(L):
                cmul(V3r, V3i, bc16(pwr, k), bc16(pwi, k), bbr[:], bbi[:], u1[:], u2[:])
                for t in range(4):
                    T.op("dve", lambda e: e.tensor_copy(
                        out=dgr, in_=Vre[:, t * 128:(t + 1) * 128].rearrange("p (g c) -> p g c", c=16)), RW, ["Vp"])
                    T.op("dve", lambda e: e.tensor_copy(
                        out=dgi, in_=Vim[:, t * 128:(t + 1) * 128].rearrange("p (g c) -> p g c", c=16)), RW, ["Vp"])
                    bk = bank(False)
                    for gl in range(8):
                        g = 8 * t + gl
                        mm(ps[bk][:, 16 * gl:16 * gl + 16], Vpr[:, gl, :], CTr[:, g * 16:(g + 1) * 16], True, False,
                           ["Vp"] + RW, [("ps", bk)])
                        mm(ps[bk][:, 16 * gl:16 * gl + 16], Vpi[:, gl, :], nCTi[:, g * 16:(g + 1) * 16], False, True,
                           ["Vp"] + RW, [("ps", bk)])
                    if k == 0:
                        stt(Kbd[:, t, k, :], ident_f[:, :], dvec[:, t:t + 1], ps[bk][:, 0:128], ALU.mult, ALU.add,
                            [("ps", bk), "ident_f", "dvec"], ["Kbd"])
                    else:
                        cp("dve", Kbd[:, t, k, :], ps[bk][:, 0:128], [("ps", bk)], ["Kbd"])
                s = L - 1 - k
                for ri, Vx in ((0, Vre), (1, Vim)):
                    bk = bank(False)
                    for t in range(4):
                        mm(ps[bk][:, t * 64:(t + 1) * 64], Vx[:, t * 128:(t + 1) * 128], ident_f[0:64, 0:64], True, True,
                           RW + ["ident_f"], [("ps", bk)])
                    for t in range(4):
                        tt("dve", Wt[:, t, s, ri, :].rearrange("p (m q) -> p m q", m=2),
                           AP(ps[bk], t * 64, [[512, 128], [0, 2], [1, 64]]),
                           AP(wmask, 0, [[2, 128], [1, 2], [0, 64]]), ALU.mult,
                           [("ps", bk), "wmask"], ["Wt"])
            EVr = sb2("EVr", [64, 512]); EVi = sb2("EVi", [64, 512])
            EVm = [sb2(f"EVm{m}", [64, 512]) for m in range(2)]
            E3r = EVr[:, :].rearrange("p (g c) -> p g c", c=16)
            E3i = EVi[:, :].rearrange("p (g c) -> p g c", c=16)
            CT3r = CTr[:, :].rearrange("p (g c) -> p g c", c=16)
            CT3i = CTi[:, :].rearrange("p (g c) -> p g c", c=16)
            for i in range(L):
                cmul(E3r, E3i, bc16(pwr, i + 1), bc16(pwi, i + 1), CT3r, CT3i, u1[:], u2[:])
                for ri, EV in ((0, EVr), (1, EVi)):
                    for m in range(2):
                        tt("dve", EVm[m][:], EV[:], parm[:, m, :], ALU.mult, RW + ["parm"], ["EVm"])
                    bk = bank(False)
                    mm(ps[bk][:, :], dup[:, 0, :], EVm[0][:], True, False, ["dup", "EVm"], [("ps", bk)])
                    mm(ps[bk][:, :], dup[:, 1, :], EVm[1][:], False, True, ["dup", "EVm"], [("ps", bk)])
                    T.op("act", lambda e, bk=bk, ri=ri, i=i: e.activation(
                        out=Et[:, :, i, ri, :], in_=ps[bk][:, :].rearrange("p (a b) -> p a b", b=32),
                        func=AF.Copy, scale=(1.0 if ri == 0 else -1.0)), [("ps", bk)], ["Et"])
            r1 = t64("r1"); uc = t64("uc"); us = t64("us")
            tt("dve", r1[:], mag[:], mag[:], ALU.mult, RW, RW)
            tt("dve", r1[:], r1[:], r1[:], ALU.mult, RW, RW)
            cp("dve", uc[:], wC[:], RW, RW)
            cp("dve", us[:], wS[:], RW, RW)
            for _ in range(2):
                tt("dve", cc[:], uc[:], uc[:], ALU.mult, RW, RW)
                tt("dve", ss_[:], us[:], us[:], ALU.mult, RW, RW)
                tt("dve", cs[:], uc[:], us[:], ALU.mult, RW, RW)
                tt("dve", uc[:], cc[:], ss_[:], ALU.subtract, RW, RW)
                ts("dve", us[:], cs[:], 2.0, None, ALU.mult, ALU.bypass, RW, RW)
            bk = bank(False)
            for i3, src in enumerate((r1, uc, us)):
                for m in range(2):
                    mm(ps[bk][:, i3 * 16:(i3 + 1) * 16], dup[:, m, :], AP(src, m, [[32, 64], [2, 16]]), m == 0, m == 1,
                       RW + ["dup"], [("ps", bk)])
            pwa = sb2("pwa", [128, 16]); pwb = sb2("pwb", [128, 16])
            x1 = sb2("x1", [128, 16, 32]); x2 = sb2("x2", [128, 16, 32])
            y1 = sb2("y1", [128, 16]); y2 = sb2("y2", [128, 16]); y3 = sb2("y3", [128, 16])
            AK = ["Atab"]
            cp("dve", rtab[:, :], ps[bk][:, 0:16], [("ps", bk)], AK)
            cp("dve", pwa[:, :], ps[bk][:, 16:32], [("ps", bk)], AK)
            cp("dve", pwb[:, :], ps[bk][:, 32:48], [("ps", bk)], AK)
            cp("dve", cosT[:, :, 0], pwa[:, :], AK, AK)
            cp("dve", sinT[:, :, 0], pwb[:, :], AK, AK)
            k = 1
            while k < NJ:
                pa = AP(pwa, 0, [[16, 128], [1, 16], [0, k]])
                pb = AP(pwb, 0, [[16, 128], [1, 16], [0, k]])
                tt("dve", x1[:, :, 0:k], cosT[:, :, 0:k], pa, ALU.mult, AK, AK)
                tt("dve", x2[:, :, 0:k], sinT[:, :, 0:k], pb, ALU.mult, AK, AK)
                tt("dve", cosT[:, :, k:2 * k], x1[:, :, 0:k], x2[:, :, 0:k], ALU.subtract, AK, AK)
                tt("dve", x1[:, :, 0:k], cosT[:, :, 0:k], pb, ALU.mult, AK, AK)
                tt("dve", x2[:, :, 0:k], sinT[:, :, 0:k], pa, ALU.mult, AK, AK)
                tt("dve", sinT[:, :, k:2 * k], x1[:, :, 0:k], x2[:, :, 0:k], ALU.add, AK, AK)
                tt("dve", y1[:], pwa[:], pwa[:], ALU.mult, AK, AK)
                tt("dve", y2[:], pwb[:], pwb[:], ALU.mult, AK, AK)
                tt("dve", y3[:], pwa[:], pwb[:], ALU.mult, AK, AK)
                tt("dve", pwa[:], y1[:], y2[:], ALU.subtract, AK, AK)
                ts("dve", pwb[:], y3[:], 2.0, None, ALU.mult, ALU.bypass, AK, AK)
                k *= 2
            memset("dve", Rt[:, :, :], 0.0, AK)
            cp("dve", Rt[:, :, 1:NJ], AP(rtab, 0, [[16, 128], [1, 16], [0, NJ - 1]]), AK, AK)
            for hh in range(2):
                sl = slice(hh * 32, (hh + 1) * 32)
                tt("dve", x1[:, :, :], cosT[:, :, sl], cosT[:, :, sl], ALU.mult, AK, AK)
                tt("dve", x2[:, :, :], sinT[:, :, sl], sinT[:, :, sl], ALU.mult, AK, AK)
                tt("dve", x1[:, :, :], x1[:, :, :], x2[:, :, :], ALU.add, AK, AK)
                ts("dve", x1[:, :, :], x1[:, :, :], -0.5, 1.5, ALU.mult, ALU.add, AK, AK)
                tt("dve", cosT[:, :, sl], cosT[:, :, sl], x1[:, :, :], ALU.mult, AK, AK)
                tt("dve", sinT[:, :, sl], sinT[:, :, sl], x1[:, :, :], ALU.mult, AK, AK)

        T.barrier()
        pinned.clear()
        if DBG.get("dump"):
            d_kbd = nc.dram_tensor("d_kbd", [128, 4 * L * 128], BF16, kind="ExternalOutput").ap()
            d_wt = nc.dram_tensor("d_wt", [128, 4 * L * 2 * 128], BF16, kind="ExternalOutput").ap()
            d_et = nc.dram_tensor("d_et", [128, 16 * L * 2 * 32], BF16, kind="ExternalOutput").ap()
            d_a = nc.dram_tensor("d_a", [128, 16 + 2 * 16 * NJ], F32, kind="ExternalOutput").ap()
            T.dma(d_kbd, Kbd[:, :, :, :].rearrange("p a b c -> p (a b c)"), "dbg", reads=["Kbd"])
            T.dma(d_wt, Wt[:, :, :, :, :].rearrange("p a b c d -> p (a b c d)"), "dbg", reads=["Wt"])
            T.dma(d_et, Et[:, :, :, :, :].rearrange("p a b c d -> p (a b c d)"), "dbg", reads=["Et"])
            T.dma(d_a[:, 0:16], rtab[:, :], "dbg", reads=["Atab"])
            T.dma(d_a[:, 16:16 + 16 * NJ], cosT[:, :, :].rearrange("p a b -> p (a b)"), "dbg", reads=["Atab"])
            T.dma(d_a[:, 16 + 16 * NJ:], sinT[:, :, :].rearrange("p a b -> p (a b)"), "dbg", reads=["Atab"])
        if DBG.get("setup_only"):
            T.finish("sp")
            build_nc.stats = (T.n_ops, T.n_waits, len(T.sems))
            return nc
        hnb = [sb(f"hnb{i}", [128, D], BF16) for i in range(1)]
        hnT = sb("hnT", [128, 8, NB], BF16)
        mixT = sb("mixT", [128, 8, NB], BF16)
        rot = [sb(f"rot{i}", [128, NT, 192], F32) for i in range(2)]
        rt1 = sb("rt1", [128, 512], F32)
        rt2 = sb("rt2", [128, 512], F32)
        qrot2 = [sb(f"qrot{i}", [128, NT, 512], BF16) for i in range(2)]
        krot2 = [sb(f"krot{i}", [128, NT, 512], BF16) for i in range(2)]
        ktd2 = [sb(f"ktd{i}", [128, NT, 512], BF16) for i in range(2)]
        vtok2 = [sb(f"vtok{i}", [128, NT, 512], BF16) for i in range(2)]
        qT = sb("qT", [128, 4, NB], BF16)
        kT = sb("kT", [128, 4, NB], BF16)
        qdT = sb("qdT", [128, 4, NB], BF16)
        sgret2 = [sb(f"sgret{i}", [128, 4, NB], BF16) for i in range(2)]
        uT2 = [sb(f"uT{i}", [128, 4, NB], BF16) for i in range(2)]
        sgssm2 = [sb(f"sgssm{i}", [128, 4, NB], BF16) for i in range(2)]
        c_sb = sb("c_sb", [128, 2, 16, NJ], F32)
        Hprev2 = [sb("Hprev0", [128, 2, 16, NJ], BF16)] * 2
        carry = sb("carry", [128, 2, 16], F32)
        ta = sb("ta", [128, 2, 4, NJ], F32)
        tb = sb("tb", [128, 2, 4, NJ], F32)
        hfin = sb("hfin", [128, NSEQ_S, 2, 16], F32)
        st3 = sb("st3", [128, 2, 16], F32)
        sTb = [sb(f"sTb{i}", [128, 256], BF16) for i in range(2)]
        Smaster = sb("Smaster", [128, 4, HD], F32)
        Sprev = [sb(f"Sprev{i}", [128, 4, HD], BF16) for i in range(2)]
        osq = sb("osq", [128, 4 * NB], BF16)
        rsd = sb("rsd", [128, 512], F32)
        ot = sb("ot", [128, 512], F32)
        zT = sb("zT", [128, 4, NB], BF16)
        g1 = [rsd[:, 0:NB]] * 2
        g2 = [ot[:, 0:NB]] * 2
        ssq = sb("ssq", [128, 8], F32)
        rstd = sb("rstd", [128, 8], F32)


        memset("pool", sTb[0][:], 0.0, ["sT0"])
        memset("pool", sTb[1][:], 0.0, ["sT1"])
        xs_ctr = [0]
        LQ = DBG.get("lq", "act")

        def next_xslot():
            i = xs_ctr[0] % NXS
            xs_ctr[0] += 1
            return i

        chunk_ctr = [0]
        blk_ctr = [0]

        def interleave2(g1, g2, n1, n2):
            c1 = c2 = 0
            d1 = d2 = False
            while not (d1 and d2):
                if d2 or (not d1 and c1 / n1 <= c2 / n2):
                    try:
                        next(g1); c1 += 1
                    except StopIteration:
                        d1 = True
                else:
                    try:
                        next(g2); c2 += 1
                    except StopIteration:
                        d2 = True
                yield

        def issue_x_load(cfg, t_, xi):
            XK = f"xslot{xi}"
            if cfg["sample"]:
                memset("pool", xslot[xi][:], 0.0, [XK])
                for half in range(2):
                    s_ = 2 * t_ + half
                    T.dma(xslot[xi][64 * half:64 * half + TS, :], xs[s_ * TS:(s_ + 1) * TS, :], f"ld_xs{xi}", writes=[XK])
            else:
                r0 = cfg["seq"] * SEQ + cfg["blk"] * NB
                T.dma(xslot[xi][:], xp[r0 + t_ * 128: r0 + (t_ + 1) * 128, :], f"ld_xs{xi}", writes=[XK])

        def issue_rot_load(cfg, rslot):
            RK = f"rot{rslot}"
            cfg["rslot"] = rslot
            if cfg["sample"]:
                for t_ in range(NT):
                    T.dma(rot[rslot][:, t_, :], crot_s, f"ld_rot{rslot}", writes=[RK])
            else:
                pos0 = cfg["blk"] * NB
                T.dma(rot[rslot][:, :, :], AP(crot_p.tensor, pos0 * 192, [[192, 128], [128 * 192, NT], [1, 192]]),
                      f"ld_rot{rslot}", writes=[RK])

        def _hdr(cfg, pb):
            sample = cfg["sample"]
            cf = 1 if sample else 0
            seq = cfg.get("seq", 0)
            blk = cfg.get("blk", 0)
            row0 = seq * SEQ + blk * NB
            gam = [1.0 - 2.0 ** (-5.0 - h) for h in range(4)]
            sdec = [g ** (16 if sample else 64) for g in gam]

            qrot, krot, ktd, vtok = qrot2[pb], krot2[pb], ktd2[pb], vtok2[pb]
            sgret, uT, sgssm, Hprev = sgret2[pb], uT2[pb], sgssm2[pb], Hprev2[pb]
            return locals()

        def phaseA(cfg, pb):
            L_ = _hdr(cfg, pb)
            sample, cf, seq, blk, row0 = L_['sample'], L_['cf'], L_['seq'], L_['blk'], L_['row0']
            qrot, krot, ktd, vtok = L_['qrot'], L_['krot'], L_['ktd'], L_['vtok']
            sgret, uT, sgssm, Hprev = L_['sgret'], L_['uT'], L_['sgssm'], L_['Hprev']
            rslot = cfg["rslot"]
            RK = f"rot{rslot}"

            for t_ in range(NT):
                xi = cfg["xslots"][t_]
                XK = f"xslot{xi}"
                hb = hnb[0]
                HK = "hnb0"
                act(hb[:], xslot[xi][:], AF.Square, [XK], [HK, "ssqA"], accum_out=ssq[:, t_:t_ + 1])
                act(ssq[:, t_:t_ + 1], ssq[:, t_:t_ + 1], AF.Ln, ["ssqA"], ["ssqA"], scale=1.0 / D, bias=EPS)
                act(rstd[:, t_:t_ + 1], ssq[:, t_:t_ + 1], AF.Exp, ["ssqA"], ["rstdA"], scale=-0.5)
                act(hb[:], xslot[xi][:], AF.Copy, [XK, "rstdA"], [HK], scale=rstd[:, t_:t_ + 1])
                bk = bank()
                for kt in range(8):
                    tr(psb[bk][:, kt * 128:(kt + 1) * 128], hb[:, kt * 128:(kt + 1) * 128], [HK, "ident_b"], [("ps", bk)])
                cp("act", hnT[:, :, t_ * 128:(t_ + 1) * 128], psb[bk][:, :].rearrange("p (k t) -> p k t", k=8),
                   [("ps", bk)], [f"hnT{t_}"])
                rel(bk)
                yield

            if DBG.get('stop', 99) <= 1:
                return
            for t_ in range(NT):
                cosb = AP(rot[rslot], t_ * 192, [[NT * 192, 128], [0, 4], [0, 2], [1, 64]])
                sinb = AP(rot[rslot], t_ * 192 + 64, [[NT * 192, 128], [0, 4], [64, 2], [1, 64]])
                for (c0, dst, dkey) in ((0, qrot, f"qrot{pb}"), (512, krot, f"krot{pb}"), (1024, None, None)):
                    bb = bank()
                    for kt in range(8):
                        mm(ps[bb][:, :], hnT[:, kt, t_ * 128:(t_ + 1) * 128], win_b[:, kt, c0:c0 + 512], kt == 0, kt == 7,
                           [f"hnT{t_}", "win_b"], [("ps", bb)])
                    yield
                    if dst is None:
                        cp("act", vtok[:, t_, :], ps[bb][:, :], [("ps", bb)], [f"vtok{pb}"])
                        rel(bb)
                        yield
                        continue
                    pv = AP(ps[bb], 0, [[512, 128], [128, 4], [64, 2], [1, 64]])
                    psw = AP(ps[bb], 64, [[512, 128], [128, 4], [-64, 2], [1, 64]])
                    tt("dve", rt1[:, :].rearrange("p (h a d) -> p h a d", h=4, a=2), pv, cosb, ALU.mult,
                       [("ps", bb), RK], ["rt1"])
                    tt("dve", rt2[:, :].rearrange("p (h a d) -> p h a d", h=4, a=2), psw, sinb, ALU.mult,
                       [("ps", bb), RK], ["rt2"])
                    rel(bb)
                    tt("dve", dst[:, t_, :], rt1[:, :], rt2[:, :], ALU.add, ["rt1", "rt2"], [dkey])
                    yield
                for h in range(4):
                    act(ktd[:, t_, h * 128:(h + 1) * 128], krot[:, t_, h * 128:(h + 1) * 128], AF.Copy,
                        [f"krot{pb}", "kdec"], [f"ktd{pb}"], scale=kdec[:, cf, h:h + 1])
                yield

            if DBG.get('stop', 99) <= 2:
                return
            for m2 in range(6):
                bk = bank()
                for half in range(2):
                    m = 2 * m2 + half
                    c0 = 1536 + m * 128
                    for kt in range(8):
                        mm(ps[bk][:, half * NB:(half + 1) * NB], win_b[:, kt, c0:c0 + 128], hnT[:, kt, :], kt == 0, kt == 7,
                           ["hnT0", "hnT1", "win_b"], [("ps", bk)])
                    yield
                for half in range(2):
                    m = 2 * m2 + half
                    src = ps[bk][:, half * NB:(half + 1) * NB]
                    if m < 4:
                        act(sgret[:, m, :], src, AF.Silu, [("ps", bk)], [f"sgret{pb}"])
                    elif m < 8:
                        cp("act", uT[:, m - 4, :], src, [("ps", bk)], [f"uT{pb}"])
                    else:
                        act(sgssm[:, m - 8, :], src, AF.Silu, [("ps", bk)], [f"sgssm{pb}"])
                rel(bk)
                yield

            nxt2 = cfg.get("next2")
            if nxt2 is not None:
                issue_rot_load(nxt2, rslot)
            yield "SPLIT"

        def s5_front(cfg, pb):
            L_ = _hdr(cfg, pb)
            sample, blk = L_['sample'], L_['blk']
            uT = L_['uT']
            cfg["front_done"] = True
            c5 = c_sb[:, :, :, :].rearrange("p r (t q) j -> p r t q j", q=4)
            for q in range(4):
                bk = bank()
                for ri in range(2):
                    for t in range(4):
                        gi = ri * 4 + t
                        for s in range(L):
                            mm(ps[bk][:, gi * NJ:(gi + 1) * NJ], Wt[32 * q:32 * q + 32, t, s, ri, :],
                               uT[32 * q:32 * q + 32, t, :].rearrange("p (j s) -> p j s", s=L)[:, :, s],
                               s == 0, s == L - 1, [f"uT{pb}", "Wt"], [("ps", bk)], tile_position=(32 * q, 0))
                cp("dve", c5[:, :, :, q, :], ps[bk][:, :].rearrange("p (r t j) -> p r t j", r=2, t=4),
                   [("ps", bk)], ["c_sb"])
                rel(bk)
                yield
            if sample:
                return
            CS = 2 * 16 * NJ
            n = NJ
            if blk == 0:
                memset("pool", carry[:], 0.0, ["carry"])
            for ph in range(4):
                Ps = slice(4 * ph, 4 * ph + 4)
                cv = c_sb[:, :, Ps, :]
                cvsw = AP(c_sb, 16 * NJ + 4 * ph * NJ, [[CS, 128], [-16 * NJ, 2], [NJ, 4], [1, n]])
                cosb = AP(cosT, 4 * ph * NJ, [[16 * NJ, 128], [0, 2], [NJ, 4], [1, n]])
                sinb = AP(sinT, 4 * ph * NJ, [[16 * NJ, 128], [0, 2], [NJ, 4], [1, n]])
                tt("dve", ta[:, :, :, :], cv, cosb, ALU.mult, ["c_sb", "Atab"], ["ta"])
                tt("dve", tb[:, :, :, :], cvsw, sinb, ALU.mult, ["c_sb", "Atab"], ["tb"])
                tt("dve", c_sb[:, 0, Ps, :], ta[:, 0, :, :], tb[:, 0, :, :], ALU.add, ["ta", "tb"], ["c_sb"])
                tt("dve", c_sb[:, 1, Ps, :], ta[:, 1, :, :], tb[:, 1, :, :], ALU.subtract, ["ta", "tb"], ["c_sb"])
                yield
            tt("dve", st3[:, :, :], carry[:, :, :], AP(rtab, 0, [[16, 128], [0, 2], [1, 16]]), ALU.mult,
               ["carry", "Atab"], ["st3"])
            tt("dve", c_sb[:, :, :, 0], c_sb[:, :, :, 0], st3[:, :, :], ALU.add, ["c_sb", "st3"], ["c_sb"])
            yield

        def phaseB(cfg, pb):
            L_ = _hdr(cfg, pb)
            sample, cf, seq, blk, row0 = L_['sample'], L_['cf'], L_['seq'], L_['blk'], L_['row0']
            qrot, krot, ktd, vtok = L_['qrot'], L_['krot'], L_['ktd'], L_['vtok']
            sgret, uT, sgssm, Hprev = L_['sgret'], L_['uT'], L_['sgssm'], L_['Hprev']
            def gen_scan():
                if DBG.get('stop', 99) <= 3:
                    return
                if not cfg.get("front_done"):
                    yield from s5_front(cfg, pb)
                if DBG.get('stop', 99) <= 4:
                    return
                CS = 2 * 16 * NJ

                def seg_scan(j0, n, init, ikey, fin_j, fin_out, fkey):
                    for ph in range(4):
                        Ps = slice(4 * ph, 4 * ph + 4)
                        cv = c_sb[:, :, Ps, j0:j0 + n]
                        cvsw = AP(c_sb, 16 * NJ + 4 * ph * NJ + j0, [[CS, 128], [-16 * NJ, 2], [NJ, 4], [1, n]])
                        cosb = AP(cosT, 4 * ph * NJ, [[16 * NJ, 128], [0, 2], [NJ, 4], [1, n]])
                        sinb = AP(sinT, 4 * ph * NJ, [[16 * NJ, 128], [0, 2], [NJ, 4], [1, n]])
                        tav = ta[:, :, :, 0:n]
                        tbv = tb[:, :, :, 0:n]
                        tt("dve", tav, cv, cosb, ALU.mult, ["c_sb", "Atab"], ["ta"])
                        tt("dve", tbv, cvsw, sinb, ALU.mult, ["c_sb", "Atab"], ["tb"])
                        tt("dve", c_sb[:, 0, Ps, j0:j0 + n], ta[:, 0, :, 0:n], tb[:, 0, :, 0:n], ALU.add, ["ta", "tb"], ["c_sb"])
                        tt("dve", c_sb[:, 1, Ps, j0:j0 + n], ta[:, 1, :, 0:n], tb[:, 1, :, 0:n], ALU.subtract, ["ta", "tb"], ["c_sb"])
                        yield
                        for ri in range(2):
                            for P in range(4 * ph, 4 * ph + 4):
                                row = c_sb[:, ri, P, j0:j0 + n]
                                T.op("dve", lambda e, row=row, P=P, ri=ri: e.tensor_tensor_scan(
                                    out=row, data0=AP(rtab, P, [[16, 128], [0, n]]), data1=row,
                                    initial=init[:, ri, P:P + 1], op0=ALU.mult, op1=ALU.add),
                                    ["c_sb", "Atab", ikey], [("c_row", ri, P)])
                            yield
                        rows = [("c_row", ri, P) for ri in range(2) for P in range(4 * ph, 4 * ph + 4)]
                        tt("dve", tav, cv, cosb, ALU.mult, ["c_sb", "Atab"] + rows, ["ta", "c_sb"])
                        tt("dve", tbv, cvsw, sinb, ALU.mult, ["c_sb", "Atab"] + rows, ["tb", "c_sb"])
                        if n > 1:
                            tt("dve", Hprev[:, 0, Ps, j0 + 1:j0 + n], ta[:, 0, :, 0:n - 1], tb[:, 0, :, 0:n - 1], ALU.subtract,
                               ["ta", "tb"], ["Hprev"])
                            tt("dve", Hprev[:, 1, Ps, j0 + 1:j0 + n], ta[:, 1, :, 0:n - 1], tb[:, 1, :, 0:n - 1], ALU.add,
                               ["ta", "tb"], ["Hprev"])
                        cp("dve", Hprev[:, :, Ps, j0], init[:, :, Ps], [ikey], ["Hprev"])
                        tt("dve", fin_out[:, 0, Ps], ta[:, 0, :, fin_j], tb[:, 0, :, fin_j], ALU.subtract, ["ta", "tb"], [fkey])
                        tt("dve", fin_out[:, 1, Ps], ta[:, 1, :, fin_j], tb[:, 1, :, fin_j], ALU.add, ["ta", "tb"], [fkey])
                        yield

                def big_scan():
                    n = NJ
                    tcv = rsd[:, :].rearrange("p (a b c) -> p a b c", a=2, b=4)
                    tdv = ot[:, :].rearrange("p (a b c) -> p a b c", a=2, b=4)
                    cp("dve", Hprev[:, :, :, 0], carry[:, :, :], ["carry"], ["Hprev"])
                    for ri in range(2):
                        row = c_sb[:, ri, :, :].rearrange("p a b -> p (a b)")
                        T.op("dve", lambda e, row=row: e.tensor_tensor_scan(
                            out=row, data0=Rt[:, :, :].rearrange("p a b -> p (a b)"), data1=row,
                            initial=0.0, op0=ALU.mult, op1=ALU.add), ["c_sb", "Atab"], ["c_sb"])
                        yield
                    for ph in range(4):
                        Ps = slice(4 * ph, 4 * ph + 4)
                        cv = c_sb[:, :, Ps, :]
                        cvsw = AP(c_sb, 16 * NJ + 4 * ph * NJ, [[CS, 128], [-16 * NJ, 2], [NJ, 4], [1, n]])
                        cosb = AP(cosT, 4 * ph * NJ, [[16 * NJ, 128], [0, 2], [NJ, 4], [1, n]])
                        sinb = AP(sinT, 4 * ph * NJ, [[16 * NJ, 128], [0, 2], [NJ, 4], [1, n]])
                        tt("dve", tcv, cv, cosb, ALU.mult, ["c_sb", "Atab"], ["rsd"])
                        tt("dve", tdv, cvsw, sinb, ALU.mult, ["c_sb", "Atab"], ["ot"])
                        tt("dve", Hprev[:, 0, Ps, 1:n], tcv[:, 0, :, 0:n - 1], tdv[:, 0, :, 0:n - 1], ALU.subtract,
                           ["rsd", "ot"], ["Hprev"])
                        tt("dve", Hprev[:, 1, Ps, 1:n], tcv[:, 1, :, 0:n - 1], tdv[:, 1, :, 0:n - 1], ALU.add,
                           ["rsd", "ot"], ["Hprev"])
                        tt("dve", carry[:, 0, Ps], tcv[:, 0, :, n - 1], tdv[:, 0, :, n - 1], ALU.subtract, ["rsd", "ot"], ["carry"])
                        tt("dve", carry[:, 1, Ps], tcv[:, 1, :, n - 1], tdv[:, 1, :, n - 1], ALU.add, ["rsd", "ot"], ["carry"])
                        yield

                if sample:
                    for s_ in range(NSEQ_S):
                        yield from seg_scan(s_ * 16, 16, h0s[:, s_, :, :], "h0s", TS // L - 1, hfin[:, s_, :, :], "hfin")
                    for s_ in range(NSEQ_S):
                        T.dma(AP(hre_s.tensor, s_ * 2048, [[1, 128], [128, 16]]), hfin[:, s_, 0, :], f"st_hs{s_}", reads=["hfin"])
                        T.dma(AP(him_s.tensor, s_ * 2048, [[1, 128], [128, 16]]), hfin[:, s_, 1, :], f"st_hs{s_}b", reads=["hfin"])
                else:
                    yield from big_scan()
                    if blk == NBLK_SEQ - 1:
                        T.dma(AP(hre_p.tensor, seq * 2048, [[1, 128], [128, 16]]), carry[:, 0, :], "st_hp", reads=["carry"])
                        T.dma(AP(him_p.tensor, seq * 2048, [[1, 128], [128, 16]]), carry[:, 1, :], "st_hpb", reads=["carry"])


            def gen_ret_tr():
                for (src, skey, dst, key, eng) in ((qrot, f"qrot{pb}", qT, "qT", "act"), (krot, f"krot{pb}", kT, "kT", "dve")):
                    bk = bank()
                    for h in range(4):
                        for t_ in range(NT):
                            tr(psb[bk][:, h * NB + t_ * 128: h * NB + (t_ + 1) * 128], src[:, t_, h * 128:(h + 1) * 128],
                               [skey, "ident_b"], [("ps", bk)])
                    cp(eng, dst[:, :, :].rearrange("p h n -> p (h n)"), psb[bk][:, :], [("ps", bk)], [key])
                    rel(bk)
                    yield
                tt("dve", qdT[:, :, :].rearrange("p h (c l) -> p h c l", l=64),
                   qT[:, :, :].rearrange("p h (c l) -> p h c l", l=64),
                   AP(qdec, cf * 256, [[512, 128], [64, 4], [0, NC], [1, 64]]), ALU.mult, ["qT", "qdec"], ["qdT"])


            def gen_ret():
                if DBG.get('stop', 99) <= 5:
                    return
                if DBG.get('stop', 99) <= 6:
                    return
                if (not sample) and blk == 0:
                    par0 = chunk_ctr[0] % 2
                    memset("pool", Smaster[:], 0.0, ["Smaster"])
                    memset("pool", Sprev[par0][:], 0.0, [f"Sprev{par0}"])
                bo = [bank(), bank()]
                chunk_par = []
                for c in range(NC):
                    t_ = c // 2
                    base = 64 * (c % 2)
                    par = chunk_ctr[0] % 2
                    chunk_ctr[0] += 1
                    chunk_par.append(par)
                    if sample:
                        T.dma(Smaster[:, :, :], AP(sret.tensor, c * 4 * HD * HD, [[HD, 128], [HD * HD, 4], [1, HD]]),
                              "ld_S", writes=["Smaster"])
                        cp("act", Sprev[par][:], Smaster[:], ["Smaster"], [f"Sprev{par}"])
                    bs = bank()
                    for h in range(4):
                        mm(ps[bs][base:base + 64, h * 64:(h + 1) * 64], kT[:, h, c * 64:(c + 1) * 64], qT[:, h, c * 64:(c + 1) * 64],
                           True, True, ["kT", "qT"], [("ps", bs)])
                    sT = sTb[c % 2]
                    yield
                    tt("dve", sT[base:base + 64, :], ps[bs][base:base + 64, 0:256], maskT[base:base + 64, cf, :], ALU.mult,
                       [("ps", bs), "maskT"], [f"sT{c % 2}"])
                    rel(bs)
                    bkv = bank()
                    for h in range(4):
                        mm(ps[bkv][:, h * 128:(h + 1) * 128], ktd[base:base + 64, t_, h * 128:(h + 1) * 128],
                           vtok[base:base + 64, t_, h * 128:(h + 1) * 128], True, True, [f"ktd{pb}", f"vtok{pb}"], [("ps", bkv)])
                    yield
                    for h in range(4):
                        ob = bo[h // 2]
                        oc = (h % 2) * NB + c * 64
                        mm(ps[ob][:, oc:oc + 64], vtok[:, t_, h * 128:(h + 1) * 128],
                           sT[:, h * 64:(h + 1) * 64], True, False, [f"vtok{pb}", f"sT{c % 2}"], [("ps", ob)])
                        mm(ps[ob][:, oc:oc + 64], Sprev[par][:, h, :], qdT[:, h, c * 64:(c + 1) * 64], False, True,
                           [f"Sprev{par}", "qdT"], [("ps", ob)])
                    yield
                    for h in range(4):
                        stt(Smaster[:, h, :], Smaster[:, h, :], sdec_t[:, cf, h:h + 1], ps[bkv][:, h * 128:(h + 1) * 128],
                            ALU.mult, ALU.add, [("ps", bkv), "Smaster", "sdec_t"], ["Smaster"])
                    rel(bkv)
                    yield
                    npar = 1 - par
                    if sample:
                        T.dma(AP(rs.tensor, c * 4 * HD * HD, [[HD, 128], [HD * HD, 4], [1, HD]]), Smaster[:, :, :], "st_rs",
                              reads=["Smaster"])
                    else:
                        cp("act", Sprev[npar][:], Smaster[:], ["Smaster"], [f"Sprev{npar}"])
                        if blk == NBLK_SEQ - 1 and c == NC - 1:
                            T.dma(AP(rp.tensor, seq * 4 * HD * HD, [[HD, 128], [HD * HD, 4], [1, HD]]), Smaster[:, :, :],
                                  "st_rp", reads=["Smaster"])

                if DBG.get('stop', 99) <= 7:
                    return
                bn = [bank(), bank()]
                for i2 in range(2):
                    act(osq[:, i2 * 512:(i2 + 1) * 512], ps[bo[i2]][:, :], AF.Square, [("ps", bo[i2])], ["osq"])
                    mm(ps[bn[i2]][:, :], ones_b[:, :], osq[:, i2 * 512:(i2 + 1) * 512], True, True, ["osq", "ones_b"],
                       [("ps", bn[i2])])
                    yield
                for i2 in range(2):
                    act(rsd[:, :], ps[bn[i2]][:, :], AF.Ln, [("ps", bn[i2])], ["rsd"], scale=1.0 / HD, bias=EPS)
                    rel(bn[i2])
                    act(rsd[:, :], rsd[:, :], AF.Exp, ["rsd"], ["rsd"], scale=-0.5)
                    tt("dve", ot[:, :], ps[bo[i2]][:, :], rsd[:, :], ALU.mult, [("ps", bo[i2]), "rsd"], ["ot"])
                    rel(bo[i2])
                    for hh in range(2):
                        h = 2 * i2 + hh
                        stt(mixT[:, h, :], ot[:, hh * NB:(hh + 1) * NB], rng_[:, h:h + 1], sgret[:, h, :], ALU.mult, ALU.mult,
                            ["ot", "rng", f"sgret{pb}"], [f"mixT{h}"])
                    yield


            yield from gen_scan()
            yield from gen_ret_tr()
            yield "SPLIT"
            yield from gen_ret()

            if DBG.get('stop', 99) <= 8:
                return
            for t2 in range(2):
                bk = bank()
                for half in range(2):
                    t = 2 * t2 + half
                    yv = ps[bk][:, half * NB:(half + 1) * NB].rearrange("p (j s) -> p j s", s=L)
                    uv = uT[:, t, :].rearrange("p (j s) -> p j s", s=L)
                    for i in range(L):
                        for k in range(i + 1):
                            mm(yv[:, :, i], Kbd[:, t, k, :], uv[:, :, i - k], k == 0, False, ["Kbd", f"uT{pb}"], [("ps", bk)])
                        for q in range(4):
                            P = 4 * t + q
                            for ri in range(2):
                                last = (ri == 1)
                                mm(ps[bk][32 * q:32 * q + 32, half * NB:(half + 1) * NB].rearrange("p (j s) -> p j s", s=L)[:, :, i],
                                   Et[:, P, i, ri, :], Hprev[:, ri, P, :], False, last, ["Et", "Hprev"], [("ps", bk)],
                                   tile_position=(0, 32 * q))
                        yield
                for half in range(2):
                    t = 2 * t2 + half
                    ysrc = ps[bk][:, half * NB:(half + 1) * NB]
                    act(zT[:, t, :], ysrc, AF.Gelu_apprx_tanh, [("ps", bk)], [f"zT{t}"])
                    if half == 1:
                        rel(bk)
                    yield

            if DBG.get('stop', 99) <= 9:
                return
            for m2 in range(2):
                bk = bank()
                for half in range(2):
                    m = 2 * m2 + half
                    for kt in range(4):
                        mm(ps[bk][:, half * NB:(half + 1) * NB], wglu_b[:, kt, m * 128:(m + 1) * 128], zT[:, kt, :],
                           kt == 0, kt == 3, ["wglu_b", f"zT{kt}"], [("ps", bk)])
                    yield
                for half in range(2):
                    m = 2 * m2 + half
                    gb = g2[half]
                    GK = "ot"
                    act(gb[:], ps[bk][:, half * NB:(half + 1) * NB], AF.Sigmoid, [("ps", bk), "bglu"], [GK],
                        bias=bglu[:, m:m + 1])
                    if half == 1:
                        rel(bk)
                    tt("dve", gb[:], gb[:], zT[:, m, :], ALU.mult, [GK, f"zT{m}"], [GK])
                    tt("dve", mixT[:, 4 + m, :], gb[:], sgssm[:, m, :], ALU.mult, [GK, f"sgssm{pb}"], [f"mixT{4 + m}"])
                    yield

            if DBG.get('stop', 99) <= 10:
                return
            for t_ in range(NT):
                xi = cfg["xslots"][t_]
                XK = f"xslot{xi}"
                for half in range(2):
                    bk = bank()
                    for kt in range(8):
                        mm(ps[bk][:, :], mixT[:, kt, t_ * 128:(t_ + 1) * 128], wout_b[:, kt, half * 512:(half + 1) * 512],
                           kt == 0, kt == 7, [f"mixT{kt}", "wout_b"], [("ps", bk)])
                    tt("dve", xslot[xi][:, half * 512:(half + 1) * 512], ps[bk][:, :], xslot[xi][:, half * 512:(half + 1) * 512],
                       ALU.add, [("ps", bk), XK], [XK])
                    rel(bk)
                    yield
                c_ = 4 + t_
                act(osq[:, :], xslot[xi][:], AF.Square, [XK], ["osq", "ssq"], accum_out=ssq[:, c_:c_ + 1])
                act(ssq[:, c_:c_ + 1], ssq[:, c_:c_ + 1], AF.Ln, ["ssq"], ["ssq"], scale=1.0 / D, bias=EPS)
                act(rstd[:, c_:c_ + 1], ssq[:, c_:c_ + 1], AF.Exp, ["ssq"], ["rstd"], scale=-0.5)
                stt(xslot[xi][:], xslot[xi][:], rstd[:, c_:c_ + 1], gfin[:], ALU.mult, ALU.mult, [XK, "rstd", "gfin"], [XK])
                if sample:
                    for half in range(2):
                        s_ = 2 * t_ + half
                        T.dma(ys[s_ * TS:(s_ + 1) * TS, :], xslot[xi][64 * half:64 * half + TS, :], f"st_y{xi}", reads=[XK])
                else:
                    T.dma(yp[row0 + t_ * 128: row0 + (t_ + 1) * 128, :], xslot[xi][:], f"st_y{xi}", reads=[XK])
                nxt2 = cfg.get("next2")
                if nxt2 is not None:
                    nxt2.setdefault("xslots", [None] * NT)[t_] = xi
                    issue_x_load(nxt2, t_, xi)
                yield


        blocks = []
        if DBG.get("sample", True):
            blocks.append(dict(sample=True))
        for seq in range(NSEQ_P):
            for blk in range(DBG.get("nblk", NBLK_SEQ)):
                blocks.append(dict(sample=False, seq=seq, blk=blk))
        for i, c_ in enumerate(blocks):
            if i + 2 < len(blocks):
                c_["next2"] = blocks[i + 2]
        for i, c_ in enumerate(blocks[:2]):
            c_["xslots"] = [2 * i, 2 * i + 1]
            for t_ in range(NT):
                issue_x_load(c_, t_, 2 * i + t_)
            issue_rot_load(c_, i)

        def drain(g):
            n = 0
            for _ in g:
                n += 1
            return n

        if DBG.get("nopipe"):
            for i, c_ in enumerate(blocks):
                drain(phaseA(c_, i % 2))
                drain(phaseB(c_, i % 2))
        else:
            est = {"a1": 39.0, "a2": 9.0, "b1": 12.0, "b2": 50.0}
            drain(phaseA(blocks[0], 0))
            pre_a = {}
            for i in range(1, len(blocks) + 1):
                gb = phaseB(blocks[i - 1], (i - 1) % 2)
                if i in pre_a:
                    ga = pre_a.pop(i)
                else:
                    ga = phaseA(blocks[i], i % 2) if i < len(blocks) else iter(())
                for stage in (1, 2):
                    if stage == 2 and i < len(blocks) and not blocks[i]["sample"] and not DBG.get("nofront"):
                        ga = s5_front(blocks[i], i % 2)
                    na, nb_ = est[f"a{stage}"], est[f"b{stage}"]
                    ca = cb = 0
                    da = db = False
                    while not (da and db):
                        if db or (not da and ca / na <= cb / nb_):
                            try:
                                r = next(ga)
                                if r == "SPLIT" and stage == 1:
                                    da = True
                                else:
                                    ca += 1
                            except StopIteration:
                                da = True
                        else:
                            try:
                                r = next(gb)
                                if r == "SPLIT" and stage == 1:
                                    db = True
                                else:
                                    cb += 1
                            except StopIteration:
                                db = True
                    if ca > 0:
                        est[f"a{stage}"] = float(ca)
                    if cb > 0:
                        est[f"b{stage}"] = float(cb)
                if i + 1 < len(blocks) and not DBG.get("nopre"):
                    gn = phaseA(blocks[i + 1], (i + 1) % 2)
                    for _ in range(NT):
                        next(gn)
                    pre_a[i + 1] = gn
        T.finish("sp")
        build_nc.stats = (T.n_ops, T.n_waits, len(T.sems))
    return nc


def _constants():
    c = {}
    c["cident"] = np.eye(128, dtype=np.float32)
    half = 64
    inv = 10000.0 ** (-np.arange(half, dtype=np.float64) / half)

    def rot_tab(pos):
        ang = pos[:, None].astype(np.float64) * inv[None, :]
        cos, sin = np.cos(ang), np.sin(ang)
        return np.concatenate([cos, -sin, sin], axis=1).astype(np.float32)

    c["crot_p"] = rot_tab(np.arange(SEQ))
    rs_ = np.zeros((128, 192), np.float32)
    tab = rot_tab(PAST + np.arange(TS))
    rs_[0:TS] = tab
    rs_[64:64 + TS] = tab
    c["crot_s"] = rs_
    gam = np.array([1.0 - 2.0 ** (-5.0 - h) for h in range(4)], dtype=np.float64)
    scale = HD ** -0.5
    cmask = np.zeros((2, 128, 4, 64), np.float64)
    cq = np.zeros((2, 4, 64), np.float64)
    ck = np.zeros((2, 128, 4), np.float64)
    for cfg, blk in ((0, 64), (1, TS)):
        for h in range(4):
            for m in range(blk):
                for l in range(m, blk):
                    cmask[cfg, m, h, l] = gam[h] ** (l - m) * scale
                    cmask[cfg, 64 + m, h, l] = gam[h] ** (l - m) * scale
            for l in range(blk):
                cq[cfg, h, l] = gam[h] ** (l + 1)
                ck[cfg, l, h] = gam[h] ** (blk - 1 - l) * scale
                ck[cfg, 64 + l, h] = gam[h] ** (blk - 1 - l) * scale
    c["cmask"] = cmask.reshape(256, 256).astype(np.float32)
    c["cqdec"] = cq.reshape(2, 256).astype(np.float32)
    c["ckdec"] = ck.reshape(256, 4).astype(np.float32)
    dup = np.zeros((64, 2, 2, 64), np.float32)
    for m in range(2):
        dup[np.arange(64), m, m, np.arange(64)] = 1.0
    c["cdup"] = dup.reshape(64, 256)
    par = np.zeros((2, 32, 16), np.float32)
    for m in range(2):
        par[m, m::2, :] = 1.0
    c["cpar"] = par.reshape(2, 512)
    wm = np.zeros((8, 16, 2), np.float32)
    for gl in range(8):
        wm[gl, :, gl % 2] = 1.0
    c["cwmask"] = wm.reshape(128, 2)
    c["csdec"] = np.array([[g ** 64 for g in gam] + [g ** TS for g in gam]], dtype=np.float32)
    return c


_NC_CACHE = {}


def kernel(x_prompt, x_sample, state_ret, state_ssm_re, state_ssm_im, norm_g, w_in, ret_norm_g,
           ssm_lambda_re, ssm_lambda_im, ssm_log_step, ssm_b_re, ssm_b_im, ssm_c_re, ssm_c_im,
           ssm_d, w_glu, b_glu, w_out, final_norm_g):
    f = lambda a: np.ascontiguousarray(np.asarray(a, dtype=np.float32))
    x_prompt = f(x_prompt); x_sample = f(x_sample)
    consts = _constants()
    shared = {
        "norm_g": f(norm_g).reshape(1, D), "w_in": f(w_in).reshape(D, 3072),
        "ret_norm_g": f(ret_norm_g).reshape(4, HD),
        "lre": f(ssm_lambda_re).reshape(32, 64), "lim": f(ssm_lambda_im).reshape(32, 64),
        "lstep": f(ssm_log_step).reshape(1, 32),
        "bre": f(ssm_b_re).reshape(2048, 16), "bim": f(ssm_b_im).reshape(2048, 16),
        "cre": f(ssm_c_re).reshape(512, 64), "cim": f(ssm_c_im).reshape(512, 64),
        "ssm_d": f(ssm_d).reshape(1, 512), "w_glu": f(w_glu).reshape(512, 512),
        "b_glu": f(b_glu).reshape(1, 512), "w_out": f(w_out).reshape(D, D),
        "fng": f(final_norm_g).reshape(1, D),
    }
    shared.update(consts)
    sr = f(state_ret)[0]; s_re = f(state_ssm_re)[0]; s_im = f(state_ssm_im)[0]
    in_maps = []
    for c in range(8):
        m = dict(shared)
        m["xp"] = x_prompt[c * NSEQ_P:(c + 1) * NSEQ_P].reshape(NSEQ_P * SEQ, D)
        m["xs"] = x_sample[c * NSEQ_S:(c + 1) * NSEQ_S].reshape(NSEQ_S * TS, D)
        m["sret"] = sr[c * NSEQ_S:(c + 1) * NSEQ_S].reshape(NSEQ_S * 4 * HD, HD)
        m["sre"] = s_re[c * NSEQ_S:(c + 1) * NSEQ_S].reshape(NSEQ_S, 2048)
        m["sim"] = s_im[c * NSEQ_S:(c + 1) * NSEQ_S].reshape(NSEQ_S, 2048)
        in_maps.append(m)
    if "nc" not in _NC_CACHE:
        _NC_CACHE["nc"] = build_nc()
    nc = _NC_CACHE["nc"]
    res = run_bass_kernel_spmd(nc, in_maps, core_ids=list(range(8)))
    R = res.results
    cat = lambda k: np.concatenate([np.asarray(r[k], dtype=np.float32) for r in R], axis=0)
    y_prompt = cat("yp").reshape(16, SEQ, D)
    y_sample = cat("ys").reshape(32, TS, D)
    ret_p = cat("rp").reshape(1, 16, 4, HD, HD)
    hre_p = cat("hre_p").reshape(1, 16, 32, 64)
    him_p = cat("him_p").reshape(1, 16, 32, 64)
    ret_s = cat("rs").reshape(1, 32, 4, HD, HD)
    hre_s = cat("hre_s").reshape(1, 32, 32, 64)
    him_s = cat("him_s").reshape(1, 32, 32, 64)
    return (y_prompt, y_sample, ret_p, hre_p, him_p, ret_s, hre_s, him_s)
```
